# Optimizing a Trainium2 kernel written in Bass

```python
import math
import jax, jax.numpy as jnp
from jax import lax
import numpy as np

D_MODEL = 1024
BATCH = 32
SEQ = 256
DEPTH = 4
DEC_BATCH = 4
DEC_SEQ = 4096
PAST_LEN = 256

GRID_W = 64
N_EVEN = (DEPTH + 1) // 2
N_ODD = DEPTH // 2
BRANCH = D_MODEL // 2
GLA_HEADS = 4
GLA_DK = BRANCH // (2 * GLA_HEADS)
GLA_DV = BRANCH // GLA_HEADS
GLA_RANK = 16
GLA_GATE_NORM = 16.0
GLA_CHUNK = 64
S5_GROUP = 16
S5_GROUPS = BRANCH // S5_GROUP
S5_STATE = 64
ATT_HEADS = 8
ATT_KV_HEADS = 2
ATT_GROUPS = ATT_HEADS // ATT_KV_HEADS
ATT_HD = BRANCH // ATT_HEADS
WINDOW = 128
BLK = 128
ROPE_BASE = 10000.0
CONV_W = 3
EPS = 1e-6
NEG = -1e30

E_SIZES = (GLA_HEADS * GLA_DK, GLA_HEADS * GLA_DK, BRANCH, GLA_RANK, GLA_RANK, BRANCH, BRANCH, BRANCH)
E_IN = sum(E_SIZES)
O_SIZES = (BRANCH, ATT_KV_HEADS * ATT_HD, ATT_KV_HEADS * ATT_HD, BRANCH, BRANCH, BRANCH, BRANCH, BRANCH)
O_IN = sum(O_SIZES)

kernel_name = "hybrid_gla_s5_swa_conv_diffusion_step"


def split_cols(x, sizes):
    out, start = [], 0
    for s in sizes:
        out.append(x[..., start:start + s])
        start += s
    return out


def rmsnorm(x, w):
    xf = x.astype(jnp.float32)
    y = xf * lax.rsqrt(jnp.mean(xf * xf, axis=-1, keepdims=True) + EPS) * w.astype(jnp.float32)
    return y.astype(x.dtype)


def adaln(cvec, w, b):
    return (jax.nn.silu(cvec) @ w + b)[:, None, :]


def modulate(x, norm_w, mod):
    shift, scale, gate = jnp.split(mod, 3, axis=-1)
    h = rmsnorm(x, norm_w) * (1.0 + scale) + shift
    return h, gate


def _rotate(x, ang):
    nf = ang.shape[-1]
    cos, sin = jnp.cos(ang), jnp.sin(ang)
    x1, x2 = x[..., :nf], x[..., nf:]
    return jnp.concatenate([x1 * cos - x2 * sin, x2 * cos + x1 * sin], axis=-1)


def rope_axial(x):
    L, d = x.shape[-2], x.shape[-1]
    rows = L // GRID_W
    row = jnp.repeat(jnp.arange(rows), GRID_W).astype(jnp.float32)
    col = jnp.tile(jnp.arange(GRID_W), rows).astype(jnp.float32)
    nf = d // 4
    freq = ROPE_BASE ** (-jnp.arange(nf, dtype=jnp.float32) / nf)
    xf = x.astype(jnp.float32)
    half = d // 2
    out = jnp.concatenate([_rotate(xf[..., :half], row[:, None] * freq),
                           _rotate(xf[..., half:], col[:, None] * freq)], axis=-1)
    return out.astype(x.dtype)


def gla_chunked(q, k, v, g, s0):
    Bsz, H, L, dk = q.shape
    dv = v.shape[-1]
    n = L // GLA_CHUNK

    def chunks(t):
        return t.astype(jnp.float32).reshape(Bsz, H, n, GLA_CHUNK, t.shape[-1]).transpose(2, 0, 1, 3, 4)

    causal = jnp.tril(jnp.ones((GLA_CHUNK, GLA_CHUNK), dtype=bool))[:, :, None]

    def step(S, inp):
        qc, kc, vc, gc = inp
        b = jnp.cumsum(gc, axis=2)
        o_inter = jnp.einsum('bhcd,bhde->bhce', qc * jnp.exp(b), S)
        diff = b[:, :, :, None, :] - b[:, :, None, :, :]
        decay = jnp.where(causal, jnp.exp(jnp.where(causal, diff, 0.0)), 0.0)
        att = jnp.einsum('bhid,bhjd,bhijd->bhij', qc, kc, decay)
        o = o_inter + jnp.einsum('bhij,bhje->bhie', att, vc)
        b_last = b[:, :, -1:, :]
        S_new = jnp.exp(b_last)[:, :, 0, :, None] * S + jnp.einsum('bhcd,bhce->bhde', kc * jnp.exp(b_last - b), vc)
        return S_new, o

    S, o = lax.scan(step, s0.astype(jnp.float32), (chunks(q), chunks(k), chunks(v), chunks(g)))
    o = o.transpose(1, 2, 0, 3, 4).reshape(Bsz, H, L, dv)
    return o, S


def gla_bidir(q, k, v, gf, gb, s0f, s0b):
    o_f, s_f = gla_chunked(q, k, v, gf, s0f)
    fl = lambda t: jnp.flip(t, axis=2)
    o_b, s_b = gla_chunked(fl(q), fl(k), fl(v), fl(gb), s0b)
    return o_f + fl(o_b), s_f, s_b


def s5_scan(u, lam_re, lam_im, log_dt, b_re, b_im, h0_re, h0_im):
    dt = jnp.exp(log_dt.astype(jnp.float32))[:, None]
    lam_re = lam_re.astype(jnp.float32)
    lam_im = lam_im.astype(jnp.float32)
    mag = jnp.exp(lam_re * dt)
    lb_re, lb_im = mag * jnp.cos(lam_im * dt), mag * jnp.sin(lam_im * dt)
    den = lam_re * lam_re + lam_im * lam_im
    nr, ni = lb_re - 1.0, lb_im
    cr = (nr * lam_re + ni * lam_im) / den
    ci = (ni * lam_re - nr * lam_im) / den
    b_re = b_re.astype(jnp.float32)
    b_im = b_im.astype(jnp.float32)
    bb_re = cr[..., None] * b_re - ci[..., None] * b_im
    bb_im = cr[..., None] * b_im + ci[..., None] * b_re
    bu_re = jnp.einsum('blgh,gph->blgp', u, bb_re)
    bu_im = jnp.einsum('blgh,gph->blgp', u, bb_im)
    h0_re = h0_re.astype(jnp.float32)
    h0_im = h0_im.astype(jnp.float32)
    bu_re = bu_re.at[:, 0].add(lb_re * h0_re - lb_im * h0_im)
    bu_im = bu_im.at[:, 0].add(lb_re * h0_im + lb_im * h0_re)
    a_re = jnp.broadcast_to(lb_re, bu_re.shape)
    a_im = jnp.broadcast_to(lb_im, bu_im.shape)

    def combine(e1, e2):
        ar1, ai1, br1, bi1 = e1
        ar2, ai2, br2, bi2 = e2
        return (ar2 * ar1 - ai2 * ai1, ar2 * ai1 + ai2 * ar1,
                ar2 * br1 - ai2 * bi1 + br2, ar2 * bi1 + ai2 * br1 + bi2)

    _, _, h_re, h_im = lax.associative_scan(combine, (a_re, a_im, bu_re, bu_im), axis=1)
    return h_re, h_im


def even_mixer(h, w_in, w_out, gla_w2, gla_b2, gla_onorm, lam_re, lam_im, log_dt, b_re, b_im,
               c_re, c_im, s5_d, w_glu, b_glu, gla_s0, s5_h0_re, s5_h0_im):
    Bsz, L, _ = h.shape
    q, k, v, lf, lb, z_gla, u, z_s5 = split_cols(h @ w_in, E_SIZES)
    heads = lambda t, d: t.reshape(Bsz, L, GLA_HEADS, d).transpose(0, 2, 1, 3)
    q = heads(q, GLA_DK) * (GLA_DK ** -0.5)
    k = heads(k, GLA_DK)
    v = heads(v, GLA_DV)
    gate = lambda lr, d: heads(jax.nn.log_sigmoid(lr.astype(jnp.float32) @ gla_w2[d].astype(jnp.float32)
                                                  + gla_b2[d].astype(jnp.float32)) / GLA_GATE_NORM, GLA_DK)
    o, s_f, s_b = gla_bidir(q, k, v, gate(lf, 0), gate(lb, 1), gla_s0[:, 0], gla_s0[:, 1])
    o = rmsnorm(o, gla_onorm).transpose(0, 2, 1, 3).reshape(Bsz, L, BRANCH)
    y_gla = o.astype(h.dtype) * jax.nn.silu(z_gla)

    uf = u.astype(jnp.float32)
    ug = uf.reshape(Bsz, L, S5_GROUPS, S5_GROUP)
    hf_re, hf_im = s5_scan(ug, lam_re[0], lam_im[0], log_dt[0], b_re, b_im, s5_h0_re[:, 0], s5_h0_im[:, 0])
    hb_re, hb_im = s5_scan(jnp.flip(ug, axis=1), lam_re[1], lam_im[1], log_dt[1], b_re, b_im,
                           s5_h0_re[:, 1], s5_h0_im[:, 1])
    last_re = jnp.stack([hf_re[:, -1], hb_re[:, -1]], axis=1)
    last_im = jnp.stack([hf_im[:, -1], hb_im[:, -1]], axis=1)
    h_re = hf_re + jnp.flip(hb_re, axis=1)
    h_im = hf_im + jnp.flip(hb_im, axis=1)
    y5 = (jnp.einsum('blgp,ghp->blgh', h_re, c_re.astype(jnp.float32))
          - jnp.einsum('blgp,ghp->blgh', h_im, c_im.astype(jnp.float32))).reshape(Bsz, L, BRANCH)
    y5 = y5 + s5_d.astype(jnp.float32) * uf
    z = jax.nn.gelu(y5)
    y5 = z * jax.nn.sigmoid(z @ w_glu.astype(jnp.float32) + b_glu.astype(jnp.float32))
    y_s5 = y5.astype(h.dtype) * jax.nn.silu(z_s5)

    out = jnp.concatenate([y_gla, y_s5], axis=-1) @ w_out
    return out, jnp.stack([s_f, s_b], axis=1), last_re, last_im


def _attend(qblk, key_sets, sink):
    qf = qblk.astype(jnp.float32) * (ATT_HD ** -0.5)
    scores = []
    for kk, vv, mask in key_sets:
        s = jnp.einsum('bkgqd,bksd->bkgqs', qf, kk.astype(jnp.float32))
        if mask is not None:
            s = jnp.where(mask, s, NEG)
        scores.append(s)
    s_sink = jnp.broadcast_to(sink.astype(jnp.float32).reshape(1, ATT_KV_HEADS, ATT_GROUPS, 1, 1),
                              scores[0].shape[:-1] + (1,))
    p = jax.nn.softmax(jnp.concatenate(scores + [s_sink], axis=-1), axis=-1)
    out, off = 0.0, 0
    for (kk, vv, _), s in zip(key_sets, scores):
        n = s.shape[-1]
        out = out + jnp.einsum('bkgqs,bksd->bkgqd', p[..., off:off + n], vv.astype(jnp.float32))
        off += n
    return out


def ctx_attention(q, k, v, sink):
    Bsz, Hk, G, L, d = q.shape
    nb = L // BLK
    qb = q.reshape(Bsz, Hk, G, nb, BLK, d).transpose(3, 0, 1, 2, 4, 5)
    o = lax.map(lambda qblk: _attend(qblk, [(k, v, None)], sink), qb)
    return o.transpose(1, 2, 3, 0, 4, 5).reshape(Bsz, Hk, G, L, d)


def latent_attention(q, k, v, k_ctx, v_ctx, sink):
    Bsz, Hk, G, L, d = q.shape
    nb = L // BLK
    kp = jnp.pad(k, ((0, 0), (0, 0), (BLK, BLK), (0, 0)))
    vp = jnp.pad(v, ((0, 0), (0, 0), (BLK, BLK), (0, 0)))
    qi = jnp.arange(BLK)[:, None]
    kj = jnp.arange(3 * BLK)[None, :]
    band = jnp.abs(kj - BLK - qi) <= WINDOW

    def one(n):
        start = n * BLK
        qblk = lax.dynamic_slice_in_dim(q, start, BLK, axis=3)
        kw = lax.dynamic_slice_in_dim(kp, start, 3 * BLK, axis=2)
        vw = lax.dynamic_slice_in_dim(vp, start, 3 * BLK, axis=2)
        kpos = start - BLK + kj
        mask = band & (kpos >= 0) & (kpos < L)
        return _attend(qblk, [(kw, vw, mask), (k_ctx, v_ctx, None)], sink)

    o = lax.map(one, jnp.arange(nb))
    return o.transpose(1, 2, 3, 0, 4, 5).reshape(Bsz, Hk, G, L, d)


def short_conv(x, w, b):
    xp = jnp.pad(x, ((0, 0), (1, 1), (0, 0)))
    return xp[:, :-2] * w[0] + xp[:, 1:-1] * w[1] + xp[:, 2:] * w[2] + b


def odd_mixer(h, w_in, w_out, q_norm_w, k_norm_w, sink, conv_w, conv_b, k_ctx=None, v_ctx=None):
    Bsz, L, _ = h.shape
    q, k, v, z_att, xc, bg, cg, z_conv = split_cols(h @ w_in, O_SIZES)
    q = rmsnorm(q.reshape(Bsz, L, ATT_KV_HEADS, ATT_GROUPS, ATT_HD), q_norm_w).transpose(0, 2, 3, 1, 4)
    k = rmsnorm(k.reshape(Bsz, L, ATT_KV_HEADS, ATT_HD), k_norm_w).transpose(0, 2, 1, 3)
    v = v.reshape(Bsz, L, ATT_KV_HEADS, ATT_HD).transpose(0, 2, 1, 3)
    if k_ctx is None:
        o = ctx_attention(q, k, v, sink)
    else:
        o = latent_attention(rope_axial(q), rope_axial(k), v, k_ctx, v_ctx, sink)
    o = o.transpose(0, 3, 1, 2, 4).reshape(Bsz, L, BRANCH).astype(h.dtype)
    y_att = o * jax.nn.silu(z_att)
    y_conv = bg * short_conv(cg * xc, conv_w, conv_b) * jax.nn.silu(z_conv)
    out = jnp.concatenate([y_att, y_conv], axis=-1) @ w_out
    return out, k, v


def setup_inputs(seed: int = 0) -> dict:
    key = jax.random.key(seed)
    ks = iter(jax.random.split(key, 48))
    nrm = lambda shape, s: jax.random.normal(next(ks), shape, jnp.float32) * s
    lam_im = jnp.pi * jnp.arange(S5_STATE, dtype=jnp.float32)
    return {
        "x_prompt": nrm((BATCH, SEQ, D_MODEL), 1.0),
        "x_sample": nrm((DEC_BATCH, DEC_SEQ, D_MODEL), 1.0),
        "c": nrm((DEC_BATCH, D_MODEL), 1.0),
        "state_gla": nrm((DEC_BATCH, N_EVEN, 2, GLA_HEADS, GLA_DK, GLA_DV), 1.0),
        "state_s5_re": nrm((DEC_BATCH, N_EVEN, 2, S5_GROUPS, S5_STATE), 1.0),
        "state_s5_im": nrm((DEC_BATCH, N_EVEN, 2, S5_GROUPS, S5_STATE), 1.0),
        "cache_k": nrm((DEC_BATCH, N_ODD, ATT_KV_HEADS, PAST_LEN, ATT_HD), 1.0),
        "cache_v": nrm((DEC_BATCH, N_ODD, ATT_KV_HEADS, PAST_LEN, ATT_HD), 1.0),
        "c_ctx": nrm((D_MODEL,), 1.0),
        "norm_w": 1.0 + nrm((DEPTH, D_MODEL), 0.02),
        "w_ada": nrm((DEPTH, D_MODEL, 3 * D_MODEL), 0.5 * D_MODEL ** -0.5),
        "b_ada": nrm((DEPTH, 3 * D_MODEL), 0.01),
        "w_in_e": nrm((N_EVEN, D_MODEL, E_IN), D_MODEL ** -0.5),
        "w_out_e": nrm((N_EVEN, 2 * BRANCH, D_MODEL), (2 * BRANCH) ** -0.5),
        "gla_w2": nrm((N_EVEN, 2, GLA_RANK, GLA_HEADS * GLA_DK), GLA_RANK ** -0.5),
        "gla_b2": nrm((N_EVEN, 2, GLA_HEADS * GLA_DK), 0.01),
        "gla_onorm": 1.0 + nrm((N_EVEN, GLA_DV), 0.02),
        "s5_lam_re": -0.5 + nrm((N_EVEN, 2, S5_GROUPS, S5_STATE), 0.01),
        "s5_lam_im": lam_im + nrm((N_EVEN, 2, S5_GROUPS, S5_STATE), 0.01),
        "s5_log_dt": jax.random.uniform(next(ks), (N_EVEN, 2, S5_GROUPS), jnp.float32,
                                        minval=math.log(1e-3), maxval=math.log(1e-1)),
        "s5_b_re": nrm((N_EVEN, S5_GROUPS, S5_STATE, S5_GROUP), (2 * S5_GROUP) ** -0.5),
        "s5_b_im": nrm((N_EVEN, S5_GROUPS, S5_STATE, S5_GROUP), (2 * S5_GROUP) ** -0.5),
        "s5_c_re": nrm((N_EVEN, S5_GROUPS, S5_GROUP, S5_STATE), (2 * S5_STATE) ** -0.5),
        "s5_c_im": nrm((N_EVEN, S5_GROUPS, S5_GROUP, S5_STATE), (2 * S5_STATE) ** -0.5),
        "s5_d": nrm((N_EVEN, BRANCH), 0.5),
        "s5_w_glu": nrm((N_EVEN, BRANCH, BRANCH), BRANCH ** -0.5),
        "s5_b_glu": nrm((N_EVEN, BRANCH), 0.01),
        "w_in_o": nrm((N_ODD, D_MODEL, O_IN), D_MODEL ** -0.5),
        "w_out_o": nrm((N_ODD, 2 * BRANCH, D_MODEL), (2 * BRANCH) ** -0.5),
        "q_norm_w": 1.0 + nrm((N_ODD, ATT_HD), 0.02),
        "k_norm_w": 1.0 + nrm((N_ODD, ATT_HD), 0.02),
        "sink": nrm((N_ODD, ATT_HEADS), 1.0),
        "conv_w": nrm((N_ODD, CONV_W, BRANCH), CONV_W ** -0.5),
        "conv_b": nrm((N_ODD, BRANCH), 0.01),
    }


def reference(x_prompt, x_sample, c, state_gla, state_s5_re, state_s5_im, cache_k, cache_v,
              c_ctx, norm_w, w_ada, b_ada, w_in_e, w_out_e, gla_w2, gla_b2, gla_onorm,
              s5_lam_re, s5_lam_im, s5_log_dt, s5_b_re, s5_b_im, s5_c_re, s5_c_im, s5_d,
              s5_w_glu, s5_b_glu, w_in_o, w_out_o, q_norm_w, k_norm_w, sink, conv_w, conv_b):
    xp, xs = x_prompt, x_sample
    bp = x_prompt.shape[0]
    new_gla, new_s5_re, new_s5_im, new_k, new_v = [], [], [], [], []
    for l in range(DEPTH):
        e = l // 2
        mod_ctx = adaln(c_ctx[None, :], w_ada[l], b_ada[l])
        mod_lat = adaln(c, w_ada[l], b_ada[l])
        hp, gp = modulate(xp, norm_w[l], mod_ctx)
        hs, gs = modulate(xs, norm_w[l], mod_lat)
        if l % 2 == 0:
            ep = (w_in_e[e], w_out_e[e], gla_w2[e], gla_b2[e], gla_onorm[e], s5_lam_re[e], s5_lam_im[e],
                  s5_log_dt[e], s5_b_re[e], s5_b_im[e], s5_c_re[e], s5_c_im[e], s5_d[e], s5_w_glu[e], s5_b_glu[e])
            z_gla = jnp.zeros((bp, 2, GLA_HEADS, GLA_DK, GLA_DV), jnp.float32)
            z_s5 = jnp.zeros((bp, 2, S5_GROUPS, S5_STATE), jnp.float32)
            op, sg, sr, si = even_mixer(hp, *ep, z_gla, z_s5, z_s5)
            os_, _, _, _ = even_mixer(hs, *ep, state_gla[:, e], state_s5_re[:, e], state_s5_im[:, e])
            new_gla.append(sg)
            new_s5_re.append(sr)
            new_s5_im.append(si)
        else:
            op_params = (w_in_o[e], w_out_o[e], q_norm_w[e], k_norm_w[e], sink[e], conv_w[e], conv_b[e])
            op, kc, vc = odd_mixer(hp, *op_params)
            os_, _, _ = odd_mixer(hs, *op_params, k_ctx=cache_k[:, e], v_ctx=cache_v[:, e])
            new_k.append(kc)
            new_v.append(vc)
        xp = xp + gp * op
        xs = xs + gs * os_
    new_state_gla = jnp.stack(new_gla, axis=1)
    new_state_s5_re = jnp.stack(new_s5_re, axis=1)
    new_state_s5_im = jnp.stack(new_s5_im, axis=1)
    new_cache_k = jnp.stack(new_k, axis=1)
    new_cache_v = jnp.stack(new_v, axis=1)
    return (xp, xs, new_state_gla, new_state_s5_re, new_state_s5_im, new_cache_k, new_cache_v)
```

```python
import math
import os
import numpy as np
import concourse.bass as bass
import concourse.mybir as mybir
from concourse.bass_utils import run_bass_kernel_spmd
from contextlib import ExitStack

F32 = mybir.dt.float32
BF16 = mybir.dt.bfloat16
I32 = mybir.dt.int32
ALU = mybir.AluOpType
AF = mybir.ActivationFunctionType
AX = mybir.AxisListType

D = 1024
EIN = 2592
OIN = 3328
EPS = 1e-6
ENGS = ("pe", "act", "dve", "pool", "sp")
SEM_LIMIT = 30000
N_DMA_SEMS = 12


class Buf:
    __slots__ = ("w", "r")

    def __init__(self):
        self.w = None
        self.r = {}


class Sched:
    def __init__(self, nc, same_engine_sync=True):
        self.nc = nc
        self.q = {e: [] for e in ENGS}
        self.epoch = {e: 0 for e in ENGS}
        self.cnt = {}
        self.seen = {e: {} for e in ENGS}
        self.same = same_engine_sync
        self.nosync = set(os.environ.get("KNOSYNC", "").split(","))
        self.semkeys = []
        for e in ENGS:
            self._newkey((e, 0))
        self.dma_pool = {e: [] for e in ENGS}
        self.dma_rr = {e: 0 for e in ENGS}
        self.n_ops = 0

    def _newkey(self, k):
        self.cnt[k] = 0
        self.semkeys.append(k)

    def _engkey(self, e):
        k = (e, self.epoch[e])
        if self.cnt[k] >= SEM_LIMIT:
            self.epoch[e] += 1
            k = (e, self.epoch[e])
            self._newkey(k)
        return k

    def _need(self, eng, waits, tok, is_dma=False):
        if tok is None:
            return
        k, v = tok
        if (not is_dma) and k[0] == eng and (eng == "pe" or eng in self.nosync):
            return
        if self.seen[eng].get(k, 0) >= v:
            return
        if waits.get(k, 0) < v:
            waits[k] = v

    def _deps(self, eng, reads, writes, is_dma):
        waits = {}
        for b in reads:
            self._need(eng, waits, b.w, is_dma)
        for b in writes:
            self._need(eng, waits, b.w, is_dma)
            for k, v in b.r.items():
                self._need(eng, waits, (k, v), is_dma)
        return waits

    def op(self, eng, fn, reads=(), writes=(), extra=None):
        if extra is None:
            waits = self._deps(eng, reads, writes, False)
        else:
            sv = self.nosync
            self.nosync = set(sv) | {eng}
            waits = self._deps(eng, reads, writes, False)
            self.nosync = sv
            for tok in extra:
                if tok is not None:
                    self._need(eng, waits, tok, True)
        for k, v in waits.items():
            self.seen[eng][k] = v
        key = self._engkey(eng)
        self.cnt[key] += 1
        tok = (key, self.cnt[key])
        for b in reads:
            b.r[key] = tok[1]
        for b in writes:
            b.w = tok
            b.r = {}
        self.q[eng].append((fn, list(waits.items()), key, 1))
        self.n_ops += 1
        return tok

    def dma(self, fn, reads=(), writes=(), eng="sp"):
        pool = self.dma_pool[eng]
        if len(pool) < N_DMA_SEMS:
            key = ("dma", eng, len(pool), 0)
            self._newkey(key)
            pool.append(key)
        else:
            idx = self.dma_rr[eng] % N_DMA_SEMS
            self.dma_rr[eng] += 1
            key = pool[idx]
            if self.cnt[key] >= SEM_LIMIT:
                key = ("dma", eng, idx, key[3] + 1)
                self._newkey(key)
                pool[idx] = key
        waits = self._deps(eng, reads, writes, True)
        if self.cnt[key] > 0:
            self._need(eng, waits, (key, self.cnt[key]), True)
        for k, v in waits.items():
            self.seen[eng][k] = v
        self.cnt[key] += 16
        tok = (key, self.cnt[key])
        for b in reads:
            b.r[key] = tok[1]
        for b in writes:
            b.w = tok
            b.r = {}
        self.q[eng].append((fn, list(waits.items()), key, 16))
        self.n_ops += 1

    def barrier(self):
        for eng in ENGS:
            waits = {}
            for k, v in self.cnt.items():
                if v > 0 and self.seen[eng].get(k, 0) < v:
                    waits[k] = v
                    self.seen[eng][k] = v
            self.q[eng].append((None, list(waits.items()), None, 0))

    def finish_waits(self, bufs, eng="sp"):
        waits = {}
        for b in bufs:
            self._need(eng, waits, b.w, True)
        self.q[eng].append((None, list(waits.items()), None, 0))

    def emit(self):
        nc = self.nc
        with ExitStack() as st:
            sems = {}
            for i, k in enumerate(self.semkeys):
                sems[k] = st.enter_context(nc.semaphore("s%d" % i))
            block = st.enter_context(nc.Block())

            def runner(ename):
                def run(e):
                    for fn, waits, key, inc in self.q[ename]:
                        for k, v in waits:
                            e.wait_ge(sems[k], v)
                        if fn is not None:
                            fn(e).then_inc(sems[key], inc)
                return run

            block.tensor(runner("pe"))
            block.scalar(runner("act"))
            block.vector(runner("dve"))
            block.gpsimd(runner("pool"))
            block.sync(runner("sp"))


class TL:
    def __init__(self, t):
        self.t = t
        self.b = Buf()

    def __getitem__(self, k):
        return self.t[k]


class View:
    def __init__(self, ap):
        self.ap = ap
        self.b = Buf()

    def __getitem__(self, k):
        return self.ap[k]


class DT:
    def __init__(self, ap, nb=1):
        self.ap = ap
        self.bs = [Buf() for _ in range(nb)]
        self.b = self.bs[0]


C_ID, C_TIF, C_TRF, C_TIB, C_TRB, C_MF, C_MB, C_ONE, C_BAND = [128 * i for i in range(9)]
C_BLK = C_BAND + 384
C_EPS = C_BLK + 256
NCST = C_EPS + 2


class _Stop(Exception):
    pass


def build(T, stop=None):
    import os
    stop = stop or os.environ.get("KSTOP")
    NT = T // 128
    NS = T // 256
    NB = T // 512
    nc = bass.Bass("TRN2", target_bir_lowering=False)
    S = Sched(nc, same_engine_sync=bool(int(os.environ.get("KSAME", "0"))))
    st = ExitStack()

    def din(name, shape, dt=F32):
        return DT(nc.dram_tensor(name, list(shape), dt, kind="ExternalInput").ap())

    def dout(name, shape, dt=F32):
        return DT(nc.dram_tensor(name, list(shape), dt, kind="ExternalOutput").ap())

    dbg = bool(os.environ.get("KDBG"))

    def dscr(name, shape, dt=F32, nb=1):
        return DT(nc.dram_tensor(name, list(shape), dt, kind="ExternalOutput" if dbg else "Internal").ap(), nb)

    ARENA_WORDS = 34000
    G_BASE = 29000
    arena_t = st.enter_context(nc.sbuf_tensor("arena", [128, ARENA_WORDS], F32))
    goff = {"G": G_BASE}

    def sb(name, shape, dt=F32, grp=None):
        if grp is None:
            return TL(st.enter_context(nc.sbuf_tensor(name, list(shape), dt)))
        n = 1
        for d_ in shape[1:]:
            n *= d_
        words = n if dt in (F32, I32) else (n + 1) // 2
        off = goff.get(grp, 0)
        goff[grp] = off + words
        assert goff[grp] <= ARENA_WORDS, (grp, name, goff[grp])
        assert grp != "S" or goff[grp] <= G_BASE, (grp, name, goff[grp])
        ap = arena_t[:, off:off + words]
        if dt != F32:
            ap = ap.bitcast(dt)[:, :n]
        if len(shape) > 2:
            names = " ".join("d%d" % i for i in range(len(shape) - 1))
            kw = {"d%d" % i: shape[i + 1] for i in range(len(shape) - 1)}
            ap = ap.rearrange("p (%s) -> p %s" % (names, names), **kw)
        if shape[0] < 128:
            ap = ap[0:shape[0]]
        return View(ap)

    def ps(name, shape, dt=F32):
        return TL(st.enter_context(nc.psum_tensor(name, list(shape), dt)))

    def bl(xs):
        return [x.b if hasattr(x, "b") else x for x in xs]

    def mm(out, lhsT, rhs, start, stop, R, W):
        S.op("pe", lambda e: e.matmul(out, lhsT=lhsT, rhs=rhs, start=start, stop=stop), bl(R), bl(W))

    def tr(out, in_, ident, R, W):
        S.op("pe", lambda e: e.transpose(out, in_, ident), bl(R), bl(W))

    def act(out, in_, func, R, W, **kw):
        S.op("act", lambda e: e.activation(out=out, in_=in_, func=func, **kw), bl(R), bl(W))

    def tt(eng, out, a, b, op, R, W):
        S.op(eng, lambda e: e.tensor_tensor(out=out, in0=a, in1=b, op=op), bl(R), bl(W))

    def ts(eng, out, a, s1, op0, R, W, s2=None, op1=None):
        if op1 is None:
            S.op(eng, lambda e: e.tensor_scalar(out=out, in0=a, scalar1=s1, scalar2=None, op0=op0), bl(R), bl(W))
        else:
            S.op(eng, lambda e: e.tensor_scalar(out=out, in0=a, scalar1=s1, scalar2=s2, op0=op0, op1=op1), bl(R), bl(W))

    def stt(out, in0, scalar, in1, op0, op1, R, W, extra=None):
        return S.op("dve", lambda e: e.scalar_tensor_tensor(out=out, in0=in0, scalar=scalar, in1=in1, op0=op0, op1=op1),
                    bl(R), bl(W), extra=extra)

    def cp(eng, out, in_, R, W):
        if eng == "act":
            S.op("act", lambda e: e.copy(out=out, in_=in_), bl(R), bl(W))
        else:
            S.op(eng, lambda e: e.tensor_copy(out=out, in_=in_), bl(R), bl(W))

    def mset(eng, ap, val, W):
        S.op(eng, lambda e: e.memset(ap, val), [], bl(W))

    def dma(out, in_, R, W, eng="sp", slow=False):
        if slow:
            S.dma(lambda e: e.dma_start(out=out, in_=in_, allow_slow_non_contiguous=True), bl(R), bl(W), eng=eng)
        else:
            S.dma(lambda e: e.dma_start(out=out, in_=in_), bl(R), bl(W), eng=eng)

    x_in = din("x", [T, D])
    cvec = din("cvec", [128, 8])
    cst = din("cst", [128, NCST])
    NMETA = 3 + 4 * NT
    meta = din("meta", [128, NMETA])
    rope = din("rope", [T, 128])
    gla0 = din("gla0", [2, 2, 4, 64, 128])
    s5h0 = din("s5h0", [2, 2, 2, 128, 16])
    ckv = din("ckv", [2, 2, 256, 2, 64])
    norm_w = din("norm_w", [4, D])
    w_ada = din("w_ada", [4, D, 3 * D])
    b_ada = din("b_ada", [4, 3 * D])
    w_in_e = din("w_in_e", [2, D, EIN])
    w_out_e = din("w_out_e", [2, D, D])
    w2cat = din("w2cat", [2, 64, 512])
    onorm = din("onorm", [2, 128, 1])
    s5p = din("s5p", [2, 2, 3, 128, 16])
    s5b = din("s5b", [2, 2, 128, 16 * 128])
    s5c = din("s5c", [2, 2, 128, 16 * 32])
    s5d = din("s5d", [2, 128, 4])
    wglu = din("wglu", [2, 512, 512])
    bglu = din("bglu", [2, 128, 4])
    w_in_o = din("w_in_o", [2, D, OIN])
    w_out_o = din("w_out_o", [2, D, D])
    qkw = din("qkw", [2, 128])
    sinkv = din("sink", [2, 8])
    convw = din("convw", [2, 128, 16])

    y_out = dout("y", [T, D])
    ngla = dout("ngla", [2, 2, NS, 4, 64, 128])
    ns5 = dout("ns5", [2, 2, 2, NS * 16, 128])
    nkv = dout("nkv", [2, 2, T, 128])

    xs = [dscr("xs0", [T, D], nb=NT), dscr("xs1", [T, D], nb=NT)]
    qkT_s = dscr("qkT", [512, T], BF16, NT)
    kv_s = dscr("kvtm", [T, 768], BF16, NT)
    sp_s = dscr("sp", [T, 512], F32, NT)
    zgT_s = dscr("zgT", [512, T], BF16, NT)
    uT_s = dscr("uT", [512, T], BF16, NT)
    z5T_s = dscr("z5T", [512, T], BF16, NT)
    oF_s = dscr("oF", [512, T], F32, NT)
    y5T_s = dscr("y5T", [512, T], F32, 16)
    yT_s = dscr("yT", [1024, T], BF16, NT)
    qT_s = dscr("qTo", [512, T], BF16, NT)
    kT_s = dscr("kTo", [256, T], BF16, NT)
    v_s = dscr("vo", [T, 128], BF16, NT)
    sz_s = dscr("szo", [T, 512], F32, NT)
    u1T_s = dscr("u1T", [512, T], F32, NT)
    u2T_s = dscr("u2T", [512, T], F32, NT)

    cs = sb("cs", [128, NCST])
    csb = sb("csb", [128, NCST], BF16)
    mt = sb("mt", [128, NMETA])
    wbf = sb("wbf", [128, 8, OIN], BF16, grp="P")
    wob = sb("wob", [128, 8, D], BF16, grp="O")
    wst1 = sb("wst0", [128, OIN], grp="P")
    wst = [wst1, wst1]
    wstO = sb("wstO", [128, D], grp="O")
    wstS = sb("wstS", [128, 512], grp="SP")
    pbfA = sb("pbfA", [128, 512], BF16, grp="A")
    modbc = sb("modbc", [128, 3 * D])
    Abc = sb("Abc", [128, D])
    sc8 = sb("sc8", [128, 8])
    screp = sb("screp", [128, 8, 128])
    xts = [sb("xt0", [128, D]), sb("xt1", [128, D])]
    xt = xts[0]
    hb = sb("hb", [128, D], BF16, grp="P")
    hT = sb("hT", [128, 8, 128], BF16, grp="P")
    small = sb("small", [128, 16])
    proj = sb("proj", [128, OIN], grp="P")
    pbf = sb("pbf", [128, OIN], BF16, grp="P")
    trs = sb("trs", [128, 8, 128], BF16)
    trf = sb("trf", [128, 4, 128])
    w1 = sb("w1", [128, 1024])
    w2 = sb("w2", [128, 1024])
    w3 = sb("w3", [128, 1024])
    lrT = sb("lrT", [64, 128], BF16, grp="P")
    w2c = sb("w2c", [64, 512], BF16, grp="P")
    w2cf = sb("w2cf", [64, 512], grp="P")

    pA = ps("pA", [128, 512]); pB = ps("pB", [128, 512]); pC = ps("pC", [128, 512])
    pD = ps("pD", [128, 512]); pE = ps("pE", [128, 512]); pF = ps("pF", [128, 512])
    pT = ps("pT", [128, 1024], BF16)
    pTf = ps("pTf", [128, 512])

    class _PT32:
        def __init__(self):
            self.ap = pT[:, :].bitcast(F32)
            self.b = pT.b

        def __getitem__(self, k):
            return self.ap[k]
    pT32 = _PT32()

    class _PTB:
        def __init__(self):
            self.ap = pE[:, :].bitcast(BF16)
            self.b = pE.b

        def __getitem__(self, k):
            return self.ap[k]
    pTb = _PTB()

    class _PTF16:
        def __init__(self):
            self.ap = pTf[:, :].bitcast(BF16)
            self.b = pTf.b

        def __getitem__(self, k):
            return self.ap[k]
    pTf16 = _PTF16()
    trsb = sb("trsb", [128, 4, 128], BF16)
    ident_b = csb[:, C_ID:C_ID + 128]
    ident_f = cs[:, C_ID:C_ID + 128]
    mflag = mt[:, 0:1]

    dma(cs[:], cst.ap[:, :], [], [cs])
    cp("dve", csb[:], cs[:], [cs], [csb])
    dma(mt[:], meta.ap[:, :], [], [mt])
    dma(sc8[:], cvec.ap[:, :], [], [sc8])
    act(sc8[:], sc8[:], AF.Silu, [sc8], [sc8])
    for kt in range(8):
        cp("dve", screp[:, kt, :], sc8[:, kt:kt + 1].to_broadcast([128, 128]), [sc8], [screp])
    band = sb("band", [128, 384])
    ts("dve", band[:], cs[:, C_BAND:C_BAND + 384], mt[:, 1:2], ALU.mult, [cs, mt], [band])

    evac_rr = [0]

    def evac(out, in_, R, W):
        e = ("act", "dve")[evac_rr[0] % 2]
        evac_rr[0] += 1
        cp(e, out, in_, R, W)

    def load_weight_bf16(dst, src_ap, ncols, wsx=None):
        wsx = wsx or wst1
        for kt in range(8):
            dma(wsx[:, :ncols], src_ap[kt * 128:(kt + 1) * 128, :], [], [wsx])
            cp("pool", dst[:, kt, :ncols], wsx[:, :ncols], [wsx], [dst])

    def adaln(l):
        banks = [pA, pB, pC, pD, pE, pF]
        for kt in range(8):
            wsx = wst[kt % 2]
            dma(wsx[:, :3 * D], w_ada.ap[l, kt * 128:(kt + 1) * 128, :], [], [wsx])
            for nb_ in range(6):
                mm(banks[nb_][:, :], screp[:, kt, :], wsx[:, nb_ * 512:(nb_ + 1) * 512], kt == 0, kt == 7,
                   [screp, wsx], [banks[nb_]])
        tmpbc = wst1
        dma(tmpbc[:, :3 * D], b_ada.ap[l:l + 1, :].partition_broadcast(128), [], [tmpbc])
        for nb_ in range(6):
            tt("dve", modbc[:, nb_ * 512:(nb_ + 1) * 512], banks[nb_][:, :], tmpbc[:, nb_ * 512:(nb_ + 1) * 512],
               ALU.add, [banks[nb_], tmpbc], [modbc])
        dma(tmpbc[:, :D], norm_w.ap[l:l + 1, :].partition_broadcast(128), [modbc], [tmpbc])
        stt(Abc[:], modbc[:, D:2 * D], 1.0, tmpbc[:, :D], ALU.add, ALU.mult, [modbc, tmpbc], [Abc])

    def load_x(ti, xsrc):
        if ti >= NT:
            return
        t0 = ti * 128
        xt = xts[ti % 2]
        dma(xt[:], xsrc.ap[t0:t0 + 128, :], [xsrc.bs[ti] if len(xsrc.bs) > 1 else xsrc.b], [xt])

    def norm_mod_T(l, ti, xsrc):
        if ti == 0:
            load_x(0, xsrc)
        load_x(ti + 1, xsrc)
        xt = xts[ti % 2]
        mset("dve", small[:, 0:1], 0.0, [small])
        act(w1[:, :D], xt[:], AF.Square, [xt], [w1, small], accum_out=small[:, 0:1])
        act(small[:, 1:2], small[:, 0:1], AF.Ln, [small], [small], scale=1.0 / D, bias=cs[:, C_EPS:C_EPS + 1])
        act(small[:, 2:3], small[:, 1:2], AF.Exp, [small], [small], scale=-0.5)
        stt(w2[:, :D], xt[:], small[:, 2:3], Abc[:], ALU.mult, ALU.mult, [xt, small, Abc], [w2])
        tt("dve", hb[:], w2[:, :D], modbc[:, 0:D], ALU.add, [w2, modbc], [hb])
        for kt in range(8):
            tr(pT[:, kt * 128:(kt + 1) * 128], hb[:, kt * 128:(kt + 1) * 128], ident_b, [hb, csb], [pT])
        cp("act", hT[:].rearrange("p a t -> p (a t)"), pT[:, :], [pT], [hT])

    def in_proj(ncols, dst=None):
        dst = dst or proj
        c0 = 0
        k = 0
        while c0 < ncols:
            cw = min(512, ncols - c0)
            pp = (pA, pB)[k % 2]
            for kt in range(8):
                mm(pp[:, :cw], hT[:, kt, :], wbf[:, kt, c0:c0 + cw], kt == 0, kt == 7, [hT, wbf], [pp])
            evac(dst[:, c0:c0 + cw], pp[:, :cw], [pp], [dst])
            c0 += cw
            k += 1

    ts_rr = [0]

    def transpose_store(src_bf_ap, nft, dst, row0, ti, R):
        t0 = ti * 128
        assert nft <= 4
        k = ts_rr[0] % 2
        ts_rr[0] += 1
        pX, tX = (pT, pTb)[k], (trs, trsb)[k]
        for a in range(nft):
            tr(pX[:, a * 128:(a + 1) * 128], src_bf_ap[:, a * 128:(a + 1) * 128], ident_b, R + [csb], [pX])
        cp(("act", "dve")[k], tX[:, :nft, :].rearrange("p a t -> p (a t)"), pX[:, :nft * 128], [pX], [tX])
        dma(dst.ap[row0:row0 + nft * 128, t0:t0 + 128].rearrange("(a p) t -> p a t", p=128), tX[:, :nft, :],
            [tX], [dst.bs[ti]])

    def transpose_store_f32(src_ap, nft, dst, row0, ti, R):
        t0 = ti * 128
        for a in range(nft):
            tr(pTf[:, a * 128:(a + 1) * 128], src_ap[:, a * 128:(a + 1) * 128], ident_f, R + [cs], [pTf])
        cp("act", trf[:, :nft, :].rearrange("p a t -> p (a t)"), pTf[:, :nft * 128], [pTf], [trf])
        dma(dst.ap[row0:row0 + nft * 128, t0:t0 + 128].rearrange("(a p) t -> p a t", p=128), trf[:, :nft, :],
            [trf], [dst.bs[ti]])

    def out_proj_residual(l, xsrc, xdst, ydt, wsrc):
        S.barrier()
        load_weight_bf16(wob, wsrc, D, wstO)
        def loads(ti):
            if ti >= NT:
                return
            t0 = ti * 128
            p = ti % 2
            yt = (sb_yt, sb_yt2)[p]
            dma(yt[:], ydt.ap[:, t0:t0 + 128].rearrange("(a p) t -> p a t", p=128), [ydt.bs[ti]], [yt])
            dma(xts[p][:], xsrc.ap[t0:t0 + 128, :], [xsrc.bs[ti] if len(xsrc.bs) > 1 else xsrc.b], [xts[p]])

        loads(0)
        for ti in range(NT):
            t0 = ti * 128
            p = ti % 2
            loads(ti + 1)
            yt, xt = (sb_yt, sb_yt2)[p], xts[p]
            o1, o2 = ((w1, w2), (w3, w4))[p]
            for cb in range(2):
                pp = ((pA, pB), (pC, pD))[p][cb]
                for kt in range(8):
                    mm(pp[:, :], yt[:, kt, :], wob[:, kt, cb * 512:(cb + 1) * 512], kt == 0, kt == 7, [yt, wob], [pp])
                tt("dve", o1[:, cb * 512:(cb + 1) * 512], pp[:, :], modbc[:, 2 * D + cb * 512:2 * D + (cb + 1) * 512],
                   ALU.mult, [pp, modbc], [o1])
            tt("dve", o2[:, :D], o1[:, :D], xt[:], ALU.add, [o1, xt], [o2])
            dma(xdst.ap[t0:t0 + 128, :], o2[:, :D], [o2], [xdst.bs[ti] if len(xdst.bs) > 1 else xdst.b])

    sb_yt = sb("yt", [128, 8, 128], BF16)
    sb_yt2 = sb("yt2", [128, 8, 128], BF16)
    w4 = sb("w4", [128, 1024])

    qk_t = sb("qk_t", [128, 4, 128], BF16, grp="G")
    kv_t = sb("kv_t", [128, 768], BF16, grp="G")
    sp_t = sb("sp_t", [128, 256], grp="G")
    E1 = sb("E1", [128, 2, 128], grp="G")
    E2 = sb("E2", [128, 2, 128], grp="G")
    E3 = sb("E3", [128, 256], grp="G")
    qbT = sb("qbT", [128, 2, 128], BF16, grp="G")
    kbT = sb("kbT", [128, 2, 128], BF16, grp="G")
    kd = sb("kd", [128, 256], BF16, grp="G")
    attm = sb("attm", [128, 4, 128], BF16, grp="G")
    Sst = sb("Sst", [128, 2, 256], grp="G")
    Sbf = sb("Sbf", [128, 2, 256], BF16, grp="G")
    Sbf1 = sb("Sbf1", [128, 2, 256], BF16, grp="G")
    o_t = sb("o_t", [128, 4, 128], grp="G")
    oF_t = sb("oF_t", [128, 4, 128], grp="G")
    zg_t = sb("zg_t", [128, 4, 128], BF16, grp="G")
    osq = sb("osq", [128, 512], BF16, grp="G")
    onc = sb("onc", [128, 1])
    TX = max(T, 2048)
    Xs = [sb("Xf", [128, 2, TX], grp="S"), sb("Xb", [128, 2, TX], grp="S")]
    u_ft = sb("u_ft", [128, T], BF16, grp="S")
    BT = sb("BT", [128, 2, 2, 16, 128], BF16, grp="S")
    CTm = sb("CTm", [128, 2, 16, 32], grp="S")
    Xblk = [[Buf() for _ in range(max(NB, 1))] for _ in range(2)]

    class _Alias:
        def __init__(self, base, shape):
            self.ap = base.ap.rearrange("p c t -> p (c t)")[:, 0:4096].rearrange("p (a j k) -> p a j k", a=2, j=16)
            self.b = base.b

        def __getitem__(self, k):
            return self.ap[k]
    Bx = _Alias(Xs[0], None)
    Bbar = _Alias(Xs[1], None)
    s5par = sb("s5par", [128, 3, 16], grp="S")
    h0t = sb("h0t", [128, 2, 2, 16], grp="S")
    NPW = 18
    PW = sb("PW", [128, 2, NPW, 3, 16], grp="S")
    PWm = sb("PWm", [128, 2, NPW, 3, 16], grp="S")
    s5tmp = sb("s5tmp", [128, 12, 16], grp="S")
    s5i = sb("s5i", [128, 16], I32, grp="S")
    STG = sb("STG", [128, 2 * 2 * NS * 16], grp="S")
    stgT = sb("stgT", [128, 128], grp="S")
    y5st = sb("y5st", [32, 512], grp="S")
    dcol = sb("dcol", [128, 4], grp="SP")
    bgl = sb("bgl", [128, 4], grp="SP")
    wgb = sb("wgb", [128, 4, 512], BF16, grp="SP")
    y5b = sb("y5b", [128, 4, 512], grp="SP")
    ub = sb("ub", [128, 4, 512], BF16, grp="SP")
    z5b = sb("z5b", [128, 4, 512], BF16, grp="SP")
    zz = sb("zz", [128, 4, 512], grp="SP")
    zzb = sb("zzb", [128, 4, 512], BF16, grp="SP")
    sg = sb("sg", [128, 4, 512], grp="SP")
    y5o = sb("y5o", [128, 4, 512], BF16, grp="SP")

    pw_exps = list(range(1, 9)) + [16, 24, 32, 40, 48, 56, 64, 128, 192, 256]
    pidx = {m: i for i, m in enumerate(pw_exps)}
    assert len(pw_exps) <= NPW

    def s5_post_setup(e):
        dma(dcol[:], s5d.ap[e], [], [dcol])
        dma(bgl[:], bglu.ap[e], [], [bgl])
        for kt in range(4):
            dma(wstS[:, :512], wglu.ap[e, kt * 128:(kt + 1) * 128, :], [], [wstS])
            cp("pool", wgb[:, kt, :], wstS[:, :512], [wstS], [wgb])

    def s5_setup(e):
        dma(h0t[:], s5h0.ap[e].rearrange("d c p j -> p d c j"), [], [h0t])
        dma(CTm[:].rearrange("p a j o -> p a (j o)"), s5c.ap[e].rearrange("a p x -> p a x"), [], [CTm])
        dma(Bx[:].rearrange("p a j k -> p a (j k)"), s5b.ap[e].rearrange("a p x -> p a x"), [], [Bx])
        for d in range(2):
            dma(s5par[:], s5p.ap[e, d].rearrange("a p j -> p a j"), [], [s5par])
            lre, lim, ldt = s5par[:, 0, :], s5par[:, 1, :], s5par[:, 2, :]
            tmp = lambda i: s5tmp[:, i, :]
            R = [s5par, s5tmp]
            W = [s5tmp]
            act(tmp(0), ldt, AF.Exp, R, W)
            tt("dve", tmp(1), lre, tmp(0), ALU.mult, R, W)
            tt("dve", tmp(2), lim, tmp(0), ALU.mult, R, W)
            act(tmp(3), tmp(1), AF.Exp, R, W)
            for (dst, shift) in ((4, 0.0), (5, math.pi / 2)):
                ts("dve", tmp(6), tmp(2), shift, ALU.add, R, W, 1.0 / (2 * math.pi), ALU.mult)
                cp("dve", s5i[:], tmp(6), R, [s5i])
                cp("dve", tmp(7), s5i[:], [s5i], W)
                ts("dve", tmp(6), tmp(2), shift, ALU.add, R, W)
                stt(tmp(6), tmp(7), -2 * math.pi, tmp(6), ALU.mult, ALU.add, R, W)
                ts("dve", tmp(6), tmp(6), math.pi, ALU.min, R, W, -math.pi, ALU.max)
                act(tmp(dst), tmp(6), AF.Sin, R, W)
            P = lambda m, c: PW[:, d, pidx[m], c, :]
            RW = [PW, s5tmp]
            tt("dve", P(1, 0), tmp(3), tmp(5), ALU.mult, RW, [PW])
            tt("dve", P(1, 1), tmp(3), tmp(4), ALU.mult, RW, [PW])
            tt("dve", tmp(6), lre, lre, ALU.mult, R, W)
            tt("dve", tmp(7), lim, lim, ALU.mult, R, W)
            tt("dve", tmp(6), tmp(6), tmp(7), ALU.add, R, W)
            S.op("dve", lambda e_, o=tmp(6): e_.reciprocal(out=o, in_=o), bl(R), bl(W))
            ts("dve", tmp(7), P(1, 0), -1.0, ALU.add, RW, W)
            tt("dve", tmp(8), tmp(7), lre, ALU.mult, R, W)
            tt("dve", tmp(9), P(1, 1), lim, ALU.mult, RW + [s5par], W)
            tt("dve", tmp(8), tmp(8), tmp(9), ALU.add, R, W)
            tt("dve", tmp(8), tmp(8), tmp(6), ALU.mult, R, W)
            tt("dve", tmp(9), P(1, 1), lre, ALU.mult, RW + [s5par], W)
            tt("dve", tmp(10), tmp(7), lim, ALU.mult, R, W)
            tt("dve", tmp(9), tmp(9), tmp(10), ALU.subtract, R, W)
            tt("dve", tmp(9), tmp(9), tmp(6), ALU.mult, R, W)
            crb = s5tmp[:, 8, :].unsqueeze(2).to_broadcast([128, 16, 128])
            cib = s5tmp[:, 9, :].unsqueeze(2).to_broadcast([128, 16, 128])
            RB = [Bx, s5tmp, Bbar]
            tt("dve", Bbar[:, 0], Bx[:, 0], crb, ALU.mult, RB, [Bbar])
            tt("dve", Bbar[:, 1], Bx[:, 1], cib, ALU.mult, RB, [Bbar])
            tt("dve", Bbar[:, 0], Bbar[:, 0], Bbar[:, 1], ALU.subtract, RB, [Bbar])
            tt("dve", Bbar[:, 1], Bx[:, 1], crb, ALU.mult, RB, [Bbar])
            for j in range(16):
                stt(Bbar[:, 1, j, :], Bx[:, 0, j, :], s5tmp[:, 9, j:j + 1], Bbar[:, 1, j, :], ALU.mult, ALU.add, RB, [Bbar])
            for c in range(2):
                for j4 in range(4):
                    for jj in range(4):
                        j = j4 * 4 + jj
                        tr(pTf[:, jj * 128:(jj + 1) * 128], Bbar[:, c, j, :], ident_f, [Bbar, cs], [pTf])
                    cp("act", BT[:, d, c, j4 * 4:(j4 + 1) * 4, :].rearrange("p j k -> p (j k)"), pTf[:, :], [pTf], [BT])
            def cmul(mo, ma, mb_):
                a_re, a_im = P(ma, 0), P(ma, 1)
                b_re, b_im = P(mb_, 0), P(mb_, 1)
                tt("dve", tmp(10), a_im, b_im, ALU.mult, RW, W)
                tt("dve", tmp(11), a_re, b_re, ALU.mult, RW, W)
                tt("dve", tmp(6), a_re, b_im, ALU.mult, RW, W)
                tt("dve", tmp(7), a_im, b_re, ALU.mult, RW, W)
                tt("dve", P(mo, 0), tmp(11), tmp(10), ALU.subtract, RW, [PW])
                tt("dve", P(mo, 1), tmp(6), tmp(7), ALU.add, RW, [PW])
            for m in range(2, 9):
                cmul(m, m - 1, 1)
            for m in range(16, 65, 8):
                cmul(m, m - 8, 8)
            cmul(128, 64, 64)
            cmul(192, 128, 64)
            cmul(256, 192, 64)
            ts("dve", PW[:, d, :, 2, :], PW[:, d, :, 1, :], -1.0, ALU.mult, [PW], [PW])
            ts("dve", PWm[:, d], PW[:, d], mflag, ALU.mult, [PW, mt], [PWm])

    def s5_view(X, comp, start, step, count, rev, inner=None):
        base = X[:, comp, 0:1]
        off = base.offset
        pstride = base.ap[0][0]
        if rev:
            o = off + (T - 1 - start)
            dims = [[pstride, 128], [-step, count]]
            if inner is not None:
                dims.append([-inner[0], inner[1]])
        else:
            o = off + start
            dims = [[pstride, 128], [step, count]]
            if inner is not None:
                dims.append([inner[0], inner[1]])
        return bass.AP(base.tensor, o, dims)

    def s5_scan(X, d, j, rev, fence0):
        XW = Xblk[d]
        XB = XW + [PW, PWm]
        stt_ = {"fence": fence0, "C": None, "D": None}

        def cmac(tgt, src, pw_tile, m, chain):
            pr = pw_tile[:, d, pidx[m], 0, j:j + 1]
            pi = pw_tile[:, d, pidx[m], 1, j:j + 1]
            pn = pw_tile[:, d, pidx[m], 2, j:j + 1]
            f = stt_["fence"]
            pc, pd = (stt_["C"], stt_["D"]) if chain else (None, None)
            ta = stt(tgt(0), src(0), pr, tgt(0), ALU.mult, ALU.add, XB, XW, extra=[f, pc])
            yield
            tb = stt(tgt(1), src(0), pi, tgt(1), ALU.mult, ALU.add, XB, XW, extra=[f, pc])
            yield
            tc = stt(tgt(0), src(1), pn, tgt(0), ALU.mult, ALU.add, XB, XW, extra=[f, pd, ta])
            yield
            td = stt(tgt(1), src(1), pr, tgt(1), ALU.mult, ALU.add, XB, XW, extra=[f, pd, tb])
            yield
            stt_["C"], stt_["D"], stt_["last"] = tc, td, td

        def fence():
            stt_["fence"] = stt_.get("last", stt_["fence"])
            stt_["C"] = stt_["D"] = None

        for (s, K) in ((1, 8), (8, 8), (64, 4)):
            n = T // (s * K)
            fence()
            for jj in range(1, K):
                tgt = lambda c, s=s, K=K, jj=jj, n=n: s5_view(X, c, (jj + 1) * s - 1, K * s, n, rev)
                src = lambda c, s=s, K=K, jj=jj, n=n: s5_view(X, c, jj * s - 1, K * s, n, rev)
                yield from cmac(tgt, src, PW, s, True)
        fence()
        for sg_ in range(1, NS):
            tgt = lambda c, sg_=sg_: s5_view(X, c, 256 * (sg_ + 1) - 1, 1, 1, rev)
            src = lambda c, sg_=sg_: s5_view(X, c, 256 * sg_ - 1, 1, 1, rev)
            yield from cmac(tgt, src, PWm, 256, True)
        fence()
        if NS > 1:
            for jj in range(3):
                tgt = lambda c, jj=jj: s5_view(X, c, 256 + 64 * (jj + 1) - 1, 256, NS - 1, rev)
                src = lambda c: s5_view(X, c, 255, 256, NS - 1, rev)
                yield from cmac(tgt, src, PWm, 64 * (jj + 1), False)
        for (s, K, nin) in ((8, 8, 4), (1, 8, 32)):
            fence()
            for jj in range(K - 1):
                m = s * (jj + 1)
                tgt = lambda c, s=s, K=K, jj=jj, nin=nin: s5_view(X, c, s * K + (jj + 1) * s - 1, 256, NS, rev, inner=(s * K, nin - 1))
                src = lambda c, s=s, K=K, nin=nin: s5_view(X, c, s * K - 1, 256, NS, rev, inner=(s * K, nin - 1))
                yield from cmac(tgt, src, PW, m, False)
                if NS > 1:
                    tgt = lambda c, s=s, jj=jj: s5_view(X, c, 256 + (jj + 1) * s - 1, 256, NS - 1, rev)
                    src = lambda c: s5_view(X, c, 255, 256, NS - 1, rev)
                    yield from cmac(tgt, src, PWm, m, False)

    def chk(tag, l):
        if stop == "%s%d" % (tag, l):
            raise _Stop()

    def chk2(tag):
        if stop == tag:
            raise _Stop()

    def even_layer(l, xsrc, xdst):
        e = l // 2
        load_weight_bf16(wbf, w_in_e.ap[e], EIN)
        chk2("W")
        adaln(l)
        chk2("ADA")
        dma(w2cf[:], w2cat.ap[e], [], [w2cf])
        cp("dve", w2c[:], w2cf[:], [w2cf], [w2c])
        dma(onc[:], onorm.ap[e], [], [onc])
        mset("dve", lrT[:], 0.0, [lrT])
        mset("dve", lrT[32:33, :], 1.0, [lrT])
        for ti in range(NT):
            t0 = ti * 128
            norm_mod_T(l, ti, xsrc)
            chk2("NM")
            in_proj(EIN, pbf)
            chk2("IP")
            transpose_store(pbf[:, 0:512], 4, qkT_s, 0, ti, [pbf])
            chk2("TS")
            dma(kv_s.ap[t0:t0 + 128, :], pbf[:, 256:1024], [pbf], [kv_s.bs[ti]])
            chk2("KV")
            tr(pT[:32, 0:128], pbf[:, 1024:1056], ident_b, [pbf, csb], [pT])
            cp("act", lrT[0:32, :], pT[:32, 0:128], [pT], [lrT])
            mm(pC[:, :], lrT[:, :], w2c[:, :], True, True, [lrT, w2c], [pC])
            act(w3[:, :512], pC[:, :], AF.Exp, [pC], [w3], scale=-1.0)
            act(w3[:, 512:1024], w3[:, :512], AF.Ln, [w3], [w3], bias=cs[:, C_ONE:C_ONE + 1])
            dma(sp_s.ap[t0:t0 + 128, :], w3[:, 512:1024], [w3], [sp_s.bs[ti]])
            chk2("GATE")
            transpose_store(pbf[:, 1056:1568], 4, zgT_s, 0, ti, [pbf])
            transpose_store(pbf[:, 1568:2080], 4, uT_s, 0, ti, [pbf])
            transpose_store(pbf[:, 2080:2592], 4, z5T_s, 0, ti, [pbf])
            chk2("T%d" % ti)
        chk('P', l)
        S.barrier()
        def gla_gen():
            for d in range(2):
                TI, TR_, MK = (C_TIF, C_TRF, C_MF) if d == 0 else (C_TIB, C_TRB, C_MB)
                mset("dve", Sst[:], 0.0, [Sst])
                yield
                for h in range(4):
                    pr, hh = h // 2, h % 2
                    dma(Sst[hh * 64:(hh + 1) * 64, pr, hh * 128:(hh + 1) * 128], gla0.ap[e, d, h], [], [Sst])
                    yield
                blkm = cs[:, C_BLK:C_BLK + 256].unsqueeze(1).to_broadcast([128, 2, 256])
                tt("pool", Sbf[:], Sst[:], blkm, ALU.mult, [Sst, cs], [Sbf])
                yield
                order = list(range(NT)) if d == 0 else list(range(NT - 1, -1, -1))
                for oi, ti in enumerate(order):
                    t0 = ti * 128
                    if oi > 0 and oi % 2 == 0:
                        ts("dve", Sst[:], Sst[:], mflag, ALU.mult, [Sst, mt], [Sst])
                        yield
                        tt("pool", Sbf[:], Sst[:], blkm, ALU.mult, [Sst, cs], [Sbf])
                        yield
                    dma(qk_t[:], qkT_s.ap[:, t0:t0 + 128].rearrange("(a p) t -> p a t", p=128), [qkT_s.bs[ti]], [qk_t])
                    yield
                    dma(kv_t[:], kv_s.ap[t0:t0 + 128, :], [kv_s.bs[ti]], [kv_t])
                    yield
                    dma(sp_t[:], sp_s.ap[t0:t0 + 128, d * 256:(d + 1) * 256], [sp_s.bs[ti]], [sp_t])
                    yield
                    yield
                    for pr in range(2):
                        mm(pC[:, pr * 128:(pr + 1) * 128], sp_t[:, pr * 128:(pr + 1) * 128], cs[:, TI:TI + 128], True, True,
                           [sp_t, cs], [pC])
                        yield
                    mm(pD[:, 0:256], cs[:, TR_:TR_ + 128], sp_t[:, :], True, True, [sp_t, cs], [pD])
                    yield
                    yield
                    act(E1[:].rearrange("p a t -> p (a t)"), pC[:, 0:256], AF.Exp, [pC], [E1])
                    yield
                    act(E2[:].rearrange("p a t -> p (a t)"), pC[:, 0:256], AF.Exp, [pC], [E2], scale=-1.0)
                    yield
                    act(E3[:], pD[:, 0:256], AF.Exp, [pD], [E3])
                    yield
                    stt(qbT[:], qk_t[:, 0:2, :], 0.125, E1[:], ALU.mult, ALU.mult, [qk_t, E1], [qbT])
                    yield
                    tt("pool", kbT[:], qk_t[:, 2:4, :], E2[:], ALU.mult, [qk_t, E2], [kbT])
                    yield
                    tt("pool", kd[:], kv_t[:, 0:256], E3[:], ALU.mult, [kv_t, E3], [kd])
                    yield
                    yield
                    for h in range(4):
                        pr, hh = h // 2, h % 2
                        pX = (pE, pA)[hh]
                        mm(pX[:, pr * 128:(pr + 1) * 128], kbT[hh * 64:(hh + 1) * 64, pr, :], qbT[hh * 64:(hh + 1) * 64, pr, :],
                           True, True, [kbT, qbT], [pX])
                        yield
                    yield
                    for hh in range(2):
                        pX = (pE, pA)[hh]
                        av = attm[:].rearrange("p (pr hh) t -> p hh pr t", hh=2)[:, hh]
                        tt("dve", av, pX[:, 0:256].rearrange("p (h t) -> p h t", h=2),
                           cs[:, MK:MK + 128].unsqueeze(1).to_broadcast([128, 2, 128]), ALU.mult, [pX, cs], [attm])
                        yield
                    yield
                    chunks = (0, 1) if d == 0 else (1, 0)
                    for ci, ch in enumerate(chunks):
                        c0 = ch * 64
                        dcolx = (c0 + 63) if d == 0 else c0
                        pUp = (pD, pA)[ch]
                        for pr in range(2):
                            mm(pUp[:, 256:512], kd[c0:c0 + 64, pr * 128:(pr + 1) * 128],
                               kv_t[c0:c0 + 64, 256 + pr * 256:256 + (pr + 1) * 256], True, True, [kd, kv_t], [pUp])
                            yield
                            stt(Sst[:, pr, :], Sst[:, pr, :], E1[:, pr, dcolx:dcolx + 1], pUp[:, 256:512], ALU.mult, ALU.add,
                                [Sst, E1, pUp], [Sst])
                            yield
                        if ci == 0:
                            tt("pool", Sbf1[:], Sst[:], blkm, ALU.mult, [Sst, cs], [Sbf1])
                            yield
                    for h in range(4):
                        pr, hh = h // 2, h % 2
                        mm(pF[:, h * 128:(h + 1) * 128], kv_t[:, 256 + h * 128:256 + (h + 1) * 128], attm[:, h, :],
                           True, False, [kv_t, attm], [pF])
                        yield
                        for ci, ch in enumerate(chunks):
                            c0 = ch * 64
                            Sx = (Sbf, Sbf1)[ci]
                            mm(pF[:, h * 128 + c0:h * 128 + c0 + 64], Sx[:, pr, hh * 128:(hh + 1) * 128],
                               qbT[:, pr, c0:c0 + 64], False, (ci == 1), [Sx, qbT], [pF])
                            yield
                    tt("pool", Sbf[:], Sst[:], blkm, ALU.mult, [Sst, cs, pF], [Sbf])
                    yield
                    yield
                    if oi % 2 == 1:
                        seg = ti // 2
                        for h in range(4):
                            pr, hh = h // 2, h % 2
                            dma(ngla.ap[e, d, seg, h], Sst[hh * 64:(hh + 1) * 64, pr, hh * 128:(hh + 1) * 128], [Sst], [ngla])
                            yield
                    if d == 0:
                        cp("act", o_t[:].rearrange("p h t -> p (h t)"), pF[:, :], [pF], [o_t])
                        yield
                        dma(oF_s.ap[:, t0:t0 + 128].rearrange("(a p) t -> p a t", p=128), o_t[:], [o_t], [oF_s.bs[ti]])
                        yield
                    else:
                        dma(oF_t[:], oF_s.ap[:, t0:t0 + 128].rearrange("(a p) t -> p a t", p=128), [oF_s.bs[ti]], [oF_t])
                        yield
                        dma(zg_t[:], zgT_s.ap[:, t0:t0 + 128].rearrange("(a p) t -> p a t", p=128), [zgT_s.bs[ti]], [zg_t])
                        yield
                        tt("dve", o_t[:].rearrange("p h t -> p (h t)"), pF[:, :], oF_t[:].rearrange("p h t -> p (h t)"),
                           ALU.add, [pF, oF_t], [o_t])
                        yield
                        of = o_t[:].rearrange("p h t -> p (h t)")
                        act(osq[:], of, AF.Square, [o_t], [osq])
                        yield
                        mm(pC[:, :], csb[:, C_ONE:C_ONE + 128], osq[:], True, True, [osq, csb], [pC])
                        yield
                        act(w1[:, :512], pC[:, :], AF.Ln, [pC], [w1], scale=1.0 / 128, bias=cs[:, C_EPS:C_EPS + 1])
                        yield
                        act(w1[:, :512], w1[:, :512], AF.Exp, [w1], [w1], scale=-0.5)
                        yield
                        tt("dve", w1[:, :512], w1[:, :512], of, ALU.mult, [w1, o_t], [w1])
                        yield
                        act(w1[:, 512:1024], zg_t[:].rearrange("p h t -> p (h t)"), AF.Silu, [zg_t], [w1])
                        yield
                        stt(trs[:, 0:4, :].rearrange("p a t -> p (a t)"), w1[:, :512], onc[:, 0:1], w1[:, 512:1024],
                            ALU.mult, ALU.mult, [w1, onc], [trs])
                        yield
                        dma(yT_s.ap[0:512, t0:t0 + 128].rearrange("(a p) t -> p a t", p=128), trs[:, 0:4, :], [trs], [yT_s.bs[ti]])
                        yield
            yield

        def s5_gen():
            s5_setup(e)
            for j in range(16):
                ft = j // 4
                if j % 4 == 0:
                    dma(u_ft[:], uT_s.ap[ft * 128:(ft + 1) * 128, :], uT_s.bs, [u_ft])
                fences = []
                for d in range(2):
                    X = Xs[d]
                    rev = (d == 1)
                    for blk in range(NB):
                        for c in range(2):
                            pp = (pB, pTf)[c]
                            mm(pp[:, :], BT[:, d, c, j, :], u_ft[:, blk * 512:(blk + 1) * 512], True, True, [BT, u_ft], [pp])
                            cp("act", X[:, c, blk * 512:(blk + 1) * 512], pp[:, :], [pp], [X, Xblk[d][blk]])
                            yield
                    v0 = lambda c: s5_view(X, c, 0, 1, 1, rev)
                    pr_ = PW[:, d, pidx[1], 0, j:j + 1]; pi_ = PW[:, d, pidx[1], 1, j:j + 1]; pn_ = PW[:, d, pidx[1], 2, j:j + 1]
                    stt(v0(0), h0t[:, d, 0, j:j + 1], pr_, v0(0), ALU.mult, ALU.add, [h0t, PW] + Xblk[d], Xblk[d])
                    stt(v0(0), h0t[:, d, 1, j:j + 1], pn_, v0(0), ALU.mult, ALU.add, [h0t, PW] + Xblk[d], Xblk[d])
                    stt(v0(1), h0t[:, d, 1, j:j + 1], pr_, v0(1), ALU.mult, ALU.add, [h0t, PW] + Xblk[d], Xblk[d])
                    fences.append(stt(v0(1), h0t[:, d, 0, j:j + 1], pi_, v0(1), ALU.mult, ALU.add, [h0t, PW] + Xblk[d], Xblk[d]))
                gens = [s5_scan(Xs[d], d, j, d == 1, fences[d]) for d in range(2)]
                alive = [True, True]
                while any(alive):
                    for d in range(2):
                        if alive[d]:
                            try:
                                next(gens[d])
                                yield
                            except StopIteration:
                                alive[d] = False
                for d in range(2):
                    X = Xs[d]
                    rev = (d == 1)
                    for c in range(2):
                        col0 = ((d * 2 + c) * NS) * 16 + j
                        dst = bass.AP(STG[:, 0:1].tensor, STG[:, col0:col0 + 1].offset, [list(STG[:, 0:1].ap[0]), [16, NS]])
                        cp("pool", dst, s5_view(X, c, 255, 256, NS, rev), Xblk[d], [STG])
                for blk in range(NB):
                    k = 0
                    for d in range(2):
                        for c in range(2):
                            mm(pT32[0:32, :], CTm[:, c, j, :], Xs[d][:, c, blk * 512:(blk + 1) * 512], k == 0, k == 3,
                               [CTm, Xblk[d][blk]], [pT32])
                            k += 1
                    cp("act", y5st[:, :], pT32[0:32, :], [pT32], [y5st])
                    dma(y5T_s.ap[j * 32:(j + 1) * 32, blk * 512:(blk + 1) * 512], y5st[:, :], [y5st], [y5T_s.bs[j]])
                    yield
            gsz = min(8, NS)
            for d in range(2):
                for c in range(2):
                    for s0 in range(0, NS, gsz):
                        col0 = ((d * 2 + c) * NS + s0) * 16
                        ncol = gsz * 16
                        tr(pTf[:ncol, 0:128], STG[:, col0:col0 + ncol], ident_f, [STG, cs], [pTf])
                        cp("act", stgT[:ncol, :], pTf[:ncol, 0:128], [pTf], [stgT])
                        dma(ns5.ap[e, d, c, s0 * 16:s0 * 16 + ncol, :], stgT[:ncol, :], [stgT], [ns5])
            yield

        gG, gS = gla_gen(), s5_gen()
        aG = aS = True
        RATIO = float(os.environ.get("KRATIO", "2.0"))
        acc = 0.0
        while aG or aS:
            if aG:
                try:
                    next(gG)
                except StopIteration:
                    aG = False
            acc += RATIO if aG else 1000.0
            while acc >= 1.0 and aS:
                acc -= 1.0
                try:
                    next(gS)
                except StopIteration:
                    aS = False
            if not aS:
                acc = 0.0

        chk('S', l)
        S.barrier()
        s5_post_setup(e)
        for blk in range(NB):
            c0 = blk * 512
            tis = list(range(blk * 4, blk * 4 + 4))
            dma(y5b[:], y5T_s.ap[:, c0:c0 + 512].rearrange("(a p) t -> p a t", p=128), y5T_s.bs, [y5b])
            dma(ub[:], uT_s.ap[:, c0:c0 + 512].rearrange("(a p) t -> p a t", p=128), [uT_s.bs[i] for i in tis], [ub])
            dma(z5b[:], z5T_s.ap[:, c0:c0 + 512].rearrange("(a p) t -> p a t", p=128), [z5T_s.bs[i] for i in tis], [z5b])
            for a in range(4):
                stt(y5b[:, a, :], ub[:, a, :], dcol[:, a:a + 1], y5b[:, a, :], ALU.mult, ALU.add, [ub, dcol, y5b], [y5b])
            act(zz[:].rearrange("p a t -> p (a t)"), y5b[:].rearrange("p a t -> p (a t)"), AF.Gelu_apprx_tanh, [y5b], [zz])
            cp("pool", zzb[:].rearrange("p a t -> p (a t)"), zz[:].rearrange("p a t -> p (a t)"), [zz], [zzb])
            for fo in range(4):
                pp = (pA, pB)[fo % 2]
                for kt in range(4):
                    mm(pp[:, :], wgb[:, kt, fo * 128:(fo + 1) * 128], zzb[:, kt, :], kt == 0, kt == 3, [wgb, zzb], [pp])
                act(sg[:, fo, :], pp[:, :], AF.Sigmoid, [pp, bgl], [sg], bias=bgl[:, fo:fo + 1])
            tt("dve", sg[:].rearrange("p a t -> p (a t)"), sg[:].rearrange("p a t -> p (a t)"),
               zz[:].rearrange("p a t -> p (a t)"), ALU.mult, [sg, zz], [sg])
            act(zz[:].rearrange("p a t -> p (a t)"), z5b[:].rearrange("p a t -> p (a t)"), AF.Silu, [z5b], [zz])
            tt("dve", y5o[:].rearrange("p a t -> p (a t)"), sg[:].rearrange("p a t -> p (a t)"),
               zz[:].rearrange("p a t -> p (a t)"), ALU.mult, [sg, zz], [y5o])
            dma(yT_s.ap[512:1024, c0:c0 + 512].rearrange("(a p) t -> p a t", p=128), y5o[:], [y5o],
                [yT_s.bs[i] for i in tis])
        chk('SP', l)
        out_proj_residual(l, xsrc, xdst, yT_s, w_out_e.ap[e])
        S.barrier()
        chk('O', l)

    qkwb = sb("qkwb", [128, 128])
    sinkb = sb("sinkb", [128, 8])
    rp_t = sb("rp_t", [128, 128], grp="P")
    qn = sb("qn", [128, 640], grp="P")
    qr = sb("qr", [128, 640], grp="P")
    kdup = sb("kdup", [128, 2, 2, 64], BF16, grp="P")
    ckT = sb("ckT", [128, 2, 256], BF16)
    cvt = sb("cvt", [128, 2, 128], BF16)
    ckf = sb("ckf", [128, 2, 128], grp="P")
    qT_t = sb("qT_t", [128, 4, 128], BF16, grp="A")
    kT_w = sb("kT_w", [128, 2, 384], BF16, grp="A")
    v_w = sb("v_w", [128, 3, 128], BF16, grp="A")
    scs2 = [sb("scs%d" % i, [128, 640], grp="A") for i in range(2)]
    pexp2 = [sb("pexp%d" % i, [128, 640], BF16, grp="A") for i in range(2)]
    pTt2 = [sb("pTt%d" % i, [128, 5, 128], BF16, grp="A") for i in range(2)]
    sm2 = [sb("sm%d" % i, [128, 8, 8], grp="A") for i in range(2)]
    bandT = sb("bandT", [128, 384], grp="A")
    oat = sb("oat", [128, 512], grp="A")
    sz_t = sb("sz_t", [128, 512], grp="A")
    cvw = sb("cvw", [128, 16])
    u1h = sb("u1h", [128, 4, 130], grp="A")
    u2t = sb("u2t", [128, 4, 128], grp="A")
    cacc = sb("cacc", [128, 4, 128], grp="A")

    def odd_layer(l, xsrc, xdst):
        e = l // 2
        load_weight_bf16(wbf, w_in_o.ap[e], OIN)
        adaln(l)
        dma(qkwb[:], qkw.ap[e:e + 1, :].partition_broadcast(128), [], [qkwb])
        ts("dve", qkwb[:, 0:64], qkwb[:, 0:64], 0.125, ALU.mult, [qkwb], [qkwb])
        dma(sinkb[:], sinkv.ap[e:e + 1, :].partition_broadcast(128), [], [sinkb])
        dma(cvw[:], convw.ap[e], [], [cvw])
        for hf in range(2):
            dma(ckf[:, 0, :], ckv.ap[e, 0, hf * 128:(hf + 1) * 128].rearrange("t k d -> t (k d)"), [], [ckf])
            dma(ckf[:, 1, :], ckv.ap[e, 1, hf * 128:(hf + 1) * 128].rearrange("t k d -> t (k d)"), [], [ckf])
            cp("dve", cvt[:, hf, :], ckf[:, 1, :], [ckf], [cvt])
            for kv in range(2):
                for c in range(2):
                    cp("dve", kdup[:, kv, c, :], ckf[:, 0, kv * 64:(kv + 1) * 64], [ckf], [kdup])
            for kv in range(2):
                tr(pT[:, kv * 128:(kv + 1) * 128], kdup[:, kv].rearrange("p c d -> p (c d)"), ident_b, [kdup, csb], [pT])
            for kv in range(2):
                cp("act", ckT[:, kv, hf * 128:(hf + 1) * 128], pT[:, kv * 128:(kv + 1) * 128], [pT], [ckT])
        for ti in range(NT):
            t0 = ti * 128
            norm_mod_T(l, ti, xsrc)
            in_proj(OIN)
            dma(rp_t[:], rope.ap[t0:t0 + 128, :], [], [rp_t])
            act(w1[:, :640], proj[:, 0:640], AF.Square, [proj], [w1])
            S.op("dve", lambda e_: e_.tensor_reduce(out=small[:, 0:10], in_=w1[:, :640].rearrange("p (h d) -> p h d", d=64),
                                                    axis=AX.X, op=ALU.add), bl([w1]), bl([small]))
            act(small[:, 0:10], small[:, 0:10], AF.Ln, [small], [small], scale=1.0 / 64, bias=cs[:, C_EPS:C_EPS + 1])
            act(small[:, 0:10], small[:, 0:10], AF.Exp, [small], [small], scale=-0.5)
            tt("dve", qn[:].rearrange("p (h d) -> p h d", d=64), proj[:, 0:640].rearrange("p (h d) -> p h d", d=64),
               small[:, 0:10].unsqueeze(2).to_broadcast([128, 10, 64]), ALU.mult, [proj, small], [qn])
            tt("dve", qn[:, 0:512].rearrange("p (h d) -> p h d", d=64), qn[:, 0:512].rearrange("p (h d) -> p h d", d=64),
               qkwb[:, 0:64].unsqueeze(1).to_broadcast([128, 8, 64]), ALU.mult, [qn, qkwb], [qn])
            tt("dve", qn[:, 512:640].rearrange("p (h d) -> p h d", d=64), qn[:, 512:640].rearrange("p (h d) -> p h d", d=64),
               qkwb[:, 64:128].unsqueeze(1).to_broadcast([128, 2, 64]), ALU.mult, [qn, qkwb], [qn])
            dma(nkv.ap[e, 0, t0:t0 + 128, :], qn[:, 512:640], [qn], [nkv])
            dma(nkv.ap[e, 1, t0:t0 + 128, :], proj[:, 640:768], [proj], [nkv])
            v5 = lambda tl: tl[:, 0:640].rearrange("p (h a b f) -> p h a b f", a=2, b=2, f=16)
            cosb = rp_t[:, 0:64].rearrange("p (a b f) -> p a b f", a=2, b=2).unsqueeze(1).to_broadcast([128, 10, 2, 2, 16])
            tt("dve", v5(qr), v5(qn), cosb, ALU.mult, [qn, rp_t], [qr])
            for b_ in range(2):
                sinb = rp_t[:, 64:128].rearrange("p (a b f) -> p a b f", a=2, b=2)[:, :, b_, :].unsqueeze(1).to_broadcast([128, 10, 2, 16])
                tt("pool", v5(w1)[:, :, :, b_, :], v5(qn)[:, :, :, 1 - b_, :], sinb, ALU.mult, [qn, rp_t], [w1])
            tt("dve", pbf[:, 0:640], qr[:, 0:640], w1[:, 0:640], ALU.add, [qr, w1], [pbf])
            transpose_store(pbf[:, 0:512], 4, qT_s, 0, ti, [pbf])
            for kv in range(2):
                for c in range(2):
                    cp("pool", kdup[:, kv, c, :], pbf[:, 512 + kv * 64:512 + (kv + 1) * 64], [pbf], [kdup])
            transpose_store(kdup[:].rearrange("p k c d -> p (k c d)"), 2, kT_s, 0, ti, [kdup])
            cp("pool", pbf[:, 640:768], proj[:, 640:768], [proj], [pbf])
            dma(v_s.ap[t0:t0 + 128, :], pbf[:, 640:768], [pbf], [v_s.bs[ti]])
            act(w2[:, :512], proj[:, 768:1280], AF.Silu, [proj], [w2])
            dma(sz_s.ap[t0:t0 + 128, :], w2[:, :512], [w2], [sz_s.bs[ti]])
            tt("dve", w3[:, 0:512], proj[:, 2304:2816], proj[:, 1280:1792], ALU.mult, [proj], [w3])
            act(w3[:, 512:1024], proj[:, 2816:3328], AF.Silu, [proj], [w3])
            tt("dve", w3[:, 512:1024], w3[:, 512:1024], proj[:, 1792:2304], ALU.mult, [w3, proj], [w3])
            transpose_store_f32(w3[:, 0:512], 4, u1T_s, 0, ti, [w3])
            transpose_store_f32(w3[:, 512:1024], 4, u2T_s, 0, ti, [w3])
        chk('P', l)
        S.barrier()
        for ti in range(NT):
            t0 = ti * 128
            tp = max(ti - 1, 0)
            tn = min(ti + 1, NT - 1)
            dma(qT_t[:], qT_s.ap[:, t0:t0 + 128].rearrange("(a p) t -> p a t", p=128), [qT_s.bs[ti]], [qT_t])
            for wi, tw in enumerate((tp, ti, tn)):
                dma(kT_w[:, :, wi * 128:(wi + 1) * 128], kT_s.ap[:, tw * 128:(tw + 1) * 128].rearrange("(k p) t -> p k t", p=128),
                    [kT_s.bs[tw]], [kT_w])
                dma(v_w[:, wi, :], v_s.ap[tw * 128:(tw + 1) * 128, :], [v_s.bs[tw]], [v_w])
            dma(sz_t[:], sz_s.ap[t0:t0 + 128, :], [sz_s.bs[ti]], [sz_t])
            flp = mt[:, 3 + ti:4 + ti]
            fln = mt[:, 3 + NT + ti:4 + NT + ti]
            ts("dve", bandT[:, 0:128], band[:, 0:128], flp, ALU.add, [band, mt], [bandT])
            cp("dve", bandT[:, 128:256], band[:, 128:256], [band], [bandT])
            ts("dve", bandT[:, 256:384], band[:, 256:384], fln, ALU.add, [band, mt], [bandT])

            def head_gen(hq, P):
                kv, pr, hh = hq // 4, hq // 2, hq % 2
                rows = slice(hh * 64, (hh + 1) * 64)
                pS1, pS2 = ((pC, pD), (pA, pB))[P]
                scsX, pexpX, pTtX, smX = scs2[P], pexp2[P], pTt2[P], sm2[P]
                pTX = (pT, pTf16)[P]
                pO = (pE, pF)[P]
                oc = (hq // 2) * 64
                mm(pS1[:, 0:384], qT_t[rows, pr, :], kT_w[rows, kv, :], True, True, [qT_t, kT_w], [pS1])
                mm(pS2[:, 0:256], qT_t[rows, pr, :], ckT[rows, kv, :], True, True, [qT_t, ckT], [pS2])
                yield
                tt("dve", scsX[:, 0:384], pS1[:, 0:384], bandT[:, :], ALU.add, [pS1, bandT], [scsX])
                yield
                ts("dve", scsX[:, 384:640], pS2[:, 0:256], mt[:, 2:3], ALU.add, [pS2, mt], [scsX])
                yield
                S.op("dve", lambda e_: e_.reduce_max(out=smX[:, hq, 0:1], in_=scsX[:, :], axis=AX.X), bl([scsX]), bl([smX]))
                yield
                tt("dve", smX[:, hq, 0:1], smX[:, hq, 0:1], sinkb[:, hq:hq + 1], ALU.max, [smX, sinkb], [smX])
                ts("dve", smX[:, hq, 1:2], smX[:, hq, 0:1], -1.0, ALU.mult, [smX], [smX])
                mset("dve", smX[:, hq, 2:3], 0.0, [smX])
                yield
                act(pexpX[:], scsX[:], AF.Exp, [scsX, smX], [pexpX, smX], bias=smX[:, hq, 1:2], accum_out=smX[:, hq, 2:3])
                act(smX[:, hq, 3:4], sinkb[:, hq:hq + 1], AF.Exp, [smX, sinkb], [smX], bias=smX[:, hq, 1:2])
                yield
                for k5 in range(5):
                    tr(pTX[:, k5 * 128:(k5 + 1) * 128], pexpX[:, k5 * 128:(k5 + 1) * 128], ident_b, [pexpX, csb], [pTX])
                yield
                cp("act", pTtX[:].rearrange("p a t -> p (a t)"), pTX[:, 0:640], [pTX], [pTtX])
                tt("dve", smX[:, hq, 4:5], smX[:, hq, 2:3], smX[:, hq, 3:4], ALU.add, [smX], [smX])
                S.op("dve", lambda e_: e_.reciprocal(out=smX[:, hq, 5:6], in_=smX[:, hq, 4:5]), bl([smX]), bl([smX]))
                yield
                for k5 in range(5):
                    vv = v_w[:, k5, kv * 64:(kv + 1) * 64] if k5 < 3 else cvt[:, k5 - 3, kv * 64:(kv + 1) * 64]
                    mm(pO[:, oc:oc + 64], pTtX[:, k5, :], vv, k5 == 0, k5 == 4, [pTtX, v_w, cvt], [pO])
                yield

            for h2 in range(0, 8, 2):
                gens = [head_gen(h2, 0), head_gen(h2 + 1, 1)]
                alive = [True, True]
                while any(alive):
                    for P in range(2):
                        if alive[P]:
                            try:
                                next(gens[P])
                            except StopIteration:
                                alive[P] = False
            for hq in range(8):
                pO = (pE, pF)[hq % 2]
                oc = (hq // 2) * 64
                stt(oat[:, hq * 64:(hq + 1) * 64], pO[:, oc:oc + 64], sm2[hq % 2][:, hq, 5:6], sz_t[:, hq * 64:(hq + 1) * 64],
                    ALU.mult, ALU.mult, [pO, sm2[hq % 2], sz_t], [oat])
            cp("pool", pbfA[:, 0:512], oat[:], [oat], [pbfA])
            transpose_store(pbfA[:, 0:512], 4, yT_s, 0, ti, [pbfA])
            dma(u1h[:, :, 1:129], u1T_s.ap[:, t0:t0 + 128].rearrange("(a p) t -> p a t", p=128), [u1T_s.bs[ti]], [u1h])
            lo = t0 - 1 if ti > 0 else 0
            hi = t0 + 128 if ti < NT - 1 else T - 1
            dma(u1h[:, :, 0:1], u1T_s.ap[:, lo:lo + 1].rearrange("(a p) t -> p a t", p=128), [u1T_s.bs[tp]], [u1h], slow=True)
            dma(u1h[:, :, 129:130], u1T_s.ap[:, hi:hi + 1].rearrange("(a p) t -> p a t", p=128), [u1T_s.bs[tn]], [u1h], slow=True)
            dma(u2t[:], u2T_s.ap[:, t0:t0 + 128].rearrange("(a p) t -> p a t", p=128), [u2T_s.bs[ti]], [u2t])
            ts("dve", u1h[:, :, 0:1], u1h[:, :, 0:1], mt[:, 3 + 2 * NT + ti:4 + 2 * NT + ti], ALU.mult, [u1h, mt], [u1h])
            ts("dve", u1h[:, :, 129:130], u1h[:, :, 129:130], mt[:, 3 + 3 * NT + ti:4 + 3 * NT + ti], ALU.mult, [u1h, mt], [u1h])
            for a in range(4):
                ts("dve", cacc[:, a, :], u1h[:, a, 1:129], cvw[:, a * 4 + 1:a * 4 + 2], ALU.mult, [u1h, cvw], [cacc],
                   cvw[:, a * 4 + 3:a * 4 + 4], ALU.add)
                stt(cacc[:, a, :], u1h[:, a, 0:128], cvw[:, a * 4:a * 4 + 1], cacc[:, a, :], ALU.mult, ALU.add, [u1h, cvw, cacc], [cacc])
                stt(cacc[:, a, :], u1h[:, a, 2:130], cvw[:, a * 4 + 2:a * 4 + 3], cacc[:, a, :], ALU.mult, ALU.add, [u1h, cvw, cacc], [cacc])
            tt("dve", trs[:, 4:8, :], cacc[:], u2t[:], ALU.mult, [cacc, u2t], [trs])
            dma(yT_s.ap[512:1024, t0:t0 + 128].rearrange("(a p) t -> p a t", p=128), trs[:, 4:8, :], [trs], [yT_s.bs[ti]])
        chk('A', l)
        out_proj_residual(l, xsrc, xdst, yT_s, w_out_o.ap[e])
        S.barrier()
        chk('O', l)

    chain = [x_in, xs[0], xs[1], xs[0], y_out]
    try:
        for l in range(4):
            if l % 2 == 0:
                even_layer(l, chain[l], chain[l + 1])
            else:
                odd_layer(l, chain[l], chain[l + 1])
    except _Stop:
        S.barrier()
    S.finish_waits([y_out.b, ngla.b, ns5.b, nkv.b] + y_out.bs)
    S.emit()
    st.close()
    return nc, S


def _consts():
    c = np.zeros((128, NCST), np.float32)
    i = np.arange(128)
    s = i[:, None]
    t = i[None, :]
    same = (s // 64) == (t // 64)
    c[:, C_ID:C_ID + 128] = np.eye(128, dtype=np.float32)
    c[:, C_TIF:C_TIF + 128] = np.where(same & (s <= t), -1.0 / 16, 0.0)
    c[:, C_TRF:C_TRF + 128] = np.where(same & (s > t), -1.0 / 16, 0.0)
    c[:, C_TIB:C_TIB + 128] = np.where(same & (s >= t), -1.0 / 16, 0.0)
    c[:, C_TRB:C_TRB + 128] = np.where(same & (s < t), -1.0 / 16, 0.0)
    c[:, C_MF:C_MF + 128] = np.where(same & (s <= t), 1.0, 0.0)
    c[:, C_MB:C_MB + 128] = np.where(same & (s >= t), 1.0, 0.0)
    c[:, C_ONE:C_ONE + 128] = 1.0
    c[:, C_EPS] = EPS
    c[:, C_EPS + 1] = 1.0
    qi = i[:, None]
    kj = i[None, :]
    NEG = -1e30
    c[:, C_BAND:C_BAND + 128] = np.where(kj >= qi, 0.0, NEG)
    c[:, C_BAND + 128:C_BAND + 256] = 0.0
    c[:, C_BAND + 256:C_BAND + 384] = np.where(kj <= qi, 0.0, NEG)
    c[0:64, C_BLK:C_BLK + 128] = 1.0
    c[64:128, C_BLK + 128:C_BLK + 256] = 1.0
    return c


def _rope_table(T, identity):
    tab = np.zeros((T, 128), np.float32)
    if identity:
        tab[:, :64] = 1.0
        return tab
    pos = np.arange(T)
    row = (pos // 64).astype(np.float32)
    col = (pos % 64).astype(np.float32)
    freq = (10000.0 ** (-np.arange(16, dtype=np.float32) / 16)).astype(np.float32)
    ar = row[:, None] * freq
    ac = col[:, None] * freq
    cos = np.concatenate([np.cos(ar), np.cos(ar), np.cos(ac), np.cos(ac)], axis=1)
    sin = np.concatenate([-np.sin(ar), np.sin(ar), -np.sin(ac), np.sin(ac)], axis=1)
    tab[:, :64] = cos
    tab[:, 64:] = sin
    return tab


def _meta(T, sample):
    NT = T // 128
    m = np.zeros((128, 3 + 4 * NT), np.float32)
    NEG = -1e30
    if sample:
        m[:, 0] = 1.0
        m[:, 1] = 1.0
        m[:, 2] = 0.0
        flp = np.zeros(NT); flp[0] = NEG
        fln = np.zeros(NT); fln[-1] = NEG
        cfl = np.ones(NT); cfl[0] = 0
        cfr = np.ones(NT); cfr[-1] = 0
    else:
        m[:, 0] = 0.0
        m[:, 1] = 0.0
        m[:, 2] = NEG
        flp = np.where(np.arange(NT) % 2 == 0, NEG, 0.0)
        fln = np.where(np.arange(NT) % 2 == 1, NEG, 0.0)
        cfl = np.where(np.arange(NT) % 2 == 0, 0.0, 1.0)
        cfr = np.where(np.arange(NT) % 2 == 1, 0.0, 1.0)
    m[:, 3:3 + NT] = flp
    m[:, 3 + NT:3 + 2 * NT] = fln
    m[:, 3 + 2 * NT:3 + 3 * NT] = cfl
    m[:, 3 + 3 * NT:3 + 4 * NT] = cfr
    return m


def _state_layout(a):
    sh = a.shape[:-2]
    b = a.reshape(sh + (16, 2, 64))
    b = np.moveaxis(b, -3, -1)
    return np.ascontiguousarray(b.reshape(sh + (128, 16)))


_NC_CACHE = {}
LAST_RESULTS = None


def run(inputs, T, n_prompt_per_core):
    f = lambda k: np.asarray(inputs[k], dtype=np.float32)
    x_prompt, x_sample, c = f("x_prompt"), f("x_sample"), f("c")
    NS = T // 256
    if T not in _NC_CACHE:
        _NC_CACHE[T] = build(T)[0]
    nc = _NC_CACHE[T]
    shared = {}
    shared["cst"] = _consts()
    shared["norm_w"] = f("norm_w")
    shared["w_ada"] = f("w_ada")
    shared["b_ada"] = f("b_ada")
    shared["w_in_e"] = f("w_in_e")
    shared["w_out_e"] = f("w_out_e")
    w2 = f("gla_w2"); b2 = f("gla_b2")
    w2cat = np.zeros((2, 64, 512), np.float32)
    w2cat[:, 0:16, 0:256] = w2[:, 0]
    w2cat[:, 16:32, 256:512] = w2[:, 1]
    w2cat[:, 32, 0:256] = b2[:, 0]
    w2cat[:, 32, 256:512] = b2[:, 1]
    shared["w2cat"] = w2cat
    shared["onorm"] = f("gla_onorm").reshape(2, 128, 1)
    lam_re, lam_im, log_dt = f("s5_lam_re"), f("s5_lam_im"), f("s5_log_dt")
    ldt = np.broadcast_to(log_dt[..., None], lam_re.shape)
    shared["s5p"] = np.stack([_state_layout(lam_re), _state_layout(lam_im), _state_layout(ldt)], axis=2)
    def expand_b(b):
        out = np.zeros((2, 128, 16, 128), np.float32)
        for g in range(32):
            j, gs = g // 2, g % 2
            k0 = 16 * (g % 8)
            out[:, gs * 64:(gs + 1) * 64, j, k0:k0 + 16] = b[:, g]
        return out.reshape(2, 128, 16 * 128)
    shared["s5b"] = np.stack([expand_b(f("s5_b_re")), expand_b(f("s5_b_im"))], axis=1)
    def expand_c(cc, sign):
        out = np.zeros((2, 128, 16, 32), np.float32)
        for g in range(32):
            j, gs = g // 2, g % 2
            out[:, gs * 64:(gs + 1) * 64, j, gs * 16:(gs + 1) * 16] = np.swapaxes(cc[:, g], 1, 2)
        if sign < 0:
            out = np.negative(out)
        return out.reshape(2, 128, 16 * 32)
    shared["s5c"] = np.stack([expand_c(f("s5_c_re"), 1), expand_c(f("s5_c_im"), -1)], axis=1)
    shared["s5d"] = np.ascontiguousarray(f("s5_d").reshape(2, 4, 128).transpose(0, 2, 1))
    shared["wglu"] = f("s5_w_glu")
    shared["bglu"] = np.ascontiguousarray(f("s5_b_glu").reshape(2, 4, 128).transpose(0, 2, 1))
    shared["w_in_o"] = f("w_in_o")
    shared["w_out_o"] = f("w_out_o")
    shared["qkw"] = np.concatenate([f("q_norm_w"), f("k_norm_w")], axis=1)
    shared["sink"] = f("sink")
    cw = f("conv_w"); cb = f("conv_b")
    cvw = np.zeros((2, 128, 4, 4), np.float32)
    for a in range(4):
        cvw[:, :, a, 0:3] = cw[:, :, a * 128:(a + 1) * 128].transpose(0, 2, 1)
        cvw[:, :, a, 3] = cb[:, a * 128:(a + 1) * 128]
    shared["convw"] = cvw.reshape(2, 128, 16)

    in_maps = []
    n_sample = x_sample.shape[0]
    for core in range(8):
        m = dict(shared)
        if core < 4:
            b = core
            m["x"] = np.ascontiguousarray(x_sample[b])
            cv = c[b]
            m["meta"] = _meta(T, True)
            m["rope"] = _rope_table(T, False)
            m["gla0"] = np.ascontiguousarray(f("state_gla")[b])
            sre = _state_layout(f("state_s5_re")[b]); sim = _state_layout(f("state_s5_im")[b])
            m["s5h0"] = np.stack([sre, sim], axis=2)
            ck = f("cache_k")[b]; cvv = f("cache_v")[b]
            m["ckv"] = np.ascontiguousarray(np.stack([ck.transpose(0, 2, 1, 3), cvv.transpose(0, 2, 1, 3)], axis=1))
        else:
            pc = core - 4
            xx = np.zeros((T, D), np.float32)
            seqs = x_prompt[pc * n_prompt_per_core:(pc + 1) * n_prompt_per_core]
            xx[:n_prompt_per_core * 256] = seqs.reshape(-1, D)
            m["x"] = xx
            cv = f("c_ctx")
            m["meta"] = _meta(T, False)
            m["rope"] = _rope_table(T, True)
            m["gla0"] = np.zeros((2, 2, 4, 64, 128), np.float32)
            m["s5h0"] = np.zeros((2, 2, 2, 128, 16), np.float32)
            m["ckv"] = np.zeros((2, 2, 256, 2, 64), np.float32)
        m["cvec"] = np.ascontiguousarray(cv.reshape(8, 128).T)
        in_maps.append(m)
    res = run_bass_kernel_spmd(nc, in_maps, core_ids=list(range(8)))
    R = res.results
    global LAST_RESULTS
    LAST_RESULTS = R
    BATCH = x_prompt.shape[0]
    y_sample = np.stack([np.asarray(R[b]["y"]) for b in range(4)], axis=0).astype(np.float32)
    y_prompt = np.zeros_like(x_prompt)
    new_gla = np.zeros((BATCH, 2, 2, 4, 64, 128), np.float32)
    new_re = np.zeros((BATCH, 2, 2, 32, 64), np.float32)
    new_im = np.zeros((BATCH, 2, 2, 32, 64), np.float32)
    new_k = np.zeros((BATCH, 2, 2, 256, 64), np.float32)
    new_v = np.zeros((BATCH, 2, 2, 256, 64), np.float32)
    for pc in range(4):
        r = R[4 + pc]
        y = np.asarray(r["y"]); g = np.asarray(r["ngla"]); s5 = np.asarray(r["ns5"]); kvo = np.asarray(r["nkv"])
        s5 = s5.reshape(2, 2, 2, NS, 16, 2, 64)
        for q in range(n_prompt_per_core):
            bi = pc * n_prompt_per_core + q
            y_prompt[bi] = y[q * 256:(q + 1) * 256]
            new_gla[bi] = g[:, :, q]
            for d in range(2):
                sig = q if d == 0 else NS - 1 - q
                new_re[bi, :, d] = s5[:, d, 0, sig].reshape(2, 32, 64)
                new_im[bi, :, d] = s5[:, d, 1, sig].reshape(2, 32, 64)
            kk = kvo[:, :, q * 256:(q + 1) * 256, :].reshape(2, 2, 256, 2, 64)
            new_k[bi] = kk[:, 0].transpose(0, 2, 1, 3)
            new_v[bi] = kk[:, 1].transpose(0, 2, 1, 3)
    return (y_prompt, y_sample, new_gla, new_re, new_im, new_k, new_v)


def kernel(**inputs):
    return run(inputs, 4096, 8)
```

```python
import math
import os
import numpy as np
import concourse.bass as bass
import concourse.mybir as mybir
from concourse.bass_utils import run_bass_kernel_spmd
from contextlib import ExitStack

F32 = mybir.dt.float32
BF16 = mybir.dt.bfloat16
I32 = mybir.dt.int32
ALU = mybir.AluOpType
AF = mybir.ActivationFunctionType
AX = mybir.AxisListType

D = 1024
EIN = 2592
OIN = 3328
EPS = 1e-6
ENGS = ("pe", "act", "dve", "pool", "sp")
SEM_LIMIT = 30000
N_DMA_SEMS = 12


class Buf:
    __slots__ = ("w", "r")

    def __init__(self):
        self.w = None
        self.r = {}


class Sched:
    def __init__(self, nc, same_engine_sync=True):
        self.nc = nc
        self.q = {e: [] for e in ENGS}
        self.epoch = {e: 0 for e in ENGS}
        self.cnt = {}
        self.seen = {e: {} for e in ENGS}
        self.same = same_engine_sync
        self.nosync = set(os.environ.get("KNOSYNC", "").split(","))
        self.semkeys = []
        for e in ENGS:
            self._newkey((e, 0))
        self.dma_pool = {e: [] for e in ENGS}
        self.dma_rr = {e: 0 for e in ENGS}
        self.n_ops = 0

    def _newkey(self, k):
        self.cnt[k] = 0
        self.semkeys.append(k)

    def _engkey(self, e):
        k = (e, self.epoch[e])
        if self.cnt[k] >= SEM_LIMIT:
            self.epoch[e] += 1
            k = (e, self.epoch[e])
            self._newkey(k)
        return k

    def _need(self, eng, waits, tok, is_dma=False):
        if tok is None:
            return
        k, v = tok
        if (not is_dma) and k[0] == eng and (eng == "pe" or eng in self.nosync):
            return
        if self.seen[eng].get(k, 0) >= v:
            return
        if waits.get(k, 0) < v:
            waits[k] = v

    def _deps(self, eng, reads, writes, is_dma):
        waits = {}
        for b in reads:
            self._need(eng, waits, b.w, is_dma)
        for b in writes:
            self._need(eng, waits, b.w, is_dma)
            for k, v in b.r.items():
                self._need(eng, waits, (k, v), is_dma)
        return waits

    def op(self, eng, fn, reads=(), writes=(), extra=None):
        if extra is None:
            waits = self._deps(eng, reads, writes, False)
        else:
            sv = self.nosync
            self.nosync = set(sv) | {eng}
            waits = self._deps(eng, reads, writes, False)
            self.nosync = sv
            for tok in extra:
                if tok is not None:
                    self._need(eng, waits, tok, True)
        for k, v in waits.items():
            self.seen[eng][k] = v
        key = self._engkey(eng)
        self.cnt[key] += 1
        tok = (key, self.cnt[key])
        for b in reads:
            b.r[key] = tok[1]
        for b in writes:
            b.w = tok
            b.r = {}
        self.q[eng].append((fn, list(waits.items()), key, 1))
        self.n_ops += 1
        return tok

    def dma(self, fn, reads=(), writes=(), eng="sp"):
        pool = self.dma_pool[eng]
        if len(pool) < N_DMA_SEMS:
            key = ("dma", eng, len(pool), 0)
            self._newkey(key)
            pool.append(key)
        else:
            idx = self.dma_rr[eng] % N_DMA_SEMS
            self.dma_rr[eng] += 1
            key = pool[idx]
            if self.cnt[key] >= SEM_LIMIT:
                key = ("dma", eng, idx, key[3] + 1)
                self._newkey(key)
                pool[idx] = key
        waits = self._deps(eng, reads, writes, True)
        if self.cnt[key] > 0:
            self._need(eng, waits, (key, self.cnt[key]), True)
        for k, v in waits.items():
            self.seen[eng][k] = v
        self.cnt[key] += 16
        tok = (key, self.cnt[key])
        for b in reads:
            b.r[key] = tok[1]
        for b in writes:
            b.w = tok
            b.r = {}
        self.q[eng].append((fn, list(waits.items()), key, 16))
        self.n_ops += 1

    def barrier(self):
        for eng in ENGS:
            waits = {}
            for k, v in self.cnt.items():
                if v > 0 and self.seen[eng].get(k, 0) < v:
                    waits[k] = v
                    self.seen[eng][k] = v
            self.q[eng].append((None, list(waits.items()), None, 0))

    def finish_waits(self, bufs, eng="sp"):
        waits = {}
        for b in bufs:
            self._need(eng, waits, b.w, True)
        self.q[eng].append((None, list(waits.items()), None, 0))

    def emit(self):
        nc = self.nc
        with ExitStack() as st:
            sems = {}
            for i, k in enumerate(self.semkeys):
                sems[k] = st.enter_context(nc.semaphore("s%d" % i))
            block = st.enter_context(nc.Block())

            def runner(ename):
                def run(e):
                    for fn, waits, key, inc in self.q[ename]:
                        for k, v in waits:
                            e.wait_ge(sems[k], v)
                        if fn is not None:
                            fn(e).then_inc(sems[key], inc)
                return run

            block.tensor(runner("pe"))
            block.scalar(runner("act"))
            block.vector(runner("dve"))
            block.gpsimd(runner("pool"))
            block.sync(runner("sp"))


class TL:
    def __init__(self, t):
        self.t = t
        self.b = Buf()

    def __getitem__(self, k):
        return self.t[k]


class View:
    def __init__(self, ap):
        self.ap = ap
        self.b = Buf()

    def __getitem__(self, k):
        return self.ap[k]


class DT:
    def __init__(self, ap, nb=1):
        self.ap = ap
        self.bs = [Buf() for _ in range(nb)]
        self.b = self.bs[0]


C_ID, C_TIF, C_TRF, C_TIB, C_TRB, C_MF, C_MB, C_ONE, C_BAND = [128 * i for i in range(9)]
C_BLK = C_BAND + 384
C_EPS = C_BLK + 256
NCST = C_EPS + 2


class _Stop(Exception):
    pass


def build(T, stop=None):
    import os
    stop = stop or os.environ.get("KSTOP")
    NT = T // 128
    NS = T // 256
    NB = T // 512
    nc = bass.Bass("TRN2", target_bir_lowering=False)
    S = Sched(nc, same_engine_sync=bool(int(os.environ.get("KSAME", "0"))))
    st = ExitStack()

    def din(name, shape, dt=F32):
        return DT(nc.dram_tensor(name, list(shape), dt, kind="ExternalInput").ap())

    def dout(name, shape, dt=F32):
        return DT(nc.dram_tensor(name, list(shape), dt, kind="ExternalOutput").ap())

    dbg = bool(os.environ.get("KDBG"))

    def dscr(name, shape, dt=F32, nb=1):
        return DT(nc.dram_tensor(name, list(shape), dt, kind="ExternalOutput" if dbg else "Internal").ap(), nb)

    ARENA_WORDS = 34000
    G_BASE = 29000
    arena_t = st.enter_context(nc.sbuf_tensor("arena", [128, ARENA_WORDS], F32))
    goff = {"G": G_BASE}

    def sb(name, shape, dt=F32, grp=None):
        if grp is None:
            return TL(st.enter_context(nc.sbuf_tensor(name, list(shape), dt)))
        n = 1
        for d_ in shape[1:]:
            n *= d_
        words = n if dt in (F32, I32) else (n + 1) // 2
        off = goff.get(grp, 0)
        goff[grp] = off + words
        assert goff[grp] <= ARENA_WORDS, (grp, name, goff[grp])
        assert grp != "S" or goff[grp] <= G_BASE, (grp, name, goff[grp])
        ap = arena_t[:, off:off + words]
        if dt != F32:
            ap = ap.bitcast(dt)[:, :n]
        if len(shape) > 2:
            names = " ".join("d%d" % i for i in range(len(shape) - 1))
            kw = {"d%d" % i: shape[i + 1] for i in range(len(shape) - 1)}
            ap = ap.rearrange("p (%s) -> p %s" % (names, names), **kw)
        if shape[0] < 128:
            ap = ap[0:shape[0]]
        return View(ap)

    def ps(name, shape, dt=F32):
        return TL(st.enter_context(nc.psum_tensor(name, list(shape), dt)))

    def bl(xs):
        return [x.b if hasattr(x, "b") else x for x in xs]

    def mm(out, lhsT, rhs, start, stop, R, W):
        S.op("pe", lambda e: e.matmul(out, lhsT=lhsT, rhs=rhs, start=start, stop=stop), bl(R), bl(W))

    def tr(out, in_, ident, R, W):
        S.op("pe", lambda e: e.transpose(out, in_, ident), bl(R), bl(W))

    def act(out, in_, func, R, W, **kw):
        S.op("act", lambda e: e.activation(out=out, in_=in_, func=func, **kw), bl(R), bl(W))

    def tt(eng, out, a, b, op, R, W):
        S.op(eng, lambda e: e.tensor_tensor(out=out, in0=a, in1=b, op=op), bl(R), bl(W))

    def ts(eng, out, a, s1, op0, R, W, s2=None, op1=None):
        if op1 is None:
            S.op(eng, lambda e: e.tensor_scalar(out=out, in0=a, scalar1=s1, scalar2=None, op0=op0), bl(R), bl(W))
        else:
            S.op(eng, lambda e: e.tensor_scalar(out=out, in0=a, scalar1=s1, scalar2=s2, op0=op0, op1=op1), bl(R), bl(W))

    def stt(out, in0, scalar, in1, op0, op1, R, W, extra=None):
        return S.op("dve", lambda e: e.scalar_tensor_tensor(out=out, in0=in0, scalar=scalar, in1=in1, op0=op0, op1=op1),
                    bl(R), bl(W), extra=extra)

    def cp(eng, out, in_, R, W):
        if eng == "act":
            S.op("act", lambda e: e.copy(out=out, in_=in_), bl(R), bl(W))
        else:
            S.op(eng, lambda e: e.tensor_copy(out=out, in_=in_), bl(R), bl(W))

    def mset(eng, ap, val, W):
        S.op(eng, lambda e: e.memset(ap, val), [], bl(W))

    def dma(out, in_, R, W, eng="sp", slow=False):
        if slow:
            S.dma(lambda e: e.dma_start(out=out, in_=in_, allow_slow_non_contiguous=True), bl(R), bl(W), eng=eng)
        else:
            S.dma(lambda e: e.dma_start(out=out, in_=in_), bl(R), bl(W), eng=eng)

    x_in = din("x", [T, D])
    cvec = din("cvec", [128, 8])
    cst = din("cst", [128, NCST])
    NMETA = 3 + 4 * NT
    meta = din("meta", [128, NMETA])
    rope = din("rope", [T, 128])
    gla0 = din("gla0", [2, 2, 4, 64, 128])
    s5h0 = din("s5h0", [2, 2, 2, 128, 16])
    ckv = din("ckv", [2, 2, 256, 2, 64])
    norm_w = din("norm_w", [4, D])
    w_ada = din("w_ada", [4, D, 3 * D])
    b_ada = din("b_ada", [4, 3 * D])
    w_in_e = din("w_in_e", [2, D, EIN])
    w_out_e = din("w_out_e", [2, D, D])
    w2cat = din("w2cat", [2, 64, 512])
    onorm = din("onorm", [2, 128, 1])
    s5p = din("s5p", [2, 2, 3, 128, 16])
    s5b = din("s5b", [2, 2, 128, 16 * 128])
    s5c = din("s5c", [2, 2, 128, 16 * 32])
    s5d = din("s5d", [2, 128, 4])
    wglu = din("wglu", [2, 512, 512])
    bglu = din("bglu", [2, 128, 4])
    w_in_o = din("w_in_o", [2, D, OIN])
    w_out_o = din("w_out_o", [2, D, D])
    qkw = din("qkw", [2, 128])
    sinkv = din("sink", [2, 8])
    convw = din("convw", [2, 128, 16])

    y_out = dout("y", [T, D])
    ngla = dout("ngla", [2, 2, NS, 4, 64, 128])
    ns5 = dout("ns5", [2, 2, 2, NS * 16, 128])
    nkv = dout("nkv", [2, 2, T, 128])

    xs = [dscr("xs0", [T, D], nb=NT), dscr("xs1", [T, D], nb=NT)]
    qkT_s = dscr("qkT", [512, T], BF16, NT)
    kv_s = dscr("kvtm", [T, 768], BF16, NT)
    sp_s = dscr("sp", [T, 512], F32, NT)
    zgT_s = dscr("zgT", [512, T], BF16, NT)
    uT_s = dscr("uT", [512, T], BF16, NT)
    z5T_s = dscr("z5T", [512, T], BF16, NT)
    oF_s = dscr("oF", [512, T], F32, NT)
    y5T_s = dscr("y5T", [512, T], F32, 16)
    yT_s = dscr("yT", [1024, T], BF16, NT)
    qT_s = dscr("qTo", [512, T], BF16, NT)
    kT_s = dscr("kTo", [256, T], BF16, NT)
    v_s = dscr("vo", [T, 128], BF16, NT)
    sz_s = dscr("szo", [T, 512], F32, NT)
    u1T_s = dscr("u1T", [512, T], F32, NT)
    u2T_s = dscr("u2T", [512, T], F32, NT)

    cs = sb("cs", [128, NCST])
    csb = sb("csb", [128, NCST], BF16)
    mt = sb("mt", [128, NMETA])
    wbf = sb("wbf", [128, 8, OIN], BF16, grp="P")
    wob = sb("wob", [128, 8, D], BF16, grp="O")
    wst1 = sb("wst0", [128, OIN], grp="P")
    wst = [wst1, wst1]
    wstO = sb("wstO", [128, D], grp="O")
    wstS = sb("wstS", [128, 512], grp="SP")
    pbfA = sb("pbfA", [128, 512], BF16, grp="A")
    modbc = sb("modbc", [128, 3 * D])
    Abc = sb("Abc", [128, D])
    sc8 = sb("sc8", [128, 8])
    screp = sb("screp", [128, 8, 128])
    xts = [sb("xt0", [128, D]), sb("xt1", [128, D])]
    xt = xts[0]
    hb = sb("hb", [128, D], BF16, grp="P")
    hT = sb("hT", [128, 8, 128], BF16, grp="P")
    small = sb("small", [128, 16])
    small2 = sb("small2", [128, 16])
    proj = sb("proj", [128, OIN], grp="P")
    pbf = sb("pbf", [128, OIN], BF16, grp="P")
    trs = sb("trs", [128, 8, 128], BF16)
    trf = sb("trf", [128, 4, 128])
    w1 = sb("w1", [128, 1024])
    w2 = sb("w2", [128, 1024])
    w3 = sb("w3", [128, 1024])
    lrT = sb("lrT", [64, 128], BF16, grp="P")
    w2c = sb("w2c", [64, 512], BF16, grp="P")
    w2cf = sb("w2cf", [64, 512], grp="P")

    projB = sb("projB", [128, OIN], grp="P")
    pbfB = sb("pbfB", [128, OIN], BF16, grp="P")
    proj2 = [proj, projB]
    pbf2 = [pbf, pbfB]
    pA = ps("pA", [128, 512]); pB = ps("pB", [128, 512]); pC = ps("pC", [128, 512])
    pD = ps("pD", [128, 512]); pE = ps("pE", [128, 512]); pF = ps("pF", [128, 512])
    pT = ps("pT", [128, 1024], BF16)
    pTf = ps("pTf", [128, 512])

    class _PT32:
        def __init__(self):
            self.ap = pT[:, :].bitcast(F32)
            self.b = pT.b

        def __getitem__(self, k):
            return self.ap[k]
    pT32 = _PT32()

    class _PTB:
        def __init__(self):
            self.ap = pE[:, :].bitcast(BF16)
            self.b = pE.b

        def __getitem__(self, k):
            return self.ap[k]
    pTb = _PTB()

    class _PTF16:
        def __init__(self):
            self.ap = pTf[:, :].bitcast(BF16)
            self.b = pTf.b

        def __getitem__(self, k):
            return self.ap[k]
    pTf16 = _PTF16()
    trsb = sb("trsb", [128, 4, 128], BF16)
    ident_b = csb[:, C_ID:C_ID + 128]
    ident_f = cs[:, C_ID:C_ID + 128]
    mflag = mt[:, 0:1]

    dma(cs[:], cst.ap[:, :], [], [cs])
    cp("dve", csb[:], cs[:], [cs], [csb])
    dma(mt[:], meta.ap[:, :], [], [mt])
    dma(sc8[:], cvec.ap[:, :], [], [sc8])
    act(sc8[:], sc8[:], AF.Silu, [sc8], [sc8])
    for kt in range(8):
        cp("dve", screp[:, kt, :], sc8[:, kt:kt + 1].to_broadcast([128, 128]), [sc8], [screp])
    band = sb("band", [128, 384])
    ts("dve", band[:], cs[:, C_BAND:C_BAND + 384], mt[:, 1:2], ALU.mult, [cs, mt], [band])

    evac_rr = [0]

    def evac(out, in_, R, W):
        e = ("act", "dve")[evac_rr[0] % 2]
        evac_rr[0] += 1
        cp(e, out, in_, R, W)

    def load_weight_bf16(dst, src_ap, ncols, wsx=None):
        wsx = wsx or wst1
        for kt in range(8):
            dma(wsx[:, :ncols], src_ap[kt * 128:(kt + 1) * 128, :], [], [wsx])
            cp("pool", dst[:, kt, :ncols], wsx[:, :ncols], [wsx], [dst])

    def adaln(l):
        banks = [pA, pB, pC, pD, pE, pF]
        for kt in range(8):
            wsx = wst[kt % 2]
            dma(wsx[:, :3 * D], w_ada.ap[l, kt * 128:(kt + 1) * 128, :], [], [wsx])
            for nb_ in range(6):
                mm(banks[nb_][:, :], screp[:, kt, :], wsx[:, nb_ * 512:(nb_ + 1) * 512], kt == 0, kt == 7,
                   [screp, wsx], [banks[nb_]])
        tmpbc = wst1
        dma(tmpbc[:, :3 * D], b_ada.ap[l:l + 1, :].partition_broadcast(128), [], [tmpbc])
        for nb_ in range(6):
            tt("dve", modbc[:, nb_ * 512:(nb_ + 1) * 512], banks[nb_][:, :], tmpbc[:, nb_ * 512:(nb_ + 1) * 512],
               ALU.add, [banks[nb_], tmpbc], [modbc])
        dma(tmpbc[:, :D], norm_w.ap[l:l + 1, :].partition_broadcast(128), [modbc], [tmpbc])
        stt(Abc[:], modbc[:, D:2 * D], 1.0, tmpbc[:, :D], ALU.add, ALU.mult, [modbc, tmpbc], [Abc])

    def load_x(ti, xsrc):
        if ti >= NT:
            return
        t0 = ti * 128
        xt = xts[ti % 2]
        dma(xt[:], xsrc.ap[t0:t0 + 128, :], [xsrc.bs[ti] if len(xsrc.bs) > 1 else xsrc.b], [xt])

    def norm_mod_T(l, ti, xsrc):
        if ti == 0:
            load_x(0, xsrc)
        load_x(ti + 1, xsrc)
        xt = xts[ti % 2]
        mset("dve", small2[:, 0:1], 0.0, [small2])
        act(hb[:], xt[:], AF.Square, [xt], [hb, small2], accum_out=small2[:, 0:1])
        act(small2[:, 1:2], small2[:, 0:1], AF.Ln, [small2], [small2], scale=1.0 / D, bias=cs[:, C_EPS:C_EPS + 1])
        act(small2[:, 2:3], small2[:, 1:2], AF.Exp, [small2], [small2], scale=-0.5)
        stt(w4[:, :D], xt[:], small2[:, 2:3], Abc[:], ALU.mult, ALU.mult, [xt, small2, Abc], [w4])
        tt("dve", hb[:], w4[:, :D], modbc[:, 0:D], ALU.add, [w4, modbc], [hb])
        for kt in range(8):
            tr(pT[:, kt * 128:(kt + 1) * 128], hb[:, kt * 128:(kt + 1) * 128], ident_b, [hb, csb], [pT])
        cp("act", hT[:].rearrange("p a t -> p (a t)"), pT[:, :], [pT], [hT])

    def in_proj(ncols, dst=None):
        dst = dst or proj
        c0 = 0
        k = 0
        while c0 < ncols:
            cw = min(512, ncols - c0)
            pp = (pA, pB)[k % 2]
            for kt in range(8):
                mm(pp[:, :cw], hT[:, kt, :], wbf[:, kt, c0:c0 + cw], kt == 0, kt == 7, [hT, wbf], [pp])
            evac(dst[:, c0:c0 + cw], pp[:, :cw], [pp], [dst])
            c0 += cw
            k += 1

    ts_rr = [0]

    def transpose_store(src_bf_ap, nft, dst, row0, ti, R):
        t0 = ti * 128
        assert nft <= 4
        k = ts_rr[0] % 2
        ts_rr[0] += 1
        pX, tX = (pT, pTb)[k], (trs, trsb)[k]
        for a in range(nft):
            tr(pX[:, a * 128:(a + 1) * 128], src_bf_ap[:, a * 128:(a + 1) * 128], ident_b, R + [csb], [pX])
        cp(("act", "dve")[k], tX[:, :nft, :].rearrange("p a t -> p (a t)"), pX[:, :nft * 128], [pX], [tX])
        dma(dst.ap[row0:row0 + nft * 128, t0:t0 + 128].rearrange("(a p) t -> p a t", p=128), tX[:, :nft, :],
            [tX], [dst.bs[ti]])

    def transpose_store_f32(src_ap, nft, dst, row0, ti, R):
        t0 = ti * 128
        for a in range(nft):
            tr(pTf[:, a * 128:(a + 1) * 128], src_ap[:, a * 128:(a + 1) * 128], ident_f, R + [cs], [pTf])
        cp("act", trf[:, :nft, :].rearrange("p a t -> p (a t)"), pTf[:, :nft * 128], [pTf], [trf])
        dma(dst.ap[row0:row0 + nft * 128, t0:t0 + 128].rearrange("(a p) t -> p a t", p=128), trf[:, :nft, :],
            [trf], [dst.bs[ti]])

    def out_proj_residual(l, xsrc, xdst, ydt, wsrc):
        S.barrier()
        load_weight_bf16(wob, wsrc, D, wstO)
        def loads(ti):
            if ti >= NT:
                return
            t0 = ti * 128
            p = ti % 2
            yt = (sb_yt, sb_yt2)[p]
            dma(yt[:], ydt.ap[:, t0:t0 + 128].rearrange("(a p) t -> p a t", p=128), [ydt.bs[ti]], [yt])
            dma(xts[p][:], xsrc.ap[t0:t0 + 128, :], [xsrc.bs[ti] if len(xsrc.bs) > 1 else xsrc.b], [xts[p]])

        loads(0)
        for ti in range(NT):
            t0 = ti * 128
            p = ti % 2
            loads(ti + 1)
            yt, xt = (sb_yt, sb_yt2)[p], xts[p]
            o1, o2 = ((w1, w2), (w3, w4))[p]
            for cb in range(2):
                pp = ((pA, pB), (pC, pD))[p][cb]
                for kt in range(8):
                    mm(pp[:, :], yt[:, kt, :], wob[:, kt, cb * 512:(cb + 1) * 512], kt == 0, kt == 7, [yt, wob], [pp])
                tt("dve", o1[:, cb * 512:(cb + 1) * 512], pp[:, :], modbc[:, 2 * D + cb * 512:2 * D + (cb + 1) * 512],
                   ALU.mult, [pp, modbc], [o1])
            tt("dve", o2[:, :D], o1[:, :D], xt[:], ALU.add, [o1, xt], [o2])
            dma(xdst.ap[t0:t0 + 128, :], o2[:, :D], [o2], [xdst.bs[ti] if len(xdst.bs) > 1 else xdst.b])

    sb_yt = sb("yt", [128, 8, 128], BF16)
    sb_yt2 = sb("yt2", [128, 8, 128], BF16)
    w4 = sb("w4", [128, 1024])

    qk_t = sb("qk_t", [128, 4, 128], BF16, grp="G")
    kv_t = sb("kv_t", [128, 768], BF16, grp="G")
    sp_t = sb("sp_t", [128, 256], grp="G")
    E1 = sb("E1", [128, 2, 128], grp="G")
    E2 = sb("E2", [128, 2, 128], grp="G")
    E3 = sb("E3", [128, 256], grp="G")
    qbT = sb("qbT", [128, 2, 128], BF16, grp="G")
    kbT = sb("kbT", [128, 2, 128], BF16, grp="G")
    kd = sb("kd", [128, 256], BF16, grp="G")
    attm = sb("attm", [128, 4, 128], BF16, grp="G")
    Sst = sb("Sst", [128, 2, 256], grp="G")
    Sbf = sb("Sbf", [128, 2, 256], BF16, grp="G")
    Sbf1 = sb("Sbf1", [128, 2, 256], BF16, grp="G")
    o_t = sb("o_t", [128, 4, 128], grp="G")
    oF_t = sb("oF_t", [128, 4, 128], grp="G")
    zg_t = sb("zg_t", [128, 4, 128], BF16, grp="G")
    osq = sb("osq", [128, 512], BF16, grp="G")
    onc = sb("onc", [128, 1])
    TX = max(T, 2048)
    Xs = [sb("Xf", [128, 2, TX], grp="S"), sb("Xb", [128, 2, TX], grp="S")]
    u_ft = sb("u_ft", [128, T], BF16, grp="S")
    BT = sb("BT", [128, 2, 2, 16, 128], BF16, grp="S")
    CTm = sb("CTm", [128, 2, 16, 32], grp="S")
    Xblk = [[Buf() for _ in range(max(NB, 1))] for _ in range(2)]

    class _Alias:
        def __init__(self, base, shape):
            self.ap = base.ap.rearrange("p c t -> p (c t)")[:, 0:4096].rearrange("p (a j k) -> p a j k", a=2, j=16)
            self.b = base.b

        def __getitem__(self, k):
            return self.ap[k]
    Bx = _Alias(Xs[0], None)
    Bbar = _Alias(Xs[1], None)
    s5par = sb("s5par", [128, 3, 16], grp="S")
    h0t = sb("h0t", [128, 2, 2, 16], grp="S")
    NPW = 18
    PW = sb("PW", [128, 2, NPW, 3, 16], grp="S")
    PWm = sb("PWm", [128, 2, NPW, 3, 16], grp="S")
    s5tmp = sb("s5tmp", [128, 12, 16], grp="S")
    s5i = sb("s5i", [128, 16], I32, grp="S")
    STG = sb("STG", [128, 2 * 2 * NS * 16], grp="S")
    stgT = sb("stgT", [128, 128], grp="S")
    y5st = sb("y5st", [32, 512], grp="S")
    dcol = sb("dcol", [128, 4], grp="SP")
    bgl = sb("bgl", [128, 4], grp="SP")
    wgb = sb("wgb", [128, 4, 512], BF16, grp="SP")
    y5b = sb("y5b", [128, 4, 512], grp="SP")
    ub = sb("ub", [128, 4, 512], BF16, grp="SP")
    z5b = sb("z5b", [128, 4, 512], BF16, grp="SP")
    zz = sb("zz", [128, 4, 512], grp="SP")
    zzb = sb("zzb", [128, 4, 512], BF16, grp="SP")
    sg = sb("sg", [128, 4, 512], grp="SP")
    y5o = sb("y5o", [128, 4, 512], BF16, grp="SP")

    pw_exps = list(range(1, 9)) + [16, 24, 32, 40, 48, 56, 64, 128, 192, 256]
    pidx = {m: i for i, m in enumerate(pw_exps)}
    assert len(pw_exps) <= NPW

    def s5_post_setup(e):
        dma(dcol[:], s5d.ap[e], [], [dcol])
        dma(bgl[:], bglu.ap[e], [], [bgl])
        for kt in range(4):
            dma(wstS[:, :512], wglu.ap[e, kt * 128:(kt + 1) * 128, :], [], [wstS])
            cp("pool", wgb[:, kt, :], wstS[:, :512], [wstS], [wgb])

    def s5_setup(e):
        dma(h0t[:], s5h0.ap[e].rearrange("d c p j -> p d c j"), [], [h0t])
        dma(CTm[:].rearrange("p a j o -> p a (j o)"), s5c.ap[e].rearrange("a p x -> p a x"), [], [CTm])
        dma(Bx[:].rearrange("p a j k -> p a (j k)"), s5b.ap[e].rearrange("a p x -> p a x"), [], [Bx])
        for d in range(2):
            dma(s5par[:], s5p.ap[e, d].rearrange("a p j -> p a j"), [], [s5par])
            lre, lim, ldt = s5par[:, 0, :], s5par[:, 1, :], s5par[:, 2, :]
            tmp = lambda i: s5tmp[:, i, :]
            R = [s5par, s5tmp]
            W = [s5tmp]
            act(tmp(0), ldt, AF.Exp, R, W)
            tt("dve", tmp(1), lre, tmp(0), ALU.mult, R, W)
            tt("dve", tmp(2), lim, tmp(0), ALU.mult, R, W)
            act(tmp(3), tmp(1), AF.Exp, R, W)
            for (dst, shift) in ((4, 0.0), (5, math.pi / 2)):
                ts("dve", tmp(6), tmp(2), shift, ALU.add, R, W, 1.0 / (2 * math.pi), ALU.mult)
                cp("dve", s5i[:], tmp(6), R, [s5i])
                cp("dve", tmp(7), s5i[:], [s5i], W)
                ts("dve", tmp(6), tmp(2), shift, ALU.add, R, W)
                stt(tmp(6), tmp(7), -2 * math.pi, tmp(6), ALU.mult, ALU.add, R, W)
                ts("dve", tmp(6), tmp(6), math.pi, ALU.min, R, W, -math.pi, ALU.max)
                act(tmp(dst), tmp(6), AF.Sin, R, W)
            P = lambda m, c: PW[:, d, pidx[m], c, :]
            RW = [PW, s5tmp]
            tt("dve", P(1, 0), tmp(3), tmp(5), ALU.mult, RW, [PW])
            tt("dve", P(1, 1), tmp(3), tmp(4), ALU.mult, RW, [PW])
            tt("dve", tmp(6), lre, lre, ALU.mult, R, W)
            tt("dve", tmp(7), lim, lim, ALU.mult, R, W)
            tt("dve", tmp(6), tmp(6), tmp(7), ALU.add, R, W)
            S.op("dve", lambda e_, o=tmp(6): e_.reciprocal(out=o, in_=o), bl(R), bl(W))
            ts("dve", tmp(7), P(1, 0), -1.0, ALU.add, RW, W)
            tt("dve", tmp(8), tmp(7), lre, ALU.mult, R, W)
            tt("dve", tmp(9), P(1, 1), lim, ALU.mult, RW + [s5par], W)
            tt("dve", tmp(8), tmp(8), tmp(9), ALU.add, R, W)
            tt("dve", tmp(8), tmp(8), tmp(6), ALU.mult, R, W)
            tt("dve", tmp(9), P(1, 1), lre, ALU.mult, RW + [s5par], W)
            tt("dve", tmp(10), tmp(7), lim, ALU.mult, R, W)
            tt("dve", tmp(9), tmp(9), tmp(10), ALU.subtract, R, W)
            tt("dve", tmp(9), tmp(9), tmp(6), ALU.mult, R, W)
            crb = s5tmp[:, 8, :].unsqueeze(2).to_broadcast([128, 16, 128])
            cib = s5tmp[:, 9, :].unsqueeze(2).to_broadcast([128, 16, 128])
            RB = [Bx, s5tmp, Bbar]
            tt("dve", Bbar[:, 0], Bx[:, 0], crb, ALU.mult, RB, [Bbar])
            tt("dve", Bbar[:, 1], Bx[:, 1], cib, ALU.mult, RB, [Bbar])
            tt("dve", Bbar[:, 0], Bbar[:, 0], Bbar[:, 1], ALU.subtract, RB, [Bbar])
            tt("dve", Bbar[:, 1], Bx[:, 1], crb, ALU.mult, RB, [Bbar])
            for j in range(16):
                stt(Bbar[:, 1, j, :], Bx[:, 0, j, :], s5tmp[:, 9, j:j + 1], Bbar[:, 1, j, :], ALU.mult, ALU.add, RB, [Bbar])
            for c in range(2):
                for j4 in range(4):
                    for jj in range(4):
                        j = j4 * 4 + jj
                        tr(pTf[:, jj * 128:(jj + 1) * 128], Bbar[:, c, j, :], ident_f, [Bbar, cs], [pTf])
                    cp("act", BT[:, d, c, j4 * 4:(j4 + 1) * 4, :].rearrange("p j k -> p (j k)"), pTf[:, :], [pTf], [BT])
            def cmul(mo, ma, mb_):
                a_re, a_im = P(ma, 0), P(ma, 1)
                b_re, b_im = P(mb_, 0), P(mb_, 1)
                tt("dve", tmp(10), a_im, b_im, ALU.mult, RW, W)
                tt("dve", tmp(11), a_re, b_re, ALU.mult, RW, W)
                tt("dve", tmp(6), a_re, b_im, ALU.mult, RW, W)
                tt("dve", tmp(7), a_im, b_re, ALU.mult, RW, W)
                tt("dve", P(mo, 0), tmp(11), tmp(10), ALU.subtract, RW, [PW])
                tt("dve", P(mo, 1), tmp(6), tmp(7), ALU.add, RW, [PW])
            for m in range(2, 9):
                cmul(m, m - 1, 1)
            for m in range(16, 65, 8):
                cmul(m, m - 8, 8)
            cmul(128, 64, 64)
            cmul(192, 128, 64)
            cmul(256, 192, 64)
            ts("dve", PW[:, d, :, 2, :], PW[:, d, :, 1, :], -1.0, ALU.mult, [PW], [PW])
            ts("dve", PWm[:, d], PW[:, d], mflag, ALU.mult, [PW, mt], [PWm])

    def s5_view(X, comp, start, step, count, rev, inner=None):
        base = X[:, comp, 0:1]
        off = base.offset
        pstride = base.ap[0][0]
        if rev:
            o = off + (T - 1 - start)
            dims = [[pstride, 128], [-step, count]]
            if inner is not None:
                dims.append([-inner[0], inner[1]])
        else:
            o = off + start
            dims = [[pstride, 128], [step, count]]
            if inner is not None:
                dims.append([inner[0], inner[1]])
        return bass.AP(base.tensor, o, dims)

    def s5_scan(X, d, j, rev, fence0):
        XW = Xblk[d]
        XB = XW + [PW, PWm]
        stt_ = {"fence": fence0, "C": None, "D": None}

        def cmac(tgt, src, pw_tile, m, chain):
            pr = pw_tile[:, d, pidx[m], 0, j:j + 1]
            pi = pw_tile[:, d, pidx[m], 1, j:j + 1]
            pn = pw_tile[:, d, pidx[m], 2, j:j + 1]
            f = stt_["fence"]
            pc, pd = (stt_["C"], stt_["D"]) if chain else (None, None)
            ta = stt(tgt(0), src(0), pr, tgt(0), ALU.mult, ALU.add, XB, XW, extra=[f, pc])
            yield
            tb = stt(tgt(1), src(0), pi, tgt(1), ALU.mult, ALU.add, XB, XW, extra=[f, pc])
            yield
            tc = stt(tgt(0), src(1), pn, tgt(0), ALU.mult, ALU.add, XB, XW, extra=[f, pd, ta])
            yield
            td = stt(tgt(1), src(1), pr, tgt(1), ALU.mult, ALU.add, XB, XW, extra=[f, pd, tb])
            yield
            stt_["C"], stt_["D"], stt_["last"] = tc, td, td

        def fence():
            stt_["fence"] = stt_.get("last", stt_["fence"])
            stt_["C"] = stt_["D"] = None

        for (s, K) in ((1, 8), (8, 8), (64, 4)):
            n = T // (s * K)
            fence()
            for jj in range(1, K):
                tgt = lambda c, s=s, K=K, jj=jj, n=n: s5_view(X, c, (jj + 1) * s - 1, K * s, n, rev)
                src = lambda c, s=s, K=K, jj=jj, n=n: s5_view(X, c, jj * s - 1, K * s, n, rev)
                yield from cmac(tgt, src, PW, s, True)
        fence()
        for sg_ in range(1, NS):
            tgt = lambda c, sg_=sg_: s5_view(X, c, 256 * (sg_ + 1) - 1, 1, 1, rev)
            src = lambda c, sg_=sg_: s5_view(X, c, 256 * sg_ - 1, 1, 1, rev)
            yield from cmac(tgt, src, PWm, 256, True)
        fence()
        if NS > 1:
            for jj in range(3):
                tgt = lambda c, jj=jj: s5_view(X, c, 256 + 64 * (jj + 1) - 1, 256, NS - 1, rev)
                src = lambda c: s5_view(X, c, 255, 256, NS - 1, rev)
                yield from cmac(tgt, src, PWm, 64 * (jj + 1), False)
        for (s, K, nin) in ((8, 8, 4), (1, 8, 32)):
            fence()
            for jj in range(K - 1):
                m = s * (jj + 1)
                tgt = lambda c, s=s, K=K, jj=jj, nin=nin: s5_view(X, c, s * K + (jj + 1) * s - 1, 256, NS, rev, inner=(s * K, nin - 1))
                src = lambda c, s=s, K=K, nin=nin: s5_view(X, c, s * K - 1, 256, NS, rev, inner=(s * K, nin - 1))
                yield from cmac(tgt, src, PW, m, False)
                if NS > 1:
                    tgt = lambda c, s=s, jj=jj: s5_view(X, c, 256 + (jj + 1) * s - 1, 256, NS - 1, rev)
                    src = lambda c: s5_view(X, c, 255, 256, NS - 1, rev)
                    yield from cmac(tgt, src, PWm, m, False)

    def chk(tag, l):
        if stop == "%s%d" % (tag, l):
            raise _Stop()

    def chk2(tag):
        if stop == tag:
            raise _Stop()

    def even_layer(l, xsrc, xdst):
        e = l // 2
        load_weight_bf16(wbf, w_in_e.ap[e], EIN)
        chk2("W")
        adaln(l)
        chk2("ADA")
        dma(w2cf[:], w2cat.ap[e], [], [w2cf])
        cp("dve", w2c[:], w2cf[:], [w2cf], [w2c])
        dma(onc[:], onorm.ap[e], [], [onc])
        mset("dve", lrT[:], 0.0, [lrT])
        mset("dve", lrT[32:33, :], 1.0, [lrT])
        def front_e(ti):
            norm_mod_T(l, ti, xsrc)
            in_proj(EIN, pbf2[ti % 2])

        front_e(0)
        for ti in range(NT):
            t0 = ti * 128
            if ti + 1 < NT:
                front_e(ti + 1)
            pbf_t = pbf2[ti % 2]
            transpose_store(pbf_t[:, 0:512], 4, qkT_s, 0, ti, [pbf_t])
            chk2("TS")
            dma(kv_s.ap[t0:t0 + 128, :], pbf_t[:, 256:1024], [pbf_t], [kv_s.bs[ti]])
            chk2("KV")
            tr(pT[:32, 0:128], pbf_t[:, 1024:1056], ident_b, [pbf_t, csb], [pT])
            cp("act", lrT[0:32, :], pT[:32, 0:128], [pT], [lrT])
            mm(pC[:, :], lrT[:, :], w2c[:, :], True, True, [lrT, w2c], [pC])
            act(w3[:, :512], pC[:, :], AF.Exp, [pC], [w3], scale=-1.0)
            act(w3[:, 512:1024], w3[:, :512], AF.Ln, [w3], [w3], bias=cs[:, C_ONE:C_ONE + 1])
            dma(sp_s.ap[t0:t0 + 128, :], w3[:, 512:1024], [w3], [sp_s.bs[ti]])
            chk2("GATE")
            transpose_store(pbf_t[:, 1056:1568], 4, zgT_s, 0, ti, [pbf_t])
            transpose_store(pbf_t[:, 1568:2080], 4, uT_s, 0, ti, [pbf_t])
            transpose_store(pbf_t[:, 2080:2592], 4, z5T_s, 0, ti, [pbf_t])
            chk2("T%d" % ti)
        chk('P', l)
        S.barrier()
        def gla_gen():
            for d in range(2):
                TI, TR_, MK = (C_TIF, C_TRF, C_MF) if d == 0 else (C_TIB, C_TRB, C_MB)
                mset("dve", Sst[:], 0.0, [Sst])
                yield
                for h in range(4):
                    pr, hh = h // 2, h % 2
                    dma(Sst[hh * 64:(hh + 1) * 64, pr, hh * 128:(hh + 1) * 128], gla0.ap[e, d, h], [], [Sst])
                    yield
                blkm = cs[:, C_BLK:C_BLK + 256].unsqueeze(1).to_broadcast([128, 2, 256])
                tt("pool", Sbf[:], Sst[:], blkm, ALU.mult, [Sst, cs], [Sbf])
                yield
                order = list(range(NT)) if d == 0 else list(range(NT - 1, -1, -1))
                for oi, ti in enumerate(order):
                    t0 = ti * 128
                    if oi > 0 and oi % 2 == 0:
                        ts("dve", Sst[:], Sst[:], mflag, ALU.mult, [Sst, mt], [Sst])
                        yield
                        tt("pool", Sbf[:], Sst[:], blkm, ALU.mult, [Sst, cs], [Sbf])
                        yield
                    dma(qk_t[:], qkT_s.ap[:, t0:t0 + 128].rearrange("(a p) t -> p a t", p=128), [qkT_s.bs[ti]], [qk_t])
                    yield
                    dma(kv_t[:], kv_s.ap[t0:t0 + 128, :], [kv_s.bs[ti]], [kv_t])
                    yield
                    dma(sp_t[:], sp_s.ap[t0:t0 + 128, d * 256:(d + 1) * 256], [sp_s.bs[ti]], [sp_t])
                    yield
                    yield
                    for pr in range(2):
                        mm(pC[:, pr * 128:(pr + 1) * 128], sp_t[:, pr * 128:(pr + 1) * 128], cs[:, TI:TI + 128], True, True,
                           [sp_t, cs], [pC])
                        yield
                    mm(pD[:, 0:256], cs[:, TR_:TR_ + 128], sp_t[:, :], True, True, [sp_t, cs], [pD])
                    yield
                    yield
                    act(E1[:].rearrange("p a t -> p (a t)"), pC[:, 0:256], AF.Exp, [pC], [E1])
                    yield
                    act(E2[:].rearrange("p a t -> p (a t)"), pC[:, 0:256], AF.Exp, [pC], [E2], scale=-1.0)
                    yield
                    act(E3[:], pD[:, 0:256], AF.Exp, [pD], [E3])
                    yield
                    stt(qbT[:], qk_t[:, 0:2, :], 0.125, E1[:], ALU.mult, ALU.mult, [qk_t, E1], [qbT])
                    yield
                    tt("pool", kbT[:], qk_t[:, 2:4, :], E2[:], ALU.mult, [qk_t, E2], [kbT])
                    yield
                    tt("pool", kd[:], kv_t[:, 0:256], E3[:], ALU.mult, [kv_t, E3], [kd])
                    yield
                    yield
                    for h in range(4):
                        pr, hh = h // 2, h % 2
                        pX = (pE, pA)[hh]
                        mm(pX[:, pr * 128:(pr + 1) * 128], kbT[hh * 64:(hh + 1) * 64, pr, :], qbT[hh * 64:(hh + 1) * 64, pr, :],
                           True, True, [kbT, qbT], [pX])
                        yield
                    yield
                    for hh in range(2):
                        pX = (pE, pA)[hh]
                        av = attm[:].rearrange("p (pr hh) t -> p hh pr t", hh=2)[:, hh]
                        tt("dve", av, pX[:, 0:256].rearrange("p (h t) -> p h t", h=2),
                           cs[:, MK:MK + 128].unsqueeze(1).to_broadcast([128, 2, 128]), ALU.mult, [pX, cs], [attm])
                        yield
                    yield
                    chunks = (0, 1) if d == 0 else (1, 0)
                    for ci, ch in enumerate(chunks):
                        c0 = ch * 64
                        dcolx = (c0 + 63) if d == 0 else c0
                        pUp = (pD, pA)[ch]
                        for pr in range(2):
                            mm(pUp[:, 256:512], kd[c0:c0 + 64, pr * 128:(pr + 1) * 128],
                               kv_t[c0:c0 + 64, 256 + pr * 256:256 + (pr + 1) * 256], True, True, [kd, kv_t], [pUp])
                            yield
                            stt(Sst[:, pr, :], Sst[:, pr, :], E1[:, pr, dcolx:dcolx + 1], pUp[:, 256:512], ALU.mult, ALU.add,
                                [Sst, E1, pUp], [Sst])
                            yield
                        if ci == 0:
                            tt("pool", Sbf1[:], Sst[:], blkm, ALU.mult, [Sst, cs], [Sbf1])
                            yield
                    for h in range(4):
                        pr, hh = h // 2, h % 2
                        mm(pF[:, h * 128:(h + 1) * 128], kv_t[:, 256 + h * 128:256 + (h + 1) * 128], attm[:, h, :],
                           True, False, [kv_t, attm], [pF])
                        yield
                        for ci, ch in enumerate(chunks):
                            c0 = ch * 64
                            Sx = (Sbf, Sbf1)[ci]
                            mm(pF[:, h * 128 + c0:h * 128 + c0 + 64], Sx[:, pr, hh * 128:(hh + 1) * 128],
                               qbT[:, pr, c0:c0 + 64], False, (ci == 1), [Sx, qbT], [pF])
                            yield
                    tt("pool", Sbf[:], Sst[:], blkm, ALU.mult, [Sst, cs, pF], [Sbf])
                    yield
                    yield
                    if oi % 2 == 1:
                        seg = ti // 2
                        for h in range(4):
                            pr, hh = h // 2, h % 2
                            dma(ngla.ap[e, d, seg, h], Sst[hh * 64:(hh + 1) * 64, pr, hh * 128:(hh + 1) * 128], [Sst], [ngla])
                            yield
                    if d == 0:
                        cp("act", o_t[:].rearrange("p h t -> p (h t)"), pF[:, :], [pF], [o_t])
                        yield
                        dma(oF_s.ap[:, t0:t0 + 128].rearrange("(a p) t -> p a t", p=128), o_t[:], [o_t], [oF_s.bs[ti]])
                        yield
                    else:
                        dma(oF_t[:], oF_s.ap[:, t0:t0 + 128].rearrange("(a p) t -> p a t", p=128), [oF_s.bs[ti]], [oF_t])
                        yield
                        dma(zg_t[:], zgT_s.ap[:, t0:t0 + 128].rearrange("(a p) t -> p a t", p=128), [zgT_s.bs[ti]], [zg_t])
                        yield
                        tt("dve", o_t[:].rearrange("p h t -> p (h t)"), pF[:, :], oF_t[:].rearrange("p h t -> p (h t)"),
                           ALU.add, [pF, oF_t], [o_t])
                        yield
                        of = o_t[:].rearrange("p h t -> p (h t)")
                        act(osq[:], of, AF.Square, [o_t], [osq])
                        yield
                        mm(pC[:, :], csb[:, C_ONE:C_ONE + 128], osq[:], True, True, [osq, csb], [pC])
                        yield
                        act(w1[:, :512], pC[:, :], AF.Ln, [pC], [w1], scale=1.0 / 128, bias=cs[:, C_EPS:C_EPS + 1])
                        yield
                        act(w1[:, :512], w1[:, :512], AF.Exp, [w1], [w1], scale=-0.5)
                        yield
                        tt("dve", w1[:, :512], w1[:, :512], of, ALU.mult, [w1, o_t], [w1])
                        yield
                        act(w1[:, 512:1024], zg_t[:].rearrange("p h t -> p (h t)"), AF.Silu, [zg_t], [w1])
                        yield
                        stt(trs[:, 0:4, :].rearrange("p a t -> p (a t)"), w1[:, :512], onc[:, 0:1], w1[:, 512:1024],
                            ALU.mult, ALU.mult, [w1, onc], [trs])
                        yield
                        dma(yT_s.ap[0:512, t0:t0 + 128].rearrange("(a p) t -> p a t", p=128), trs[:, 0:4, :], [trs], [yT_s.bs[ti]])
                        yield
            yield

        def s5_gen():
            s5_setup(e)
            for j in range(16):
                ft = j // 4
                if j % 4 == 0:
                    dma(u_ft[:], uT_s.ap[ft * 128:(ft + 1) * 128, :], uT_s.bs, [u_ft])
                fences = []
                for d in range(2):
                    X = Xs[d]
                    rev = (d == 1)
                    for blk in range(NB):
                        for c in range(2):
                            pp = (pB, pTf)[c]
                            mm(pp[:, :], BT[:, d, c, j, :], u_ft[:, blk * 512:(blk + 1) * 512], True, True, [BT, u_ft], [pp])
                            cp("act", X[:, c, blk * 512:(blk + 1) * 512], pp[:, :], [pp], [X, Xblk[d][blk]])
                            yield
                    v0 = lambda c: s5_view(X, c, 0, 1, 1, rev)
                    pr_ = PW[:, d, pidx[1], 0, j:j + 1]; pi_ = PW[:, d, pidx[1], 1, j:j + 1]; pn_ = PW[:, d, pidx[1], 2, j:j + 1]
                    stt(v0(0), h0t[:, d, 0, j:j + 1], pr_, v0(0), ALU.mult, ALU.add, [h0t, PW] + Xblk[d], Xblk[d])
                    stt(v0(0), h0t[:, d, 1, j:j + 1], pn_, v0(0), ALU.mult, ALU.add, [h0t, PW] + Xblk[d], Xblk[d])
                    stt(v0(1), h0t[:, d, 1, j:j + 1], pr_, v0(1), ALU.mult, ALU.add, [h0t, PW] + Xblk[d], Xblk[d])
                    fences.append(stt(v0(1), h0t[:, d, 0, j:j + 1], pi_, v0(1), ALU.mult, ALU.add, [h0t, PW] + Xblk[d], Xblk[d]))
                gens = [s5_scan(Xs[d], d, j, d == 1, fences[d]) for d in range(2)]
                alive = [True, True]
                while any(alive):
                    for d in range(2):
                        if alive[d]:
                            try:
                                next(gens[d])
                                yield
                            except StopIteration:
                                alive[d] = False
                for d in range(2):
                    X = Xs[d]
                    rev = (d == 1)
                    for c in range(2):
                        col0 = ((d * 2 + c) * NS) * 16 + j
                        dst = bass.AP(STG[:, 0:1].tensor, STG[:, col0:col0 + 1].offset, [list(STG[:, 0:1].ap[0]), [16, NS]])
                        cp("pool", dst, s5_view(X, c, 255, 256, NS, rev), Xblk[d], [STG])
                for blk in range(NB):
                    k = 0
                    for d in range(2):
                        for c in range(2):
                            mm(pT32[0:32, :], CTm[:, c, j, :], Xs[d][:, c, blk * 512:(blk + 1) * 512], k == 0, k == 3,
                               [CTm, Xblk[d][blk]], [pT32])
                            k += 1
                    cp("act", y5st[:, :], pT32[0:32, :], [pT32], [y5st])
                    dma(y5T_s.ap[j * 32:(j + 1) * 32, blk * 512:(blk + 1) * 512], y5st[:, :], [y5st], [y5T_s.bs[j]])
                    yield
            gsz = min(8, NS)
            for d in range(2):
                for c in range(2):
                    for s0 in range(0, NS, gsz):
                        col0 = ((d * 2 + c) * NS + s0) * 16
                        ncol = gsz * 16
                        tr(pTf[:ncol, 0:128], STG[:, col0:col0 + ncol], ident_f, [STG, cs], [pTf])
                        cp("act", stgT[:ncol, :], pTf[:ncol, 0:128], [pTf], [stgT])
                        dma(ns5.ap[e, d, c, s0 * 16:s0 * 16 + ncol, :], stgT[:ncol, :], [stgT], [ns5])
            yield

        gG, gS = gla_gen(), s5_gen()
        aG = aS = True
        RATIO = float(os.environ.get("KRATIO", "2.0"))
        acc = 0.0
        while aG or aS:
            if aG:
                try:
                    next(gG)
                except StopIteration:
                    aG = False
            acc += RATIO if aG else 1000.0
            while acc >= 1.0 and aS:
                acc -= 1.0
                try:
                    next(gS)
                except StopIteration:
                    aS = False
            if not aS:
                acc = 0.0

        chk('S', l)
        S.barrier()
        s5_post_setup(e)
        for blk in range(NB):
            c0 = blk * 512
            tis = list(range(blk * 4, blk * 4 + 4))
            dma(y5b[:], y5T_s.ap[:, c0:c0 + 512].rearrange("(a p) t -> p a t", p=128), y5T_s.bs, [y5b])
            dma(ub[:], uT_s.ap[:, c0:c0 + 512].rearrange("(a p) t -> p a t", p=128), [uT_s.bs[i] for i in tis], [ub])
            dma(z5b[:], z5T_s.ap[:, c0:c0 + 512].rearrange("(a p) t -> p a t", p=128), [z5T_s.bs[i] for i in tis], [z5b])
            for a in range(4):
                stt(y5b[:, a, :], ub[:, a, :], dcol[:, a:a + 1], y5b[:, a, :], ALU.mult, ALU.add, [ub, dcol, y5b], [y5b])
            act(zz[:].rearrange("p a t -> p (a t)"), y5b[:].rearrange("p a t -> p (a t)"), AF.Gelu_apprx_tanh, [y5b], [zz])
            cp("pool", zzb[:].rearrange("p a t -> p (a t)"), zz[:].rearrange("p a t -> p (a t)"), [zz], [zzb])
            for fo in range(4):
                pp = (pA, pB)[fo % 2]
                for kt in range(4):
                    mm(pp[:, :], wgb[:, kt, fo * 128:(fo + 1) * 128], zzb[:, kt, :], kt == 0, kt == 3, [wgb, zzb], [pp])
                act(sg[:, fo, :], pp[:, :], AF.Sigmoid, [pp, bgl], [sg], bias=bgl[:, fo:fo + 1])
            tt("dve", sg[:].rearrange("p a t -> p (a t)"), sg[:].rearrange("p a t -> p (a t)"),
               zz[:].rearrange("p a t -> p (a t)"), ALU.mult, [sg, zz], [sg])
            act(zz[:].rearrange("p a t -> p (a t)"), z5b[:].rearrange("p a t -> p (a t)"), AF.Silu, [z5b], [zz])
            tt("dve", y5o[:].rearrange("p a t -> p (a t)"), sg[:].rearrange("p a t -> p (a t)"),
               zz[:].rearrange("p a t -> p (a t)"), ALU.mult, [sg, zz], [y5o])
            dma(yT_s.ap[512:1024, c0:c0 + 512].rearrange("(a p) t -> p a t", p=128), y5o[:], [y5o],
                [yT_s.bs[i] for i in tis])
        chk('SP', l)
        out_proj_residual(l, xsrc, xdst, yT_s, w_out_e.ap[e])
        S.barrier()
        chk('O', l)

    qkwb = sb("qkwb", [128, 128])
    sinkb = sb("sinkb", [128, 8])
    rp_t = sb("rp_t", [128, 128], grp="P")
    qn = sb("qn", [128, 640], grp="P")
    qr = sb("qr", [128, 640], grp="P")
    kdup = sb("kdup", [128, 2, 2, 64], BF16, grp="P")
    ckT = sb("ckT", [128, 2, 256], BF16)
    cvt = sb("cvt", [128, 2, 128], BF16)
    ckf = sb("ckf", [128, 2, 128], grp="P")
    qT_t = sb("qT_t", [128, 4, 128], BF16, grp="A")
    kT_w = sb("kT_w", [128, 2, 384], BF16, grp="A")
    v_w = sb("v_w", [128, 3, 128], BF16, grp="A")
    scs2 = [sb("scs%d" % i, [128, 640], grp="A") for i in range(2)]
    pexp2 = [sb("pexp%d" % i, [128, 640], BF16, grp="A") for i in range(2)]
    pTt2 = [sb("pTt%d" % i, [128, 5, 128], BF16, grp="A") for i in range(2)]
    sm2 = [sb("sm%d" % i, [128, 8, 8], grp="A") for i in range(2)]
    bandT = sb("bandT", [128, 384], grp="A")
    oat = sb("oat", [128, 512], grp="A")
    sz_t = sb("sz_t", [128, 512], grp="A")
    cvw = sb("cvw", [128, 16])
    u1h = sb("u1h", [128, 4, 130], grp="A")
    u2t = sb("u2t", [128, 4, 128], grp="A")
    cacc = sb("cacc", [128, 4, 128], grp="A")

    def odd_layer(l, xsrc, xdst):
        e = l // 2
        load_weight_bf16(wbf, w_in_o.ap[e], OIN)
        adaln(l)
        dma(qkwb[:], qkw.ap[e:e + 1, :].partition_broadcast(128), [], [qkwb])
        ts("dve", qkwb[:, 0:64], qkwb[:, 0:64], 0.125, ALU.mult, [qkwb], [qkwb])
        dma(sinkb[:], sinkv.ap[e:e + 1, :].partition_broadcast(128), [], [sinkb])
        dma(cvw[:], convw.ap[e], [], [cvw])
        for hf in range(2):
            dma(ckf[:, 0, :], ckv.ap[e, 0, hf * 128:(hf + 1) * 128].rearrange("t k d -> t (k d)"), [], [ckf])
            dma(ckf[:, 1, :], ckv.ap[e, 1, hf * 128:(hf + 1) * 128].rearrange("t k d -> t (k d)"), [], [ckf])
            cp("dve", cvt[:, hf, :], ckf[:, 1, :], [ckf], [cvt])
            for kv in range(2):
                for c in range(2):
                    cp("dve", kdup[:, kv, c, :], ckf[:, 0, kv * 64:(kv + 1) * 64], [ckf], [kdup])
            for kv in range(2):
                tr(pT[:, kv * 128:(kv + 1) * 128], kdup[:, kv].rearrange("p c d -> p (c d)"), ident_b, [kdup, csb], [pT])
            for kv in range(2):
                cp("act", ckT[:, kv, hf * 128:(hf + 1) * 128], pT[:, kv * 128:(kv + 1) * 128], [pT], [ckT])
        def front_o(ti):
            norm_mod_T(l, ti, xsrc)
            in_proj(OIN, proj2[ti % 2])

        front_o(0)
        for ti in range(NT):
            t0 = ti * 128
            if ti + 1 < NT:
                front_o(ti + 1)
            proj_t = proj2[ti % 2]
            pbf_t = pbf2[ti % 2]
            dma(rp_t[:], rope.ap[t0:t0 + 128, :], [], [rp_t])
            act(w1[:, :640], proj_t[:, 0:640], AF.Square, [proj_t], [w1])
            S.op("dve", lambda e_: e_.tensor_reduce(out=small[:, 0:10], in_=w1[:, :640].rearrange("p (h d) -> p h d", d=64),
                                                    axis=AX.X, op=ALU.add), bl([w1]), bl([small]))
            act(small[:, 0:10], small[:, 0:10], AF.Ln, [small], [small], scale=1.0 / 64, bias=cs[:, C_EPS:C_EPS + 1])
            act(small[:, 0:10], small[:, 0:10], AF.Exp, [small], [small], scale=-0.5)
            tt("dve", qn[:].rearrange("p (h d) -> p h d", d=64), proj_t[:, 0:640].rearrange("p (h d) -> p h d", d=64),
               small[:, 0:10].unsqueeze(2).to_broadcast([128, 10, 64]), ALU.mult, [proj_t, small], [qn])
            tt("dve", qn[:, 0:512].rearrange("p (h d) -> p h d", d=64), qn[:, 0:512].rearrange("p (h d) -> p h d", d=64),
               qkwb[:, 0:64].unsqueeze(1).to_broadcast([128, 8, 64]), ALU.mult, [qn, qkwb], [qn])
            tt("dve", qn[:, 512:640].rearrange("p (h d) -> p h d", d=64), qn[:, 512:640].rearrange("p (h d) -> p h d", d=64),
               qkwb[:, 64:128].unsqueeze(1).to_broadcast([128, 2, 64]), ALU.mult, [qn, qkwb], [qn])
            dma(nkv.ap[e, 0, t0:t0 + 128, :], qn[:, 512:640], [qn], [nkv])
            dma(nkv.ap[e, 1, t0:t0 + 128, :], proj_t[:, 640:768], [proj_t], [nkv])
            v5 = lambda tl: tl[:, 0:640].rearrange("p (h a b f) -> p h a b f", a=2, b=2, f=16)
            cosb = rp_t[:, 0:64].rearrange("p (a b f) -> p a b f", a=2, b=2).unsqueeze(1).to_broadcast([128, 10, 2, 2, 16])
            tt("dve", v5(qr), v5(qn), cosb, ALU.mult, [qn, rp_t], [qr])
            for b_ in range(2):
                sinb = rp_t[:, 64:128].rearrange("p (a b f) -> p a b f", a=2, b=2)[:, :, b_, :].unsqueeze(1).to_broadcast([128, 10, 2, 16])
                tt("pool", v5(w1)[:, :, :, b_, :], v5(qn)[:, :, :, 1 - b_, :], sinb, ALU.mult, [qn, rp_t], [w1])
            tt("dve", pbf_t[:, 0:640], qr[:, 0:640], w1[:, 0:640], ALU.add, [qr, w1], [pbf_t])
            transpose_store(pbf_t[:, 0:512], 4, qT_s, 0, ti, [pbf_t])
            for kv in range(2):
                for c in range(2):
                    cp("pool", kdup[:, kv, c, :], pbf_t[:, 512 + kv * 64:512 + (kv + 1) * 64], [pbf_t], [kdup])
            transpose_store(kdup[:].rearrange("p k c d -> p (k c d)"), 2, kT_s, 0, ti, [kdup])
            cp("pool", pbf_t[:, 640:768], proj_t[:, 640:768], [proj_t], [pbf_t])
            dma(v_s.ap[t0:t0 + 128, :], pbf_t[:, 640:768], [pbf_t], [v_s.bs[ti]])
            act(w2[:, :512], proj_t[:, 768:1280], AF.Silu, [proj_t], [w2])
            dma(sz_s.ap[t0:t0 + 128, :], w2[:, :512], [w2], [sz_s.bs[ti]])
            tt("dve", w3[:, 0:512], proj_t[:, 2304:2816], proj_t[:, 1280:1792], ALU.mult, [proj_t], [w3])
            act(w3[:, 512:1024], proj_t[:, 2816:3328], AF.Silu, [proj_t], [w3])
            tt("dve", w3[:, 512:1024], w3[:, 512:1024], proj_t[:, 1792:2304], ALU.mult, [w3, proj_t], [w3])
            transpose_store_f32(w3[:, 0:512], 4, u1T_s, 0, ti, [w3])
            transpose_store_f32(w3[:, 512:1024], 4, u2T_s, 0, ti, [w3])
        chk('P', l)
        S.barrier()
        for ti in range(NT):
            t0 = ti * 128
            tp = max(ti - 1, 0)
            tn = min(ti + 1, NT - 1)
            dma(qT_t[:], qT_s.ap[:, t0:t0 + 128].rearrange("(a p) t -> p a t", p=128), [qT_s.bs[ti]], [qT_t])
            for wi, tw in enumerate((tp, ti, tn)):
                dma(kT_w[:, :, wi * 128:(wi + 1) * 128], kT_s.ap[:, tw * 128:(tw + 1) * 128].rearrange("(k p) t -> p k t", p=128),
                    [kT_s.bs[tw]], [kT_w])
                dma(v_w[:, wi, :], v_s.ap[tw * 128:(tw + 1) * 128, :], [v_s.bs[tw]], [v_w])
            dma(sz_t[:], sz_s.ap[t0:t0 + 128, :], [sz_s.bs[ti]], [sz_t])
            flp = mt[:, 3 + ti:4 + ti]
            fln = mt[:, 3 + NT + ti:4 + NT + ti]
            ts("dve", bandT[:, 0:128], band[:, 0:128], flp, ALU.add, [band, mt], [bandT])
            cp("dve", bandT[:, 128:256], band[:, 128:256], [band], [bandT])
            ts("dve", bandT[:, 256:384], band[:, 256:384], fln, ALU.add, [band, mt], [bandT])

            def head_gen(hq, P):
                kv, pr, hh = hq // 4, hq // 2, hq % 2
                rows = slice(hh * 64, (hh + 1) * 64)
                pS1, pS2 = ((pC, pD), (pA, pB))[P]
                scsX, pexpX, pTtX, smX = scs2[P], pexp2[P], pTt2[P], sm2[P]
                pTX = (pT, pTf16)[P]
                pO = (pE, pF)[P]
                oc = (hq // 2) * 64
                mm(pS1[:, 0:384], qT_t[rows, pr, :], kT_w[rows, kv, :], True, True, [qT_t, kT_w], [pS1])
                mm(pS2[:, 0:256], qT_t[rows, pr, :], ckT[rows, kv, :], True, True, [qT_t, ckT], [pS2])
                yield
                tt("dve", scsX[:, 0:384], pS1[:, 0:384], bandT[:, :], ALU.add, [pS1, bandT], [scsX])
                yield
                ts("dve", scsX[:, 384:640], pS2[:, 0:256], mt[:, 2:3], ALU.add, [pS2, mt], [scsX])
                yield
                S.op("dve", lambda e_: e_.reduce_max(out=smX[:, hq, 0:1], in_=scsX[:, :], axis=AX.X), bl([scsX]), bl([smX]))
                yield
                tt("dve", smX[:, hq, 0:1], smX[:, hq, 0:1], sinkb[:, hq:hq + 1], ALU.max, [smX, sinkb], [smX])
                ts("dve", smX[:, hq, 1:2], smX[:, hq, 0:1], -1.0, ALU.mult, [smX], [smX])
                mset("dve", smX[:, hq, 2:3], 0.0, [smX])
                yield
                act(pexpX[:], scsX[:], AF.Exp, [scsX, smX], [pexpX, smX], bias=smX[:, hq, 1:2], accum_out=smX[:, hq, 2:3])
                act(smX[:, hq, 3:4], sinkb[:, hq:hq + 1], AF.Exp, [smX, sinkb], [smX], bias=smX[:, hq, 1:2])
                yield
                for k5 in range(5):
                    tr(pTX[:, k5 * 128:(k5 + 1) * 128], pexpX[:, k5 * 128:(k5 + 1) * 128], ident_b, [pexpX, csb], [pTX])
                yield
                cp("act", pTtX[:].rearrange("p a t -> p (a t)"), pTX[:, 0:640], [pTX], [pTtX])
                tt("dve", smX[:, hq, 4:5], smX[:, hq, 2:3], smX[:, hq, 3:4], ALU.add, [smX], [smX])
                S.op("dve", lambda e_: e_.reciprocal(out=smX[:, hq, 5:6], in_=smX[:, hq, 4:5]), bl([smX]), bl([smX]))
                yield
                for k5 in range(5):
                    vv = v_w[:, k5, kv * 64:(kv + 1) * 64] if k5 < 3 else cvt[:, k5 - 3, kv * 64:(kv + 1) * 64]
                    mm(pO[:, oc:oc + 64], pTtX[:, k5, :], vv, k5 == 0, k5 == 4, [pTtX, v_w, cvt], [pO])
                yield

            for h2 in range(0, 8, 2):
                gens = [head_gen(h2, 0), head_gen(h2 + 1, 1)]
                alive = [True, True]
                while any(alive):
                    for P in range(2):
                        if alive[P]:
                            try:
                                next(gens[P])
                            except StopIteration:
                                alive[P] = False
            for hq in range(8):
                pO = (pE, pF)[hq % 2]
                oc = (hq // 2) * 64
                stt(oat[:, hq * 64:(hq + 1) * 64], pO[:, oc:oc + 64], sm2[hq % 2][:, hq, 5:6], sz_t[:, hq * 64:(hq + 1) * 64],
                    ALU.mult, ALU.mult, [pO, sm2[hq % 2], sz_t], [oat])
            cp("pool", pbfA[:, 0:512], oat[:], [oat], [pbfA])
            transpose_store(pbfA[:, 0:512], 4, yT_s, 0, ti, [pbfA])
            dma(u1h[:, :, 1:129], u1T_s.ap[:, t0:t0 + 128].rearrange("(a p) t -> p a t", p=128), [u1T_s.bs[ti]], [u1h])
            lo = t0 - 1 if ti > 0 else 0
            hi = t0 + 128 if ti < NT - 1 else T - 1
            dma(u1h[:, :, 0:1], u1T_s.ap[:, lo:lo + 1].rearrange("(a p) t -> p a t", p=128), [u1T_s.bs[tp]], [u1h], slow=True)
            dma(u1h[:, :, 129:130], u1T_s.ap[:, hi:hi + 1].rearrange("(a p) t -> p a t", p=128), [u1T_s.bs[tn]], [u1h], slow=True)
            dma(u2t[:], u2T_s.ap[:, t0:t0 + 128].rearrange("(a p) t -> p a t", p=128), [u2T_s.bs[ti]], [u2t])
            ts("dve", u1h[:, :, 0:1], u1h[:, :, 0:1], mt[:, 3 + 2 * NT + ti:4 + 2 * NT + ti], ALU.mult, [u1h, mt], [u1h])
            ts("dve", u1h[:, :, 129:130], u1h[:, :, 129:130], mt[:, 3 + 3 * NT + ti:4 + 3 * NT + ti], ALU.mult, [u1h, mt], [u1h])
            for a in range(4):
                ts("dve", cacc[:, a, :], u1h[:, a, 1:129], cvw[:, a * 4 + 1:a * 4 + 2], ALU.mult, [u1h, cvw], [cacc],
                   cvw[:, a * 4 + 3:a * 4 + 4], ALU.add)
                stt(cacc[:, a, :], u1h[:, a, 0:128], cvw[:, a * 4:a * 4 + 1], cacc[:, a, :], ALU.mult, ALU.add, [u1h, cvw, cacc], [cacc])
                stt(cacc[:, a, :], u1h[:, a, 2:130], cvw[:, a * 4 + 2:a * 4 + 3], cacc[:, a, :], ALU.mult, ALU.add, [u1h, cvw, cacc], [cacc])
            tt("dve", trs[:, 4:8, :], cacc[:], u2t[:], ALU.mult, [cacc, u2t], [trs])
            dma(yT_s.ap[512:1024, t0:t0 + 128].rearrange("(a p) t -> p a t", p=128), trs[:, 4:8, :], [trs], [yT_s.bs[ti]])
        chk('A', l)
        out_proj_residual(l, xsrc, xdst, yT_s, w_out_o.ap[e])
        S.barrier()
        chk('O', l)

    chain = [x_in, xs[0], xs[1], xs[0], y_out]
    try:
        for l in range(4):
            if l % 2 == 0:
                even_layer(l, chain[l], chain[l + 1])
            else:
                odd_layer(l, chain[l], chain[l + 1])
    except _Stop:
        S.barrier()
    S.finish_waits([y_out.b, ngla.b, ns5.b, nkv.b] + y_out.bs)
    S.emit()
    st.close()
    return nc, S


def _consts():
    c = np.zeros((128, NCST), np.float32)
    i = np.arange(128)
    s = i[:, None]
    t = i[None, :]
    same = (s // 64) == (t // 64)
    c[:, C_ID:C_ID + 128] = np.eye(128, dtype=np.float32)
    c[:, C_TIF:C_TIF + 128] = np.where(same & (s <= t), -1.0 / 16, 0.0)
    c[:, C_TRF:C_TRF + 128] = np.where(same & (s > t), -1.0 / 16, 0.0)
    c[:, C_TIB:C_TIB + 128] = np.where(same & (s >= t), -1.0 / 16, 0.0)
    c[:, C_TRB:C_TRB + 128] = np.where(same & (s < t), -1.0 / 16, 0.0)
    c[:, C_MF:C_MF + 128] = np.where(same & (s <= t), 1.0, 0.0)
    c[:, C_MB:C_MB + 128] = np.where(same & (s >= t), 1.0, 0.0)
    c[:, C_ONE:C_ONE + 128] = 1.0
    c[:, C_EPS] = EPS
    c[:, C_EPS + 1] = 1.0
    qi = i[:, None]
    kj = i[None, :]
    NEG = -1e30
    c[:, C_BAND:C_BAND + 128] = np.where(kj >= qi, 0.0, NEG)
    c[:, C_BAND + 128:C_BAND + 256] = 0.0
    c[:, C_BAND + 256:C_BAND + 384] = np.where(kj <= qi, 0.0, NEG)
    c[0:64, C_BLK:C_BLK + 128] = 1.0
    c[64:128, C_BLK + 128:C_BLK + 256] = 1.0
    return c


def _rope_table(T, identity):
    tab = np.zeros((T, 128), np.float32)
    if identity:
        tab[:, :64] = 1.0
        return tab
    pos = np.arange(T)
    row = (pos // 64).astype(np.float32)
    col = (pos % 64).astype(np.float32)
    freq = (10000.0 ** (-np.arange(16, dtype=np.float32) / 16)).astype(np.float32)
    ar = row[:, None] * freq
    ac = col[:, None] * freq
    cos = np.concatenate([np.cos(ar), np.cos(ar), np.cos(ac), np.cos(ac)], axis=1)
    sin = np.concatenate([-np.sin(ar), np.sin(ar), -np.sin(ac), np.sin(ac)], axis=1)
    tab[:, :64] = cos
    tab[:, 64:] = sin
    return tab


def _meta(T, sample):
    NT = T // 128
    m = np.zeros((128, 3 + 4 * NT), np.float32)
    NEG = -1e30
    if sample:
        m[:, 0] = 1.0
        m[:, 1] = 1.0
        m[:, 2] = 0.0
        flp = np.zeros(NT); flp[0] = NEG
        fln = np.zeros(NT); fln[-1] = NEG
        cfl = np.ones(NT); cfl[0] = 0
        cfr = np.ones(NT); cfr[-1] = 0
    else:
        m[:, 0] = 0.0
        m[:, 1] = 0.0
        m[:, 2] = NEG
        flp = np.where(np.arange(NT) % 2 == 0, NEG, 0.0)
        fln = np.where(np.arange(NT) % 2 == 1, NEG, 0.0)
        cfl = np.where(np.arange(NT) % 2 == 0, 0.0, 1.0)
        cfr = np.where(np.arange(NT) % 2 == 1, 0.0, 1.0)
    m[:, 3:3 + NT] = flp
    m[:, 3 + NT:3 + 2 * NT] = fln
    m[:, 3 + 2 * NT:3 + 3 * NT] = cfl
    m[:, 3 + 3 * NT:3 + 4 * NT] = cfr
    return m


def _state_layout(a):
    sh = a.shape[:-2]
    b = a.reshape(sh + (16, 2, 64))
    b = np.moveaxis(b, -3, -1)
    return np.ascontiguousarray(b.reshape(sh + (128, 16)))


_NC_CACHE = {}
LAST_RESULTS = None


def run(inputs, T, n_prompt_per_core):
    f = lambda k: np.asarray(inputs[k], dtype=np.float32)
    x_prompt, x_sample, c = f("x_prompt"), f("x_sample"), f("c")
    NS = T // 256
    if T not in _NC_CACHE:
        _NC_CACHE[T] = build(T)[0]
    nc = _NC_CACHE[T]
    shared = {}
    shared["cst"] = _consts()
    shared["norm_w"] = f("norm_w")
    shared["w_ada"] = f("w_ada")
    shared["b_ada"] = f("b_ada")
    shared["w_in_e"] = f("w_in_e")
    shared["w_out_e"] = f("w_out_e")
    w2 = f("gla_w2"); b2 = f("gla_b2")
    w2cat = np.zeros((2, 64, 512), np.float32)
    w2cat[:, 0:16, 0:256] = w2[:, 0]
    w2cat[:, 16:32, 256:512] = w2[:, 1]
    w2cat[:, 32, 0:256] = b2[:, 0]
    w2cat[:, 32, 256:512] = b2[:, 1]
    shared["w2cat"] = w2cat
    shared["onorm"] = f("gla_onorm").reshape(2, 128, 1)
    lam_re, lam_im, log_dt = f("s5_lam_re"), f("s5_lam_im"), f("s5_log_dt")
    ldt = np.broadcast_to(log_dt[..., None], lam_re.shape)
    shared["s5p"] = np.stack([_state_layout(lam_re), _state_layout(lam_im), _state_layout(ldt)], axis=2)
    def expand_b(b):
        out = np.zeros((2, 128, 16, 128), np.float32)
        for g in range(32):
            j, gs = g // 2, g % 2
            k0 = 16 * (g % 8)
            out[:, gs * 64:(gs + 1) * 64, j, k0:k0 + 16] = b[:, g]
        return out.reshape(2, 128, 16 * 128)
    shared["s5b"] = np.stack([expand_b(f("s5_b_re")), expand_b(f("s5_b_im"))], axis=1)
    def expand_c(cc, sign):
        out = np.zeros((2, 128, 16, 32), np.float32)
        for g in range(32):
            j, gs = g // 2, g % 2
            out[:, gs * 64:(gs + 1) * 64, j, gs * 16:(gs + 1) * 16] = np.swapaxes(cc[:, g], 1, 2)
        if sign < 0:
            out = np.negative(out)
        return out.reshape(2, 128, 16 * 32)
    shared["s5c"] = np.stack([expand_c(f("s5_c_re"), 1), expand_c(f("s5_c_im"), -1)], axis=1)
    shared["s5d"] = np.ascontiguousarray(f("s5_d").reshape(2, 4, 128).transpose(0, 2, 1))
    shared["wglu"] = f("s5_w_glu")
    shared["bglu"] = np.ascontiguousarray(f("s5_b_glu").reshape(2, 4, 128).transpose(0, 2, 1))
    shared["w_in_o"] = f("w_in_o")
    shared["w_out_o"] = f("w_out_o")
    shared["qkw"] = np.concatenate([f("q_norm_w"), f("k_norm_w")], axis=1)
    shared["sink"] = f("sink")
    cw = f("conv_w"); cb = f("conv_b")
    cvw = np.zeros((2, 128, 4, 4), np.float32)
    for a in range(4):
        cvw[:, :, a, 0:3] = cw[:, :, a * 128:(a + 1) * 128].transpose(0, 2, 1)
        cvw[:, :, a, 3] = cb[:, a * 128:(a + 1) * 128]
    shared["convw"] = cvw.reshape(2, 128, 16)

    in_maps = []
    n_sample = x_sample.shape[0]
    for core in range(8):
        m = dict(shared)
        if core < 4:
            b = core
            m["x"] = np.ascontiguousarray(x_sample[b])
            cv = c[b]
            m["meta"] = _meta(T, True)
            m["rope"] = _rope_table(T, False)
            m["gla0"] = np.ascontiguousarray(f("state_gla")[b])
            sre = _state_layout(f("state_s5_re")[b]); sim = _state_layout(f("state_s5_im")[b])
            m["s5h0"] = np.stack([sre, sim], axis=2)
            ck = f("cache_k")[b]; cvv = f("cache_v")[b]
            m["ckv"] = np.ascontiguousarray(np.stack([ck.transpose(0, 2, 1, 3), cvv.transpose(0, 2, 1, 3)], axis=1))
        else:
            pc = core - 4
            xx = np.zeros((T, D), np.float32)
            seqs = x_prompt[pc * n_prompt_per_core:(pc + 1) * n_prompt_per_core]
            xx[:n_prompt_per_core * 256] = seqs.reshape(-1, D)
            m["x"] = xx
            cv = f("c_ctx")
            m["meta"] = _meta(T, False)
            m["rope"] = _rope_table(T, True)
            m["gla0"] = np.zeros((2, 2, 4, 64, 128), np.float32)
            m["s5h0"] = np.zeros((2, 2, 2, 128, 16), np.float32)
            m["ckv"] = np.zeros((2, 2, 256, 2, 64), np.float32)
        m["cvec"] = np.ascontiguousarray(cv.reshape(8, 128).T)
        in_maps.append(m)
    res = run_bass_kernel_spmd(nc, in_maps, core_ids=list(range(8)))
    R = res.results
    global LAST_RESULTS
    LAST_RESULTS = R
    BATCH = x_prompt.shape[0]
    y_sample = np.stack([np.asarray(R[b]["y"]) for b in range(4)], axis=0).astype(np.float32)
    y_prompt = np.zeros_like(x_prompt)
    new_gla = np.zeros((BATCH, 2, 2, 4, 64, 128), np.float32)
    new_re = np.zeros((BATCH, 2, 2, 32, 64), np.float32)
    new_im = np.zeros((BATCH, 2, 2, 32, 64), np.float32)
    new_k = np.zeros((BATCH, 2, 2, 256, 64), np.float32)
    new_v = np.zeros((BATCH, 2, 2, 256, 64), np.float32)
    for pc in range(4):
        r = R[4 + pc]
        y = np.asarray(r["y"]); g = np.asarray(r["ngla"]); s5 = np.asarray(r["ns5"]); kvo = np.asarray(r["nkv"])
        s5 = s5.reshape(2, 2, 2, NS, 16, 2, 64)
        for q in range(n_prompt_per_core):
            bi = pc * n_prompt_per_core + q
            y_prompt[bi] = y[q * 256:(q + 1) * 256]
            new_gla[bi] = g[:, :, q]
            for d in range(2):
                sig = q if d == 0 else NS - 1 - q
                new_re[bi, :, d] = s5[:, d, 0, sig].reshape(2, 32, 64)
                new_im[bi, :, d] = s5[:, d, 1, sig].reshape(2, 32, 64)
            kk = kvo[:, :, q * 256:(q + 1) * 256, :].reshape(2, 2, 256, 2, 64)
            new_k[bi] = kk[:, 0].transpose(0, 2, 1, 3)
            new_v[bi] = kk[:, 1].transpose(0, 2, 1, 3)
    return (y_prompt, y_sample, new_gla, new_re, new_im, new_k, new_v)


def kernel(**inputs):
    return run(inputs, 4096, 8)
```

```python
import math
import os
import numpy as np
import concourse.bass as bass
import concourse.mybir as mybir
from concourse.bass_utils import run_bass_kernel_spmd
from contextlib import ExitStack

F32 = mybir.dt.float32
BF16 = mybir.dt.bfloat16
I32 = mybir.dt.int32
ALU = mybir.AluOpType
AF = mybir.ActivationFunctionType
AX = mybir.AxisListType

D = 1024
EIN = 2592
OIN = 3328
EPS = 1e-6
ENGS = ("pe", "act", "dve", "pool", "sp")
SEM_LIMIT = 30000
N_DMA_SEMS = 12


class Buf:
    __slots__ = ("w", "r")

    def __init__(self):
        self.w = None
        self.r = {}


class Sched:
    def __init__(self, nc, same_engine_sync=True):
        self.nc = nc
        self.q = {e: [] for e in ENGS}
        self.epoch = {e: 0 for e in ENGS}
        self.cnt = {}
        self.seen = {e: {} for e in ENGS}
        self.same = same_engine_sync
        self.nosync = set(os.environ.get("KNOSYNC", "").split(","))
        self.semkeys = []
        for e in ENGS:
            self._newkey((e, 0))
        self.dma_pool = {e: [] for e in ENGS}
        self.dma_rr = {e: 0 for e in ENGS}
        self.n_ops = 0

    def _newkey(self, k):
        self.cnt[k] = 0
        self.semkeys.append(k)

    def _engkey(self, e):
        k = (e, self.epoch[e])
        if self.cnt[k] >= SEM_LIMIT:
            self.epoch[e] += 1
            k = (e, self.epoch[e])
            self._newkey(k)
        return k

    def _need(self, eng, waits, tok, is_dma=False):
        if tok is None:
            return
        k, v = tok
        if (not is_dma) and k[0] == eng and (eng == "pe" or eng in self.nosync):
            return
        if self.seen[eng].get(k, 0) >= v:
            return
        if waits.get(k, 0) < v:
            waits[k] = v

    def _deps(self, eng, reads, writes, is_dma):
        waits = {}
        for b in reads:
            self._need(eng, waits, b.w, is_dma)
        for b in writes:
            self._need(eng, waits, b.w, is_dma)
            for k, v in b.r.items():
                self._need(eng, waits, (k, v), is_dma)
        return waits

    def op(self, eng, fn, reads=(), writes=(), extra=None):
        if extra is None:
            waits = self._deps(eng, reads, writes, False)
        else:
            sv = self.nosync
            self.nosync = set(sv) | {eng}
            waits = self._deps(eng, reads, writes, False)
            self.nosync = sv
            for tok in extra:
                if tok is not None:
                    self._need(eng, waits, tok, True)
        for k, v in waits.items():
            self.seen[eng][k] = v
        key = self._engkey(eng)
        self.cnt[key] += 1
        tok = (key, self.cnt[key])
        for b in reads:
            b.r[key] = tok[1]
        for b in writes:
            b.w = tok
            b.r = {}
        self.q[eng].append((fn, list(waits.items()), key, 1))
        self.n_ops += 1
        return tok

    def dma(self, fn, reads=(), writes=(), eng="sp"):
        pool = self.dma_pool[eng]
        if len(pool) < N_DMA_SEMS:
            key = ("dma", eng, len(pool), 0)
            self._newkey(key)
            pool.append(key)
        else:
            idx = self.dma_rr[eng] % N_DMA_SEMS
            self.dma_rr[eng] += 1
            key = pool[idx]
            if self.cnt[key] >= SEM_LIMIT:
                key = ("dma", eng, idx, key[3] + 1)
                self._newkey(key)
                pool[idx] = key
        waits = self._deps(eng, reads, writes, True)
        if self.cnt[key] > 0:
            self._need(eng, waits, (key, self.cnt[key]), True)
        for k, v in waits.items():
            self.seen[eng][k] = v
        self.cnt[key] += 16
        tok = (key, self.cnt[key])
        for b in reads:
            b.r[key] = tok[1]
        for b in writes:
            b.w = tok
            b.r = {}
        self.q[eng].append((fn, list(waits.items()), key, 16))
        self.n_ops += 1

    def barrier(self):
        for eng in ENGS:
            waits = {}
            for k, v in self.cnt.items():
                if v > 0 and self.seen[eng].get(k, 0) < v:
                    waits[k] = v
                    self.seen[eng][k] = v
            self.q[eng].append((None, list(waits.items()), None, 0))

    def finish_waits(self, bufs, eng="sp"):
        waits = {}
        for b in bufs:
            self._need(eng, waits, b.w, True)
        self.q[eng].append((None, list(waits.items()), None, 0))

    def emit(self):
        nc = self.nc
        with ExitStack() as st:
            sems = {}
            for i, k in enumerate(self.semkeys):
                sems[k] = st.enter_context(nc.semaphore("s%d" % i))
            block = st.enter_context(nc.Block())

            def runner(ename):
                def run(e):
                    for fn, waits, key, inc in self.q[ename]:
                        for k, v in waits:
                            e.wait_ge(sems[k], v)
                        if fn is not None:
                            fn(e).then_inc(sems[key], inc)
                return run

            block.tensor(runner("pe"))
            block.scalar(runner("act"))
            block.vector(runner("dve"))
            block.gpsimd(runner("pool"))
            block.sync(runner("sp"))


class TL:
    def __init__(self, t):
        self.t = t
        self.b = Buf()

    def __getitem__(self, k):
        return self.t[k]


class View:
    def __init__(self, ap):
        self.ap = ap
        self.b = Buf()

    def __getitem__(self, k):
        return self.ap[k]


class DT:
    def __init__(self, ap, nb=1):
        self.ap = ap
        self.bs = [Buf() for _ in range(nb)]
        self.b = self.bs[0]


C_ID, C_TIF, C_TRF, C_TIB, C_TRB, C_MF, C_MB, C_ONE, C_BAND = [128 * i for i in range(9)]
C_BLK = C_BAND + 384
C_EPS = C_BLK + 256
NCST = C_EPS + 2


class _Stop(Exception):
    pass


def build(T, stop=None):
    import os
    stop = stop or os.environ.get("KSTOP")
    NT = T // 128
    NS = T // 256
    NB = T // 512
    nc = bass.Bass("TRN2", target_bir_lowering=False)
    S = Sched(nc, same_engine_sync=bool(int(os.environ.get("KSAME", "0"))))
    st = ExitStack()

    def din(name, shape, dt=F32):
        return DT(nc.dram_tensor(name, list(shape), dt, kind="ExternalInput").ap())

    def dout(name, shape, dt=F32):
        return DT(nc.dram_tensor(name, list(shape), dt, kind="ExternalOutput").ap())

    dbg = bool(os.environ.get("KDBG"))

    def dscr(name, shape, dt=F32, nb=1):
        return DT(nc.dram_tensor(name, list(shape), dt, kind="ExternalOutput" if dbg else "Internal").ap(), nb)

    ARENA_WORDS = 34000
    G_BASE = 29000
    arena_t = st.enter_context(nc.sbuf_tensor("arena", [128, ARENA_WORDS], F32))
    goff = {"G": G_BASE}

    def sb(name, shape, dt=F32, grp=None):
        if grp is None:
            return TL(st.enter_context(nc.sbuf_tensor(name, list(shape), dt)))
        n = 1
        for d_ in shape[1:]:
            n *= d_
        words = n if dt in (F32, I32) else (n + 1) // 2
        off = goff.get(grp, 0)
        goff[grp] = off + words
        assert goff[grp] <= ARENA_WORDS, (grp, name, goff[grp])
        assert grp != "S" or goff[grp] <= G_BASE, (grp, name, goff[grp])
        ap = arena_t[:, off:off + words]
        if dt != F32:
            ap = ap.bitcast(dt)[:, :n]
        if len(shape) > 2:
            names = " ".join("d%d" % i for i in range(len(shape) - 1))
            kw = {"d%d" % i: shape[i + 1] for i in range(len(shape) - 1)}
            ap = ap.rearrange("p (%s) -> p %s" % (names, names), **kw)
        if shape[0] < 128:
            ap = ap[0:shape[0]]
        return View(ap)

    def ps(name, shape, dt=F32):
        return TL(st.enter_context(nc.psum_tensor(name, list(shape), dt)))

    def bl(xs):
        return [x.b if hasattr(x, "b") else x for x in xs]

    def mm(out, lhsT, rhs, start, stop, R, W):
        S.op("pe", lambda e: e.matmul(out, lhsT=lhsT, rhs=rhs, start=start, stop=stop), bl(R), bl(W))

    def tr(out, in_, ident, R, W):
        S.op("pe", lambda e: e.transpose(out, in_, ident), bl(R), bl(W))

    def act(out, in_, func, R, W, **kw):
        S.op("act", lambda e: e.activation(out=out, in_=in_, func=func, **kw), bl(R), bl(W))

    def tt(eng, out, a, b, op, R, W):
        S.op(eng, lambda e: e.tensor_tensor(out=out, in0=a, in1=b, op=op), bl(R), bl(W))

    def ts(eng, out, a, s1, op0, R, W, s2=None, op1=None):
        if op1 is None:
            S.op(eng, lambda e: e.tensor_scalar(out=out, in0=a, scalar1=s1, scalar2=None, op0=op0), bl(R), bl(W))
        else:
            S.op(eng, lambda e: e.tensor_scalar(out=out, in0=a, scalar1=s1, scalar2=s2, op0=op0, op1=op1), bl(R), bl(W))

    def stt(out, in0, scalar, in1, op0, op1, R, W, extra=None):
        return S.op("dve", lambda e: e.scalar_tensor_tensor(out=out, in0=in0, scalar=scalar, in1=in1, op0=op0, op1=op1),
                    bl(R), bl(W), extra=extra)

    def cp(eng, out, in_, R, W):
        if eng == "act":
            S.op("act", lambda e: e.copy(out=out, in_=in_), bl(R), bl(W))
        else:
            S.op(eng, lambda e: e.tensor_copy(out=out, in_=in_), bl(R), bl(W))

    def mset(eng, ap, val, W):
        S.op(eng, lambda e: e.memset(ap, val), [], bl(W))

    def dma(out, in_, R, W, eng="sp", slow=False):
        if slow:
            S.dma(lambda e: e.dma_start(out=out, in_=in_, allow_slow_non_contiguous=True), bl(R), bl(W), eng=eng)
        else:
            S.dma(lambda e: e.dma_start(out=out, in_=in_), bl(R), bl(W), eng=eng)

    x_in = din("x", [T, D])
    cvec = din("cvec", [128, 8])
    cst = din("cst", [128, NCST])
    NMETA = 3 + 4 * NT
    meta = din("meta", [128, NMETA])
    rope = din("rope", [T, 128])
    gla0 = din("gla0", [2, 2, 4, 64, 128])
    s5h0 = din("s5h0", [2, 2, 2, 128, 16])
    ckv = din("ckv", [2, 2, 256, 2, 64])
    norm_w = din("norm_w", [4, D])
    w_ada = din("w_ada", [4, D, 3 * D])
    b_ada = din("b_ada", [4, 3 * D])
    w_in_e = din("w_in_e", [2, D, EIN])
    w_out_e = din("w_out_e", [2, D, D])
    w2cat = din("w2cat", [2, 64, 512])
    onorm = din("onorm", [2, 128, 1])
    s5p = din("s5p", [2, 2, 3, 128, 16])
    s5b = din("s5b", [2, 2, 128, 16 * 128])
    s5c = din("s5c", [2, 2, 128, 16 * 32])
    s5d = din("s5d", [2, 128, 4])
    wglu = din("wglu", [2, 512, 512])
    bglu = din("bglu", [2, 128, 4])
    w_in_o = din("w_in_o", [2, D, OIN])
    w_out_o = din("w_out_o", [2, D, D])
    qkw = din("qkw", [2, 128])
    sinkv = din("sink", [2, 8])
    convw = din("convw", [2, 128, 16])

    y_out = dout("y", [T, D])
    ngla = dout("ngla", [2, 2, NS, 4, 64, 128])
    ns5 = dout("ns5", [2, 2, 2, NS * 16, 128])
    nkv = dout("nkv", [2, 2, T, 128])

    xs = [dscr("xs0", [T, D], nb=NT), dscr("xs1", [T, D], nb=NT)]
    qkT_s = dscr("qkT", [512, T], BF16, NT)
    kv_s = dscr("kvtm", [T, 768], BF16, NT)
    sp_s = dscr("sp", [T, 512], F32, NT)
    zgT_s = dscr("zgT", [512, T], BF16, NT)
    uT_s = dscr("uT", [512, T], BF16, NT)
    z5T_s = dscr("z5T", [512, T], BF16, NT)
    oF_s = dscr("oF", [512, T], F32, NT)
    y5T_s = dscr("y5T", [512, T], F32, 16)
    yT_s = dscr("yT", [1024, T], BF16, NT)
    qT_s = dscr("qTo", [512, T], BF16, NT)
    kT_s = dscr("kTo", [256, T], BF16, NT)
    v_s = dscr("vo", [T, 128], BF16, NT)
    sz_s = dscr("szo", [T, 512], F32, NT)
    u1T_s = dscr("u1T", [512, T], F32, NT)
    u2T_s = dscr("u2T", [512, T], F32, NT)

    cs = sb("cs", [128, NCST])
    csb = sb("csb", [128, NCST], BF16)
    mt = sb("mt", [128, NMETA])
    wbf = sb("wbf", [128, 8, OIN], BF16, grp="P")
    wob = sb("wob", [128, 8, D], BF16, grp="O")
    wst1 = sb("wst0", [128, OIN], grp="P")
    wst = [wst1, wst1]
    wstO = sb("wstO", [128, D], grp="O")
    wstS = sb("wstS", [128, 512], grp="SP")
    pbfA = sb("pbfA", [128, 512], BF16, grp="A")
    modbc = sb("modbc", [128, 3 * D])
    Abc = sb("Abc", [128, D])
    sc8 = sb("sc8", [128, 8])
    screp = sb("screp", [128, 8, 128])
    xts = [sb("xt0", [128, D]), sb("xt1", [128, D])]
    xt = xts[0]
    hb = sb("hb", [128, D], BF16, grp="P")
    hT = sb("hT", [128, 8, 128], BF16, grp="P")
    small = sb("small", [128, 16])
    small2 = sb("small2", [128, 16])
    proj = sb("proj", [128, OIN], grp="P")
    pbf = sb("pbf", [128, OIN], BF16, grp="P")
    trs = sb("trs", [128, 8, 128], BF16)
    trf = sb("trf", [128, 4, 128])
    w1 = sb("w1", [128, 1024])
    w2 = sb("w2", [128, 1024])
    w3 = sb("w3", [128, 1024])
    lrT = sb("lrT", [64, 128], BF16, grp="P")
    w2c = sb("w2c", [64, 512], BF16, grp="P")
    w2cf = sb("w2cf", [64, 512], grp="P")

    projB = sb("projB", [128, OIN], grp="P")
    pbfB = sb("pbfB", [128, OIN], BF16, grp="P")
    proj2 = [proj, projB]
    pbf2 = [pbf, pbfB]
    pA = ps("pA", [128, 512]); pB = ps("pB", [128, 512]); pC = ps("pC", [128, 512])
    pD = ps("pD", [128, 512]); pE = ps("pE", [128, 512]); pF = ps("pF", [128, 512])
    pT = ps("pT", [128, 1024], BF16)
    pTf = ps("pTf", [128, 512])

    class _PT32:
        def __init__(self):
            self.ap = pT[:, :].bitcast(F32)
            self.b = pT.b

        def __getitem__(self, k):
            return self.ap[k]
    pT32 = _PT32()

    class _PTB:
        def __init__(self):
            self.ap = pE[:, :].bitcast(BF16)
            self.b = pE.b

        def __getitem__(self, k):
            return self.ap[k]
    pTb = _PTB()

    class _PTF16:
        def __init__(self):
            self.ap = pTf[:, :].bitcast(BF16)
            self.b = pTf.b

        def __getitem__(self, k):
            return self.ap[k]
    pTf16 = _PTF16()
    trsb = sb("trsb", [128, 4, 128], BF16)
    ident_b = csb[:, C_ID:C_ID + 128]
    ident_f = cs[:, C_ID:C_ID + 128]
    mflag = mt[:, 0:1]

    dma(cs[:], cst.ap[:, :], [], [cs])
    cp("dve", csb[:], cs[:], [cs], [csb])
    dma(mt[:], meta.ap[:, :], [], [mt])
    dma(sc8[:], cvec.ap[:, :], [], [sc8])
    act(sc8[:], sc8[:], AF.Silu, [sc8], [sc8])
    for kt in range(8):
        cp("dve", screp[:, kt, :], sc8[:, kt:kt + 1].to_broadcast([128, 128]), [sc8], [screp])
    band = sb("band", [128, 384])
    ts("dve", band[:], cs[:, C_BAND:C_BAND + 384], mt[:, 1:2], ALU.mult, [cs, mt], [band])

    evac_rr = [0]

    def evac(out, in_, R, W):
        e = ("act", "dve")[evac_rr[0] % 2]
        evac_rr[0] += 1
        cp(e, out, in_, R, W)

    def load_weight_bf16(dst, src_ap, ncols, wsx=None):
        wsx = wsx or wst1
        for kt in range(8):
            dma(wsx[:, :ncols], src_ap[kt * 128:(kt + 1) * 128, :], [], [wsx])
            cp("pool", dst[:, kt, :ncols], wsx[:, :ncols], [wsx], [dst])

    def adaln(l):
        banks = [pA, pB, pC, pD, pE, pF]
        for kt in range(8):
            wsx = wst[kt % 2]
            dma(wsx[:, :3 * D], w_ada.ap[l, kt * 128:(kt + 1) * 128, :], [], [wsx])
            for nb_ in range(6):
                mm(banks[nb_][:, :], screp[:, kt, :], wsx[:, nb_ * 512:(nb_ + 1) * 512], kt == 0, kt == 7,
                   [screp, wsx], [banks[nb_]])
        tmpbc = wst1
        dma(tmpbc[:, :3 * D], b_ada.ap[l:l + 1, :].partition_broadcast(128), [], [tmpbc])
        for nb_ in range(6):
            tt("dve", modbc[:, nb_ * 512:(nb_ + 1) * 512], banks[nb_][:, :], tmpbc[:, nb_ * 512:(nb_ + 1) * 512],
               ALU.add, [banks[nb_], tmpbc], [modbc])
        dma(tmpbc[:, :D], norm_w.ap[l:l + 1, :].partition_broadcast(128), [modbc], [tmpbc])
        stt(Abc[:], modbc[:, D:2 * D], 1.0, tmpbc[:, :D], ALU.add, ALU.mult, [modbc, tmpbc], [Abc])

    def load_x(ti, xsrc):
        if ti >= NT:
            return
        t0 = ti * 128
        xt = xts[ti % 2]
        dma(xt[:], xsrc.ap[t0:t0 + 128, :], [xsrc.bs[ti] if len(xsrc.bs) > 1 else xsrc.b], [xt])

    def norm_mod_T(l, ti, xsrc):
        if ti == 0:
            load_x(0, xsrc)
        load_x(ti + 1, xsrc)
        xt = xts[ti % 2]
        mset("dve", small2[:, 0:1], 0.0, [small2])
        act(hb[:], xt[:], AF.Square, [xt], [hb, small2], accum_out=small2[:, 0:1])
        act(small2[:, 1:2], small2[:, 0:1], AF.Ln, [small2], [small2], scale=1.0 / D, bias=cs[:, C_EPS:C_EPS + 1])
        act(small2[:, 2:3], small2[:, 1:2], AF.Exp, [small2], [small2], scale=-0.5)
        stt(w4[:, :D], xt[:], small2[:, 2:3], Abc[:], ALU.mult, ALU.mult, [xt, small2, Abc], [w4])
        tt("dve", hb[:], w4[:, :D], modbc[:, 0:D], ALU.add, [w4, modbc], [hb])
        for kt in range(8):
            tr(pT[:, kt * 128:(kt + 1) * 128], hb[:, kt * 128:(kt + 1) * 128], ident_b, [hb, csb], [pT])
        cp("act", hT[:].rearrange("p a t -> p (a t)"), pT[:, :], [pT], [hT])

    def in_proj(ncols, dst=None):
        dst = dst or proj
        c0 = 0
        k = 0
        while c0 < ncols:
            cw = min(512, ncols - c0)
            pp = (pA, pB)[k % 2]
            for kt in range(8):
                mm(pp[:, :cw], hT[:, kt, :], wbf[:, kt, c0:c0 + cw], kt == 0, kt == 7, [hT, wbf], [pp])
            evac(dst[:, c0:c0 + cw], pp[:, :cw], [pp], [dst])
            c0 += cw
            k += 1

    ts_rr = [0]

    def transpose_store(src_bf_ap, nft, dst, row0, ti, R):
        t0 = ti * 128
        assert nft <= 4
        k = ts_rr[0] % 2
        ts_rr[0] += 1
        pX, tX = (pT, pTb)[k], (trs, trsb)[k]
        for a in range(nft):
            tr(pX[:, a * 128:(a + 1) * 128], src_bf_ap[:, a * 128:(a + 1) * 128], ident_b, R + [csb], [pX])
        cp(("act", "dve")[k], tX[:, :nft, :].rearrange("p a t -> p (a t)"), pX[:, :nft * 128], [pX], [tX])
        dma(dst.ap[row0:row0 + nft * 128, t0:t0 + 128].rearrange("(a p) t -> p a t", p=128), tX[:, :nft, :],
            [tX], [dst.bs[ti]])

    def transpose_store_f32(src_ap, nft, dst, row0, ti, R):
        t0 = ti * 128
        for a in range(nft):
            tr(pTf[:, a * 128:(a + 1) * 128], src_ap[:, a * 128:(a + 1) * 128], ident_f, R + [cs], [pTf])
        cp("act", trf[:, :nft, :].rearrange("p a t -> p (a t)"), pTf[:, :nft * 128], [pTf], [trf])
        dma(dst.ap[row0:row0 + nft * 128, t0:t0 + 128].rearrange("(a p) t -> p a t", p=128), trf[:, :nft, :],
            [trf], [dst.bs[ti]])

    def out_proj_residual(l, xsrc, xdst, ydt, wsrc):
        S.barrier()
        load_weight_bf16(wob, wsrc, D, wstO)
        def loads(ti):
            if ti >= NT:
                return
            t0 = ti * 128
            p = ti % 2
            yt = (sb_yt, sb_yt2)[p]
            dma(yt[:], ydt.ap[:, t0:t0 + 128].rearrange("(a p) t -> p a t", p=128), [ydt.bs[ti]], [yt])
            dma(xts[p][:], xsrc.ap[t0:t0 + 128, :], [xsrc.bs[ti] if len(xsrc.bs) > 1 else xsrc.b], [xts[p]])

        loads(0)
        for ti in range(NT):
            t0 = ti * 128
            p = ti % 2
            loads(ti + 1)
            yt, xt = (sb_yt, sb_yt2)[p], xts[p]
            o1, o2 = ((w1, w2), (w3, w4))[p]
            for cb in range(2):
                pp = ((pA, pB), (pC, pD))[p][cb]
                for kt in range(8):
                    mm(pp[:, :], yt[:, kt, :], wob[:, kt, cb * 512:(cb + 1) * 512], kt == 0, kt == 7, [yt, wob], [pp])
                tt("dve", o1[:, cb * 512:(cb + 1) * 512], pp[:, :], modbc[:, 2 * D + cb * 512:2 * D + (cb + 1) * 512],
                   ALU.mult, [pp, modbc], [o1])
            tt("dve", o2[:, :D], o1[:, :D], xt[:], ALU.add, [o1, xt], [o2])
            dma(xdst.ap[t0:t0 + 128, :], o2[:, :D], [o2], [xdst.bs[ti] if len(xdst.bs) > 1 else xdst.b])

    sb_yt = sb("yt", [128, 8, 128], BF16)
    sb_yt2 = sb("yt2", [128, 8, 128], BF16)
    w4 = sb("w4", [128, 1024])

    qk_t = sb("qk_t", [128, 4, 128], BF16, grp="G")
    kv_t = sb("kv_t", [128, 768], BF16, grp="G")
    sp_t = sb("sp_t", [128, 256], grp="G")
    E1 = sb("E1", [128, 2, 128], grp="G")
    E2 = sb("E2", [128, 2, 128], grp="G")
    E3 = sb("E3", [128, 256], grp="G")
    qbT = sb("qbT", [128, 2, 128], BF16, grp="G")
    kbT = sb("kbT", [128, 2, 128], BF16, grp="G")
    kd = sb("kd", [128, 256], BF16, grp="G")
    attm = sb("attm", [128, 4, 128], BF16, grp="G")
    Sst = sb("Sst", [128, 2, 256], grp="G")
    Sbf = sb("Sbf", [128, 2, 256], BF16, grp="G")
    Sbf1 = sb("Sbf1", [128, 2, 256], BF16, grp="G")
    o_t = sb("o_t", [128, 4, 128], grp="G")
    oF_t = sb("oF_t", [128, 4, 128], grp="G")
    zg_t = sb("zg_t", [128, 4, 128], BF16, grp="G")
    osq = sb("osq", [128, 512], BF16, grp="G")
    onc = sb("onc", [128, 1])
    TX = max(T, 2048)
    Xs = [sb("Xf", [128, 2, TX], grp="S"), sb("Xb", [128, 2, TX], grp="S")]
    u_ft = sb("u_ft", [128, T], BF16, grp="S")
    BT = sb("BT", [128, 2, 2, 16, 128], BF16, grp="S")
    CTm = sb("CTm", [128, 2, 16, 32], grp="S")
    Xblk = [[Buf() for _ in range(max(NB, 1))] for _ in range(2)]

    class _Alias:
        def __init__(self, base, shape):
            self.ap = base.ap.rearrange("p c t -> p (c t)")[:, 0:4096].rearrange("p (a j k) -> p a j k", a=2, j=16)
            self.b = base.b

        def __getitem__(self, k):
            return self.ap[k]
    Bx = _Alias(Xs[0], None)
    Bbar = _Alias(Xs[1], None)
    s5par = sb("s5par", [128, 3, 16], grp="S")
    h0t = sb("h0t", [128, 2, 2, 16], grp="S")
    NPW = 18
    PW = sb("PW", [128, 2, NPW, 3, 16], grp="S")
    PWm = sb("PWm", [128, 2, NPW, 3, 16], grp="S")
    s5tmp = sb("s5tmp", [128, 12, 16], grp="S")
    s5i = sb("s5i", [128, 16], I32, grp="S")
    STG = sb("STG", [128, 2 * 2 * NS * 16], grp="S")
    stgT = sb("stgT", [128, 128], grp="S")
    y5st = sb("y5st", [32, 512], grp="S")
    dcol = sb("dcol", [128, 4], grp="SP")
    bgl = sb("bgl", [128, 4], grp="SP")
    wgb = sb("wgb", [128, 4, 512], BF16, grp="SP")
    y5b = sb("y5b", [128, 4, 512], grp="SP")
    ub = sb("ub", [128, 4, 512], BF16, grp="SP")
    z5b = sb("z5b", [128, 4, 512], BF16, grp="SP")
    zz = sb("zz", [128, 4, 512], grp="SP")
    zzb = sb("zzb", [128, 4, 512], BF16, grp="SP")
    sg = sb("sg", [128, 4, 512], grp="SP")
    y5o = sb("y5o", [128, 4, 512], BF16, grp="SP")

    pw_exps = list(range(1, 9)) + [16, 24, 32, 40, 48, 56, 64, 128, 192, 256]
    pidx = {m: i for i, m in enumerate(pw_exps)}
    assert len(pw_exps) <= NPW

    def s5_post_setup(e):
        dma(dcol[:], s5d.ap[e], [], [dcol])
        dma(bgl[:], bglu.ap[e], [], [bgl])
        for kt in range(4):
            dma(wstS[:, :512], wglu.ap[e, kt * 128:(kt + 1) * 128, :], [], [wstS])
            cp("pool", wgb[:, kt, :], wstS[:, :512], [wstS], [wgb])

    def s5_setup(e):
        dma(h0t[:], s5h0.ap[e].rearrange("d c p j -> p d c j"), [], [h0t])
        dma(CTm[:].rearrange("p a j o -> p a (j o)"), s5c.ap[e].rearrange("a p x -> p a x"), [], [CTm])
        dma(Bx[:].rearrange("p a j k -> p a (j k)"), s5b.ap[e].rearrange("a p x -> p a x"), [], [Bx])
        for d in range(2):
            dma(s5par[:], s5p.ap[e, d].rearrange("a p j -> p a j"), [], [s5par])
            lre, lim, ldt = s5par[:, 0, :], s5par[:, 1, :], s5par[:, 2, :]
            tmp = lambda i: s5tmp[:, i, :]
            R = [s5par, s5tmp]
            W = [s5tmp]
            act(tmp(0), ldt, AF.Exp, R, W)
            tt("dve", tmp(1), lre, tmp(0), ALU.mult, R, W)
            tt("dve", tmp(2), lim, tmp(0), ALU.mult, R, W)
            act(tmp(3), tmp(1), AF.Exp, R, W)
            for (dst, shift) in ((4, 0.0), (5, math.pi / 2)):
                ts("dve", tmp(6), tmp(2), shift, ALU.add, R, W, 1.0 / (2 * math.pi), ALU.mult)
                cp("dve", s5i[:], tmp(6), R, [s5i])
                cp("dve", tmp(7), s5i[:], [s5i], W)
                ts("dve", tmp(6), tmp(2), shift, ALU.add, R, W)
                stt(tmp(6), tmp(7), -2 * math.pi, tmp(6), ALU.mult, ALU.add, R, W)
                ts("dve", tmp(6), tmp(6), math.pi, ALU.min, R, W, -math.pi, ALU.max)
                act(tmp(dst), tmp(6), AF.Sin, R, W)
            P = lambda m, c: PW[:, d, pidx[m], c, :]
            RW = [PW, s5tmp]
            tt("dve", P(1, 0), tmp(3), tmp(5), ALU.mult, RW, [PW])
            tt("dve", P(1, 1), tmp(3), tmp(4), ALU.mult, RW, [PW])
            tt("dve", tmp(6), lre, lre, ALU.mult, R, W)
            tt("dve", tmp(7), lim, lim, ALU.mult, R, W)
            tt("dve", tmp(6), tmp(6), tmp(7), ALU.add, R, W)
            S.op("dve", lambda e_, o=tmp(6): e_.reciprocal(out=o, in_=o), bl(R), bl(W))
            ts("dve", tmp(7), P(1, 0), -1.0, ALU.add, RW, W)
            tt("dve", tmp(8), tmp(7), lre, ALU.mult, R, W)
            tt("dve", tmp(9), P(1, 1), lim, ALU.mult, RW + [s5par], W)
            tt("dve", tmp(8), tmp(8), tmp(9), ALU.add, R, W)
            tt("dve", tmp(8), tmp(8), tmp(6), ALU.mult, R, W)
            tt("dve", tmp(9), P(1, 1), lre, ALU.mult, RW + [s5par], W)
            tt("dve", tmp(10), tmp(7), lim, ALU.mult, R, W)
            tt("dve", tmp(9), tmp(9), tmp(10), ALU.subtract, R, W)
            tt("dve", tmp(9), tmp(9), tmp(6), ALU.mult, R, W)
            crb = s5tmp[:, 8, :].unsqueeze(2).to_broadcast([128, 16, 128])
            cib = s5tmp[:, 9, :].unsqueeze(2).to_broadcast([128, 16, 128])
            RB = [Bx, s5tmp, Bbar]
            tt("dve", Bbar[:, 0], Bx[:, 0], crb, ALU.mult, RB, [Bbar])
            tt("dve", Bbar[:, 1], Bx[:, 1], cib, ALU.mult, RB, [Bbar])
            tt("dve", Bbar[:, 0], Bbar[:, 0], Bbar[:, 1], ALU.subtract, RB, [Bbar])
            tt("dve", Bbar[:, 1], Bx[:, 1], crb, ALU.mult, RB, [Bbar])
            for j in range(16):
                stt(Bbar[:, 1, j, :], Bx[:, 0, j, :], s5tmp[:, 9, j:j + 1], Bbar[:, 1, j, :], ALU.mult, ALU.add, RB, [Bbar])
            for c in range(2):
                for j4 in range(4):
                    for jj in range(4):
                        j = j4 * 4 + jj
                        tr(pTf[:, jj * 128:(jj + 1) * 128], Bbar[:, c, j, :], ident_f, [Bbar, cs], [pTf])
                    cp("act", BT[:, d, c, j4 * 4:(j4 + 1) * 4, :].rearrange("p j k -> p (j k)"), pTf[:, :], [pTf], [BT])
            def cmul(mo, ma, mb_):
                a_re, a_im = P(ma, 0), P(ma, 1)
                b_re, b_im = P(mb_, 0), P(mb_, 1)
                tt("dve", tmp(10), a_im, b_im, ALU.mult, RW, W)
                tt("dve", tmp(11), a_re, b_re, ALU.mult, RW, W)
                tt("dve", tmp(6), a_re, b_im, ALU.mult, RW, W)
                tt("dve", tmp(7), a_im, b_re, ALU.mult, RW, W)
                tt("dve", P(mo, 0), tmp(11), tmp(10), ALU.subtract, RW, [PW])
                tt("dve", P(mo, 1), tmp(6), tmp(7), ALU.add, RW, [PW])
            for m in range(2, 9):
                cmul(m, m - 1, 1)
            for m in range(16, 65, 8):
                cmul(m, m - 8, 8)
            cmul(128, 64, 64)
            cmul(192, 128, 64)
            cmul(256, 192, 64)
            ts("dve", PW[:, d, :, 2, :], PW[:, d, :, 1, :], -1.0, ALU.mult, [PW], [PW])
            ts("dve", PWm[:, d], PW[:, d], mflag, ALU.mult, [PW, mt], [PWm])

    def s5_view(X, comp, start, step, count, rev, inner=None):
        base = X[:, comp, 0:1]
        off = base.offset
        pstride = base.ap[0][0]
        if rev:
            o = off + (T - 1 - start)
            dims = [[pstride, 128], [-step, count]]
            if inner is not None:
                dims.append([-inner[0], inner[1]])
        else:
            o = off + start
            dims = [[pstride, 128], [step, count]]
            if inner is not None:
                dims.append([inner[0], inner[1]])
        return bass.AP(base.tensor, o, dims)

    def s5_scan(X, d, j, rev, fence0):
        XW = Xblk[d]
        XB = XW + [PW, PWm]
        stt_ = {"fence": fence0, "C": None, "D": None}

        def cmac(tgt, src, pw_tile, m, chain):
            pr = pw_tile[:, d, pidx[m], 0, j:j + 1]
            pi = pw_tile[:, d, pidx[m], 1, j:j + 1]
            pn = pw_tile[:, d, pidx[m], 2, j:j + 1]
            f = stt_["fence"]
            pc, pd = (stt_["C"], stt_["D"]) if chain else (None, None)
            ta = stt(tgt(0), src(0), pr, tgt(0), ALU.mult, ALU.add, XB, XW, extra=[f, pc])
            yield
            tb = stt(tgt(1), src(0), pi, tgt(1), ALU.mult, ALU.add, XB, XW, extra=[f, pc])
            yield
            tc = stt(tgt(0), src(1), pn, tgt(0), ALU.mult, ALU.add, XB, XW, extra=[f, pd, ta])
            yield
            td = stt(tgt(1), src(1), pr, tgt(1), ALU.mult, ALU.add, XB, XW, extra=[f, pd, tb])
            yield
            stt_["C"], stt_["D"], stt_["last"] = tc, td, td

        def fence():
            stt_["fence"] = stt_.get("last", stt_["fence"])
            stt_["C"] = stt_["D"] = None

        for (s, K) in ((1, 8), (8, 8), (64, 4)):
            n = T // (s * K)
            fence()
            for jj in range(1, K):
                tgt = lambda c, s=s, K=K, jj=jj, n=n: s5_view(X, c, (jj + 1) * s - 1, K * s, n, rev)
                src = lambda c, s=s, K=K, jj=jj, n=n: s5_view(X, c, jj * s - 1, K * s, n, rev)
                yield from cmac(tgt, src, PW, s, True)
        fence()
        for sg_ in range(1, NS):
            tgt = lambda c, sg_=sg_: s5_view(X, c, 256 * (sg_ + 1) - 1, 1, 1, rev)
            src = lambda c, sg_=sg_: s5_view(X, c, 256 * sg_ - 1, 1, 1, rev)
            yield from cmac(tgt, src, PWm, 256, True)
        fence()
        if NS > 1:
            for jj in range(3):
                tgt = lambda c, jj=jj: s5_view(X, c, 256 + 64 * (jj + 1) - 1, 256, NS - 1, rev)
                src = lambda c: s5_view(X, c, 255, 256, NS - 1, rev)
                yield from cmac(tgt, src, PWm, 64 * (jj + 1), False)
        for (s, K, nin) in ((8, 8, 4), (1, 8, 32)):
            fence()
            for jj in range(K - 1):
                m = s * (jj + 1)
                tgt = lambda c, s=s, K=K, jj=jj, nin=nin: s5_view(X, c, s * K + (jj + 1) * s - 1, 256, NS, rev, inner=(s * K, nin - 1))
                src = lambda c, s=s, K=K, nin=nin: s5_view(X, c, s * K - 1, 256, NS, rev, inner=(s * K, nin - 1))
                yield from cmac(tgt, src, PW, m, False)
                if NS > 1:
                    tgt = lambda c, s=s, jj=jj: s5_view(X, c, 256 + (jj + 1) * s - 1, 256, NS - 1, rev)
                    src = lambda c: s5_view(X, c, 255, 256, NS - 1, rev)
                    yield from cmac(tgt, src, PWm, m, False)

    def chk(tag, l):
        if stop == "%s%d" % (tag, l):
            raise _Stop()

    def chk2(tag):
        if stop == tag:
            raise _Stop()

    def even_layer(l, xsrc, xdst):
        e = l // 2
        load_weight_bf16(wbf, w_in_e.ap[e], EIN)
        chk2("W")
        adaln(l)
        chk2("ADA")
        dma(w2cf[:], w2cat.ap[e], [], [w2cf])
        cp("dve", w2c[:], w2cf[:], [w2cf], [w2c])
        dma(onc[:], onorm.ap[e], [], [onc])
        mset("dve", lrT[:], 0.0, [lrT])
        mset("dve", lrT[32:33, :], 1.0, [lrT])
        def front_e(ti):
            norm_mod_T(l, ti, xsrc)
            in_proj(EIN, pbf2[ti % 2])

        front_e(0)
        for ti in range(NT):
            t0 = ti * 128
            if ti + 1 < NT:
                front_e(ti + 1)
            pbf_t = pbf2[ti % 2]
            transpose_store(pbf_t[:, 0:512], 4, qkT_s, 0, ti, [pbf_t])
            chk2("TS")
            dma(kv_s.ap[t0:t0 + 128, :], pbf_t[:, 256:1024], [pbf_t], [kv_s.bs[ti]])
            chk2("KV")
            tr(pT[:32, 0:128], pbf_t[:, 1024:1056], ident_b, [pbf_t, csb], [pT])
            cp("act", lrT[0:32, :], pT[:32, 0:128], [pT], [lrT])
            mm(pC[:, :], lrT[:, :], w2c[:, :], True, True, [lrT, w2c], [pC])
            act(w3[:, :512], pC[:, :], AF.Exp, [pC], [w3], scale=-1.0)
            act(w3[:, 512:1024], w3[:, :512], AF.Ln, [w3], [w3], bias=cs[:, C_ONE:C_ONE + 1])
            dma(sp_s.ap[t0:t0 + 128, :], w3[:, 512:1024], [w3], [sp_s.bs[ti]])
            chk2("GATE")
            transpose_store(pbf_t[:, 1056:1568], 4, zgT_s, 0, ti, [pbf_t])
            transpose_store(pbf_t[:, 1568:2080], 4, uT_s, 0, ti, [pbf_t])
            transpose_store(pbf_t[:, 2080:2592], 4, z5T_s, 0, ti, [pbf_t])
            chk2("T%d" % ti)
        chk('P', l)
        S.barrier()
        def gla_gen():
            for d in range(2):
                TI, TR_, MK = (C_TIF, C_TRF, C_MF) if d == 0 else (C_TIB, C_TRB, C_MB)
                mset("dve", Sst[:], 0.0, [Sst])
                yield
                for h in range(4):
                    pr, hh = h // 2, h % 2
                    dma(Sst[hh * 64:(hh + 1) * 64, pr, hh * 128:(hh + 1) * 128], gla0.ap[e, d, h], [], [Sst])
                    yield
                blkm = cs[:, C_BLK:C_BLK + 256].unsqueeze(1).to_broadcast([128, 2, 256])
                tt("pool", Sbf[:], Sst[:], blkm, ALU.mult, [Sst, cs], [Sbf])
                yield
                order = list(range(NT)) if d == 0 else list(range(NT - 1, -1, -1))
                for oi, ti in enumerate(order):
                    t0 = ti * 128
                    if oi > 0 and oi % 2 == 0:
                        ts("dve", Sst[:], Sst[:], mflag, ALU.mult, [Sst, mt], [Sst])
                        yield
                        tt("pool", Sbf[:], Sst[:], blkm, ALU.mult, [Sst, cs], [Sbf])
                        yield
                    dma(qk_t[:], qkT_s.ap[:, t0:t0 + 128].rearrange("(a p) t -> p a t", p=128), [qkT_s.bs[ti]], [qk_t])
                    yield
                    dma(kv_t[:], kv_s.ap[t0:t0 + 128, :], [kv_s.bs[ti]], [kv_t])
                    yield
                    dma(sp_t[:], sp_s.ap[t0:t0 + 128, d * 256:(d + 1) * 256], [sp_s.bs[ti]], [sp_t])
                    yield
                    yield
                    for pr in range(2):
                        mm(pC[:, pr * 128:(pr + 1) * 128], sp_t[:, pr * 128:(pr + 1) * 128], cs[:, TI:TI + 128], True, True,
                           [sp_t, cs], [pC])
                        yield
                    mm(pD[:, 0:256], cs[:, TR_:TR_ + 128], sp_t[:, :], True, True, [sp_t, cs], [pD])
                    yield
                    yield
                    act(E1[:].rearrange("p a t -> p (a t)"), pC[:, 0:256], AF.Exp, [pC], [E1])
                    yield
                    act(E2[:].rearrange("p a t -> p (a t)"), pC[:, 0:256], AF.Exp, [pC], [E2], scale=-1.0)
                    yield
                    act(E3[:], pD[:, 0:256], AF.Exp, [pD], [E3])
                    yield
                    stt(qbT[:], qk_t[:, 0:2, :], 0.125, E1[:], ALU.mult, ALU.mult, [qk_t, E1], [qbT])
                    yield
                    tt("pool", kbT[:], qk_t[:, 2:4, :], E2[:], ALU.mult, [qk_t, E2], [kbT])
                    yield
                    tt("pool", kd[:], kv_t[:, 0:256], E3[:], ALU.mult, [kv_t, E3], [kd])
                    yield
                    yield
                    for h in range(4):
                        pr, hh = h // 2, h % 2
                        pX = (pE, pA)[hh]
                        mm(pX[:, pr * 128:(pr + 1) * 128], kbT[hh * 64:(hh + 1) * 64, pr, :], qbT[hh * 64:(hh + 1) * 64, pr, :],
                           True, True, [kbT, qbT], [pX])
                        yield
                    yield
                    for hh in range(2):
                        pX = (pE, pA)[hh]
                        av = attm[:].rearrange("p (pr hh) t -> p hh pr t", hh=2)[:, hh]
                        tt("dve", av, pX[:, 0:256].rearrange("p (h t) -> p h t", h=2),
                           cs[:, MK:MK + 128].unsqueeze(1).to_broadcast([128, 2, 128]), ALU.mult, [pX, cs], [attm])
                        yield
                    yield
                    chunks = (0, 1) if d == 0 else (1, 0)
                    for ci, ch in enumerate(chunks):
                        c0 = ch * 64
                        dcolx = (c0 + 63) if d == 0 else c0
                        pUp = (pD, pA)[ch]
                        for pr in range(2):
                            mm(pUp[:, 256:512], kd[c0:c0 + 64, pr * 128:(pr + 1) * 128],
                               kv_t[c0:c0 + 64, 256 + pr * 256:256 + (pr + 1) * 256], True, True, [kd, kv_t], [pUp])
                            yield
                            stt(Sst[:, pr, :], Sst[:, pr, :], E1[:, pr, dcolx:dcolx + 1], pUp[:, 256:512], ALU.mult, ALU.add,
                                [Sst, E1, pUp], [Sst])
                            yield
                        if ci == 0:
                            tt("pool", Sbf1[:], Sst[:], blkm, ALU.mult, [Sst, cs], [Sbf1])
                            yield
                    for h in range(4):
                        pr, hh = h // 2, h % 2
                        mm(pF[:, h * 128:(h + 1) * 128], kv_t[:, 256 + h * 128:256 + (h + 1) * 128], attm[:, h, :],
                           True, False, [kv_t, attm], [pF])
                        yield
                        for ci, ch in enumerate(chunks):
                            c0 = ch * 64
                            Sx = (Sbf, Sbf1)[ci]
                            mm(pF[:, h * 128 + c0:h * 128 + c0 + 64], Sx[:, pr, hh * 128:(hh + 1) * 128],
                               qbT[:, pr, c0:c0 + 64], False, (ci == 1), [Sx, qbT], [pF])
                            yield
                    tt("pool", Sbf[:], Sst[:], blkm, ALU.mult, [Sst, cs, pF], [Sbf])
                    yield
                    yield
                    if oi % 2 == 1:
                        seg = ti // 2
                        for h in range(4):
                            pr, hh = h // 2, h % 2
                            dma(ngla.ap[e, d, seg, h], Sst[hh * 64:(hh + 1) * 64, pr, hh * 128:(hh + 1) * 128], [Sst], [ngla])
                            yield
                    if d == 0:
                        cp("act", o_t[:].rearrange("p h t -> p (h t)"), pF[:, :], [pF], [o_t])
                        yield
                        dma(oF_s.ap[:, t0:t0 + 128].rearrange("(a p) t -> p a t", p=128), o_t[:], [o_t], [oF_s.bs[ti]])
                        yield
                    else:
                        dma(oF_t[:], oF_s.ap[:, t0:t0 + 128].rearrange("(a p) t -> p a t", p=128), [oF_s.bs[ti]], [oF_t])
                        yield
                        dma(zg_t[:], zgT_s.ap[:, t0:t0 + 128].rearrange("(a p) t -> p a t", p=128), [zgT_s.bs[ti]], [zg_t])
                        yield
                        tt("dve", o_t[:].rearrange("p h t -> p (h t)"), pF[:, :], oF_t[:].rearrange("p h t -> p (h t)"),
                           ALU.add, [pF, oF_t], [o_t])
                        yield
                        of = o_t[:].rearrange("p h t -> p (h t)")
                        act(osq[:], of, AF.Square, [o_t], [osq])
                        yield
                        mm(pC[:, :], csb[:, C_ONE:C_ONE + 128], osq[:], True, True, [osq, csb], [pC])
                        yield
                        act(w1[:, :512], pC[:, :], AF.Ln, [pC], [w1], scale=1.0 / 128, bias=cs[:, C_EPS:C_EPS + 1])
                        yield
                        act(w1[:, :512], w1[:, :512], AF.Exp, [w1], [w1], scale=-0.5)
                        yield
                        tt("dve", w1[:, :512], w1[:, :512], of, ALU.mult, [w1, o_t], [w1])
                        yield
                        act(w1[:, 512:1024], zg_t[:].rearrange("p h t -> p (h t)"), AF.Silu, [zg_t], [w1])
                        yield
                        stt(trs[:, 0:4, :].rearrange("p a t -> p (a t)"), w1[:, :512], onc[:, 0:1], w1[:, 512:1024],
                            ALU.mult, ALU.mult, [w1, onc], [trs])
                        yield
                        dma(yT_s.ap[0:512, t0:t0 + 128].rearrange("(a p) t -> p a t", p=128), trs[:, 0:4, :], [trs], [yT_s.bs[ti]])
                        yield
            yield

        def s5_gen():
            s5_setup(e)
            for j in range(16):
                ft = j // 4
                if j % 4 == 0:
                    dma(u_ft[:], uT_s.ap[ft * 128:(ft + 1) * 128, :], uT_s.bs, [u_ft])
                fences = []
                for d in range(2):
                    X = Xs[d]
                    rev = (d == 1)
                    for blk in range(NB):
                        for c in range(2):
                            pp = (pB, pTf)[c]
                            mm(pp[:, :], BT[:, d, c, j, :], u_ft[:, blk * 512:(blk + 1) * 512], True, True, [BT, u_ft], [pp])
                            cp("act", X[:, c, blk * 512:(blk + 1) * 512], pp[:, :], [pp], [X, Xblk[d][blk]])
                            yield
                    v0 = lambda c: s5_view(X, c, 0, 1, 1, rev)
                    pr_ = PW[:, d, pidx[1], 0, j:j + 1]; pi_ = PW[:, d, pidx[1], 1, j:j + 1]; pn_ = PW[:, d, pidx[1], 2, j:j + 1]
                    stt(v0(0), h0t[:, d, 0, j:j + 1], pr_, v0(0), ALU.mult, ALU.add, [h0t, PW] + Xblk[d], Xblk[d])
                    stt(v0(0), h0t[:, d, 1, j:j + 1], pn_, v0(0), ALU.mult, ALU.add, [h0t, PW] + Xblk[d], Xblk[d])
                    stt(v0(1), h0t[:, d, 1, j:j + 1], pr_, v0(1), ALU.mult, ALU.add, [h0t, PW] + Xblk[d], Xblk[d])
                    fences.append(stt(v0(1), h0t[:, d, 0, j:j + 1], pi_, v0(1), ALU.mult, ALU.add, [h0t, PW] + Xblk[d], Xblk[d]))
                gens = [s5_scan(Xs[d], d, j, d == 1, fences[d]) for d in range(2)]
                alive = [True, True]
                while any(alive):
                    for d in range(2):
                        if alive[d]:
                            try:
                                next(gens[d])
                                yield
                            except StopIteration:
                                alive[d] = False
                for d in range(2):
                    X = Xs[d]
                    rev = (d == 1)
                    for c in range(2):
                        col0 = ((d * 2 + c) * NS) * 16 + j
                        dst = bass.AP(STG[:, 0:1].tensor, STG[:, col0:col0 + 1].offset, [list(STG[:, 0:1].ap[0]), [16, NS]])
                        cp("pool", dst, s5_view(X, c, 255, 256, NS, rev), Xblk[d], [STG])
                for blk in range(NB):
                    k = 0
                    for d in range(2):
                        for c in range(2):
                            mm(pT32[0:32, :], CTm[:, c, j, :], Xs[d][:, c, blk * 512:(blk + 1) * 512], k == 0, k == 3,
                               [CTm, Xblk[d][blk]], [pT32])
                            k += 1
                    cp("act", y5st[:, :], pT32[0:32, :], [pT32], [y5st])
                    dma(y5T_s.ap[j * 32:(j + 1) * 32, blk * 512:(blk + 1) * 512], y5st[:, :], [y5st], [y5T_s.bs[j]])
                    yield
            gsz = min(8, NS)
            for d in range(2):
                for c in range(2):
                    for s0 in range(0, NS, gsz):
                        col0 = ((d * 2 + c) * NS + s0) * 16
                        ncol = gsz * 16
                        tr(pTf[:ncol, 0:128], STG[:, col0:col0 + ncol], ident_f, [STG, cs], [pTf])
                        cp("act", stgT[:ncol, :], pTf[:ncol, 0:128], [pTf], [stgT])
                        dma(ns5.ap[e, d, c, s0 * 16:s0 * 16 + ncol, :], stgT[:ncol, :], [stgT], [ns5])
            yield

        gG, gS = gla_gen(), s5_gen()
        aG = aS = True
        RATIO = float(os.environ.get("KRATIO", "2.0"))
        acc = 0.0
        while aG or aS:
            if aG:
                try:
                    next(gG)
                except StopIteration:
                    aG = False
            acc += RATIO if aG else 1000.0
            while acc >= 1.0 and aS:
                acc -= 1.0
                try:
                    next(gS)
                except StopIteration:
                    aS = False
            if not aS:
                acc = 0.0

        chk('S', l)
        S.barrier()
        s5_post_setup(e)
        for blk in range(NB):
            c0 = blk * 512
            tis = list(range(blk * 4, blk * 4 + 4))
            dma(y5b[:], y5T_s.ap[:, c0:c0 + 512].rearrange("(a p) t -> p a t", p=128), y5T_s.bs, [y5b])
            dma(ub[:], uT_s.ap[:, c0:c0 + 512].rearrange("(a p) t -> p a t", p=128), [uT_s.bs[i] for i in tis], [ub])
            dma(z5b[:], z5T_s.ap[:, c0:c0 + 512].rearrange("(a p) t -> p a t", p=128), [z5T_s.bs[i] for i in tis], [z5b])
            for a in range(4):
                stt(y5b[:, a, :], ub[:, a, :], dcol[:, a:a + 1], y5b[:, a, :], ALU.mult, ALU.add, [ub, dcol, y5b], [y5b])
            act(zz[:].rearrange("p a t -> p (a t)"), y5b[:].rearrange("p a t -> p (a t)"), AF.Gelu_apprx_tanh, [y5b], [zz])
            cp("pool", zzb[:].rearrange("p a t -> p (a t)"), zz[:].rearrange("p a t -> p (a t)"), [zz], [zzb])
            for fo in range(4):
                pp = (pA, pB)[fo % 2]
                for kt in range(4):
                    mm(pp[:, :], wgb[:, kt, fo * 128:(fo + 1) * 128], zzb[:, kt, :], kt == 0, kt == 3, [wgb, zzb], [pp])
                act(sg[:, fo, :], pp[:, :], AF.Sigmoid, [pp, bgl], [sg], bias=bgl[:, fo:fo + 1])
            tt("dve", sg[:].rearrange("p a t -> p (a t)"), sg[:].rearrange("p a t -> p (a t)"),
               zz[:].rearrange("p a t -> p (a t)"), ALU.mult, [sg, zz], [sg])
            act(zz[:].rearrange("p a t -> p (a t)"), z5b[:].rearrange("p a t -> p (a t)"), AF.Silu, [z5b], [zz])
            tt("dve", y5o[:].rearrange("p a t -> p (a t)"), sg[:].rearrange("p a t -> p (a t)"),
               zz[:].rearrange("p a t -> p (a t)"), ALU.mult, [sg, zz], [y5o])
            dma(yT_s.ap[512:1024, c0:c0 + 512].rearrange("(a p) t -> p a t", p=128), y5o[:], [y5o],
                [yT_s.bs[i] for i in tis])
        chk('SP', l)
        out_proj_residual(l, xsrc, xdst, yT_s, w_out_e.ap[e])
        S.barrier()
        chk('O', l)

    qkwb = sb("qkwb", [128, 128])
    sinkb = sb("sinkb", [128, 8])
    rp_t = sb("rp_t", [128, 128], grp="P")
    qn = sb("qn", [128, 640], grp="P")
    qr = sb("qr", [128, 640], grp="P")
    kdup = sb("kdup", [128, 2, 2, 64], BF16, grp="P")
    ckT = sb("ckT", [128, 2, 256], BF16)
    cvt = sb("cvt", [128, 2, 128], BF16)
    ckf = sb("ckf", [128, 2, 128], grp="P")
    qT_t2 = [sb("qT_t%d" % i, [128, 4, 128], BF16, grp="A") for i in range(2)]
    kT_w2 = [sb("kT_w%d" % i, [128, 2, 384], BF16, grp="A") for i in range(2)]
    v_w2 = [sb("v_w%d" % i, [128, 3, 128], BF16, grp="A") for i in range(2)]
    scs2 = [sb("scs%d" % i, [128, 640], grp="A") for i in range(2)]
    pexp2 = [sb("pexp%d" % i, [128, 640], BF16, grp="A") for i in range(2)]
    pTt2 = [sb("pTt%d" % i, [128, 5, 128], BF16, grp="A") for i in range(2)]
    sm2 = [sb("sm%d" % i, [128, 8, 8], grp="A") for i in range(2)]
    bandT = sb("bandT", [128, 384], grp="A")
    oat = sb("oat", [128, 512], grp="A")
    sz_t2 = [sb("sz_t%d" % i, [128, 512], grp="A") for i in range(2)]
    cvw = sb("cvw", [128, 16])
    u1h2 = [sb("u1h%d" % i, [128, 4, 130], grp="A") for i in range(2)]
    u2t2 = [sb("u2t%d" % i, [128, 4, 128], grp="A") for i in range(2)]
    cacc = sb("cacc", [128, 4, 128], grp="A")

    def odd_layer(l, xsrc, xdst):
        e = l // 2
        load_weight_bf16(wbf, w_in_o.ap[e], OIN)
        adaln(l)
        dma(qkwb[:], qkw.ap[e:e + 1, :].partition_broadcast(128), [], [qkwb])
        ts("dve", qkwb[:, 0:64], qkwb[:, 0:64], 0.125, ALU.mult, [qkwb], [qkwb])
        dma(sinkb[:], sinkv.ap[e:e + 1, :].partition_broadcast(128), [], [sinkb])
        dma(cvw[:], convw.ap[e], [], [cvw])
        for hf in range(2):
            dma(ckf[:, 0, :], ckv.ap[e, 0, hf * 128:(hf + 1) * 128].rearrange("t k d -> t (k d)"), [], [ckf])
            dma(ckf[:, 1, :], ckv.ap[e, 1, hf * 128:(hf + 1) * 128].rearrange("t k d -> t (k d)"), [], [ckf])
            cp("dve", cvt[:, hf, :], ckf[:, 1, :], [ckf], [cvt])
            for kv in range(2):
                for c in range(2):
                    cp("dve", kdup[:, kv, c, :], ckf[:, 0, kv * 64:(kv + 1) * 64], [ckf], [kdup])
            for kv in range(2):
                tr(pT[:, kv * 128:(kv + 1) * 128], kdup[:, kv].rearrange("p c d -> p (c d)"), ident_b, [kdup, csb], [pT])
            for kv in range(2):
                cp("act", ckT[:, kv, hf * 128:(hf + 1) * 128], pT[:, kv * 128:(kv + 1) * 128], [pT], [ckT])
        def front_o(ti):
            norm_mod_T(l, ti, xsrc)
            in_proj(OIN, proj2[ti % 2])

        front_o(0)
        for ti in range(NT):
            t0 = ti * 128
            if ti + 1 < NT:
                front_o(ti + 1)
            proj_t = proj2[ti % 2]
            pbf_t = pbf2[ti % 2]
            dma(rp_t[:], rope.ap[t0:t0 + 128, :], [], [rp_t])
            act(w1[:, :640], proj_t[:, 0:640], AF.Square, [proj_t], [w1])
            S.op("dve", lambda e_: e_.tensor_reduce(out=small[:, 0:10], in_=w1[:, :640].rearrange("p (h d) -> p h d", d=64),
                                                    axis=AX.X, op=ALU.add), bl([w1]), bl([small]))
            act(small[:, 0:10], small[:, 0:10], AF.Ln, [small], [small], scale=1.0 / 64, bias=cs[:, C_EPS:C_EPS + 1])
            act(small[:, 0:10], small[:, 0:10], AF.Exp, [small], [small], scale=-0.5)
            tt("dve", qn[:].rearrange("p (h d) -> p h d", d=64), proj_t[:, 0:640].rearrange("p (h d) -> p h d", d=64),
               small[:, 0:10].unsqueeze(2).to_broadcast([128, 10, 64]), ALU.mult, [proj_t, small], [qn])
            tt("dve", qn[:, 0:512].rearrange("p (h d) -> p h d", d=64), qn[:, 0:512].rearrange("p (h d) -> p h d", d=64),
               qkwb[:, 0:64].unsqueeze(1).to_broadcast([128, 8, 64]), ALU.mult, [qn, qkwb], [qn])
            tt("dve", qn[:, 512:640].rearrange("p (h d) -> p h d", d=64), qn[:, 512:640].rearrange("p (h d) -> p h d", d=64),
               qkwb[:, 64:128].unsqueeze(1).to_broadcast([128, 2, 64]), ALU.mult, [qn, qkwb], [qn])
            dma(nkv.ap[e, 0, t0:t0 + 128, :], qn[:, 512:640], [qn], [nkv])
            dma(nkv.ap[e, 1, t0:t0 + 128, :], proj_t[:, 640:768], [proj_t], [nkv])
            v5 = lambda tl: tl[:, 0:640].rearrange("p (h a b f) -> p h a b f", a=2, b=2, f=16)
            cosb = rp_t[:, 0:64].rearrange("p (a b f) -> p a b f", a=2, b=2).unsqueeze(1).to_broadcast([128, 10, 2, 2, 16])
            tt("dve", v5(qr), v5(qn), cosb, ALU.mult, [qn, rp_t], [qr])
            for b_ in range(2):
                sinb = rp_t[:, 64:128].rearrange("p (a b f) -> p a b f", a=2, b=2)[:, :, b_, :].unsqueeze(1).to_broadcast([128, 10, 2, 16])
                tt("pool", v5(w1)[:, :, :, b_, :], v5(qn)[:, :, :, 1 - b_, :], sinb, ALU.mult, [qn, rp_t], [w1])
            tt("dve", pbf_t[:, 0:640], qr[:, 0:640], w1[:, 0:640], ALU.add, [qr, w1], [pbf_t])
            transpose_store(pbf_t[:, 0:512], 4, qT_s, 0, ti, [pbf_t])
            for kv in range(2):
                for c in range(2):
                    cp("pool", kdup[:, kv, c, :], pbf_t[:, 512 + kv * 64:512 + (kv + 1) * 64], [pbf_t], [kdup])
            transpose_store(kdup[:].rearrange("p k c d -> p (k c d)"), 2, kT_s, 0, ti, [kdup])
            cp("pool", pbf_t[:, 640:768], proj_t[:, 640:768], [proj_t], [pbf_t])
            dma(v_s.ap[t0:t0 + 128, :], pbf_t[:, 640:768], [pbf_t], [v_s.bs[ti]])
            act(w2[:, :512], proj_t[:, 768:1280], AF.Silu, [proj_t], [w2])
            dma(sz_s.ap[t0:t0 + 128, :], w2[:, :512], [w2], [sz_s.bs[ti]])
            tt("dve", w3[:, 0:512], proj_t[:, 2304:2816], proj_t[:, 1280:1792], ALU.mult, [proj_t], [w3])
            act(w3[:, 512:1024], proj_t[:, 2816:3328], AF.Silu, [proj_t], [w3])
            tt("dve", w3[:, 512:1024], w3[:, 512:1024], proj_t[:, 1792:2304], ALU.mult, [w3, proj_t], [w3])
            transpose_store_f32(w3[:, 0:512], 4, u1T_s, 0, ti, [w3])
            transpose_store_f32(w3[:, 512:1024], 4, u2T_s, 0, ti, [w3])
        chk('P', l)
        S.barrier()
        def loadsA(ti):
            if ti >= NT:
                return
            t0 = ti * 128
            tp = max(ti - 1, 0)
            tn = min(ti + 1, NT - 1)
            qT_t, kT_w, v_w, sz_t, u1h, u2t = (x[ti % 2] for x in (qT_t2, kT_w2, v_w2, sz_t2, u1h2, u2t2))
            dma(qT_t[:], qT_s.ap[:, t0:t0 + 128].rearrange("(a p) t -> p a t", p=128), [qT_s.bs[ti]], [qT_t])
            for wi, tw in enumerate((tp, ti, tn)):
                dma(kT_w[:, :, wi * 128:(wi + 1) * 128], kT_s.ap[:, tw * 128:(tw + 1) * 128].rearrange("(k p) t -> p k t", p=128),
                    [kT_s.bs[tw]], [kT_w])
                dma(v_w[:, wi, :], v_s.ap[tw * 128:(tw + 1) * 128, :], [v_s.bs[tw]], [v_w])
            dma(sz_t[:], sz_s.ap[t0:t0 + 128, :], [sz_s.bs[ti]], [sz_t])
            dma(u1h[:, :, 1:129], u1T_s.ap[:, t0:t0 + 128].rearrange("(a p) t -> p a t", p=128), [u1T_s.bs[ti]], [u1h])
            lo = t0 - 1 if ti > 0 else 0
            hi = t0 + 128 if ti < NT - 1 else T - 1
            dma(u1h[:, :, 0:1], u1T_s.ap[:, lo:lo + 1].rearrange("(a p) t -> p a t", p=128), [u1T_s.bs[tp]], [u1h], slow=True)
            dma(u1h[:, :, 129:130], u1T_s.ap[:, hi:hi + 1].rearrange("(a p) t -> p a t", p=128), [u1T_s.bs[tn]], [u1h], slow=True)
            dma(u2t[:], u2T_s.ap[:, t0:t0 + 128].rearrange("(a p) t -> p a t", p=128), [u2T_s.bs[ti]], [u2t])

        loadsA(0)
        for ti in range(NT):
            t0 = ti * 128
            loadsA(ti + 1)
            qT_t, kT_w, v_w, sz_t, u1h, u2t = (x[ti % 2] for x in (qT_t2, kT_w2, v_w2, sz_t2, u1h2, u2t2))
            flp = mt[:, 3 + ti:4 + ti]
            fln = mt[:, 3 + NT + ti:4 + NT + ti]
            ts("dve", bandT[:, 0:128], band[:, 0:128], flp, ALU.add, [band, mt], [bandT])
            cp("dve", bandT[:, 128:256], band[:, 128:256], [band], [bandT])
            ts("dve", bandT[:, 256:384], band[:, 256:384], fln, ALU.add, [band, mt], [bandT])

            def head_gen(hq, P):
                kv, pr, hh = hq // 4, hq // 2, hq % 2
                rows = slice(hh * 64, (hh + 1) * 64)
                pS1, pS2 = ((pC, pD), (pA, pB))[P]
                scsX, pexpX, pTtX, smX = scs2[P], pexp2[P], pTt2[P], sm2[P]
                pTX = (pT, pTf16)[P]
                pO = (pE, pF)[P]
                oc = (hq // 2) * 64
                mm(pS1[:, 0:384], qT_t[rows, pr, :], kT_w[rows, kv, :], True, True, [qT_t, kT_w], [pS1])
                mm(pS2[:, 0:256], qT_t[rows, pr, :], ckT[rows, kv, :], True, True, [qT_t, ckT], [pS2])
                yield
                tt("dve", scsX[:, 0:384], pS1[:, 0:384], bandT[:, :], ALU.add, [pS1, bandT], [scsX])
                yield
                ts("dve", scsX[:, 384:640], pS2[:, 0:256], mt[:, 2:3], ALU.add, [pS2, mt], [scsX])
                yield
                S.op("dve", lambda e_: e_.reduce_max(out=smX[:, hq, 0:1], in_=scsX[:, :], axis=AX.X), bl([scsX]), bl([smX]))
                yield
                tt("dve", smX[:, hq, 0:1], smX[:, hq, 0:1], sinkb[:, hq:hq + 1], ALU.max, [smX, sinkb], [smX])
                ts("dve", smX[:, hq, 1:2], smX[:, hq, 0:1], -1.0, ALU.mult, [smX], [smX])
                mset("dve", smX[:, hq, 2:3], 0.0, [smX])
                yield
                act(pexpX[:], scsX[:], AF.Exp, [scsX, smX], [pexpX, smX], bias=smX[:, hq, 1:2], accum_out=smX[:, hq, 2:3])
                act(smX[:, hq, 3:4], sinkb[:, hq:hq + 1], AF.Exp, [smX, sinkb], [smX], bias=smX[:, hq, 1:2])
                yield
                for k5 in range(5):
                    tr(pTX[:, k5 * 128:(k5 + 1) * 128], pexpX[:, k5 * 128:(k5 + 1) * 128], ident_b, [pexpX, csb], [pTX])
                yield
                cp("act", pTtX[:].rearrange("p a t -> p (a t)"), pTX[:, 0:640], [pTX], [pTtX])
                tt("dve", smX[:, hq, 4:5], smX[:, hq, 2:3], smX[:, hq, 3:4], ALU.add, [smX], [smX])
                S.op("dve", lambda e_: e_.reciprocal(out=smX[:, hq, 5:6], in_=smX[:, hq, 4:5]), bl([smX]), bl([smX]))
                yield
                for k5 in range(5):
                    vv = v_w[:, k5, kv * 64:(kv + 1) * 64] if k5 < 3 else cvt[:, k5 - 3, kv * 64:(kv + 1) * 64]
                    mm(pO[:, oc:oc + 64], pTtX[:, k5, :], vv, k5 == 0, k5 == 4, [pTtX, v_w, cvt], [pO])
                yield

            for h2 in range(0, 8, 2):
                gens = [head_gen(h2, 0), head_gen(h2 + 1, 1)]
                alive = [True, True]
                while any(alive):
                    for P in range(2):
                        if alive[P]:
                            try:
                                next(gens[P])
                            except StopIteration:
                                alive[P] = False
            for hq in range(8):
                pO = (pE, pF)[hq % 2]
                oc = (hq // 2) * 64
                stt(oat[:, hq * 64:(hq + 1) * 64], pO[:, oc:oc + 64], sm2[hq % 2][:, hq, 5:6], sz_t[:, hq * 64:(hq + 1) * 64],
                    ALU.mult, ALU.mult, [pO, sm2[hq % 2], sz_t], [oat])
            cp("pool", pbfA[:, 0:512], oat[:], [oat], [pbfA])
            transpose_store(pbfA[:, 0:512], 4, yT_s, 0, ti, [pbfA])
            ts("dve", u1h[:, :, 0:1], u1h[:, :, 0:1], mt[:, 3 + 2 * NT + ti:4 + 2 * NT + ti], ALU.mult, [u1h, mt], [u1h])
            ts("dve", u1h[:, :, 129:130], u1h[:, :, 129:130], mt[:, 3 + 3 * NT + ti:4 + 3 * NT + ti], ALU.mult, [u1h, mt], [u1h])
            for a in range(4):
                ts("dve", cacc[:, a, :], u1h[:, a, 1:129], cvw[:, a * 4 + 1:a * 4 + 2], ALU.mult, [u1h, cvw], [cacc],
                   cvw[:, a * 4 + 3:a * 4 + 4], ALU.add)
                stt(cacc[:, a, :], u1h[:, a, 0:128], cvw[:, a * 4:a * 4 + 1], cacc[:, a, :], ALU.mult, ALU.add, [u1h, cvw, cacc], [cacc])
                stt(cacc[:, a, :], u1h[:, a, 2:130], cvw[:, a * 4 + 2:a * 4 + 3], cacc[:, a, :], ALU.mult, ALU.add, [u1h, cvw, cacc], [cacc])
            tt("dve", trs[:, 4:8, :], cacc[:], u2t[:], ALU.mult, [cacc, u2t], [trs])
            dma(yT_s.ap[512:1024, t0:t0 + 128].rearrange("(a p) t -> p a t", p=128), trs[:, 4:8, :], [trs], [yT_s.bs[ti]])
        chk('A', l)
        out_proj_residual(l, xsrc, xdst, yT_s, w_out_o.ap[e])
        S.barrier()
        chk('O', l)

    chain = [x_in, xs[0], xs[1], xs[0], y_out]
    try:
        for l in range(4):
            if l % 2 == 0:
                even_layer(l, chain[l], chain[l + 1])
            else:
                odd_layer(l, chain[l], chain[l + 1])
    except _Stop:
        S.barrier()
    S.finish_waits([y_out.b, ngla.b, ns5.b, nkv.b] + y_out.bs)
    S.emit()
    st.close()
    return nc, S


def _consts():
    c = np.zeros((128, NCST), np.float32)
    i = np.arange(128)
    s = i[:, None]
    t = i[None, :]
    same = (s // 64) == (t // 64)
    c[:, C_ID:C_ID + 128] = np.eye(128, dtype=np.float32)
    c[:, C_TIF:C_TIF + 128] = np.where(same & (s <= t), -1.0 / 16, 0.0)
    c[:, C_TRF:C_TRF + 128] = np.where(same & (s > t), -1.0 / 16, 0.0)
    c[:, C_TIB:C_TIB + 128] = np.where(same & (s >= t), -1.0 / 16, 0.0)
    c[:, C_TRB:C_TRB + 128] = np.where(same & (s < t), -1.0 / 16, 0.0)
    c[:, C_MF:C_MF + 128] = np.where(same & (s <= t), 1.0, 0.0)
    c[:, C_MB:C_MB + 128] = np.where(same & (s >= t), 1.0, 0.0)
    c[:, C_ONE:C_ONE + 128] = 1.0
    c[:, C_EPS] = EPS
    c[:, C_EPS + 1] = 1.0
    qi = i[:, None]
    kj = i[None, :]
    NEG = -1e30
    c[:, C_BAND:C_BAND + 128] = np.where(kj >= qi, 0.0, NEG)
    c[:, C_BAND + 128:C_BAND + 256] = 0.0
    c[:, C_BAND + 256:C_BAND + 384] = np.where(kj <= qi, 0.0, NEG)
    c[0:64, C_BLK:C_BLK + 128] = 1.0
    c[64:128, C_BLK + 128:C_BLK + 256] = 1.0
    return c


def _rope_table(T, identity):
    tab = np.zeros((T, 128), np.float32)
    if identity:
        tab[:, :64] = 1.0
        return tab
    pos = np.arange(T)
    row = (pos // 64).astype(np.float32)
    col = (pos % 64).astype(np.float32)
    freq = (10000.0 ** (-np.arange(16, dtype=np.float32) / 16)).astype(np.float32)
    ar = row[:, None] * freq
    ac = col[:, None] * freq
    cos = np.concatenate([np.cos(ar), np.cos(ar), np.cos(ac), np.cos(ac)], axis=1)
    sin = np.concatenate([-np.sin(ar), np.sin(ar), -np.sin(ac), np.sin(ac)], axis=1)
    tab[:, :64] = cos
    tab[:, 64:] = sin
    return tab


def _meta(T, sample):
    NT = T // 128
    m = np.zeros((128, 3 + 4 * NT), np.float32)
    NEG = -1e30
    if sample:
        m[:, 0] = 1.0
        m[:, 1] = 1.0
        m[:, 2] = 0.0
        flp = np.zeros(NT); flp[0] = NEG
        fln = np.zeros(NT); fln[-1] = NEG
        cfl = np.ones(NT); cfl[0] = 0
        cfr = np.ones(NT); cfr[-1] = 0
    else:
        m[:, 0] = 0.0
        m[:, 1] = 0.0
        m[:, 2] = NEG
        flp = np.where(np.arange(NT) % 2 == 0, NEG, 0.0)
        fln = np.where(np.arange(NT) % 2 == 1, NEG, 0.0)
        cfl = np.where(np.arange(NT) % 2 == 0, 0.0, 1.0)
        cfr = np.where(np.arange(NT) % 2 == 1, 0.0, 1.0)
    m[:, 3:3 + NT] = flp
    m[:, 3 + NT:3 + 2 * NT] = fln
    m[:, 3 + 2 * NT:3 + 3 * NT] = cfl
    m[:, 3 + 3 * NT:3 + 4 * NT] = cfr
    return m


def _state_layout(a):
    sh = a.shape[:-2]
    b = a.reshape(sh + (16, 2, 64))
    b = np.moveaxis(b, -3, -1)
    return np.ascontiguousarray(b.reshape(sh + (128, 16)))


_NC_CACHE = {}
LAST_RESULTS = None


def run(inputs, T, n_prompt_per_core):
    f = lambda k: np.asarray(inputs[k], dtype=np.float32)
    x_prompt, x_sample, c = f("x_prompt"), f("x_sample"), f("c")
    NS = T // 256
    if T not in _NC_CACHE:
        _NC_CACHE[T] = build(T)[0]
    nc = _NC_CACHE[T]
    shared = {}
    shared["cst"] = _consts()
    shared["norm_w"] = f("norm_w")
    shared["w_ada"] = f("w_ada")
    shared["b_ada"] = f("b_ada")
    shared["w_in_e"] = f("w_in_e")
    shared["w_out_e"] = f("w_out_e")
    w2 = f("gla_w2"); b2 = f("gla_b2")
    w2cat = np.zeros((2, 64, 512), np.float32)
    w2cat[:, 0:16, 0:256] = w2[:, 0]
    w2cat[:, 16:32, 256:512] = w2[:, 1]
    w2cat[:, 32, 0:256] = b2[:, 0]
    w2cat[:, 32, 256:512] = b2[:, 1]
    shared["w2cat"] = w2cat
    shared["onorm"] = f("gla_onorm").reshape(2, 128, 1)
    lam_re, lam_im, log_dt = f("s5_lam_re"), f("s5_lam_im"), f("s5_log_dt")
    ldt = np.broadcast_to(log_dt[..., None], lam_re.shape)
    shared["s5p"] = np.stack([_state_layout(lam_re), _state_layout(lam_im), _state_layout(ldt)], axis=2)
    def expand_b(b):
        out = np.zeros((2, 128, 16, 128), np.float32)
        for g in range(32):
            j, gs = g // 2, g % 2
            k0 = 16 * (g % 8)
            out[:, gs * 64:(gs + 1) * 64, j, k0:k0 + 16] = b[:, g]
        return out.reshape(2, 128, 16 * 128)
    shared["s5b"] = np.stack([expand_b(f("s5_b_re")), expand_b(f("s5_b_im"))], axis=1)
    def expand_c(cc, sign):
        out = np.zeros((2, 128, 16, 32), np.float32)
        for g in range(32):
            j, gs = g // 2, g % 2
            out[:, gs * 64:(gs + 1) * 64, j, gs * 16:(gs + 1) * 16] = np.swapaxes(cc[:, g], 1, 2)
        if sign < 0:
            out = np.negative(out)
        return out.reshape(2, 128, 16 * 32)
    shared["s5c"] = np.stack([expand_c(f("s5_c_re"), 1), expand_c(f("s5_c_im"), -1)], axis=1)
    shared["s5d"] = np.ascontiguousarray(f("s5_d").reshape(2, 4, 128).transpose(0, 2, 1))
    shared["wglu"] = f("s5_w_glu")
    shared["bglu"] = np.ascontiguousarray(f("s5_b_glu").reshape(2, 4, 128).transpose(0, 2, 1))
    shared["w_in_o"] = f("w_in_o")
    shared["w_out_o"] = f("w_out_o")
    shared["qkw"] = np.concatenate([f("q_norm_w"), f("k_norm_w")], axis=1)
    shared["sink"] = f("sink")
    cw = f("conv_w"); cb = f("conv_b")
    cvw = np.zeros((2, 128, 4, 4), np.float32)
    for a in range(4):
        cvw[:, :, a, 0:3] = cw[:, :, a * 128:(a + 1) * 128].transpose(0, 2, 1)
        cvw[:, :, a, 3] = cb[:, a * 128:(a + 1) * 128]
    shared["convw"] = cvw.reshape(2, 128, 16)

    in_maps = []
    n_sample = x_sample.shape[0]
    for core in range(8):
        m = dict(shared)
        if core < 4:
            b = core
            m["x"] = np.ascontiguousarray(x_sample[b])
            cv = c[b]
            m["meta"] = _meta(T, True)
            m["rope"] = _rope_table(T, False)
            m["gla0"] = np.ascontiguousarray(f("state_gla")[b])
            sre = _state_layout(f("state_s5_re")[b]); sim = _state_layout(f("state_s5_im")[b])
            m["s5h0"] = np.stack([sre, sim], axis=2)
            ck = f("cache_k")[b]; cvv = f("cache_v")[b]
            m["ckv"] = np.ascontiguousarray(np.stack([ck.transpose(0, 2, 1, 3), cvv.transpose(0, 2, 1, 3)], axis=1))
        else:
            pc = core - 4
            xx = np.zeros((T, D), np.float32)
            seqs = x_prompt[pc * n_prompt_per_core:(pc + 1) * n_prompt_per_core]
            xx[:n_prompt_per_core * 256] = seqs.reshape(-1, D)
            m["x"] = xx
            cv = f("c_ctx")
            m["meta"] = _meta(T, False)
            m["rope"] = _rope_table(T, True)
            m["gla0"] = np.zeros((2, 2, 4, 64, 128), np.float32)
            m["s5h0"] = np.zeros((2, 2, 2, 128, 16), np.float32)
            m["ckv"] = np.zeros((2, 2, 256, 2, 64), np.float32)
        m["cvec"] = np.ascontiguousarray(cv.reshape(8, 128).T)
        in_maps.append(m)
    res = run_bass_kernel_spmd(nc, in_maps, core_ids=list(range(8)))
    R = res.results
    global LAST_RESULTS
    LAST_RESULTS = R
    BATCH = x_prompt.shape[0]
    y_sample = np.stack([np.asarray(R[b]["y"]) for b in range(4)], axis=0).astype(np.float32)
    y_prompt = np.zeros_like(x_prompt)
    new_gla = np.zeros((BATCH, 2, 2, 4, 64, 128), np.float32)
    new_re = np.zeros((BATCH, 2, 2, 32, 64), np.float32)
    new_im = np.zeros((BATCH, 2, 2, 32, 64), np.float32)
    new_k = np.zeros((BATCH, 2, 2, 256, 64), np.float32)
    new_v = np.zeros((BATCH, 2, 2, 256, 64), np.float32)
    for pc in range(4):
        r = R[4 + pc]
        y = np.asarray(r["y"]); g = np.asarray(r["ngla"]); s5 = np.asarray(r["ns5"]); kvo = np.asarray(r["nkv"])
        s5 = s5.reshape(2, 2, 2, NS, 16, 2, 64)
        for q in range(n_prompt_per_core):
            bi = pc * n_prompt_per_core + q
            y_prompt[bi] = y[q * 256:(q + 1) * 256]
            new_gla[bi] = g[:, :, q]
            for d in range(2):
                sig = q if d == 0 else NS - 1 - q
                new_re[bi, :, d] = s5[:, d, 0, sig].reshape(2, 32, 64)
                new_im[bi, :, d] = s5[:, d, 1, sig].reshape(2, 32, 64)
            kk = kvo[:, :, q * 256:(q + 1) * 256, :].reshape(2, 2, 256, 2, 64)
            new_k[bi] = kk[:, 0].transpose(0, 2, 1, 3)
            new_v[bi] = kk[:, 1].transpose(0, 2, 1, 3)
    return (y_prompt, y_sample, new_gla, new_re, new_im, new_k, new_v)


def kernel(**inputs):
    return run(inputs, 4096, 8)
```

```python
import math
import os
import numpy as np
import concourse.bass as bass
import concourse.mybir as mybir
from concourse.bass_utils import run_bass_kernel_spmd
from contextlib import ExitStack

F32 = mybir.dt.float32
BF16 = mybir.dt.bfloat16
I32 = mybir.dt.int32
ALU = mybir.AluOpType
AF = mybir.ActivationFunctionType
AX = mybir.AxisListType

D = 1024
EIN = 2592
OIN = 3328
EPS = 1e-6
ENGS = ("pe", "act", "dve", "pool", "sp")
SEM_LIMIT = 30000
N_DMA_SEMS = 12


class Buf:
    __slots__ = ("w", "r")

    def __init__(self):
        self.w = None
        self.r = {}


class Sched:
    def __init__(self, nc, same_engine_sync=True):
        self.nc = nc
        self.q = {e: [] for e in ENGS}
        self.epoch = {e: 0 for e in ENGS}
        self.cnt = {}
        self.seen = {e: {} for e in ENGS}
        self.same = same_engine_sync
        self.nosync = set(os.environ.get("KNOSYNC", "").split(","))
        self.semkeys = []
        for e in ENGS:
            self._newkey((e, 0))
        self.dma_pool = {e: [] for e in ENGS}
        self.dma_rr = {e: 0 for e in ENGS}
        self.n_ops = 0

    def _newkey(self, k):
        self.cnt[k] = 0
        self.semkeys.append(k)

    def _engkey(self, e):
        k = (e, self.epoch[e])
        if self.cnt[k] >= SEM_LIMIT:
            self.epoch[e] += 1
            k = (e, self.epoch[e])
            self._newkey(k)
        return k

    def _need(self, eng, waits, tok, is_dma=False):
        if tok is None:
            return
        k, v = tok
        if (not is_dma) and k[0] == eng and (eng == "pe" or eng in self.nosync):
            return
        if self.seen[eng].get(k, 0) >= v:
            return
        if waits.get(k, 0) < v:
            waits[k] = v

    def _deps(self, eng, reads, writes, is_dma):
        waits = {}
        for b in reads:
            self._need(eng, waits, b.w, is_dma)
        for b in writes:
            self._need(eng, waits, b.w, is_dma)
            for k, v in b.r.items():
                self._need(eng, waits, (k, v), is_dma)
        return waits

    def op(self, eng, fn, reads=(), writes=(), extra=None):
        if extra is None:
            waits = self._deps(eng, reads, writes, False)
        else:
            sv = self.nosync
            self.nosync = set(sv) | {eng}
            waits = self._deps(eng, reads, writes, False)
            self.nosync = sv
            for tok in extra:
                if tok is not None:
                    self._need(eng, waits, tok, True)
        for k, v in waits.items():
            self.seen[eng][k] = v
        key = self._engkey(eng)
        self.cnt[key] += 1
        tok = (key, self.cnt[key])
        for b in reads:
            b.r[key] = tok[1]
        for b in writes:
            b.w = tok
            b.r = {}
        self.q[eng].append((fn, list(waits.items()), key, 1))
        self.n_ops += 1
        return tok

    def dma(self, fn, reads=(), writes=(), eng="sp"):
        pool = self.dma_pool[eng]
        if len(pool) < N_DMA_SEMS:
            key = ("dma", eng, len(pool), 0)
            self._newkey(key)
            pool.append(key)
        else:
            idx = self.dma_rr[eng] % N_DMA_SEMS
            self.dma_rr[eng] += 1
            key = pool[idx]
            if self.cnt[key] >= SEM_LIMIT:
                key = ("dma", eng, idx, key[3] + 1)
                self._newkey(key)
                pool[idx] = key
        waits = self._deps(eng, reads, writes, True)
        if self.cnt[key] > 0:
            self._need(eng, waits, (key, self.cnt[key]), True)
        for k, v in waits.items():
            self.seen[eng][k] = v
        self.cnt[key] += 16
        tok = (key, self.cnt[key])
        for b in reads:
            b.r[key] = tok[1]
        for b in writes:
            b.w = tok
            b.r = {}
        self.q[eng].append((fn, list(waits.items()), key, 16))
        self.n_ops += 1

    def barrier(self):
        for eng in ENGS:
            waits = {}
            for k, v in self.cnt.items():
                if v > 0 and self.seen[eng].get(k, 0) < v:
                    waits[k] = v
                    self.seen[eng][k] = v
            self.q[eng].append((None, list(waits.items()), None, 0))

    def finish_waits(self, bufs, eng="sp"):
        waits = {}
        for b in bufs:
            self._need(eng, waits, b.w, True)
        self.q[eng].append((None, list(waits.items()), None, 0))

    def emit(self):
        nc = self.nc
        with ExitStack() as st:
            sems = {}
            for i, k in enumerate(self.semkeys):
                sems[k] = st.enter_context(nc.semaphore("s%d" % i))
            block = st.enter_context(nc.Block())

            def runner(ename):
                def run(e):
                    for fn, waits, key, inc in self.q[ename]:
                        for k, v in waits:
                            e.wait_ge(sems[k], v)
                        if fn is not None:
                            fn(e).then_inc(sems[key], inc)
                return run

            block.tensor(runner("pe"))
            block.scalar(runner("act"))
            block.vector(runner("dve"))
            block.gpsimd(runner("pool"))
            block.sync(runner("sp"))


class TL:
    def __init__(self, t):
        self.t = t
        self.b = Buf()

    def __getitem__(self, k):
        return self.t[k]


class View:
    def __init__(self, ap):
        self.ap = ap
        self.b = Buf()

    def __getitem__(self, k):
        return self.ap[k]


class DT:
    def __init__(self, ap, nb=1):
        self.ap = ap
        self.bs = [Buf() for _ in range(nb)]
        self.b = self.bs[0]


C_ID, C_TIF, C_TRF, C_TIB, C_TRB, C_MF, C_MB, C_ONE, C_BAND = [128 * i for i in range(9)]
C_BLK = C_BAND + 384
C_EPS = C_BLK + 256
NCST = C_EPS + 2


class _Stop(Exception):
    pass


def build(T, stop=None):
    import os
    stop = stop or os.environ.get("KSTOP")
    NT = T // 128
    NS = T // 256
    NB = T // 512
    nc = bass.Bass("TRN2", target_bir_lowering=False)
    S = Sched(nc, same_engine_sync=bool(int(os.environ.get("KSAME", "0"))))
    st = ExitStack()

    def din(name, shape, dt=F32):
        return DT(nc.dram_tensor(name, list(shape), dt, kind="ExternalInput").ap())

    def dout(name, shape, dt=F32):
        return DT(nc.dram_tensor(name, list(shape), dt, kind="ExternalOutput").ap())

    dbg = bool(os.environ.get("KDBG"))

    def dscr(name, shape, dt=F32, nb=1):
        return DT(nc.dram_tensor(name, list(shape), dt, kind="ExternalOutput" if dbg else "Internal").ap(), nb)

    ARENA_WORDS = 34000
    G_BASE = 29000
    arena_t = st.enter_context(nc.sbuf_tensor("arena", [128, ARENA_WORDS], F32))
    goff = {"G": G_BASE}

    def sb(name, shape, dt=F32, grp=None):
        if grp is None:
            return TL(st.enter_context(nc.sbuf_tensor(name, list(shape), dt)))
        n = 1
        for d_ in shape[1:]:
            n *= d_
        words = n if dt in (F32, I32) else (n + 1) // 2
        off = goff.get(grp, 0)
        goff[grp] = off + words
        assert goff[grp] <= ARENA_WORDS, (grp, name, goff[grp])
        assert grp != "S" or goff[grp] <= G_BASE, (grp, name, goff[grp])
        ap = arena_t[:, off:off + words]
        if dt != F32:
            ap = ap.bitcast(dt)[:, :n]
        if len(shape) > 2:
            names = " ".join("d%d" % i for i in range(len(shape) - 1))
            kw = {"d%d" % i: shape[i + 1] for i in range(len(shape) - 1)}
            ap = ap.rearrange("p (%s) -> p %s" % (names, names), **kw)
        if shape[0] < 128:
            ap = ap[0:shape[0]]
        return View(ap)

    def ps(name, shape, dt=F32):
        return TL(st.enter_context(nc.psum_tensor(name, list(shape), dt)))

    def bl(xs):
        return [x.b if hasattr(x, "b") else x for x in xs]

    def mm(out, lhsT, rhs, start, stop, R, W):
        S.op("pe", lambda e: e.matmul(out, lhsT=lhsT, rhs=rhs, start=start, stop=stop), bl(R), bl(W))

    def tr(out, in_, ident, R, W):
        S.op("pe", lambda e: e.transpose(out, in_, ident), bl(R), bl(W))

    def act(out, in_, func, R, W, **kw):
        S.op("act", lambda e: e.activation(out=out, in_=in_, func=func, **kw), bl(R), bl(W))

    def tt(eng, out, a, b, op, R, W):
        S.op(eng, lambda e: e.tensor_tensor(out=out, in0=a, in1=b, op=op), bl(R), bl(W))

    def ts(eng, out, a, s1, op0, R, W, s2=None, op1=None):
        if op1 is None:
            S.op(eng, lambda e: e.tensor_scalar(out=out, in0=a, scalar1=s1, scalar2=None, op0=op0), bl(R), bl(W))
        else:
            S.op(eng, lambda e: e.tensor_scalar(out=out, in0=a, scalar1=s1, scalar2=s2, op0=op0, op1=op1), bl(R), bl(W))

    def stt(out, in0, scalar, in1, op0, op1, R, W, extra=None):
        return S.op("dve", lambda e: e.scalar_tensor_tensor(out=out, in0=in0, scalar=scalar, in1=in1, op0=op0, op1=op1),
                    bl(R), bl(W), extra=extra)

    def cp(eng, out, in_, R, W):
        if eng == "act":
            S.op("act", lambda e: e.copy(out=out, in_=in_), bl(R), bl(W))
        else:
            S.op(eng, lambda e: e.tensor_copy(out=out, in_=in_), bl(R), bl(W))

    def mset(eng, ap, val, W):
        S.op(eng, lambda e: e.memset(ap, val), [], bl(W))

    def dma(out, in_, R, W, eng="sp", slow=False):
        if slow:
            S.dma(lambda e: e.dma_start(out=out, in_=in_, allow_slow_non_contiguous=True), bl(R), bl(W), eng=eng)
        else:
            S.dma(lambda e: e.dma_start(out=out, in_=in_), bl(R), bl(W), eng=eng)

    x_in = din("x", [T, D])
    cvec = din("cvec", [128, 8])
    cst = din("cst", [128, NCST])
    NMETA = 3 + 4 * NT
    meta = din("meta", [128, NMETA])
    rope = din("rope", [T, 128])
    gla0 = din("gla0", [2, 2, 4, 64, 128])
    s5h0 = din("s5h0", [2, 2, 2, 128, 16])
    ckv = din("ckv", [2, 2, 256, 2, 64])
    norm_w = din("norm_w", [4, D])
    w_ada = din("w_ada", [4, D, 3 * D])
    b_ada = din("b_ada", [4, 3 * D])
    w_in_e = din("w_in_e", [2, D, EIN])
    w_out_e = din("w_out_e", [2, D, D])
    w2cat = din("w2cat", [2, 64, 512])
    onorm = din("onorm", [2, 128, 1])
    s5p = din("s5p", [2, 2, 3, 128, 16])
    s5b = din("s5b", [2, 2, 128, 16 * 128])
    s5c = din("s5c", [2, 2, 128, 16 * 32])
    s5d = din("s5d", [2, 128, 4])
    wglu = din("wglu", [2, 512, 512])
    bglu = din("bglu", [2, 128, 4])
    w_in_o = din("w_in_o", [2, D, OIN])
    w_out_o = din("w_out_o", [2, D, D])
    qkw = din("qkw", [2, 128])
    sinkv = din("sink", [2, 8])
    convw = din("convw", [2, 128, 16])

    y_out = dout("y", [T, D])
    ngla = dout("ngla", [2, 2, NS, 4, 64, 128])
    ns5 = dout("ns5", [2, 2, 2, NS * 16, 128])
    nkv = dout("nkv", [2, 2, T, 128])

    xs = [dscr("xs0", [T, D], nb=NT), dscr("xs1", [T, D], nb=NT)]
    qkT_s = dscr("qkT", [512, T], BF16, NT)
    kv_s = dscr("kvtm", [T, 768], BF16, NT)
    sp_s = dscr("sp", [T, 512], F32, NT)
    zgT_s = dscr("zgT", [512, T], BF16, NT)
    uT_s = dscr("uT", [512, T], BF16, NT)
    z5T_s = dscr("z5T", [512, T], BF16, NT)
    oF_s = dscr("oF", [512, T], F32, NT)
    y5T_s = dscr("y5T", [512, T], F32, 16)
    yT_s = dscr("yT", [1024, T], BF16, NT)
    qT_s = dscr("qTo", [512, T], BF16, NT)
    kT_s = dscr("kTo", [256, T], BF16, NT)
    v_s = dscr("vo", [T, 128], BF16, NT)
    sz_s = dscr("szo", [T, 512], F32, NT)
    u1T_s = dscr("u1T", [512, T], F32, NT)
    u2T_s = dscr("u2T", [512, T], F32, NT)

    cs = sb("cs", [128, NCST])
    csb = sb("csb", [128, NCST], BF16)
    mt = sb("mt", [128, NMETA])
    wbf = sb("wbf", [128, 8, OIN], BF16, grp="P")
    wob = sb("wob", [128, 8, D], BF16, grp="O")
    wst1 = sb("wst0", [128, OIN], grp="P")
    wst = [wst1, wst1]
    wstO = sb("wstO", [128, D], grp="O")
    wstS = sb("wstS", [128, 512], grp="SP")
    pbfA = sb("pbfA", [128, 512], BF16, grp="A")
    modbc = sb("modbc", [128, 3 * D])
    Abc = sb("Abc", [128, D])
    sc8 = sb("sc8", [128, 8])
    screp = sb("screp", [128, 8, 128])
    xts = [sb("xt0", [128, D]), sb("xt1", [128, D])]
    xt = xts[0]
    hb = sb("hb", [128, D], BF16, grp="P")
    hT = sb("hT", [128, 8, 128], BF16, grp="P")
    small = sb("small", [128, 16])
    small2 = sb("small2", [128, 16])
    proj = sb("proj", [128, OIN], grp="P")
    pbf = sb("pbf", [128, OIN], BF16, grp="P")
    trs = sb("trs", [128, 8, 128], BF16)
    trf = sb("trf", [128, 4, 128])
    w1 = sb("w1", [128, 1024])
    w2 = sb("w2", [128, 1024])
    w3 = sb("w3", [128, 1024])
    lrT = sb("lrT", [64, 128], BF16, grp="P")
    w2c = sb("w2c", [64, 512], BF16, grp="P")
    w2cf = sb("w2cf", [64, 512], grp="P")

    projB = sb("projB", [128, OIN], grp="P")
    pbfB = sb("pbfB", [128, OIN], BF16, grp="P")
    proj2 = [proj, projB]
    pbf2 = [pbf, pbfB]
    pA = ps("pA", [128, 512]); pB = ps("pB", [128, 512]); pC = ps("pC", [128, 512])
    pD = ps("pD", [128, 512]); pE = ps("pE", [128, 512]); pF = ps("pF", [128, 512])
    pT = ps("pT", [128, 1024], BF16)
    pTf = ps("pTf", [128, 512])

    class _PT32:
        def __init__(self):
            self.ap = pT[:, :].bitcast(F32)
            self.b = pT.b

        def __getitem__(self, k):
            return self.ap[k]
    pT32 = _PT32()

    class _PTB:
        def __init__(self):
            self.ap = pE[:, :].bitcast(BF16)
            self.b = pE.b

        def __getitem__(self, k):
            return self.ap[k]
    pTb = _PTB()

    class _PTF16:
        def __init__(self):
            self.ap = pTf[:, :].bitcast(BF16)
            self.b = pTf.b

        def __getitem__(self, k):
            return self.ap[k]
    pTf16 = _PTF16()
    trsb = sb("trsb", [128, 4, 128], BF16)
    ident_b = csb[:, C_ID:C_ID + 128]
    ident_f = cs[:, C_ID:C_ID + 128]
    mflag = mt[:, 0:1]

    dma(cs[:], cst.ap[:, :], [], [cs])
    cp("dve", csb[:], cs[:], [cs], [csb])
    dma(mt[:], meta.ap[:, :], [], [mt])
    dma(sc8[:], cvec.ap[:, :], [], [sc8])
    act(sc8[:], sc8[:], AF.Silu, [sc8], [sc8])
    for kt in range(8):
        cp("dve", screp[:, kt, :], sc8[:, kt:kt + 1].to_broadcast([128, 128]), [sc8], [screp])
    band = sb("band", [128, 384])
    ts("dve", band[:], cs[:, C_BAND:C_BAND + 384], mt[:, 1:2], ALU.mult, [cs, mt], [band])

    evac_rr = [0]

    def evac(out, in_, R, W):
        e = ("act", "dve")[evac_rr[0] % 2]
        evac_rr[0] += 1
        cp(e, out, in_, R, W)

    def load_weight_bf16(dst, src_ap, ncols, wsx=None):
        wsx = wsx or wst1
        for kt in range(8):
            dma(wsx[:, :ncols], src_ap[kt * 128:(kt + 1) * 128, :], [], [wsx])
            cp(("act", "dve")[kt % 2], dst[:, kt, :ncols], wsx[:, :ncols], [wsx], [dst])

    def adaln(l):
        banks = [pA, pB, pC, pD, pE, pF]
        for kt in range(8):
            wsx = wst[kt % 2]
            dma(wsx[:, :3 * D], w_ada.ap[l, kt * 128:(kt + 1) * 128, :], [], [wsx])
            for nb_ in range(6):
                mm(banks[nb_][:, :], screp[:, kt, :], wsx[:, nb_ * 512:(nb_ + 1) * 512], kt == 0, kt == 7,
                   [screp, wsx], [banks[nb_]])
        tmpbc = wst1
        dma(tmpbc[:, :3 * D], b_ada.ap[l:l + 1, :].partition_broadcast(128), [], [tmpbc])
        for nb_ in range(6):
            tt("dve", modbc[:, nb_ * 512:(nb_ + 1) * 512], banks[nb_][:, :], tmpbc[:, nb_ * 512:(nb_ + 1) * 512],
               ALU.add, [banks[nb_], tmpbc], [modbc])
        dma(tmpbc[:, :D], norm_w.ap[l:l + 1, :].partition_broadcast(128), [modbc], [tmpbc])
        stt(Abc[:], modbc[:, D:2 * D], 1.0, tmpbc[:, :D], ALU.add, ALU.mult, [modbc, tmpbc], [Abc])

    def load_x(ti, xsrc):
        if ti >= NT:
            return
        t0 = ti * 128
        xt = xts[ti % 2]
        dma(xt[:], xsrc.ap[t0:t0 + 128, :], [xsrc.bs[ti] if len(xsrc.bs) > 1 else xsrc.b], [xt])

    def norm_mod_T(l, ti, xsrc):
        if ti == 0:
            load_x(0, xsrc)
        load_x(ti + 1, xsrc)
        xt = xts[ti % 2]
        mset("dve", small2[:, 0:1], 0.0, [small2])
        act(hb[:], xt[:], AF.Square, [xt], [hb, small2], accum_out=small2[:, 0:1])
        act(small2[:, 1:2], small2[:, 0:1], AF.Ln, [small2], [small2], scale=1.0 / D, bias=cs[:, C_EPS:C_EPS + 1])
        act(small2[:, 2:3], small2[:, 1:2], AF.Exp, [small2], [small2], scale=-0.5)
        stt(w4[:, :D], xt[:], small2[:, 2:3], Abc[:], ALU.mult, ALU.mult, [xt, small2, Abc], [w4])
        tt("dve", hb[:], w4[:, :D], modbc[:, 0:D], ALU.add, [w4, modbc], [hb])
        for kt in range(8):
            tr(pT[:, kt * 128:(kt + 1) * 128], hb[:, kt * 128:(kt + 1) * 128], ident_b, [hb, csb], [pT])
        cp("act", hT[:].rearrange("p a t -> p (a t)"), pT[:, :], [pT], [hT])

    def in_proj(ncols, dst=None):
        dst = dst or proj
        c0 = 0
        k = 0
        while c0 < ncols:
            cw = min(512, ncols - c0)
            pp = (pA, pB)[k % 2]
            for kt in range(8):
                mm(pp[:, :cw], hT[:, kt, :], wbf[:, kt, c0:c0 + cw], kt == 0, kt == 7, [hT, wbf], [pp])
            evac(dst[:, c0:c0 + cw], pp[:, :cw], [pp], [dst])
            c0 += cw
            k += 1

    ts_rr = [0]

    def transpose_store(src_bf_ap, nft, dst, row0, ti, R):
        t0 = ti * 128
        assert nft <= 4
        k = ts_rr[0] % 2
        ts_rr[0] += 1
        pX, tX = (pT, pTb)[k], (trs, trsb)[k]
        for a in range(nft):
            tr(pX[:, a * 128:(a + 1) * 128], src_bf_ap[:, a * 128:(a + 1) * 128], ident_b, R + [csb], [pX])
        cp(("act", "dve")[k], tX[:, :nft, :].rearrange("p a t -> p (a t)"), pX[:, :nft * 128], [pX], [tX])
        dma(dst.ap[row0:row0 + nft * 128, t0:t0 + 128].rearrange("(a p) t -> p a t", p=128), tX[:, :nft, :],
            [tX], [dst.bs[ti]])

    def transpose_store_f32(src_ap, nft, dst, row0, ti, R):
        t0 = ti * 128
        for a in range(nft):
            tr(pTf[:, a * 128:(a + 1) * 128], src_ap[:, a * 128:(a + 1) * 128], ident_f, R + [cs], [pTf])
        cp("act", trf[:, :nft, :].rearrange("p a t -> p (a t)"), pTf[:, :nft * 128], [pTf], [trf])
        dma(dst.ap[row0:row0 + nft * 128, t0:t0 + 128].rearrange("(a p) t -> p a t", p=128), trf[:, :nft, :],
            [trf], [dst.bs[ti]])

    def out_proj_residual(l, xsrc, xdst, ydt, wsrc):
        S.barrier()
        load_weight_bf16(wob, wsrc, D, wstO)
        def loads(ti):
            if ti >= NT:
                return
            t0 = ti * 128
            p = ti % 2
            yt = (sb_yt, sb_yt2)[p]
            dma(yt[:], ydt.ap[:, t0:t0 + 128].rearrange("(a p) t -> p a t", p=128), [ydt.bs[ti]], [yt])
            dma(xts[p][:], xsrc.ap[t0:t0 + 128, :], [xsrc.bs[ti] if len(xsrc.bs) > 1 else xsrc.b], [xts[p]])

        loads(0)
        for ti in range(NT):
            t0 = ti * 128
            p = ti % 2
            loads(ti + 1)
            yt, xt = (sb_yt, sb_yt2)[p], xts[p]
            o1, o2 = ((w1, w2), (w3, w4))[p]
            for cb in range(2):
                pp = ((pA, pB), (pC, pD))[p][cb]
                for kt in range(8):
                    mm(pp[:, :], yt[:, kt, :], wob[:, kt, cb * 512:(cb + 1) * 512], kt == 0, kt == 7, [yt, wob], [pp])
                tt("dve", o1[:, cb * 512:(cb + 1) * 512], pp[:, :], modbc[:, 2 * D + cb * 512:2 * D + (cb + 1) * 512],
                   ALU.mult, [pp, modbc], [o1])
            tt("dve", o2[:, :D], o1[:, :D], xt[:], ALU.add, [o1, xt], [o2])
            dma(xdst.ap[t0:t0 + 128, :], o2[:, :D], [o2], [xdst.bs[ti] if len(xdst.bs) > 1 else xdst.b])

    sb_yt = sb("yt", [128, 8, 128], BF16)
    sb_yt2 = sb("yt2", [128, 8, 128], BF16)
    w4 = sb("w4", [128, 1024])

    qk_t = sb("qk_t", [128, 4, 128], BF16, grp="G")
    kv_t = sb("kv_t", [128, 768], BF16, grp="G")
    sp_t = sb("sp_t", [128, 256], grp="G")
    E1 = sb("E1", [128, 2, 128], grp="G")
    E2 = sb("E2", [128, 2, 128], grp="G")
    E3 = sb("E3", [128, 256], grp="G")
    qbT = sb("qbT", [128, 2, 128], BF16, grp="G")
    kbT = sb("kbT", [128, 2, 128], BF16, grp="G")
    kd = sb("kd", [128, 256], BF16, grp="G")
    attm = sb("attm", [128, 4, 128], BF16, grp="G")
    Sst = sb("Sst", [128, 2, 256], grp="G")
    Sbf = sb("Sbf", [128, 2, 256], BF16, grp="G")
    Sbf1 = sb("Sbf1", [128, 2, 256], BF16, grp="G")
    o_t = sb("o_t", [128, 4, 128], grp="G")
    oF_t = sb("oF_t", [128, 4, 128], grp="G")
    zg_t = sb("zg_t", [128, 4, 128], BF16, grp="G")
    osq = sb("osq", [128, 512], BF16, grp="G")
    onc = sb("onc", [128, 1])
    TX = max(T, 2048)
    Xs = [sb("Xf", [128, 2, TX], grp="S"), sb("Xb", [128, 2, TX], grp="S")]
    u_ft = sb("u_ft", [128, T], BF16, grp="S")
    BT = sb("BT", [128, 2, 2, 16, 128], BF16, grp="S")
    CTm = sb("CTm", [128, 2, 16, 32], grp="S")
    Xblk = [[Buf() for _ in range(max(NB, 1))] for _ in range(2)]

    class _Alias:
        def __init__(self, base, shape):
            self.ap = base.ap.rearrange("p c t -> p (c t)")[:, 0:4096].rearrange("p (a j k) -> p a j k", a=2, j=16)
            self.b = base.b

        def __getitem__(self, k):
            return self.ap[k]
    Bx = _Alias(Xs[0], None)
    Bbar = _Alias(Xs[1], None)
    s5par = sb("s5par", [128, 3, 16], grp="S")
    h0t = sb("h0t", [128, 2, 2, 16], grp="S")
    NPW = 18
    PW = sb("PW", [128, 2, NPW, 3, 16], grp="S")
    PWm = sb("PWm", [128, 2, NPW, 3, 16], grp="S")
    s5tmp = sb("s5tmp", [128, 12, 16], grp="S")
    s5i = sb("s5i", [128, 16], I32, grp="S")
    STG = sb("STG", [128, 2 * 2 * NS * 16], grp="S")
    stgT = sb("stgT", [128, 128], grp="S")
    y5st = sb("y5st", [32, 512], grp="S")
    dcol = sb("dcol", [128, 4], grp="SP")
    bgl = sb("bgl", [128, 4], grp="SP")
    wgb = sb("wgb", [128, 4, 512], BF16, grp="SP")
    y5b = sb("y5b", [128, 4, 512], grp="SP")
    ub = sb("ub", [128, 4, 512], BF16, grp="SP")
    z5b = sb("z5b", [128, 4, 512], BF16, grp="SP")
    zz = sb("zz", [128, 4, 512], grp="SP")
    zzb = sb("zzb", [128, 4, 512], BF16, grp="SP")
    sg = sb("sg", [128, 4, 512], grp="SP")
    y5o = sb("y5o", [128, 4, 512], BF16, grp="SP")

    pw_exps = list(range(1, 9)) + [16, 24, 32, 40, 48, 56, 64, 128, 192, 256]
    pidx = {m: i for i, m in enumerate(pw_exps)}
    assert len(pw_exps) <= NPW

    def s5_post_setup(e):
        dma(dcol[:], s5d.ap[e], [], [dcol])
        dma(bgl[:], bglu.ap[e], [], [bgl])
        for kt in range(4):
            dma(wstS[:, :512], wglu.ap[e, kt * 128:(kt + 1) * 128, :], [], [wstS])
            cp("pool", wgb[:, kt, :], wstS[:, :512], [wstS], [wgb])

    def s5_setup(e):
        dma(h0t[:], s5h0.ap[e].rearrange("d c p j -> p d c j"), [], [h0t])
        dma(CTm[:].rearrange("p a j o -> p a (j o)"), s5c.ap[e].rearrange("a p x -> p a x"), [], [CTm])
        dma(Bx[:].rearrange("p a j k -> p a (j k)"), s5b.ap[e].rearrange("a p x -> p a x"), [], [Bx])
        for d in range(2):
            dma(s5par[:], s5p.ap[e, d].rearrange("a p j -> p a j"), [], [s5par])
            lre, lim, ldt = s5par[:, 0, :], s5par[:, 1, :], s5par[:, 2, :]
            tmp = lambda i: s5tmp[:, i, :]
            R = [s5par, s5tmp]
            W = [s5tmp]
            act(tmp(0), ldt, AF.Exp, R, W)
            tt("dve", tmp(1), lre, tmp(0), ALU.mult, R, W)
            tt("dve", tmp(2), lim, tmp(0), ALU.mult, R, W)
            act(tmp(3), tmp(1), AF.Exp, R, W)
            for (dst, shift) in ((4, 0.0), (5, math.pi / 2)):
                ts("dve", tmp(6), tmp(2), shift, ALU.add, R, W, 1.0 / (2 * math.pi), ALU.mult)
                cp("dve", s5i[:], tmp(6), R, [s5i])
                cp("dve", tmp(7), s5i[:], [s5i], W)
                ts("dve", tmp(6), tmp(2), shift, ALU.add, R, W)
                stt(tmp(6), tmp(7), -2 * math.pi, tmp(6), ALU.mult, ALU.add, R, W)
                ts("dve", tmp(6), tmp(6), math.pi, ALU.min, R, W, -math.pi, ALU.max)
                act(tmp(dst), tmp(6), AF.Sin, R, W)
            P = lambda m, c: PW[:, d, pidx[m], c, :]
            RW = [PW, s5tmp]
            tt("dve", P(1, 0), tmp(3), tmp(5), ALU.mult, RW, [PW])
            tt("dve", P(1, 1), tmp(3), tmp(4), ALU.mult, RW, [PW])
            tt("dve", tmp(6), lre, lre, ALU.mult, R, W)
            tt("dve", tmp(7), lim, lim, ALU.mult, R, W)
            tt("dve", tmp(6), tmp(6), tmp(7), ALU.add, R, W)
            S.op("dve", lambda e_, o=tmp(6): e_.reciprocal(out=o, in_=o), bl(R), bl(W))
            ts("dve", tmp(7), P(1, 0), -1.0, ALU.add, RW, W)
            tt("dve", tmp(8), tmp(7), lre, ALU.mult, R, W)
            tt("dve", tmp(9), P(1, 1), lim, ALU.mult, RW + [s5par], W)
            tt("dve", tmp(8), tmp(8), tmp(9), ALU.add, R, W)
            tt("dve", tmp(8), tmp(8), tmp(6), ALU.mult, R, W)
            tt("dve", tmp(9), P(1, 1), lre, ALU.mult, RW + [s5par], W)
            tt("dve", tmp(10), tmp(7), lim, ALU.mult, R, W)
            tt("dve", tmp(9), tmp(9), tmp(10), ALU.subtract, R, W)
            tt("dve", tmp(9), tmp(9), tmp(6), ALU.mult, R, W)
            crb = s5tmp[:, 8, :].unsqueeze(2).to_broadcast([128, 16, 128])
            cib = s5tmp[:, 9, :].unsqueeze(2).to_broadcast([128, 16, 128])
            RB = [Bx, s5tmp, Bbar]
            tt("dve", Bbar[:, 0], Bx[:, 0], crb, ALU.mult, RB, [Bbar])
            tt("dve", Bbar[:, 1], Bx[:, 1], cib, ALU.mult, RB, [Bbar])
            tt("dve", Bbar[:, 0], Bbar[:, 0], Bbar[:, 1], ALU.subtract, RB, [Bbar])
            tt("dve", Bbar[:, 1], Bx[:, 1], crb, ALU.mult, RB, [Bbar])
            for j in range(16):
                stt(Bbar[:, 1, j, :], Bx[:, 0, j, :], s5tmp[:, 9, j:j + 1], Bbar[:, 1, j, :], ALU.mult, ALU.add, RB, [Bbar])
            for c in range(2):
                for j4 in range(4):
                    for jj in range(4):
                        j = j4 * 4 + jj
                        tr(pTf[:, jj * 128:(jj + 1) * 128], Bbar[:, c, j, :], ident_f, [Bbar, cs], [pTf])
                    cp("act", BT[:, d, c, j4 * 4:(j4 + 1) * 4, :].rearrange("p j k -> p (j k)"), pTf[:, :], [pTf], [BT])
            def cmul(mo, ma, mb_):
                a_re, a_im = P(ma, 0), P(ma, 1)
                b_re, b_im = P(mb_, 0), P(mb_, 1)
                tt("dve", tmp(10), a_im, b_im, ALU.mult, RW, W)
                tt("dve", tmp(11), a_re, b_re, ALU.mult, RW, W)
                tt("dve", tmp(6), a_re, b_im, ALU.mult, RW, W)
                tt("dve", tmp(7), a_im, b_re, ALU.mult, RW, W)
                tt("dve", P(mo, 0), tmp(11), tmp(10), ALU.subtract, RW, [PW])
                tt("dve", P(mo, 1), tmp(6), tmp(7), ALU.add, RW, [PW])
            for m in range(2, 9):
                cmul(m, m - 1, 1)
            for m in range(16, 65, 8):
                cmul(m, m - 8, 8)
            cmul(128, 64, 64)
            cmul(192, 128, 64)
            cmul(256, 192, 64)
            ts("dve", PW[:, d, :, 2, :], PW[:, d, :, 1, :], -1.0, ALU.mult, [PW], [PW])
            ts("dve", PWm[:, d], PW[:, d], mflag, ALU.mult, [PW, mt], [PWm])

    def s5_view(X, comp, start, step, count, rev, inner=None):
        base = X[:, comp, 0:1]
        off = base.offset
        pstride = base.ap[0][0]
        if rev:
            o = off + (T - 1 - start)
            dims = [[pstride, 128], [-step, count]]
            if inner is not None:
                dims.append([-inner[0], inner[1]])
        else:
            o = off + start
            dims = [[pstride, 128], [step, count]]
            if inner is not None:
                dims.append([inner[0], inner[1]])
        return bass.AP(base.tensor, o, dims)

    def s5_scan(X, d, j, rev, fence0):
        XW = Xblk[d]
        XB = XW + [PW, PWm]
        stt_ = {"fence": fence0, "C": None, "D": None}

        def cmac(tgt, src, pw_tile, m, chain):
            pr = pw_tile[:, d, pidx[m], 0, j:j + 1]
            pi = pw_tile[:, d, pidx[m], 1, j:j + 1]
            pn = pw_tile[:, d, pidx[m], 2, j:j + 1]
            f = stt_["fence"]
            pc, pd = (stt_["C"], stt_["D"]) if chain else (None, None)
            ta = stt(tgt(0), src(0), pr, tgt(0), ALU.mult, ALU.add, XB, XW, extra=[f, pc])
            yield
            tb = stt(tgt(1), src(0), pi, tgt(1), ALU.mult, ALU.add, XB, XW, extra=[f, pc])
            yield
            tc = stt(tgt(0), src(1), pn, tgt(0), ALU.mult, ALU.add, XB, XW, extra=[f, pd, ta])
            yield
            td = stt(tgt(1), src(1), pr, tgt(1), ALU.mult, ALU.add, XB, XW, extra=[f, pd, tb])
            yield
            stt_["C"], stt_["D"], stt_["last"] = tc, td, td

        def fence():
            stt_["fence"] = stt_.get("last", stt_["fence"])
            stt_["C"] = stt_["D"] = None

        for (s, K) in ((1, 8), (8, 8), (64, 4)):
            n = T // (s * K)
            fence()
            for jj in range(1, K):
                tgt = lambda c, s=s, K=K, jj=jj, n=n: s5_view(X, c, (jj + 1) * s - 1, K * s, n, rev)
                src = lambda c, s=s, K=K, jj=jj, n=n: s5_view(X, c, jj * s - 1, K * s, n, rev)
                yield from cmac(tgt, src, PW, s, True)
        fence()
        for sg_ in range(1, NS):
            tgt = lambda c, sg_=sg_: s5_view(X, c, 256 * (sg_ + 1) - 1, 1, 1, rev)
            src = lambda c, sg_=sg_: s5_view(X, c, 256 * sg_ - 1, 1, 1, rev)
            yield from cmac(tgt, src, PWm, 256, True)
        fence()
        if NS > 1:
            for jj in range(3):
                tgt = lambda c, jj=jj: s5_view(X, c, 256 + 64 * (jj + 1) - 1, 256, NS - 1, rev)
                src = lambda c: s5_view(X, c, 255, 256, NS - 1, rev)
                yield from cmac(tgt, src, PWm, 64 * (jj + 1), False)
        for (s, K, nin) in ((8, 8, 4), (1, 8, 32)):
            fence()
            for jj in range(K - 1):
                m = s * (jj + 1)
                tgt = lambda c, s=s, K=K, jj=jj, nin=nin: s5_view(X, c, s * K + (jj + 1) * s - 1, 256, NS, rev, inner=(s * K, nin - 1))
                src = lambda c, s=s, K=K, nin=nin: s5_view(X, c, s * K - 1, 256, NS, rev, inner=(s * K, nin - 1))
                yield from cmac(tgt, src, PW, m, False)
                if NS > 1:
                    tgt = lambda c, s=s, jj=jj: s5_view(X, c, 256 + (jj + 1) * s - 1, 256, NS - 1, rev)
                    src = lambda c: s5_view(X, c, 255, 256, NS - 1, rev)
                    yield from cmac(tgt, src, PWm, m, False)

    def chk(tag, l):
        if stop == "%s%d" % (tag, l):
            raise _Stop()

    def chk2(tag):
        if stop == tag:
            raise _Stop()

    def even_layer(l, xsrc, xdst):
        e = l // 2
        load_weight_bf16(wbf, w_in_e.ap[e], EIN)
        chk2("W")
        adaln(l)
        chk2("ADA")
        dma(w2cf[:], w2cat.ap[e], [], [w2cf])
        cp("dve", w2c[:], w2cf[:], [w2cf], [w2c])
        dma(onc[:], onorm.ap[e], [], [onc])
        mset("dve", lrT[:], 0.0, [lrT])
        mset("dve", lrT[32:33, :], 1.0, [lrT])
        def front_e(ti):
            norm_mod_T(l, ti, xsrc)
            in_proj(EIN, pbf2[ti % 2])

        front_e(0)
        for ti in range(NT):
            t0 = ti * 128
            if ti + 1 < NT:
                front_e(ti + 1)
            pbf_t = pbf2[ti % 2]
            transpose_store(pbf_t[:, 0:512], 4, qkT_s, 0, ti, [pbf_t])
            chk2("TS")
            dma(kv_s.ap[t0:t0 + 128, :], pbf_t[:, 256:1024], [pbf_t], [kv_s.bs[ti]])
            chk2("KV")
            tr(pT[:32, 0:128], pbf_t[:, 1024:1056], ident_b, [pbf_t, csb], [pT])
            cp("act", lrT[0:32, :], pT[:32, 0:128], [pT], [lrT])
            mm(pC[:, :], lrT[:, :], w2c[:, :], True, True, [lrT, w2c], [pC])
            act(w3[:, :512], pC[:, :], AF.Exp, [pC], [w3], scale=-1.0)
            act(w3[:, 512:1024], w3[:, :512], AF.Ln, [w3], [w3], bias=cs[:, C_ONE:C_ONE + 1])
            dma(sp_s.ap[t0:t0 + 128, :], w3[:, 512:1024], [w3], [sp_s.bs[ti]])
            chk2("GATE")
            transpose_store(pbf_t[:, 1056:1568], 4, zgT_s, 0, ti, [pbf_t])
            transpose_store(pbf_t[:, 1568:2080], 4, uT_s, 0, ti, [pbf_t])
            transpose_store(pbf_t[:, 2080:2592], 4, z5T_s, 0, ti, [pbf_t])
            chk2("T%d" % ti)
        chk('P', l)
        S.barrier()
        def gla_gen():
            for d in range(2):
                TI, TR_, MK = (C_TIF, C_TRF, C_MF) if d == 0 else (C_TIB, C_TRB, C_MB)
                mset("dve", Sst[:], 0.0, [Sst])
                yield
                for h in range(4):
                    pr, hh = h // 2, h % 2
                    dma(Sst[hh * 64:(hh + 1) * 64, pr, hh * 128:(hh + 1) * 128], gla0.ap[e, d, h], [], [Sst])
                    yield
                blkm = cs[:, C_BLK:C_BLK + 256].unsqueeze(1).to_broadcast([128, 2, 256])
                tt("pool", Sbf[:], Sst[:], blkm, ALU.mult, [Sst, cs], [Sbf])
                yield
                order = list(range(NT)) if d == 0 else list(range(NT - 1, -1, -1))
                for oi, ti in enumerate(order):
                    t0 = ti * 128
                    if oi > 0 and oi % 2 == 0:
                        ts("dve", Sst[:], Sst[:], mflag, ALU.mult, [Sst, mt], [Sst])
                        yield
                        tt("pool", Sbf[:], Sst[:], blkm, ALU.mult, [Sst, cs], [Sbf])
                        yield
                    dma(qk_t[:], qkT_s.ap[:, t0:t0 + 128].rearrange("(a p) t -> p a t", p=128), [qkT_s.bs[ti]], [qk_t])
                    yield
                    dma(kv_t[:], kv_s.ap[t0:t0 + 128, :], [kv_s.bs[ti]], [kv_t])
                    yield
                    dma(sp_t[:], sp_s.ap[t0:t0 + 128, d * 256:(d + 1) * 256], [sp_s.bs[ti]], [sp_t])
                    yield
                    yield
                    for pr in range(2):
                        mm(pC[:, pr * 128:(pr + 1) * 128], sp_t[:, pr * 128:(pr + 1) * 128], cs[:, TI:TI + 128], True, True,
                           [sp_t, cs], [pC])
                        yield
                    mm(pD[:, 0:256], cs[:, TR_:TR_ + 128], sp_t[:, :], True, True, [sp_t, cs], [pD])
                    yield
                    yield
                    act(E1[:].rearrange("p a t -> p (a t)"), pC[:, 0:256], AF.Exp, [pC], [E1])
                    yield
                    act(E2[:].rearrange("p a t -> p (a t)"), pC[:, 0:256], AF.Exp, [pC], [E2], scale=-1.0)
                    yield
                    act(E3[:], pD[:, 0:256], AF.Exp, [pD], [E3])
                    yield
                    stt(qbT[:], qk_t[:, 0:2, :], 0.125, E1[:], ALU.mult, ALU.mult, [qk_t, E1], [qbT])
                    yield
                    tt("pool", kbT[:], qk_t[:, 2:4, :], E2[:], ALU.mult, [qk_t, E2], [kbT])
                    yield
                    tt("pool", kd[:], kv_t[:, 0:256], E3[:], ALU.mult, [kv_t, E3], [kd])
                    yield
                    yield
                    for h in range(4):
                        pr, hh = h // 2, h % 2
                        pX = (pE, pA)[hh]
                        mm(pX[:, pr * 128:(pr + 1) * 128], kbT[hh * 64:(hh + 1) * 64, pr, :], qbT[hh * 64:(hh + 1) * 64, pr, :],
                           True, True, [kbT, qbT], [pX])
                        yield
                    yield
                    for hh in range(2):
                        pX = (pE, pA)[hh]
                        av = attm[:].rearrange("p (pr hh) t -> p hh pr t", hh=2)[:, hh]
                        tt("dve", av, pX[:, 0:256].rearrange("p (h t) -> p h t", h=2),
                           cs[:, MK:MK + 128].unsqueeze(1).to_broadcast([128, 2, 128]), ALU.mult, [pX, cs], [attm])
                        yield
                    yield
                    chunks = (0, 1) if d == 0 else (1, 0)
                    for ci, ch in enumerate(chunks):
                        c0 = ch * 64
                        dcolx = (c0 + 63) if d == 0 else c0
                        pUp = (pD, pA)[ch]
                        for pr in range(2):
                            mm(pUp[:, 256:512], kd[c0:c0 + 64, pr * 128:(pr + 1) * 128],
                               kv_t[c0:c0 + 64, 256 + pr * 256:256 + (pr + 1) * 256], True, True, [kd, kv_t], [pUp])
                            yield
                            stt(Sst[:, pr, :], Sst[:, pr, :], E1[:, pr, dcolx:dcolx + 1], pUp[:, 256:512], ALU.mult, ALU.add,
                                [Sst, E1, pUp], [Sst])
                            yield
                        if ci == 0:
                            tt("pool", Sbf1[:], Sst[:], blkm, ALU.mult, [Sst, cs], [Sbf1])
                            yield
                    for h in range(4):
                        pr, hh = h // 2, h % 2
                        mm(pF[:, h * 128:(h + 1) * 128], kv_t[:, 256 + h * 128:256 + (h + 1) * 128], attm[:, h, :],
                           True, False, [kv_t, attm], [pF])
                        yield
                        for ci, ch in enumerate(chunks):
                            c0 = ch * 64
                            Sx = (Sbf, Sbf1)[ci]
                            mm(pF[:, h * 128 + c0:h * 128 + c0 + 64], Sx[:, pr, hh * 128:(hh + 1) * 128],
                               qbT[:, pr, c0:c0 + 64], False, (ci == 1), [Sx, qbT], [pF])
                            yield
                    tt("pool", Sbf[:], Sst[:], blkm, ALU.mult, [Sst, cs, pF], [Sbf])
                    yield
                    yield
                    if oi % 2 == 1:
                        seg = ti // 2
                        for h in range(4):
                            pr, hh = h // 2, h % 2
                            dma(ngla.ap[e, d, seg, h], Sst[hh * 64:(hh + 1) * 64, pr, hh * 128:(hh + 1) * 128], [Sst], [ngla])
                            yield
                    if d == 0:
                        cp("act", o_t[:].rearrange("p h t -> p (h t)"), pF[:, :], [pF], [o_t])
                        yield
                        dma(oF_s.ap[:, t0:t0 + 128].rearrange("(a p) t -> p a t", p=128), o_t[:], [o_t], [oF_s.bs[ti]])
                        yield
                    else:
                        dma(oF_t[:], oF_s.ap[:, t0:t0 + 128].rearrange("(a p) t -> p a t", p=128), [oF_s.bs[ti]], [oF_t])
                        yield
                        dma(zg_t[:], zgT_s.ap[:, t0:t0 + 128].rearrange("(a p) t -> p a t", p=128), [zgT_s.bs[ti]], [zg_t])
                        yield
                        tt("dve", o_t[:].rearrange("p h t -> p (h t)"), pF[:, :], oF_t[:].rearrange("p h t -> p (h t)"),
                           ALU.add, [pF, oF_t], [o_t])
                        yield
                        of = o_t[:].rearrange("p h t -> p (h t)")
                        act(osq[:], of, AF.Square, [o_t], [osq])
                        yield
                        mm(pC[:, :], csb[:, C_ONE:C_ONE + 128], osq[:], True, True, [osq, csb], [pC])
                        yield
                        act(w1[:, :512], pC[:, :], AF.Ln, [pC], [w1], scale=1.0 / 128, bias=cs[:, C_EPS:C_EPS + 1])
                        yield
                        act(w1[:, :512], w1[:, :512], AF.Exp, [w1], [w1], scale=-0.5)
                        yield
                        tt("dve", w1[:, :512], w1[:, :512], of, ALU.mult, [w1, o_t], [w1])
                        yield
                        act(w1[:, 512:1024], zg_t[:].rearrange("p h t -> p (h t)"), AF.Silu, [zg_t], [w1])
                        yield
                        stt(trs[:, 0:4, :].rearrange("p a t -> p (a t)"), w1[:, :512], onc[:, 0:1], w1[:, 512:1024],
                            ALU.mult, ALU.mult, [w1, onc], [trs])
                        yield
                        dma(yT_s.ap[0:512, t0:t0 + 128].rearrange("(a p) t -> p a t", p=128), trs[:, 0:4, :], [trs], [yT_s.bs[ti]])
                        yield
            yield

        def s5_gen():
            s5_setup(e)
            for j in range(16):
                ft = j // 4
                if j % 4 == 0:
                    dma(u_ft[:], uT_s.ap[ft * 128:(ft + 1) * 128, :], uT_s.bs, [u_ft])
                fences = []
                for d in range(2):
                    X = Xs[d]
                    rev = (d == 1)
                    for blk in range(NB):
                        for c in range(2):
                            pp = (pB, pTf)[c]
                            mm(pp[:, :], BT[:, d, c, j, :], u_ft[:, blk * 512:(blk + 1) * 512], True, True, [BT, u_ft], [pp])
                            cp("act", X[:, c, blk * 512:(blk + 1) * 512], pp[:, :], [pp], [X, Xblk[d][blk]])
                            yield
                    v0 = lambda c: s5_view(X, c, 0, 1, 1, rev)
                    pr_ = PW[:, d, pidx[1], 0, j:j + 1]; pi_ = PW[:, d, pidx[1], 1, j:j + 1]; pn_ = PW[:, d, pidx[1], 2, j:j + 1]
                    stt(v0(0), h0t[:, d, 0, j:j + 1], pr_, v0(0), ALU.mult, ALU.add, [h0t, PW] + Xblk[d], Xblk[d])
                    stt(v0(0), h0t[:, d, 1, j:j + 1], pn_, v0(0), ALU.mult, ALU.add, [h0t, PW] + Xblk[d], Xblk[d])
                    stt(v0(1), h0t[:, d, 1, j:j + 1], pr_, v0(1), ALU.mult, ALU.add, [h0t, PW] + Xblk[d], Xblk[d])
                    fences.append(stt(v0(1), h0t[:, d, 0, j:j + 1], pi_, v0(1), ALU.mult, ALU.add, [h0t, PW] + Xblk[d], Xblk[d]))
                gens = [s5_scan(Xs[d], d, j, d == 1, fences[d]) for d in range(2)]
                alive = [True, True]
                while any(alive):
                    for d in range(2):
                        if alive[d]:
                            try:
                                next(gens[d])
                                yield
                            except StopIteration:
                                alive[d] = False
                for d in range(2):
                    X = Xs[d]
                    rev = (d == 1)
                    for c in range(2):
                        col0 = ((d * 2 + c) * NS) * 16 + j
                        dst = bass.AP(STG[:, 0:1].tensor, STG[:, col0:col0 + 1].offset, [list(STG[:, 0:1].ap[0]), [16, NS]])
                        cp("pool", dst, s5_view(X, c, 255, 256, NS, rev), Xblk[d], [STG])
                for blk in range(NB):
                    k = 0
                    for d in range(2):
                        for c in range(2):
                            mm(pT32[0:32, :], CTm[:, c, j, :], Xs[d][:, c, blk * 512:(blk + 1) * 512], k == 0, k == 3,
                               [CTm, Xblk[d][blk]], [pT32])
                            k += 1
                    cp("act", y5st[:, :], pT32[0:32, :], [pT32], [y5st])
                    dma(y5T_s.ap[j * 32:(j + 1) * 32, blk * 512:(blk + 1) * 512], y5st[:, :], [y5st], [y5T_s.bs[j]])
                    yield
            gsz = min(8, NS)
            for d in range(2):
                for c in range(2):
                    for s0 in range(0, NS, gsz):
                        col0 = ((d * 2 + c) * NS + s0) * 16
                        ncol = gsz * 16
                        tr(pTf[:ncol, 0:128], STG[:, col0:col0 + ncol], ident_f, [STG, cs], [pTf])
                        cp("act", stgT[:ncol, :], pTf[:ncol, 0:128], [pTf], [stgT])
                        dma(ns5.ap[e, d, c, s0 * 16:s0 * 16 + ncol, :], stgT[:ncol, :], [stgT], [ns5])
            yield

        gG, gS = gla_gen(), s5_gen()
        aG = aS = True
        RATIO = float(os.environ.get("KRATIO", "2.0"))
        acc = 0.0
        while aG or aS:
            if aG:
                try:
                    next(gG)
                except StopIteration:
                    aG = False
            acc += RATIO if aG else 1000.0
            while acc >= 1.0 and aS:
                acc -= 1.0
                try:
                    next(gS)
                except StopIteration:
                    aS = False
            if not aS:
                acc = 0.0

        chk('S', l)
        S.barrier()
        s5_post_setup(e)
        for blk in range(NB):
            c0 = blk * 512
            tis = list(range(blk * 4, blk * 4 + 4))
            dma(y5b[:], y5T_s.ap[:, c0:c0 + 512].rearrange("(a p) t -> p a t", p=128), y5T_s.bs, [y5b])
            dma(ub[:], uT_s.ap[:, c0:c0 + 512].rearrange("(a p) t -> p a t", p=128), [uT_s.bs[i] for i in tis], [ub])
            dma(z5b[:], z5T_s.ap[:, c0:c0 + 512].rearrange("(a p) t -> p a t", p=128), [z5T_s.bs[i] for i in tis], [z5b])
            for a in range(4):
                stt(y5b[:, a, :], ub[:, a, :], dcol[:, a:a + 1], y5b[:, a, :], ALU.mult, ALU.add, [ub, dcol, y5b], [y5b])
            act(zz[:].rearrange("p a t -> p (a t)"), y5b[:].rearrange("p a t -> p (a t)"), AF.Gelu_apprx_tanh, [y5b], [zz])
            cp("pool", zzb[:].rearrange("p a t -> p (a t)"), zz[:].rearrange("p a t -> p (a t)"), [zz], [zzb])
            for fo in range(4):
                pp = (pA, pB)[fo % 2]
                for kt in range(4):
                    mm(pp[:, :], wgb[:, kt, fo * 128:(fo + 1) * 128], zzb[:, kt, :], kt == 0, kt == 3, [wgb, zzb], [pp])
                act(sg[:, fo, :], pp[:, :], AF.Sigmoid, [pp, bgl], [sg], bias=bgl[:, fo:fo + 1])
            tt("dve", sg[:].rearrange("p a t -> p (a t)"), sg[:].rearrange("p a t -> p (a t)"),
               zz[:].rearrange("p a t -> p (a t)"), ALU.mult, [sg, zz], [sg])
            act(zz[:].rearrange("p a t -> p (a t)"), z5b[:].rearrange("p a t -> p (a t)"), AF.Silu, [z5b], [zz])
            tt("dve", y5o[:].rearrange("p a t -> p (a t)"), sg[:].rearrange("p a t -> p (a t)"),
               zz[:].rearrange("p a t -> p (a t)"), ALU.mult, [sg, zz], [y5o])
            dma(yT_s.ap[512:1024, c0:c0 + 512].rearrange("(a p) t -> p a t", p=128), y5o[:], [y5o],
                [yT_s.bs[i] for i in tis])
        chk('SP', l)
        out_proj_residual(l, xsrc, xdst, yT_s, w_out_e.ap[e])
        S.barrier()
        chk('O', l)

    qkwb = sb("qkwb", [128, 128])
    sinkb = sb("sinkb", [128, 8])
    rp_t = sb("rp_t", [128, 128], grp="P")
    qn = sb("qn", [128, 640], grp="P")
    qr = sb("qr", [128, 640], grp="P")
    kdup = sb("kdup", [128, 2, 2, 64], BF16, grp="P")
    ckT = sb("ckT", [128, 2, 256], BF16)
    cvt = sb("cvt", [128, 2, 128], BF16)
    ckf = sb("ckf", [128, 2, 128], grp="P")
    qT_t2 = [sb("qT_t%d" % i, [128, 4, 128], BF16, grp="A") for i in range(2)]
    kT_w2 = [sb("kT_w%d" % i, [128, 2, 384], BF16, grp="A") for i in range(2)]
    v_w2 = [sb("v_w%d" % i, [128, 3, 128], BF16, grp="A") for i in range(2)]
    scs2 = [sb("scs%d" % i, [128, 640], grp="A") for i in range(2)]
    pexp2 = [sb("pexp%d" % i, [128, 640], BF16, grp="A") for i in range(2)]
    pTt2 = [sb("pTt%d" % i, [128, 5, 128], BF16, grp="A") for i in range(2)]
    sm2 = [sb("sm%d" % i, [128, 8, 8], grp="A") for i in range(2)]
    bandT = sb("bandT", [128, 384], grp="A")
    oat = sb("oat", [128, 512], grp="A")
    sz_t2 = [sb("sz_t%d" % i, [128, 512], grp="A") for i in range(2)]
    cvw = sb("cvw", [128, 16])
    u1h2 = [sb("u1h%d" % i, [128, 4, 130], grp="A") for i in range(2)]
    u2t2 = [sb("u2t%d" % i, [128, 4, 128], grp="A") for i in range(2)]
    cacc = sb("cacc", [128, 4, 128], grp="A")

    def odd_layer(l, xsrc, xdst):
        e = l // 2
        load_weight_bf16(wbf, w_in_o.ap[e], OIN)
        adaln(l)
        dma(qkwb[:], qkw.ap[e:e + 1, :].partition_broadcast(128), [], [qkwb])
        ts("dve", qkwb[:, 0:64], qkwb[:, 0:64], 0.125, ALU.mult, [qkwb], [qkwb])
        dma(sinkb[:], sinkv.ap[e:e + 1, :].partition_broadcast(128), [], [sinkb])
        dma(cvw[:], convw.ap[e], [], [cvw])
        for hf in range(2):
            dma(ckf[:, 0, :], ckv.ap[e, 0, hf * 128:(hf + 1) * 128].rearrange("t k d -> t (k d)"), [], [ckf])
            dma(ckf[:, 1, :], ckv.ap[e, 1, hf * 128:(hf + 1) * 128].rearrange("t k d -> t (k d)"), [], [ckf])
            cp("dve", cvt[:, hf, :], ckf[:, 1, :], [ckf], [cvt])
            for kv in range(2):
                for c in range(2):
                    cp("dve", kdup[:, kv, c, :], ckf[:, 0, kv * 64:(kv + 1) * 64], [ckf], [kdup])
            for kv in range(2):
                tr(pT[:, kv * 128:(kv + 1) * 128], kdup[:, kv].rearrange("p c d -> p (c d)"), ident_b, [kdup, csb], [pT])
            for kv in range(2):
                cp("act", ckT[:, kv, hf * 128:(hf + 1) * 128], pT[:, kv * 128:(kv + 1) * 128], [pT], [ckT])
        def front_o(ti):
            norm_mod_T(l, ti, xsrc)
            in_proj(OIN, proj2[ti % 2])

        front_o(0)
        for ti in range(NT):
            t0 = ti * 128
            if ti + 1 < NT:
                front_o(ti + 1)
            proj_t = proj2[ti % 2]
            pbf_t = pbf2[ti % 2]
            dma(rp_t[:], rope.ap[t0:t0 + 128, :], [], [rp_t])
            act(w1[:, :640], proj_t[:, 0:640], AF.Square, [proj_t], [w1])
            S.op("dve", lambda e_: e_.tensor_reduce(out=small[:, 0:10], in_=w1[:, :640].rearrange("p (h d) -> p h d", d=64),
                                                    axis=AX.X, op=ALU.add), bl([w1]), bl([small]))
            act(small[:, 0:10], small[:, 0:10], AF.Ln, [small], [small], scale=1.0 / 64, bias=cs[:, C_EPS:C_EPS + 1])
            act(small[:, 0:10], small[:, 0:10], AF.Exp, [small], [small], scale=-0.5)
            tt("dve", qn[:].rearrange("p (h d) -> p h d", d=64), proj_t[:, 0:640].rearrange("p (h d) -> p h d", d=64),
               small[:, 0:10].unsqueeze(2).to_broadcast([128, 10, 64]), ALU.mult, [proj_t, small], [qn])
            tt("dve", qn[:, 0:512].rearrange("p (h d) -> p h d", d=64), qn[:, 0:512].rearrange("p (h d) -> p h d", d=64),
               qkwb[:, 0:64].unsqueeze(1).to_broadcast([128, 8, 64]), ALU.mult, [qn, qkwb], [qn])
            tt("dve", qn[:, 512:640].rearrange("p (h d) -> p h d", d=64), qn[:, 512:640].rearrange("p (h d) -> p h d", d=64),
               qkwb[:, 64:128].unsqueeze(1).to_broadcast([128, 2, 64]), ALU.mult, [qn, qkwb], [qn])
            dma(nkv.ap[e, 0, t0:t0 + 128, :], qn[:, 512:640], [qn], [nkv])
            dma(nkv.ap[e, 1, t0:t0 + 128, :], proj_t[:, 640:768], [proj_t], [nkv])
            v5 = lambda tl: tl[:, 0:640].rearrange("p (h a b f) -> p h a b f", a=2, b=2, f=16)
            cosb = rp_t[:, 0:64].rearrange("p (a b f) -> p a b f", a=2, b=2).unsqueeze(1).to_broadcast([128, 10, 2, 2, 16])
            tt("dve", v5(qr), v5(qn), cosb, ALU.mult, [qn, rp_t], [qr])
            for b_ in range(2):
                sinb = rp_t[:, 64:128].rearrange("p (a b f) -> p a b f", a=2, b=2)[:, :, b_, :].unsqueeze(1).to_broadcast([128, 10, 2, 16])
                tt("dve", v5(w1)[:, :, :, b_, :], v5(qn)[:, :, :, 1 - b_, :], sinb, ALU.mult, [qn, rp_t], [w1])
            tt("dve", pbf_t[:, 0:640], qr[:, 0:640], w1[:, 0:640], ALU.add, [qr, w1], [pbf_t])
            transpose_store(pbf_t[:, 0:512], 4, qT_s, 0, ti, [pbf_t])
            for kv in range(2):
                for c in range(2):
                    cp("act", kdup[:, kv, c, :], pbf_t[:, 512 + kv * 64:512 + (kv + 1) * 64], [pbf_t], [kdup])
            transpose_store(kdup[:].rearrange("p k c d -> p (k c d)"), 2, kT_s, 0, ti, [kdup])
            cp("pool", pbf_t[:, 640:768], proj_t[:, 640:768], [proj_t], [pbf_t])
            dma(v_s.ap[t0:t0 + 128, :], pbf_t[:, 640:768], [pbf_t], [v_s.bs[ti]])
            act(w2[:, :512], proj_t[:, 768:1280], AF.Silu, [proj_t], [w2])
            dma(sz_s.ap[t0:t0 + 128, :], w2[:, :512], [w2], [sz_s.bs[ti]])
            tt("dve", w3[:, 0:512], proj_t[:, 2304:2816], proj_t[:, 1280:1792], ALU.mult, [proj_t], [w3])
            act(w3[:, 512:1024], proj_t[:, 2816:3328], AF.Silu, [proj_t], [w3])
            tt("dve", w3[:, 512:1024], w3[:, 512:1024], proj_t[:, 1792:2304], ALU.mult, [w3, proj_t], [w3])
            transpose_store_f32(w3[:, 0:512], 4, u1T_s, 0, ti, [w3])
            transpose_store_f32(w3[:, 512:1024], 4, u2T_s, 0, ti, [w3])
        chk('P', l)
        S.barrier()
        def loadsA(ti):
            if ti >= NT:
                return
            t0 = ti * 128
            tp = max(ti - 1, 0)
            tn = min(ti + 1, NT - 1)
            qT_t, kT_w, v_w, sz_t, u1h, u2t = (x[ti % 2] for x in (qT_t2, kT_w2, v_w2, sz_t2, u1h2, u2t2))
            dma(qT_t[:], qT_s.ap[:, t0:t0 + 128].rearrange("(a p) t -> p a t", p=128), [qT_s.bs[ti]], [qT_t])
            for wi, tw in enumerate((tp, ti, tn)):
                dma(kT_w[:, :, wi * 128:(wi + 1) * 128], kT_s.ap[:, tw * 128:(tw + 1) * 128].rearrange("(k p) t -> p k t", p=128),
                    [kT_s.bs[tw]], [kT_w])
                dma(v_w[:, wi, :], v_s.ap[tw * 128:(tw + 1) * 128, :], [v_s.bs[tw]], [v_w])
            dma(sz_t[:], sz_s.ap[t0:t0 + 128, :], [sz_s.bs[ti]], [sz_t])
            dma(u1h[:, :, 1:129], u1T_s.ap[:, t0:t0 + 128].rearrange("(a p) t -> p a t", p=128), [u1T_s.bs[ti]], [u1h])
            lo = t0 - 1 if ti > 0 else 0
            hi = t0 + 128 if ti < NT - 1 else T - 1
            dma(u1h[:, :, 0:1], u1T_s.ap[:, lo:lo + 1].rearrange("(a p) t -> p a t", p=128), [u1T_s.bs[tp]], [u1h], slow=True)
            dma(u1h[:, :, 129:130], u1T_s.ap[:, hi:hi + 1].rearrange("(a p) t -> p a t", p=128), [u1T_s.bs[tn]], [u1h], slow=True)
            dma(u2t[:], u2T_s.ap[:, t0:t0 + 128].rearrange("(a p) t -> p a t", p=128), [u2T_s.bs[ti]], [u2t])

        loadsA(0)
        for ti in range(NT):
            t0 = ti * 128
            loadsA(ti + 1)
            qT_t, kT_w, v_w, sz_t, u1h, u2t = (x[ti % 2] for x in (qT_t2, kT_w2, v_w2, sz_t2, u1h2, u2t2))
            flp = mt[:, 3 + ti:4 + ti]
            fln = mt[:, 3 + NT + ti:4 + NT + ti]
            ts("dve", bandT[:, 0:128], band[:, 0:128], flp, ALU.add, [band, mt], [bandT])
            cp("dve", bandT[:, 128:256], band[:, 128:256], [band], [bandT])
            ts("dve", bandT[:, 256:384], band[:, 256:384], fln, ALU.add, [band, mt], [bandT])

            def head_gen(hq, P):
                kv, pr, hh = hq // 4, hq // 2, hq % 2
                rows = slice(hh * 64, (hh + 1) * 64)
                pS1, pS2 = ((pC, pD), (pA, pB))[P]
                scsX, pexpX, pTtX, smX = scs2[P], pexp2[P], pTt2[P], sm2[P]
                pTX = (pT, pTf16)[P]
                pO = (pE, pF)[P]
                oc = (hq // 2) * 64
                mm(pS1[:, 0:384], qT_t[rows, pr, :], kT_w[rows, kv, :], True, True, [qT_t, kT_w], [pS1])
                mm(pS2[:, 0:256], qT_t[rows, pr, :], ckT[rows, kv, :], True, True, [qT_t, ckT], [pS2])
                yield
                tt("dve", scsX[:, 0:384], pS1[:, 0:384], bandT[:, :], ALU.add, [pS1, bandT], [scsX])
                yield
                ts("dve", scsX[:, 384:640], pS2[:, 0:256], mt[:, 2:3], ALU.add, [pS2, mt], [scsX])
                yield
                S.op("dve", lambda e_: e_.reduce_max(out=smX[:, hq, 0:1], in_=scsX[:, :], axis=AX.X), bl([scsX]), bl([smX]))
                yield
                tt("dve", smX[:, hq, 0:1], smX[:, hq, 0:1], sinkb[:, hq:hq + 1], ALU.max, [smX, sinkb], [smX])
                ts("dve", smX[:, hq, 1:2], smX[:, hq, 0:1], -1.0, ALU.mult, [smX], [smX])
                mset("dve", smX[:, hq, 2:3], 0.0, [smX])
                yield
                act(pexpX[:], scsX[:], AF.Exp, [scsX, smX], [pexpX, smX], bias=smX[:, hq, 1:2], accum_out=smX[:, hq, 2:3])
                act(smX[:, hq, 3:4], sinkb[:, hq:hq + 1], AF.Exp, [smX, sinkb], [smX], bias=smX[:, hq, 1:2])
                yield
                for k5 in range(5):
                    tr(pTX[:, k5 * 128:(k5 + 1) * 128], pexpX[:, k5 * 128:(k5 + 1) * 128], ident_b, [pexpX, csb], [pTX])
                yield
                cp("act", pTtX[:].rearrange("p a t -> p (a t)"), pTX[:, 0:640], [pTX], [pTtX])
                tt("dve", smX[:, hq, 4:5], smX[:, hq, 2:3], smX[:, hq, 3:4], ALU.add, [smX], [smX])
                S.op("dve", lambda e_: e_.reciprocal(out=smX[:, hq, 5:6], in_=smX[:, hq, 4:5]), bl([smX]), bl([smX]))
                yield
                for k5 in range(5):
                    vv = v_w[:, k5, kv * 64:(kv + 1) * 64] if k5 < 3 else cvt[:, k5 - 3, kv * 64:(kv + 1) * 64]
                    mm(pO[:, oc:oc + 64], pTtX[:, k5, :], vv, k5 == 0, k5 == 4, [pTtX, v_w, cvt], [pO])
                yield

            for h2 in range(0, 8, 2):
                gens = [head_gen(h2, 0), head_gen(h2 + 1, 1)]
                alive = [True, True]
                while any(alive):
                    for P in range(2):
                        if alive[P]:
                            try:
                                next(gens[P])
                            except StopIteration:
                                alive[P] = False
            for hq in range(8):
                pO = (pE, pF)[hq % 2]
                oc = (hq // 2) * 64
                stt(oat[:, hq * 64:(hq + 1) * 64], pO[:, oc:oc + 64], sm2[hq % 2][:, hq, 5:6], sz_t[:, hq * 64:(hq + 1) * 64],
                    ALU.mult, ALU.mult, [pO, sm2[hq % 2], sz_t], [oat])
            cp("pool", pbfA[:, 0:512], oat[:], [oat], [pbfA])
            transpose_store(pbfA[:, 0:512], 4, yT_s, 0, ti, [pbfA])
            ts("dve", u1h[:, :, 0:1], u1h[:, :, 0:1], mt[:, 3 + 2 * NT + ti:4 + 2 * NT + ti], ALU.mult, [u1h, mt], [u1h])
            ts("dve", u1h[:, :, 129:130], u1h[:, :, 129:130], mt[:, 3 + 3 * NT + ti:4 + 3 * NT + ti], ALU.mult, [u1h, mt], [u1h])
            for a in range(4):
                ts("dve", cacc[:, a, :], u1h[:, a, 1:129], cvw[:, a * 4 + 1:a * 4 + 2], ALU.mult, [u1h, cvw], [cacc],
                   cvw[:, a * 4 + 3:a * 4 + 4], ALU.add)
                stt(cacc[:, a, :], u1h[:, a, 0:128], cvw[:, a * 4:a * 4 + 1], cacc[:, a, :], ALU.mult, ALU.add, [u1h, cvw, cacc], [cacc])
                stt(cacc[:, a, :], u1h[:, a, 2:130], cvw[:, a * 4 + 2:a * 4 + 3], cacc[:, a, :], ALU.mult, ALU.add, [u1h, cvw, cacc], [cacc])
            tt("dve", trs[:, 4:8, :], cacc[:], u2t[:], ALU.mult, [cacc, u2t], [trs])
            dma(yT_s.ap[512:1024, t0:t0 + 128].rearrange("(a p) t -> p a t", p=128), trs[:, 4:8, :], [trs], [yT_s.bs[ti]])
        chk('A', l)
        out_proj_residual(l, xsrc, xdst, yT_s, w_out_o.ap[e])
        S.barrier()
        chk('O', l)

    chain = [x_in, xs[0], xs[1], xs[0], y_out]
    try:
        for l in range(4):
            if l % 2 == 0:
                even_layer(l, chain[l], chain[l + 1])
            else:
                odd_layer(l, chain[l], chain[l + 1])
    except _Stop:
        S.barrier()
    S.finish_waits([y_out.b, ngla.b, ns5.b, nkv.b] + y_out.bs)
    S.emit()
    st.close()
    return nc, S


def _consts():
    c = np.zeros((128, NCST), np.float32)
    i = np.arange(128)
    s = i[:, None]
    t = i[None, :]
    same = (s // 64) == (t // 64)
    c[:, C_ID:C_ID + 128] = np.eye(128, dtype=np.float32)
    c[:, C_TIF:C_TIF + 128] = np.where(same & (s <= t), -1.0 / 16, 0.0)
    c[:, C_TRF:C_TRF + 128] = np.where(same & (s > t), -1.0 / 16, 0.0)
    c[:, C_TIB:C_TIB + 128] = np.where(same & (s >= t), -1.0 / 16, 0.0)
    c[:, C_TRB:C_TRB + 128] = np.where(same & (s < t), -1.0 / 16, 0.0)
    c[:, C_MF:C_MF + 128] = np.where(same & (s <= t), 1.0, 0.0)
    c[:, C_MB:C_MB + 128] = np.where(same & (s >= t), 1.0, 0.0)
    c[:, C_ONE:C_ONE + 128] = 1.0
    c[:, C_EPS] = EPS
    c[:, C_EPS + 1] = 1.0
    qi = i[:, None]
    kj = i[None, :]
    NEG = -1e30
    c[:, C_BAND:C_BAND + 128] = np.where(kj >= qi, 0.0, NEG)
    c[:, C_BAND + 128:C_BAND + 256] = 0.0
    c[:, C_BAND + 256:C_BAND + 384] = np.where(kj <= qi, 0.0, NEG)
    c[0:64, C_BLK:C_BLK + 128] = 1.0
    c[64:128, C_BLK + 128:C_BLK + 256] = 1.0
    return c


def _rope_table(T, identity):
    tab = np.zeros((T, 128), np.float32)
    if identity:
        tab[:, :64] = 1.0
        return tab
    pos = np.arange(T)
    row = (pos // 64).astype(np.float32)
    col = (pos % 64).astype(np.float32)
    freq = (10000.0 ** (-np.arange(16, dtype=np.float32) / 16)).astype(np.float32)
    ar = row[:, None] * freq
    ac = col[:, None] * freq
    cos = np.concatenate([np.cos(ar), np.cos(ar), np.cos(ac), np.cos(ac)], axis=1)
    sin = np.concatenate([-np.sin(ar), np.sin(ar), -np.sin(ac), np.sin(ac)], axis=1)
    tab[:, :64] = cos
    tab[:, 64:] = sin
    return tab


def _meta(T, sample):
    NT = T // 128
    m = np.zeros((128, 3 + 4 * NT), np.float32)
    NEG = -1e30
    if sample:
        m[:, 0] = 1.0
        m[:, 1] = 1.0
        m[:, 2] = 0.0
        flp = np.zeros(NT); flp[0] = NEG
        fln = np.zeros(NT); fln[-1] = NEG
        cfl = np.ones(NT); cfl[0] = 0
        cfr = np.ones(NT); cfr[-1] = 0
    else:
        m[:, 0] = 0.0
        m[:, 1] = 0.0
        m[:, 2] = NEG
        flp = np.where(np.arange(NT) % 2 == 0, NEG, 0.0)
        fln = np.where(np.arange(NT) % 2 == 1, NEG, 0.0)
        cfl = np.where(np.arange(NT) % 2 == 0, 0.0, 1.0)
        cfr = np.where(np.arange(NT) % 2 == 1, 0.0, 1.0)
    m[:, 3:3 + NT] = flp
    m[:, 3 + NT:3 + 2 * NT] = fln
    m[:, 3 + 2 * NT:3 + 3 * NT] = cfl
    m[:, 3 + 3 * NT:3 + 4 * NT] = cfr
    return m


def _state_layout(a):
    sh = a.shape[:-2]
    b = a.reshape(sh + (16, 2, 64))
    b = np.moveaxis(b, -3, -1)
    return np.ascontiguousarray(b.reshape(sh + (128, 16)))


_NC_CACHE = {}
LAST_RESULTS = None


def run(inputs, T, n_prompt_per_core):
    f = lambda k: np.asarray(inputs[k], dtype=np.float32)
    x_prompt, x_sample, c = f("x_prompt"), f("x_sample"), f("c")
    NS = T // 256
    if T not in _NC_CACHE:
        _NC_CACHE[T] = build(T)[0]
    nc = _NC_CACHE[T]
    shared = {}
    shared["cst"] = _consts()
    shared["norm_w"] = f("norm_w")
    shared["w_ada"] = f("w_ada")
    shared["b_ada"] = f("b_ada")
    shared["w_in_e"] = f("w_in_e")
    shared["w_out_e"] = f("w_out_e")
    w2 = f("gla_w2"); b2 = f("gla_b2")
    w2cat = np.zeros((2, 64, 512), np.float32)
    w2cat[:, 0:16, 0:256] = w2[:, 0]
    w2cat[:, 16:32, 256:512] = w2[:, 1]
    w2cat[:, 32, 0:256] = b2[:, 0]
    w2cat[:, 32, 256:512] = b2[:, 1]
    shared["w2cat"] = w2cat
    shared["onorm"] = f("gla_onorm").reshape(2, 128, 1)
    lam_re, lam_im, log_dt = f("s5_lam_re"), f("s5_lam_im"), f("s5_log_dt")
    ldt = np.broadcast_to(log_dt[..., None], lam_re.shape)
    shared["s5p"] = np.stack([_state_layout(lam_re), _state_layout(lam_im), _state_layout(ldt)], axis=2)
    def expand_b(b):
        out = np.zeros((2, 128, 16, 128), np.float32)
        for g in range(32):
            j, gs = g // 2, g % 2
            k0 = 16 * (g % 8)
            out[:, gs * 64:(gs + 1) * 64, j, k0:k0 + 16] = b[:, g]
        return out.reshape(2, 128, 16 * 128)
    shared["s5b"] = np.stack([expand_b(f("s5_b_re")), expand_b(f("s5_b_im"))], axis=1)
    def expand_c(cc, sign):
        out = np.zeros((2, 128, 16, 32), np.float32)
        for g in range(32):
            j, gs = g // 2, g % 2
            out[:, gs * 64:(gs + 1) * 64, j, gs * 16:(gs + 1) * 16] = np.swapaxes(cc[:, g], 1, 2)
        if sign < 0:
            out = np.negative(out)
        return out.reshape(2, 128, 16 * 32)
    shared["s5c"] = np.stack([expand_c(f("s5_c_re"), 1), expand_c(f("s5_c_im"), -1)], axis=1)
    shared["s5d"] = np.ascontiguousarray(f("s5_d").reshape(2, 4, 128).transpose(0, 2, 1))
    shared["wglu"] = f("s5_w_glu")
    shared["bglu"] = np.ascontiguousarray(f("s5_b_glu").reshape(2, 4, 128).transpose(0, 2, 1))
    shared["w_in_o"] = f("w_in_o")
    shared["w_out_o"] = f("w_out_o")
    shared["qkw"] = np.concatenate([f("q_norm_w"), f("k_norm_w")], axis=1)
    shared["sink"] = f("sink")
    cw = f("conv_w"); cb = f("conv_b")
    cvw = np.zeros((2, 128, 4, 4), np.float32)
    for a in range(4):
        cvw[:, :, a, 0:3] = cw[:, :, a * 128:(a + 1) * 128].transpose(0, 2, 1)
        cvw[:, :, a, 3] = cb[:, a * 128:(a + 1) * 128]
    shared["convw"] = cvw.reshape(2, 128, 16)

    in_maps = []
    n_sample = x_sample.shape[0]
    for core in range(8):
        m = dict(shared)
        if core < 4:
            b = core
            m["x"] = np.ascontiguousarray(x_sample[b])
            cv = c[b]
            m["meta"] = _meta(T, True)
            m["rope"] = _rope_table(T, False)
            m["gla0"] = np.ascontiguousarray(f("state_gla")[b])
            sre = _state_layout(f("state_s5_re")[b]); sim = _state_layout(f("state_s5_im")[b])
            m["s5h0"] = np.stack([sre, sim], axis=2)
            ck = f("cache_k")[b]; cvv = f("cache_v")[b]
            m["ckv"] = np.ascontiguousarray(np.stack([ck.transpose(0, 2, 1, 3), cvv.transpose(0, 2, 1, 3)], axis=1))
        else:
            pc = core - 4
            xx = np.zeros((T, D), np.float32)
            seqs = x_prompt[pc * n_prompt_per_core:(pc + 1) * n_prompt_per_core]
            xx[:n_prompt_per_core * 256] = seqs.reshape(-1, D)
            m["x"] = xx
            cv = f("c_ctx")
            m["meta"] = _meta(T, False)
            m["rope"] = _rope_table(T, True)
            m["gla0"] = np.zeros((2, 2, 4, 64, 128), np.float32)
            m["s5h0"] = np.zeros((2, 2, 2, 128, 16), np.float32)
            m["ckv"] = np.zeros((2, 2, 256, 2, 64), np.float32)
        m["cvec"] = np.ascontiguousarray(cv.reshape(8, 128).T)
        in_maps.append(m)
    res = run_bass_kernel_spmd(nc, in_maps, core_ids=list(range(8)))
    R = res.results
    global LAST_RESULTS
    LAST_RESULTS = R
    BATCH = x_prompt.shape[0]
    y_sample = np.stack([np.asarray(R[b]["y"]) for b in range(4)], axis=0).astype(np.float32)
    y_prompt = np.zeros_like(x_prompt)
    new_gla = np.zeros((BATCH, 2, 2, 4, 64, 128), np.float32)
    new_re = np.zeros((BATCH, 2, 2, 32, 64), np.float32)
    new_im = np.zeros((BATCH, 2, 2, 32, 64), np.float32)
    new_k = np.zeros((BATCH, 2, 2, 256, 64), np.float32)
    new_v = np.zeros((BATCH, 2, 2, 256, 64), np.float32)
    for pc in range(4):
        r = R[4 + pc]
        y = np.asarray(r["y"]); g = np.asarray(r["ngla"]); s5 = np.asarray(r["ns5"]); kvo = np.asarray(r["nkv"])
        s5 = s5.reshape(2, 2, 2, NS, 16, 2, 64)
        for q in range(n_prompt_per_core):
            bi = pc * n_prompt_per_core + q
            y_prompt[bi] = y[q * 256:(q + 1) * 256]
            new_gla[bi] = g[:, :, q]
            for d in range(2):
                sig = q if d == 0 else NS - 1 - q
                new_re[bi, :, d] = s5[:, d, 0, sig].reshape(2, 32, 64)
                new_im[bi, :, d] = s5[:, d, 1, sig].reshape(2, 32, 64)
            kk = kvo[:, :, q * 256:(q + 1) * 256, :].reshape(2, 2, 256, 2, 64)
            new_k[bi] = kk[:, 0].transpose(0, 2, 1, 3)
            new_v[bi] = kk[:, 1].transpose(0, 2, 1, 3)
    return (y_prompt, y_sample, new_gla, new_re, new_im, new_k, new_v)


def kernel(**inputs):
    return run(inputs, 4096, 8)
```

```python
import math
import os
import numpy as np
import concourse.bass as bass
import concourse.mybir as mybir
from concourse.bass_utils import run_bass_kernel_spmd
from contextlib import ExitStack

F32 = mybir.dt.float32
BF16 = mybir.dt.bfloat16
I32 = mybir.dt.int32
ALU = mybir.AluOpType
AF = mybir.ActivationFunctionType
AX = mybir.AxisListType

D = 1024
EIN = 2592
OIN = 3328
EPS = 1e-6
ENGS = ("pe", "act", "dve", "pool", "sp")
SEM_LIMIT = 30000
N_DMA_SEMS = 12


class Buf:
    __slots__ = ("w", "r")

    def __init__(self):
        self.w = None
        self.r = {}


class Sched:
    def __init__(self, nc, same_engine_sync=True):
        self.nc = nc
        self.q = {e: [] for e in ENGS}
        self.epoch = {e: 0 for e in ENGS}
        self.cnt = {}
        self.seen = {e: {} for e in ENGS}
        self.same = same_engine_sync
        self.nosync = set(os.environ.get("KNOSYNC", "").split(","))
        self.semkeys = []
        for e in ENGS:
            self._newkey((e, 0))
        self.dma_pool = {e: [] for e in ENGS}
        self.dma_rr = {e: 0 for e in ENGS}
        self.n_ops = 0

    def _newkey(self, k):
        self.cnt[k] = 0
        self.semkeys.append(k)

    def _engkey(self, e):
        k = (e, self.epoch[e])
        if self.cnt[k] >= SEM_LIMIT:
            self.epoch[e] += 1
            k = (e, self.epoch[e])
            self._newkey(k)
        return k

    def _need(self, eng, waits, tok, is_dma=False):
        if tok is None:
            return
        k, v = tok
        if (not is_dma) and k[0] == eng and (eng == "pe" or eng in self.nosync):
            return
        if self.seen[eng].get(k, 0) >= v:
            return
        if waits.get(k, 0) < v:
            waits[k] = v

    def _deps(self, eng, reads, writes, is_dma):
        waits = {}
        for b in reads:
            self._need(eng, waits, b.w, is_dma)
        for b in writes:
            self._need(eng, waits, b.w, is_dma)
            for k, v in b.r.items():
                self._need(eng, waits, (k, v), is_dma)
        return waits

    def op(self, eng, fn, reads=(), writes=(), extra=None):
        if extra is None:
            waits = self._deps(eng, reads, writes, False)
        else:
            sv = self.nosync
            self.nosync = set(sv) | {eng}
            waits = self._deps(eng, reads, writes, False)
            self.nosync = sv
            for tok in extra:
                if tok is not None:
                    self._need(eng, waits, tok, True)
        for k, v in waits.items():
            self.seen[eng][k] = v
        key = self._engkey(eng)
        self.cnt[key] += 1
        tok = (key, self.cnt[key])
        for b in reads:
            b.r[key] = tok[1]
        for b in writes:
            b.w = tok
            b.r = {}
        self.q[eng].append((fn, list(waits.items()), key, 1))
        self.n_ops += 1
        return tok

    def dma(self, fn, reads=(), writes=(), eng="sp"):
        pool = self.dma_pool[eng]
        if len(pool) < N_DMA_SEMS:
            key = ("dma", eng, len(pool), 0)
            self._newkey(key)
            pool.append(key)
        else:
            idx = self.dma_rr[eng] % N_DMA_SEMS
            self.dma_rr[eng] += 1
            key = pool[idx]
            if self.cnt[key] >= SEM_LIMIT:
                key = ("dma", eng, idx, key[3] + 1)
                self._newkey(key)
                pool[idx] = key
        waits = self._deps(eng, reads, writes, True)
        if self.cnt[key] > 0:
            self._need(eng, waits, (key, self.cnt[key]), True)
        for k, v in waits.items():
            self.seen[eng][k] = v
        self.cnt[key] += 16
        tok = (key, self.cnt[key])
        for b in reads:
            b.r[key] = tok[1]
        for b in writes:
            b.w = tok
            b.r = {}
        self.q[eng].append((fn, list(waits.items()), key, 16))
        self.n_ops += 1

    def barrier(self):
        for eng in ENGS:
            waits = {}
            for k, v in self.cnt.items():
                if v > 0 and self.seen[eng].get(k, 0) < v:
                    waits[k] = v
                    self.seen[eng][k] = v
            self.q[eng].append((None, list(waits.items()), None, 0))

    def finish_waits(self, bufs, eng="sp"):
        waits = {}
        for b in bufs:
            self._need(eng, waits, b.w, True)
        self.q[eng].append((None, list(waits.items()), None, 0))

    def emit(self):
        nc = self.nc
        with ExitStack() as st:
            sems = {}
            for i, k in enumerate(self.semkeys):
                sems[k] = st.enter_context(nc.semaphore("s%d" % i))
            block = st.enter_context(nc.Block())

            def runner(ename):
                def run(e):
                    for fn, waits, key, inc in self.q[ename]:
                        for k, v in waits:
                            e.wait_ge(sems[k], v)
                        if fn is not None:
                            fn(e).then_inc(sems[key], inc)
                return run

            block.tensor(runner("pe"))
            block.scalar(runner("act"))
            block.vector(runner("dve"))
            block.gpsimd(runner("pool"))
            block.sync(runner("sp"))


class TL:
    def __init__(self, t):
        self.t = t
        self.b = Buf()

    def __getitem__(self, k):
        return self.t[k]


class View:
    def __init__(self, ap):
        self.ap = ap
        self.b = Buf()

    def __getitem__(self, k):
        return self.ap[k]


class DT:
    def __init__(self, ap, nb=1):
        self.ap = ap
        self.bs = [Buf() for _ in range(nb)]
        self.b = self.bs[0]


C_ID, C_TIF, C_TRF, C_TIB, C_TRB, C_MF, C_MB, C_ONE, C_BAND = [128 * i for i in range(9)]
C_BLK = C_BAND + 384
C_EPS = C_BLK + 256
NCST = C_EPS + 2


class _Stop(Exception):
    pass


def build(T, stop=None):
    import os
    stop = stop or os.environ.get("KSTOP")
    NT = T // 128
    NS = T // 256
    NB = T // 512
    nc = bass.Bass("TRN2", target_bir_lowering=False)
    S = Sched(nc, same_engine_sync=bool(int(os.environ.get("KSAME", "0"))))
    st = ExitStack()

    def din(name, shape, dt=F32):
        return DT(nc.dram_tensor(name, list(shape), dt, kind="ExternalInput").ap())

    def dout(name, shape, dt=F32):
        return DT(nc.dram_tensor(name, list(shape), dt, kind="ExternalOutput").ap())

    dbg = bool(os.environ.get("KDBG"))

    def dscr(name, shape, dt=F32, nb=1):
        return DT(nc.dram_tensor(name, list(shape), dt, kind="ExternalOutput" if dbg else "Internal").ap(), nb)

    ARENA_WORDS = 34000
    G_BASE = 29000
    arena_t = st.enter_context(nc.sbuf_tensor("arena", [128, ARENA_WORDS], F32))
    goff = {"G": G_BASE}

    def sb(name, shape, dt=F32, grp=None):
        if grp is None:
            return TL(st.enter_context(nc.sbuf_tensor(name, list(shape), dt)))
        n = 1
        for d_ in shape[1:]:
            n *= d_
        words = n if dt in (F32, I32) else (n + 1) // 2
        off = goff.get(grp, 0)
        goff[grp] = off + words
        assert goff[grp] <= ARENA_WORDS, (grp, name, goff[grp])
        assert grp != "S" or goff[grp] <= G_BASE, (grp, name, goff[grp])
        ap = arena_t[:, off:off + words]
        if dt != F32:
            ap = ap.bitcast(dt)[:, :n]
        if len(shape) > 2:
            names = " ".join("d%d" % i for i in range(len(shape) - 1))
            kw = {"d%d" % i: shape[i + 1] for i in range(len(shape) - 1)}
            ap = ap.rearrange("p (%s) -> p %s" % (names, names), **kw)
        if shape[0] < 128:
            ap = ap[0:shape[0]]
        return View(ap)

    def ps(name, shape, dt=F32):
        return TL(st.enter_context(nc.psum_tensor(name, list(shape), dt)))

    def bl(xs):
        return [x.b if hasattr(x, "b") else x for x in xs]

    def mm(out, lhsT, rhs, start, stop, R, W):
        S.op("pe", lambda e: e.matmul(out, lhsT=lhsT, rhs=rhs, start=start, stop=stop), bl(R), bl(W))

    def tr(out, in_, ident, R, W):
        S.op("pe", lambda e: e.transpose(out, in_, ident), bl(R), bl(W))

    def act(out, in_, func, R, W, **kw):
        S.op("act", lambda e: e.activation(out=out, in_=in_, func=func, **kw), bl(R), bl(W))

    def tt(eng, out, a, b, op, R, W):
        S.op(eng, lambda e: e.tensor_tensor(out=out, in0=a, in1=b, op=op), bl(R), bl(W))

    def ts(eng, out, a, s1, op0, R, W, s2=None, op1=None):
        if op1 is None:
            S.op(eng, lambda e: e.tensor_scalar(out=out, in0=a, scalar1=s1, scalar2=None, op0=op0), bl(R), bl(W))
        else:
            S.op(eng, lambda e: e.tensor_scalar(out=out, in0=a, scalar1=s1, scalar2=s2, op0=op0, op1=op1), bl(R), bl(W))

    def stt(out, in0, scalar, in1, op0, op1, R, W, extra=None):
        return S.op("dve", lambda e: e.scalar_tensor_tensor(out=out, in0=in0, scalar=scalar, in1=in1, op0=op0, op1=op1),
                    bl(R), bl(W), extra=extra)

    def cp(eng, out, in_, R, W):
        if eng == "act":
            S.op("act", lambda e: e.copy(out=out, in_=in_), bl(R), bl(W))
        else:
            S.op(eng, lambda e: e.tensor_copy(out=out, in_=in_), bl(R), bl(W))

    def mset(eng, ap, val, W):
        S.op(eng, lambda e: e.memset(ap, val), [], bl(W))

    def dma(out, in_, R, W, eng="sp", slow=False):
        if slow:
            S.dma(lambda e: e.dma_start(out=out, in_=in_, allow_slow_non_contiguous=True), bl(R), bl(W), eng=eng)
        else:
            S.dma(lambda e: e.dma_start(out=out, in_=in_), bl(R), bl(W), eng=eng)

    x_in = din("x", [T, D])
    cvec = din("cvec", [128, 8])
    cst = din("cst", [128, NCST])
    NMETA = 3 + 4 * NT
    meta = din("meta", [128, NMETA])
    rope = din("rope", [T, 128])
    gla0 = din("gla0", [2, 2, 4, 64, 128])
    s5h0 = din("s5h0", [2, 2, 2, 128, 16])
    ckv = din("ckv", [2, 2, 256, 2, 64])
    norm_w = din("norm_w", [4, D])
    w_ada = din("w_ada", [4, D, 3 * D])
    b_ada = din("b_ada", [4, 3 * D])
    w_in_e = din("w_in_e", [2, D, EIN])
    w_out_e = din("w_out_e", [2, D, D])
    w2cat = din("w2cat", [2, 64, 512])
    onorm = din("onorm", [2, 128, 1])
    s5p = din("s5p", [2, 2, 3, 128, 16])
    s5b = din("s5b", [2, 2, 128, 16 * 128])
    s5c = din("s5c", [2, 2, 128, 16 * 32])
    s5d = din("s5d", [2, 128, 4])
    wglu = din("wglu", [2, 512, 512])
    bglu = din("bglu", [2, 128, 4])
    w_in_o = din("w_in_o", [2, D, OIN])
    w_out_o = din("w_out_o", [2, D, D])
    qkw = din("qkw", [2, 128])
    sinkv = din("sink", [2, 8])
    convw = din("convw", [2, 128, 16])

    y_out = dout("y", [T, D])
    ngla = dout("ngla", [2, 2, NS, 4, 64, 128])
    ns5 = dout("ns5", [2, 2, 2, NS * 16, 128])
    nkv = dout("nkv", [2, 2, T, 128])

    xs = [dscr("xs0", [T, D], nb=NT), dscr("xs1", [T, D], nb=NT)]
    qkT_s = dscr("qkT", [512, T], BF16, NT)
    kv_s = dscr("kvtm", [T, 768], BF16, NT)
    sp_s = dscr("sp", [T, 512], F32, NT)
    zgT_s = dscr("zgT", [512, T], BF16, NT)
    uT_s = dscr("uT", [512, T], BF16, NT)
    z5T_s = dscr("z5T", [512, T], BF16, NT)
    oF_s = dscr("oF", [512, T], F32, NT)
    y5T_s = dscr("y5T", [512, T], F32, 16)
    yT_s = dscr("yT", [1024, T], BF16, NT)
    qT_s = dscr("qTo", [512, T], BF16, NT)
    kT_s = dscr("kTo", [256, T], BF16, NT)
    v_s = dscr("vo", [T, 128], BF16, NT)
    sz_s = dscr("szo", [T, 512], F32, NT)
    u1T_s = dscr("u1T", [512, T], F32, NT)
    u2T_s = dscr("u2T", [512, T], F32, NT)

    cs = sb("cs", [128, NCST])
    csb = sb("csb", [128, NCST], BF16)
    mt = sb("mt", [128, NMETA])
    wbf = sb("wbf", [128, 8, OIN], BF16, grp="P")
    wob = sb("wob", [128, 8, D], BF16, grp="O")
    wst1 = sb("wst0", [128, OIN], grp="P")
    wst = [wst1, wst1]
    wstO = sb("wstO", [128, D], grp="O")
    wstS = sb("wstS", [128, 512], grp="SP")
    pbfA = sb("pbfA", [128, 512], BF16, grp="A")
    modbc = sb("modbc", [128, 3 * D])
    Abc = sb("Abc", [128, D])
    sc8 = sb("sc8", [128, 8])
    screp = sb("screp", [128, 8, 128])
    xts = [sb("xt0", [128, D]), sb("xt1", [128, D])]
    xt = xts[0]
    hb = sb("hb", [128, D], BF16, grp="P")
    hT = sb("hT", [128, 8, 128], BF16, grp="P")
    small = sb("small", [128, 16])
    small2 = sb("small2", [128, 16])
    proj = sb("proj", [128, OIN], grp="P")
    pbf = sb("pbf", [128, OIN], BF16, grp="P")
    trs = sb("trs", [128, 8, 128], BF16)
    trf = sb("trf", [128, 4, 128])
    w1 = sb("w1", [128, 1024])
    w2 = sb("w2", [128, 1024])
    w3 = sb("w3", [128, 1024])
    lrT = sb("lrT", [64, 128], BF16, grp="P")
    w2c = sb("w2c", [64, 512], BF16, grp="P")
    w2cf = sb("w2cf", [64, 512], grp="P")

    projB = sb("projB", [128, OIN], grp="P")
    pbfB = sb("pbfB", [128, OIN], BF16, grp="P")
    proj2 = [proj, projB]
    pbf2 = [pbf, pbfB]
    pA = ps("pA", [128, 512]); pB = ps("pB", [128, 512]); pC = ps("pC", [128, 512])
    pD = ps("pD", [128, 512]); pE = ps("pE", [128, 512]); pF = ps("pF", [128, 512])
    pT = ps("pT", [128, 1024], BF16)
    pTf = ps("pTf", [128, 512])

    class _PT32:
        def __init__(self):
            self.ap = pT[:, :].bitcast(F32)
            self.b = pT.b

        def __getitem__(self, k):
            return self.ap[k]
    pT32 = _PT32()

    class _PTB:
        def __init__(self):
            self.ap = pE[:, :].bitcast(BF16)
            self.b = pE.b

        def __getitem__(self, k):
            return self.ap[k]
    pTb = _PTB()

    class _PTF16:
        def __init__(self):
            self.ap = pTf[:, :].bitcast(BF16)
            self.b = pTf.b

        def __getitem__(self, k):
            return self.ap[k]
    pTf16 = _PTF16()
    trsb = sb("trsb", [128, 4, 128], BF16)
    ident_b = csb[:, C_ID:C_ID + 128]
    ident_f = cs[:, C_ID:C_ID + 128]
    mflag = mt[:, 0:1]

    dma(cs[:], cst.ap[:, :], [], [cs])
    cp("dve", csb[:], cs[:], [cs], [csb])
    dma(mt[:], meta.ap[:, :], [], [mt])
    dma(sc8[:], cvec.ap[:, :], [], [sc8])
    act(sc8[:], sc8[:], AF.Silu, [sc8], [sc8])
    for kt in range(8):
        cp("dve", screp[:, kt, :], sc8[:, kt:kt + 1].to_broadcast([128, 128]), [sc8], [screp])
    band = sb("band", [128, 384])
    ts("dve", band[:], cs[:, C_BAND:C_BAND + 384], mt[:, 1:2], ALU.mult, [cs, mt], [band])

    evac_rr = [0]

    def evac(out, in_, R, W):
        e = ("act", "dve")[evac_rr[0] % 2]
        evac_rr[0] += 1
        cp(e, out, in_, R, W)

    def load_weight_bf16(dst, src_ap, ncols, wsx=None):
        wsx = wsx or wst1
        for kt in range(8):
            dma(wsx[:, :ncols], src_ap[kt * 128:(kt + 1) * 128, :], [], [wsx])
            cp(("act", "dve")[kt % 2], dst[:, kt, :ncols], wsx[:, :ncols], [wsx], [dst])

    def adaln(l):
        banks = [pA, pB, pC, pD, pE, pF]
        for kt in range(8):
            wsx = wst[kt % 2]
            dma(wsx[:, :3 * D], w_ada.ap[l, kt * 128:(kt + 1) * 128, :], [], [wsx])
            for nb_ in range(6):
                mm(banks[nb_][:, :], screp[:, kt, :], wsx[:, nb_ * 512:(nb_ + 1) * 512], kt == 0, kt == 7,
                   [screp, wsx], [banks[nb_]])
        tmpbc = wst1
        dma(tmpbc[:, :3 * D], b_ada.ap[l:l + 1, :].partition_broadcast(128), [], [tmpbc])
        for nb_ in range(6):
            tt("dve", modbc[:, nb_ * 512:(nb_ + 1) * 512], banks[nb_][:, :], tmpbc[:, nb_ * 512:(nb_ + 1) * 512],
               ALU.add, [banks[nb_], tmpbc], [modbc])
        dma(tmpbc[:, :D], norm_w.ap[l:l + 1, :].partition_broadcast(128), [modbc], [tmpbc])
        stt(Abc[:], modbc[:, D:2 * D], 1.0, tmpbc[:, :D], ALU.add, ALU.mult, [modbc, tmpbc], [Abc])

    def load_x(ti, xsrc):
        if ti >= NT:
            return
        t0 = ti * 128
        xt = xts[ti % 2]
        dma(xt[:], xsrc.ap[t0:t0 + 128, :], [xsrc.bs[ti] if len(xsrc.bs) > 1 else xsrc.b], [xt])

    def norm_mod_T(l, ti, xsrc):
        if ti == 0:
            load_x(0, xsrc)
        load_x(ti + 1, xsrc)
        xt = xts[ti % 2]
        mset("dve", small2[:, 0:1], 0.0, [small2])
        act(hb[:], xt[:], AF.Square, [xt], [hb, small2], accum_out=small2[:, 0:1])
        act(small2[:, 1:2], small2[:, 0:1], AF.Ln, [small2], [small2], scale=1.0 / D, bias=cs[:, C_EPS:C_EPS + 1])
        act(small2[:, 2:3], small2[:, 1:2], AF.Exp, [small2], [small2], scale=-0.5)
        stt(w4[:, :D], xt[:], small2[:, 2:3], Abc[:], ALU.mult, ALU.mult, [xt, small2, Abc], [w4])
        tt("dve", hb[:], w4[:, :D], modbc[:, 0:D], ALU.add, [w4, modbc], [hb])
        for kt in range(8):
            tr(pT[:, kt * 128:(kt + 1) * 128], hb[:, kt * 128:(kt + 1) * 128], ident_b, [hb, csb], [pT])
        cp("act", hT[:].rearrange("p a t -> p (a t)"), pT[:, :], [pT], [hT])

    def in_proj(ncols, dst=None):
        dst = dst or proj
        c0 = 0
        k = 0
        while c0 < ncols:
            cw = min(512, ncols - c0)
            pp = (pA, pB)[k % 2]
            for kt in range(8):
                mm(pp[:, :cw], hT[:, kt, :], wbf[:, kt, c0:c0 + cw], kt == 0, kt == 7, [hT, wbf], [pp])
            evac(dst[:, c0:c0 + cw], pp[:, :cw], [pp], [dst])
            c0 += cw
            k += 1

    ts_rr = [0]

    def transpose_store(src_bf_ap, nft, dst, row0, ti, R):
        t0 = ti * 128
        assert nft <= 4
        k = ts_rr[0] % 2
        ts_rr[0] += 1
        pX, tX = (pT, pTb)[k], (trs, trsb)[k]
        for a in range(nft):
            tr(pX[:, a * 128:(a + 1) * 128], src_bf_ap[:, a * 128:(a + 1) * 128], ident_b, R + [csb], [pX])
        cp(("act", "dve")[k], tX[:, :nft, :].rearrange("p a t -> p (a t)"), pX[:, :nft * 128], [pX], [tX])
        dma(dst.ap[row0:row0 + nft * 128, t0:t0 + 128].rearrange("(a p) t -> p a t", p=128), tX[:, :nft, :],
            [tX], [dst.bs[ti]])

    def transpose_store_f32(src_ap, nft, dst, row0, ti, R):
        t0 = ti * 128
        for a in range(nft):
            tr(pTf[:, a * 128:(a + 1) * 128], src_ap[:, a * 128:(a + 1) * 128], ident_f, R + [cs], [pTf])
        cp("act", trf[:, :nft, :].rearrange("p a t -> p (a t)"), pTf[:, :nft * 128], [pTf], [trf])
        dma(dst.ap[row0:row0 + nft * 128, t0:t0 + 128].rearrange("(a p) t -> p a t", p=128), trf[:, :nft, :],
            [trf], [dst.bs[ti]])

    def out_proj_residual(l, xsrc, xdst, ydt, wsrc):
        S.barrier()
        load_weight_bf16(wob, wsrc, D, wstO)
        def loads(ti):
            if ti >= NT:
                return
            t0 = ti * 128
            p = ti % 2
            yt = (sb_yt, sb_yt2)[p]
            dma(yt[:], ydt.ap[:, t0:t0 + 128].rearrange("(a p) t -> p a t", p=128), [ydt.bs[ti]], [yt])
            dma(xts[p][:], xsrc.ap[t0:t0 + 128, :], [xsrc.bs[ti] if len(xsrc.bs) > 1 else xsrc.b], [xts[p]])

        loads(0)
        for ti in range(NT):
            t0 = ti * 128
            p = ti % 2
            loads(ti + 1)
            yt, xt = (sb_yt, sb_yt2)[p], xts[p]
            o1, o2 = ((w1, w2), (w3, w4))[p]
            for cb in range(2):
                pp = ((pA, pB), (pC, pD))[p][cb]
                for kt in range(8):
                    mm(pp[:, :], yt[:, kt, :], wob[:, kt, cb * 512:(cb + 1) * 512], kt == 0, kt == 7, [yt, wob], [pp])
                tt("dve", o1[:, cb * 512:(cb + 1) * 512], pp[:, :], modbc[:, 2 * D + cb * 512:2 * D + (cb + 1) * 512],
                   ALU.mult, [pp, modbc], [o1])
            tt("dve", o2[:, :D], o1[:, :D], xt[:], ALU.add, [o1, xt], [o2])
            dma(xdst.ap[t0:t0 + 128, :], o2[:, :D], [o2], [xdst.bs[ti] if len(xdst.bs) > 1 else xdst.b])

    sb_yt = sb("yt", [128, 8, 128], BF16)
    sb_yt2 = sb("yt2", [128, 8, 128], BF16)
    w4 = sb("w4", [128, 1024])

    qk_t = sb("qk_t", [128, 4, 128], BF16, grp="G")
    kv_t = sb("kv_t", [128, 768], BF16, grp="G")
    sp_t = sb("sp_t", [128, 256], grp="G")
    E1 = sb("E1", [128, 2, 128], grp="G")
    E2 = sb("E2", [128, 2, 128], grp="G")
    E3 = sb("E3", [128, 256], grp="G")
    qbT = sb("qbT", [128, 2, 128], BF16, grp="G")
    kbT = sb("kbT", [128, 2, 128], BF16, grp="G")
    kd = sb("kd", [128, 256], BF16, grp="G")
    attm = sb("attm", [128, 4, 128], BF16, grp="G")
    Sst = sb("Sst", [128, 2, 256], grp="G")
    Sbf = sb("Sbf", [128, 2, 256], BF16, grp="G")
    Sbf1 = sb("Sbf1", [128, 2, 256], BF16, grp="G")
    o_t = sb("o_t", [128, 4, 128], grp="G")
    oF_t = sb("oF_t", [128, 4, 128], grp="G")
    zg_t = sb("zg_t", [128, 4, 128], BF16, grp="G")
    osq = sb("osq", [128, 512], BF16, grp="G")
    onc = sb("onc", [128, 1])
    TX = max(T, 2048)
    Xs = [sb("Xf", [128, 2, TX], grp="S"), sb("Xb", [128, 2, TX], grp="S")]
    u_ft = sb("u_ft", [128, T], BF16, grp="S")
    BT = sb("BT", [128, 2, 2, 16, 128], BF16, grp="S")
    CTm = sb("CTm", [128, 2, 16, 32], grp="S")
    Xblk = [[Buf() for _ in range(max(NB, 1))] for _ in range(2)]

    class _Alias:
        def __init__(self, base, shape):
            self.ap = base.ap.rearrange("p c t -> p (c t)")[:, 0:4096].rearrange("p (a j k) -> p a j k", a=2, j=16)
            self.b = base.b

        def __getitem__(self, k):
            return self.ap[k]
    Bx = _Alias(Xs[0], None)
    Bbar = _Alias(Xs[1], None)
    s5par = sb("s5par", [128, 3, 16], grp="S")
    h0t = sb("h0t", [128, 2, 2, 16], grp="S")
    NPW = 18
    PW = sb("PW", [128, 2, NPW, 3, 16], grp="S")
    PWm = sb("PWm", [128, 2, NPW, 3, 16], grp="S")
    s5tmp = sb("s5tmp", [128, 12, 16], grp="S")
    s5i = sb("s5i", [128, 16], I32, grp="S")
    STG = sb("STG", [128, 2 * 2 * NS * 16], grp="S")
    stgT = sb("stgT", [128, 128], grp="S")
    y5st = sb("y5st", [32, 512], grp="S")
    dcol = sb("dcol", [128, 4], grp="SP")
    bgl = sb("bgl", [128, 4], grp="SP")
    wgb = sb("wgb", [128, 4, 512], BF16, grp="SP")
    y5b = sb("y5b", [128, 4, 512], grp="SP")
    ub = sb("ub", [128, 4, 512], BF16, grp="SP")
    z5b = sb("z5b", [128, 4, 512], BF16, grp="SP")
    zz = sb("zz", [128, 4, 512], grp="SP")
    zzb = sb("zzb", [128, 4, 512], BF16, grp="SP")
    sg = sb("sg", [128, 4, 512], grp="SP")
    y5o = sb("y5o", [128, 4, 512], BF16, grp="SP")

    pw_exps = list(range(1, 9)) + [16, 24, 32, 40, 48, 56, 64, 128, 192, 256]
    pidx = {m: i for i, m in enumerate(pw_exps)}
    assert len(pw_exps) <= NPW

    def s5_post_setup(e):
        dma(dcol[:], s5d.ap[e], [], [dcol])
        dma(bgl[:], bglu.ap[e], [], [bgl])
        for kt in range(4):
            dma(wstS[:, :512], wglu.ap[e, kt * 128:(kt + 1) * 128, :], [], [wstS])
            cp("act", wgb[:, kt, :], wstS[:, :512], [wstS], [wgb])

    def s5_setup(e):
        dma(h0t[:], s5h0.ap[e].rearrange("d c p j -> p d c j"), [], [h0t])
        dma(CTm[:].rearrange("p a j o -> p a (j o)"), s5c.ap[e].rearrange("a p x -> p a x"), [], [CTm])
        dma(Bx[:].rearrange("p a j k -> p a (j k)"), s5b.ap[e].rearrange("a p x -> p a x"), [], [Bx])
        for d in range(2):
            dma(s5par[:], s5p.ap[e, d].rearrange("a p j -> p a j"), [], [s5par])
            lre, lim, ldt = s5par[:, 0, :], s5par[:, 1, :], s5par[:, 2, :]
            tmp = lambda i: s5tmp[:, i, :]
            R = [s5par, s5tmp]
            W = [s5tmp]
            act(tmp(0), ldt, AF.Exp, R, W)
            tt("dve", tmp(1), lre, tmp(0), ALU.mult, R, W)
            tt("dve", tmp(2), lim, tmp(0), ALU.mult, R, W)
            act(tmp(3), tmp(1), AF.Exp, R, W)
            for (dst, shift) in ((4, 0.0), (5, math.pi / 2)):
                ts("dve", tmp(6), tmp(2), shift, ALU.add, R, W, 1.0 / (2 * math.pi), ALU.mult)
                cp("dve", s5i[:], tmp(6), R, [s5i])
                cp("dve", tmp(7), s5i[:], [s5i], W)
                ts("dve", tmp(6), tmp(2), shift, ALU.add, R, W)
                stt(tmp(6), tmp(7), -2 * math.pi, tmp(6), ALU.mult, ALU.add, R, W)
                ts("dve", tmp(6), tmp(6), math.pi, ALU.min, R, W, -math.pi, ALU.max)
                act(tmp(dst), tmp(6), AF.Sin, R, W)
            P = lambda m, c: PW[:, d, pidx[m], c, :]
            RW = [PW, s5tmp]
            tt("dve", P(1, 0), tmp(3), tmp(5), ALU.mult, RW, [PW])
            tt("dve", P(1, 1), tmp(3), tmp(4), ALU.mult, RW, [PW])
            tt("dve", tmp(6), lre, lre, ALU.mult, R, W)
            tt("dve", tmp(7), lim, lim, ALU.mult, R, W)
            tt("dve", tmp(6), tmp(6), tmp(7), ALU.add, R, W)
            S.op("dve", lambda e_, o=tmp(6): e_.reciprocal(out=o, in_=o), bl(R), bl(W))
            ts("dve", tmp(7), P(1, 0), -1.0, ALU.add, RW, W)
            tt("dve", tmp(8), tmp(7), lre, ALU.mult, R, W)
            tt("dve", tmp(9), P(1, 1), lim, ALU.mult, RW + [s5par], W)
            tt("dve", tmp(8), tmp(8), tmp(9), ALU.add, R, W)
            tt("dve", tmp(8), tmp(8), tmp(6), ALU.mult, R, W)
            tt("dve", tmp(9), P(1, 1), lre, ALU.mult, RW + [s5par], W)
            tt("dve", tmp(10), tmp(7), lim, ALU.mult, R, W)
            tt("dve", tmp(9), tmp(9), tmp(10), ALU.subtract, R, W)
            tt("dve", tmp(9), tmp(9), tmp(6), ALU.mult, R, W)
            crb = s5tmp[:, 8, :].unsqueeze(2).to_broadcast([128, 16, 128])
            cib = s5tmp[:, 9, :].unsqueeze(2).to_broadcast([128, 16, 128])
            RB = [Bx, s5tmp, Bbar]
            tt("dve", Bbar[:, 0], Bx[:, 0], crb, ALU.mult, RB, [Bbar])
            tt("dve", Bbar[:, 1], Bx[:, 1], cib, ALU.mult, RB, [Bbar])
            tt("dve", Bbar[:, 0], Bbar[:, 0], Bbar[:, 1], ALU.subtract, RB, [Bbar])
            tt("dve", Bbar[:, 1], Bx[:, 1], crb, ALU.mult, RB, [Bbar])
            for j in range(16):
                stt(Bbar[:, 1, j, :], Bx[:, 0, j, :], s5tmp[:, 9, j:j + 1], Bbar[:, 1, j, :], ALU.mult, ALU.add, RB, [Bbar])
            for c in range(2):
                for j4 in range(4):
                    for jj in range(4):
                        j = j4 * 4 + jj
                        tr(pTf[:, jj * 128:(jj + 1) * 128], Bbar[:, c, j, :], ident_f, [Bbar, cs], [pTf])
                    cp("act", BT[:, d, c, j4 * 4:(j4 + 1) * 4, :].rearrange("p j k -> p (j k)"), pTf[:, :], [pTf], [BT])
            def cmul(mo, ma, mb_):
                a_re, a_im = P(ma, 0), P(ma, 1)
                b_re, b_im = P(mb_, 0), P(mb_, 1)
                tt("dve", tmp(10), a_im, b_im, ALU.mult, RW, W)
                tt("dve", tmp(11), a_re, b_re, ALU.mult, RW, W)
                tt("dve", tmp(6), a_re, b_im, ALU.mult, RW, W)
                tt("dve", tmp(7), a_im, b_re, ALU.mult, RW, W)
                tt("dve", P(mo, 0), tmp(11), tmp(10), ALU.subtract, RW, [PW])
                tt("dve", P(mo, 1), tmp(6), tmp(7), ALU.add, RW, [PW])
            for m in range(2, 9):
                cmul(m, m - 1, 1)
            for m in range(16, 65, 8):
                cmul(m, m - 8, 8)
            cmul(128, 64, 64)
            cmul(192, 128, 64)
            cmul(256, 192, 64)
            ts("dve", PW[:, d, :, 2, :], PW[:, d, :, 1, :], -1.0, ALU.mult, [PW], [PW])
            ts("dve", PWm[:, d], PW[:, d], mflag, ALU.mult, [PW, mt], [PWm])

    def s5_view(X, comp, start, step, count, rev, inner=None):
        base = X[:, comp, 0:1]
        off = base.offset
        pstride = base.ap[0][0]
        if rev:
            o = off + (T - 1 - start)
            dims = [[pstride, 128], [-step, count]]
            if inner is not None:
                dims.append([-inner[0], inner[1]])
        else:
            o = off + start
            dims = [[pstride, 128], [step, count]]
            if inner is not None:
                dims.append([inner[0], inner[1]])
        return bass.AP(base.tensor, o, dims)

    def s5_scan(X, d, j, rev, fence0):
        XW = Xblk[d]
        XB = XW + [PW, PWm]
        stt_ = {"fence": fence0, "C": None, "D": None}

        def cmac(tgt, src, pw_tile, m, chain):
            pr = pw_tile[:, d, pidx[m], 0, j:j + 1]
            pi = pw_tile[:, d, pidx[m], 1, j:j + 1]
            pn = pw_tile[:, d, pidx[m], 2, j:j + 1]
            f = stt_["fence"]
            pc, pd = (stt_["C"], stt_["D"]) if chain else (None, None)
            ta = stt(tgt(0), src(0), pr, tgt(0), ALU.mult, ALU.add, XB, XW, extra=[f, pc])
            yield
            tb = stt(tgt(1), src(0), pi, tgt(1), ALU.mult, ALU.add, XB, XW, extra=[f, pc])
            yield
            tc = stt(tgt(0), src(1), pn, tgt(0), ALU.mult, ALU.add, XB, XW, extra=[f, pd, ta])
            yield
            td = stt(tgt(1), src(1), pr, tgt(1), ALU.mult, ALU.add, XB, XW, extra=[f, pd, tb])
            yield
            stt_["C"], stt_["D"], stt_["last"] = tc, td, td

        def fence():
            stt_["fence"] = stt_.get("last", stt_["fence"])
            stt_["C"] = stt_["D"] = None

        for (s, K) in ((1, 8), (8, 8), (64, 4)):
            n = T // (s * K)
            fence()
            for jj in range(1, K):
                tgt = lambda c, s=s, K=K, jj=jj, n=n: s5_view(X, c, (jj + 1) * s - 1, K * s, n, rev)
                src = lambda c, s=s, K=K, jj=jj, n=n: s5_view(X, c, jj * s - 1, K * s, n, rev)
                yield from cmac(tgt, src, PW, s, True)
        fence()
        for sg_ in range(1, NS):
            tgt = lambda c, sg_=sg_: s5_view(X, c, 256 * (sg_ + 1) - 1, 1, 1, rev)
            src = lambda c, sg_=sg_: s5_view(X, c, 256 * sg_ - 1, 1, 1, rev)
            yield from cmac(tgt, src, PWm, 256, True)
        fence()
        if NS > 1:
            for jj in range(3):
                tgt = lambda c, jj=jj: s5_view(X, c, 256 + 64 * (jj + 1) - 1, 256, NS - 1, rev)
                src = lambda c: s5_view(X, c, 255, 256, NS - 1, rev)
                yield from cmac(tgt, src, PWm, 64 * (jj + 1), False)
        for (s, K, nin) in ((8, 8, 4), (1, 8, 32)):
            fence()
            for jj in range(K - 1):
                m = s * (jj + 1)
                tgt = lambda c, s=s, K=K, jj=jj, nin=nin: s5_view(X, c, s * K + (jj + 1) * s - 1, 256, NS, rev, inner=(s * K, nin - 1))
                src = lambda c, s=s, K=K, nin=nin: s5_view(X, c, s * K - 1, 256, NS, rev, inner=(s * K, nin - 1))
                yield from cmac(tgt, src, PW, m, False)
                if NS > 1:
                    tgt = lambda c, s=s, jj=jj: s5_view(X, c, 256 + (jj + 1) * s - 1, 256, NS - 1, rev)
                    src = lambda c: s5_view(X, c, 255, 256, NS - 1, rev)
                    yield from cmac(tgt, src, PWm, m, False)

    def chk(tag, l):
        if stop == "%s%d" % (tag, l):
            raise _Stop()

    def chk2(tag):
        if stop == tag:
            raise _Stop()

    def even_layer(l, xsrc, xdst):
        e = l // 2
        load_weight_bf16(wbf, w_in_e.ap[e], EIN)
        chk2("W")
        adaln(l)
        chk2("ADA")
        dma(w2cf[:], w2cat.ap[e], [], [w2cf])
        cp("dve", w2c[:], w2cf[:], [w2cf], [w2c])
        dma(onc[:], onorm.ap[e], [], [onc])
        mset("dve", lrT[:], 0.0, [lrT])
        mset("dve", lrT[32:33, :], 1.0, [lrT])
        def front_e(ti):
            norm_mod_T(l, ti, xsrc)
            in_proj(EIN, pbf2[ti % 2])

        front_e(0)
        for ti in range(NT):
            t0 = ti * 128
            if ti + 1 < NT:
                front_e(ti + 1)
            pbf_t = pbf2[ti % 2]
            transpose_store(pbf_t[:, 0:512], 4, qkT_s, 0, ti, [pbf_t])
            chk2("TS")
            dma(kv_s.ap[t0:t0 + 128, :], pbf_t[:, 256:1024], [pbf_t], [kv_s.bs[ti]])
            chk2("KV")
            tr(pT[:32, 0:128], pbf_t[:, 1024:1056], ident_b, [pbf_t, csb], [pT])
            cp("act", lrT[0:32, :], pT[:32, 0:128], [pT], [lrT])
            mm(pC[:, :], lrT[:, :], w2c[:, :], True, True, [lrT, w2c], [pC])
            act(w3[:, :512], pC[:, :], AF.Exp, [pC], [w3], scale=-1.0)
            act(w3[:, 512:1024], w3[:, :512], AF.Ln, [w3], [w3], bias=cs[:, C_ONE:C_ONE + 1])
            dma(sp_s.ap[t0:t0 + 128, :], w3[:, 512:1024], [w3], [sp_s.bs[ti]])
            chk2("GATE")
            transpose_store(pbf_t[:, 1056:1568], 4, zgT_s, 0, ti, [pbf_t])
            transpose_store(pbf_t[:, 1568:2080], 4, uT_s, 0, ti, [pbf_t])
            transpose_store(pbf_t[:, 2080:2592], 4, z5T_s, 0, ti, [pbf_t])
            chk2("T%d" % ti)
        chk('P', l)
        S.barrier()
        def gla_gen():
            for d in range(2):
                TI, TR_, MK = (C_TIF, C_TRF, C_MF) if d == 0 else (C_TIB, C_TRB, C_MB)
                mset("dve", Sst[:], 0.0, [Sst])
                yield
                for h in range(4):
                    pr, hh = h // 2, h % 2
                    dma(Sst[hh * 64:(hh + 1) * 64, pr, hh * 128:(hh + 1) * 128], gla0.ap[e, d, h], [], [Sst])
                    yield
                blkm = cs[:, C_BLK:C_BLK + 256].unsqueeze(1).to_broadcast([128, 2, 256])
                tt("pool", Sbf[:], Sst[:], blkm, ALU.mult, [Sst, cs], [Sbf])
                yield
                order = list(range(NT)) if d == 0 else list(range(NT - 1, -1, -1))
                for oi, ti in enumerate(order):
                    t0 = ti * 128
                    if oi > 0 and oi % 2 == 0:
                        ts("dve", Sst[:], Sst[:], mflag, ALU.mult, [Sst, mt], [Sst])
                        yield
                        tt("pool", Sbf[:], Sst[:], blkm, ALU.mult, [Sst, cs], [Sbf])
                        yield
                    dma(qk_t[:], qkT_s.ap[:, t0:t0 + 128].rearrange("(a p) t -> p a t", p=128), [qkT_s.bs[ti]], [qk_t])
                    yield
                    dma(kv_t[:], kv_s.ap[t0:t0 + 128, :], [kv_s.bs[ti]], [kv_t])
                    yield
                    dma(sp_t[:], sp_s.ap[t0:t0 + 128, d * 256:(d + 1) * 256], [sp_s.bs[ti]], [sp_t])
                    yield
                    yield
                    for pr in range(2):
                        mm(pC[:, pr * 128:(pr + 1) * 128], sp_t[:, pr * 128:(pr + 1) * 128], cs[:, TI:TI + 128], True, True,
                           [sp_t, cs], [pC])
                        yield
                    mm(pD[:, 0:256], cs[:, TR_:TR_ + 128], sp_t[:, :], True, True, [sp_t, cs], [pD])
                    yield
                    yield
                    act(E1[:].rearrange("p a t -> p (a t)"), pC[:, 0:256], AF.Exp, [pC], [E1])
                    yield
                    act(E2[:].rearrange("p a t -> p (a t)"), pC[:, 0:256], AF.Exp, [pC], [E2], scale=-1.0)
                    yield
                    act(E3[:], pD[:, 0:256], AF.Exp, [pD], [E3])
                    yield
                    stt(qbT[:], qk_t[:, 0:2, :], 0.125, E1[:], ALU.mult, ALU.mult, [qk_t, E1], [qbT])
                    yield
                    tt("pool", kbT[:], qk_t[:, 2:4, :], E2[:], ALU.mult, [qk_t, E2], [kbT])
                    yield
                    tt("pool", kd[:], kv_t[:, 0:256], E3[:], ALU.mult, [kv_t, E3], [kd])
                    yield
                    yield
                    for h in range(4):
                        pr, hh = h // 2, h % 2
                        pX = (pE, pA)[hh]
                        mm(pX[:, pr * 128:(pr + 1) * 128], kbT[hh * 64:(hh + 1) * 64, pr, :], qbT[hh * 64:(hh + 1) * 64, pr, :],
                           True, True, [kbT, qbT], [pX])
                        yield
                    yield
                    for hh in range(2):
                        pX = (pE, pA)[hh]
                        av = attm[:].rearrange("p (pr hh) t -> p hh pr t", hh=2)[:, hh]
                        tt("dve", av, pX[:, 0:256].rearrange("p (h t) -> p h t", h=2),
                           cs[:, MK:MK + 128].unsqueeze(1).to_broadcast([128, 2, 128]), ALU.mult, [pX, cs], [attm])
                        yield
                    yield
                    chunks = (0, 1) if d == 0 else (1, 0)
                    for ci, ch in enumerate(chunks):
                        c0 = ch * 64
                        dcolx = (c0 + 63) if d == 0 else c0
                        pUp = (pD, pA)[ch]
                        for pr in range(2):
                            mm(pUp[:, 256:512], kd[c0:c0 + 64, pr * 128:(pr + 1) * 128],
                               kv_t[c0:c0 + 64, 256 + pr * 256:256 + (pr + 1) * 256], True, True, [kd, kv_t], [pUp])
                            yield
                            stt(Sst[:, pr, :], Sst[:, pr, :], E1[:, pr, dcolx:dcolx + 1], pUp[:, 256:512], ALU.mult, ALU.add,
                                [Sst, E1, pUp], [Sst])
                            yield
                        if ci == 0:
                            tt("pool", Sbf1[:], Sst[:], blkm, ALU.mult, [Sst, cs], [Sbf1])
                            yield
                    for h in range(4):
                        pr, hh = h // 2, h % 2
                        mm(pF[:, h * 128:(h + 1) * 128], kv_t[:, 256 + h * 128:256 + (h + 1) * 128], attm[:, h, :],
                           True, False, [kv_t, attm], [pF])
                        yield
                        for ci, ch in enumerate(chunks):
                            c0 = ch * 64
                            Sx = (Sbf, Sbf1)[ci]
                            mm(pF[:, h * 128 + c0:h * 128 + c0 + 64], Sx[:, pr, hh * 128:(hh + 1) * 128],
                               qbT[:, pr, c0:c0 + 64], False, (ci == 1), [Sx, qbT], [pF])
                            yield
                    tt("pool", Sbf[:], Sst[:], blkm, ALU.mult, [Sst, cs, pF], [Sbf])
                    yield
                    yield
                    if oi % 2 == 1:
                        seg = ti // 2
                        for h in range(4):
                            pr, hh = h // 2, h % 2
                            dma(ngla.ap[e, d, seg, h], Sst[hh * 64:(hh + 1) * 64, pr, hh * 128:(hh + 1) * 128], [Sst], [ngla])
                            yield
                    if d == 0:
                        cp("act", o_t[:].rearrange("p h t -> p (h t)"), pF[:, :], [pF], [o_t])
                        yield
                        dma(oF_s.ap[:, t0:t0 + 128].rearrange("(a p) t -> p a t", p=128), o_t[:], [o_t], [oF_s.bs[ti]])
                        yield
                    else:
                        dma(oF_t[:], oF_s.ap[:, t0:t0 + 128].rearrange("(a p) t -> p a t", p=128), [oF_s.bs[ti]], [oF_t])
                        yield
                        dma(zg_t[:], zgT_s.ap[:, t0:t0 + 128].rearrange("(a p) t -> p a t", p=128), [zgT_s.bs[ti]], [zg_t])
                        yield
                        tt("dve", o_t[:].rearrange("p h t -> p (h t)"), pF[:, :], oF_t[:].rearrange("p h t -> p (h t)"),
                           ALU.add, [pF, oF_t], [o_t])
                        yield
                        of = o_t[:].rearrange("p h t -> p (h t)")
                        act(osq[:], of, AF.Square, [o_t], [osq])
                        yield
                        mm(pC[:, :], csb[:, C_ONE:C_ONE + 128], osq[:], True, True, [osq, csb], [pC])
                        yield
                        act(w1[:, :512], pC[:, :], AF.Ln, [pC], [w1], scale=1.0 / 128, bias=cs[:, C_EPS:C_EPS + 1])
                        yield
                        act(w1[:, :512], w1[:, :512], AF.Exp, [w1], [w1], scale=-0.5)
                        yield
                        tt("dve", w1[:, :512], w1[:, :512], of, ALU.mult, [w1, o_t], [w1])
                        yield
                        act(w1[:, 512:1024], zg_t[:].rearrange("p h t -> p (h t)"), AF.Silu, [zg_t], [w1])
                        yield
                        stt(trs[:, 0:4, :].rearrange("p a t -> p (a t)"), w1[:, :512], onc[:, 0:1], w1[:, 512:1024],
                            ALU.mult, ALU.mult, [w1, onc], [trs])
                        yield
                        dma(yT_s.ap[0:512, t0:t0 + 128].rearrange("(a p) t -> p a t", p=128), trs[:, 0:4, :], [trs], [yT_s.bs[ti]])
                        yield
            yield

        def s5_gen():
            s5_setup(e)
            for j in range(16):
                ft = j // 4
                if j % 4 == 0:
                    dma(u_ft[:], uT_s.ap[ft * 128:(ft + 1) * 128, :], uT_s.bs, [u_ft])
                fences = []
                for d in range(2):
                    X = Xs[d]
                    rev = (d == 1)
                    for blk in range(NB):
                        for c in range(2):
                            pp = (pB, pTf)[c]
                            mm(pp[:, :], BT[:, d, c, j, :], u_ft[:, blk * 512:(blk + 1) * 512], True, True, [BT, u_ft], [pp])
                            cp("act", X[:, c, blk * 512:(blk + 1) * 512], pp[:, :], [pp], [X, Xblk[d][blk]])
                            yield
                    v0 = lambda c: s5_view(X, c, 0, 1, 1, rev)
                    pr_ = PW[:, d, pidx[1], 0, j:j + 1]; pi_ = PW[:, d, pidx[1], 1, j:j + 1]; pn_ = PW[:, d, pidx[1], 2, j:j + 1]
                    stt(v0(0), h0t[:, d, 0, j:j + 1], pr_, v0(0), ALU.mult, ALU.add, [h0t, PW] + Xblk[d], Xblk[d])
                    stt(v0(0), h0t[:, d, 1, j:j + 1], pn_, v0(0), ALU.mult, ALU.add, [h0t, PW] + Xblk[d], Xblk[d])
                    stt(v0(1), h0t[:, d, 1, j:j + 1], pr_, v0(1), ALU.mult, ALU.add, [h0t, PW] + Xblk[d], Xblk[d])
                    fences.append(stt(v0(1), h0t[:, d, 0, j:j + 1], pi_, v0(1), ALU.mult, ALU.add, [h0t, PW] + Xblk[d], Xblk[d]))
                gens = [s5_scan(Xs[d], d, j, d == 1, fences[d]) for d in range(2)]
                alive = [True, True]
                while any(alive):
                    for d in range(2):
                        if alive[d]:
                            try:
                                next(gens[d])
                                yield
                            except StopIteration:
                                alive[d] = False
                for d in range(2):
                    X = Xs[d]
                    rev = (d == 1)
                    for c in range(2):
                        col0 = ((d * 2 + c) * NS) * 16 + j
                        dst = bass.AP(STG[:, 0:1].tensor, STG[:, col0:col0 + 1].offset, [list(STG[:, 0:1].ap[0]), [16, NS]])
                        cp("pool", dst, s5_view(X, c, 255, 256, NS, rev), Xblk[d], [STG])
                for blk in range(NB):
                    k = 0
                    for d in range(2):
                        for c in range(2):
                            mm(pT32[0:32, :], CTm[:, c, j, :], Xs[d][:, c, blk * 512:(blk + 1) * 512], k == 0, k == 3,
                               [CTm, Xblk[d][blk]], [pT32])
                            k += 1
                    cp("act", y5st[:, :], pT32[0:32, :], [pT32], [y5st])
                    dma(y5T_s.ap[j * 32:(j + 1) * 32, blk * 512:(blk + 1) * 512], y5st[:, :], [y5st], [y5T_s.bs[j]])
                    yield
            gsz = min(8, NS)
            for d in range(2):
                for c in range(2):
                    for s0 in range(0, NS, gsz):
                        col0 = ((d * 2 + c) * NS + s0) * 16
                        ncol = gsz * 16
                        tr(pTf[:ncol, 0:128], STG[:, col0:col0 + ncol], ident_f, [STG, cs], [pTf])
                        cp("act", stgT[:ncol, :], pTf[:ncol, 0:128], [pTf], [stgT])
                        dma(ns5.ap[e, d, c, s0 * 16:s0 * 16 + ncol, :], stgT[:ncol, :], [stgT], [ns5])
            yield

        gG, gS = gla_gen(), s5_gen()
        aG = aS = True
        RATIO = float(os.environ.get("KRATIO", "2.0"))
        acc = 0.0
        while aG or aS:
            if aG:
                try:
                    next(gG)
                except StopIteration:
                    aG = False
            acc += RATIO if aG else 1000.0
            while acc >= 1.0 and aS:
                acc -= 1.0
                try:
                    next(gS)
                except StopIteration:
                    aS = False
            if not aS:
                acc = 0.0

        chk('S', l)
        S.barrier()
        s5_post_setup(e)
        for blk in range(NB):
            c0 = blk * 512
            tis = list(range(blk * 4, blk * 4 + 4))
            dma(y5b[:], y5T_s.ap[:, c0:c0 + 512].rearrange("(a p) t -> p a t", p=128), y5T_s.bs, [y5b])
            dma(ub[:], uT_s.ap[:, c0:c0 + 512].rearrange("(a p) t -> p a t", p=128), [uT_s.bs[i] for i in tis], [ub])
            dma(z5b[:], z5T_s.ap[:, c0:c0 + 512].rearrange("(a p) t -> p a t", p=128), [z5T_s.bs[i] for i in tis], [z5b])
            for a in range(4):
                stt(y5b[:, a, :], ub[:, a, :], dcol[:, a:a + 1], y5b[:, a, :], ALU.mult, ALU.add, [ub, dcol, y5b], [y5b])
            act(zz[:].rearrange("p a t -> p (a t)"), y5b[:].rearrange("p a t -> p (a t)"), AF.Gelu_apprx_tanh, [y5b], [zz])
            cp("dve", zzb[:].rearrange("p a t -> p (a t)"), zz[:].rearrange("p a t -> p (a t)"), [zz], [zzb])
            for fo in range(4):
                pp = (pA, pB)[fo % 2]
                for kt in range(4):
                    mm(pp[:, :], wgb[:, kt, fo * 128:(fo + 1) * 128], zzb[:, kt, :], kt == 0, kt == 3, [wgb, zzb], [pp])
                act(sg[:, fo, :], pp[:, :], AF.Sigmoid, [pp, bgl], [sg], bias=bgl[:, fo:fo + 1])
            tt("dve", sg[:].rearrange("p a t -> p (a t)"), sg[:].rearrange("p a t -> p (a t)"),
               zz[:].rearrange("p a t -> p (a t)"), ALU.mult, [sg, zz], [sg])
            act(zz[:].rearrange("p a t -> p (a t)"), z5b[:].rearrange("p a t -> p (a t)"), AF.Silu, [z5b], [zz])
            tt("dve", y5o[:].rearrange("p a t -> p (a t)"), sg[:].rearrange("p a t -> p (a t)"),
               zz[:].rearrange("p a t -> p (a t)"), ALU.mult, [sg, zz], [y5o])
            dma(yT_s.ap[512:1024, c0:c0 + 512].rearrange("(a p) t -> p a t", p=128), y5o[:], [y5o],
                [yT_s.bs[i] for i in tis])
        chk('SP', l)
        out_proj_residual(l, xsrc, xdst, yT_s, w_out_e.ap[e])
        S.barrier()
        chk('O', l)

    qkwb = sb("qkwb", [128, 128])
    sinkb = sb("sinkb", [128, 8])
    rp_t = sb("rp_t", [128, 128], grp="P")
    qn = sb("qn", [128, 640], grp="P")
    qr = sb("qr", [128, 640], grp="P")
    kdup = sb("kdup", [128, 2, 2, 64], BF16, grp="P")
    ckT = sb("ckT", [128, 2, 256], BF16)
    cvt = sb("cvt", [128, 2, 128], BF16)
    ckf = sb("ckf", [128, 2, 128], grp="P")
    qT_t2 = [sb("qT_t%d" % i, [128, 4, 128], BF16, grp="A") for i in range(2)]
    kT_w2 = [sb("kT_w%d" % i, [128, 2, 384], BF16, grp="A") for i in range(2)]
    v_w2 = [sb("v_w%d" % i, [128, 3, 128], BF16, grp="A") for i in range(2)]
    scs2 = [sb("scs%d" % i, [128, 640], grp="A") for i in range(2)]
    pexp2 = [sb("pexp%d" % i, [128, 640], BF16, grp="A") for i in range(2)]
    pTt2 = [sb("pTt%d" % i, [128, 5, 128], BF16, grp="A") for i in range(2)]
    sm2 = [sb("sm%d" % i, [128, 8, 8], grp="A") for i in range(2)]
    bandT = sb("bandT", [128, 384], grp="A")
    oat = sb("oat", [128, 512], grp="A")
    sz_t2 = [sb("sz_t%d" % i, [128, 512], grp="A") for i in range(2)]
    cvw = sb("cvw", [128, 16])
    u1h2 = [sb("u1h%d" % i, [128, 4, 130], grp="A") for i in range(2)]
    u2t2 = [sb("u2t%d" % i, [128, 4, 128], grp="A") for i in range(2)]
    cacc = sb("cacc", [128, 4, 128], grp="A")

    def odd_layer(l, xsrc, xdst):
        e = l // 2
        load_weight_bf16(wbf, w_in_o.ap[e], OIN)
        adaln(l)
        dma(qkwb[:], qkw.ap[e:e + 1, :].partition_broadcast(128), [], [qkwb])
        ts("dve", qkwb[:, 0:64], qkwb[:, 0:64], 0.125, ALU.mult, [qkwb], [qkwb])
        dma(sinkb[:], sinkv.ap[e:e + 1, :].partition_broadcast(128), [], [sinkb])
        dma(cvw[:], convw.ap[e], [], [cvw])
        for hf in range(2):
            dma(ckf[:, 0, :], ckv.ap[e, 0, hf * 128:(hf + 1) * 128].rearrange("t k d -> t (k d)"), [], [ckf])
            dma(ckf[:, 1, :], ckv.ap[e, 1, hf * 128:(hf + 1) * 128].rearrange("t k d -> t (k d)"), [], [ckf])
            cp("dve", cvt[:, hf, :], ckf[:, 1, :], [ckf], [cvt])
            for kv in range(2):
                for c in range(2):
                    cp("dve", kdup[:, kv, c, :], ckf[:, 0, kv * 64:(kv + 1) * 64], [ckf], [kdup])
            for kv in range(2):
                tr(pT[:, kv * 128:(kv + 1) * 128], kdup[:, kv].rearrange("p c d -> p (c d)"), ident_b, [kdup, csb], [pT])
            for kv in range(2):
                cp("act", ckT[:, kv, hf * 128:(hf + 1) * 128], pT[:, kv * 128:(kv + 1) * 128], [pT], [ckT])
        def front_o(ti):
            norm_mod_T(l, ti, xsrc)
            in_proj(OIN, proj2[ti % 2])

        front_o(0)
        for ti in range(NT):
            t0 = ti * 128
            if ti + 1 < NT:
                front_o(ti + 1)
            proj_t = proj2[ti % 2]
            pbf_t = pbf2[ti % 2]
            dma(rp_t[:], rope.ap[t0:t0 + 128, :], [], [rp_t])
            act(w1[:, :640], proj_t[:, 0:640], AF.Square, [proj_t], [w1])
            S.op("dve", lambda e_: e_.tensor_reduce(out=small[:, 0:10], in_=w1[:, :640].rearrange("p (h d) -> p h d", d=64),
                                                    axis=AX.X, op=ALU.add), bl([w1]), bl([small]))
            act(small[:, 0:10], small[:, 0:10], AF.Ln, [small], [small], scale=1.0 / 64, bias=cs[:, C_EPS:C_EPS + 1])
            act(small[:, 0:10], small[:, 0:10], AF.Exp, [small], [small], scale=-0.5)
            tt("dve", qn[:].rearrange("p (h d) -> p h d", d=64), proj_t[:, 0:640].rearrange("p (h d) -> p h d", d=64),
               small[:, 0:10].unsqueeze(2).to_broadcast([128, 10, 64]), ALU.mult, [proj_t, small], [qn])
            tt("dve", qn[:, 0:512].rearrange("p (h d) -> p h d", d=64), qn[:, 0:512].rearrange("p (h d) -> p h d", d=64),
               qkwb[:, 0:64].unsqueeze(1).to_broadcast([128, 8, 64]), ALU.mult, [qn, qkwb], [qn])
            tt("dve", qn[:, 512:640].rearrange("p (h d) -> p h d", d=64), qn[:, 512:640].rearrange("p (h d) -> p h d", d=64),
               qkwb[:, 64:128].unsqueeze(1).to_broadcast([128, 2, 64]), ALU.mult, [qn, qkwb], [qn])
            dma(nkv.ap[e, 0, t0:t0 + 128, :], qn[:, 512:640], [qn], [nkv])
            dma(nkv.ap[e, 1, t0:t0 + 128, :], proj_t[:, 640:768], [proj_t], [nkv])
            v5 = lambda tl: tl[:, 0:640].rearrange("p (h a b f) -> p h a b f", a=2, b=2, f=16)
            cosb = rp_t[:, 0:64].rearrange("p (a b f) -> p a b f", a=2, b=2).unsqueeze(1).to_broadcast([128, 10, 2, 2, 16])
            tt("dve", v5(qr), v5(qn), cosb, ALU.mult, [qn, rp_t], [qr])
            for b_ in range(2):
                sinb = rp_t[:, 64:128].rearrange("p (a b f) -> p a b f", a=2, b=2)[:, :, b_, :].unsqueeze(1).to_broadcast([128, 10, 2, 16])
                tt("dve", v5(w1)[:, :, :, b_, :], v5(qn)[:, :, :, 1 - b_, :], sinb, ALU.mult, [qn, rp_t], [w1])
            tt("dve", pbf_t[:, 0:640], qr[:, 0:640], w1[:, 0:640], ALU.add, [qr, w1], [pbf_t])
            transpose_store(pbf_t[:, 0:512], 4, qT_s, 0, ti, [pbf_t])
            for kv in range(2):
                for c in range(2):
                    cp("act", kdup[:, kv, c, :], pbf_t[:, 512 + kv * 64:512 + (kv + 1) * 64], [pbf_t], [kdup])
            transpose_store(kdup[:].rearrange("p k c d -> p (k c d)"), 2, kT_s, 0, ti, [kdup])
            cp("pool", pbf_t[:, 640:768], proj_t[:, 640:768], [proj_t], [pbf_t])
            dma(v_s.ap[t0:t0 + 128, :], pbf_t[:, 640:768], [pbf_t], [v_s.bs[ti]])
            act(w2[:, :512], proj_t[:, 768:1280], AF.Silu, [proj_t], [w2])
            dma(sz_s.ap[t0:t0 + 128, :], w2[:, :512], [w2], [sz_s.bs[ti]])
            tt("dve", w3[:, 0:512], proj_t[:, 2304:2816], proj_t[:, 1280:1792], ALU.mult, [proj_t], [w3])
            act(w3[:, 512:1024], proj_t[:, 2816:3328], AF.Silu, [proj_t], [w3])
            tt("dve", w3[:, 512:1024], w3[:, 512:1024], proj_t[:, 1792:2304], ALU.mult, [w3, proj_t], [w3])
            transpose_store_f32(w3[:, 0:512], 4, u1T_s, 0, ti, [w3])
            transpose_store_f32(w3[:, 512:1024], 4, u2T_s, 0, ti, [w3])
        chk('P', l)
        S.barrier()
        def loadsA(ti):
            if ti >= NT:
                return
            t0 = ti * 128
            tp = max(ti - 1, 0)
            tn = min(ti + 1, NT - 1)
            qT_t, kT_w, v_w, sz_t, u1h, u2t = (x[ti % 2] for x in (qT_t2, kT_w2, v_w2, sz_t2, u1h2, u2t2))
            dma(qT_t[:], qT_s.ap[:, t0:t0 + 128].rearrange("(a p) t -> p a t", p=128), [qT_s.bs[ti]], [qT_t])
            for wi, tw in enumerate((tp, ti, tn)):
                dma(kT_w[:, :, wi * 128:(wi + 1) * 128], kT_s.ap[:, tw * 128:(tw + 1) * 128].rearrange("(k p) t -> p k t", p=128),
                    [kT_s.bs[tw]], [kT_w])
                dma(v_w[:, wi, :], v_s.ap[tw * 128:(tw + 1) * 128, :], [v_s.bs[tw]], [v_w])
            dma(sz_t[:], sz_s.ap[t0:t0 + 128, :], [sz_s.bs[ti]], [sz_t])
            dma(u1h[:, :, 1:129], u1T_s.ap[:, t0:t0 + 128].rearrange("(a p) t -> p a t", p=128), [u1T_s.bs[ti]], [u1h])
            lo = t0 - 1 if ti > 0 else 0
            hi = t0 + 128 if ti < NT - 1 else T - 1
            dma(u1h[:, :, 0:1], u1T_s.ap[:, lo:lo + 1].rearrange("(a p) t -> p a t", p=128), [u1T_s.bs[tp]], [u1h], slow=True)
            dma(u1h[:, :, 129:130], u1T_s.ap[:, hi:hi + 1].rearrange("(a p) t -> p a t", p=128), [u1T_s.bs[tn]], [u1h], slow=True)
            dma(u2t[:], u2T_s.ap[:, t0:t0 + 128].rearrange("(a p) t -> p a t", p=128), [u2T_s.bs[ti]], [u2t])

        loadsA(0)
        for ti in range(NT):
            t0 = ti * 128
            loadsA(ti + 1)
            qT_t, kT_w, v_w, sz_t, u1h, u2t = (x[ti % 2] for x in (qT_t2, kT_w2, v_w2, sz_t2, u1h2, u2t2))
            flp = mt[:, 3 + ti:4 + ti]
            fln = mt[:, 3 + NT + ti:4 + NT + ti]
            ts("dve", bandT[:, 0:128], band[:, 0:128], flp, ALU.add, [band, mt], [bandT])
            cp("dve", bandT[:, 128:256], band[:, 128:256], [band], [bandT])
            ts("dve", bandT[:, 256:384], band[:, 256:384], fln, ALU.add, [band, mt], [bandT])

            def head_gen(hq, P):
                kv, pr, hh = hq // 4, hq // 2, hq % 2
                rows = slice(hh * 64, (hh + 1) * 64)
                pS1, pS2 = ((pC, pD), (pA, pB))[P]
                scsX, pexpX, pTtX, smX = scs2[P], pexp2[P], pTt2[P], sm2[P]
                pTX = (pT, pTf16)[P]
                pO = (pE, pF)[P]
                oc = (hq // 2) * 64
                mm(pS1[:, 0:384], qT_t[rows, pr, :], kT_w[rows, kv, :], True, True, [qT_t, kT_w], [pS1])
                mm(pS2[:, 0:256], qT_t[rows, pr, :], ckT[rows, kv, :], True, True, [qT_t, ckT], [pS2])
                yield
                tt("dve", scsX[:, 0:384], pS1[:, 0:384], bandT[:, :], ALU.add, [pS1, bandT], [scsX])
                yield
                ts("dve", scsX[:, 384:640], pS2[:, 0:256], mt[:, 2:3], ALU.add, [pS2, mt], [scsX])
                yield
                S.op("dve", lambda e_: e_.reduce_max(out=smX[:, hq, 0:1], in_=scsX[:, :], axis=AX.X), bl([scsX]), bl([smX]))
                yield
                tt("dve", smX[:, hq, 0:1], smX[:, hq, 0:1], sinkb[:, hq:hq + 1], ALU.max, [smX, sinkb], [smX])
                ts("dve", smX[:, hq, 1:2], smX[:, hq, 0:1], -1.0, ALU.mult, [smX], [smX])
                mset("dve", smX[:, hq, 2:3], 0.0, [smX])
                yield
                act(pexpX[:], scsX[:], AF.Exp, [scsX, smX], [pexpX, smX], bias=smX[:, hq, 1:2], accum_out=smX[:, hq, 2:3])
                act(smX[:, hq, 3:4], sinkb[:, hq:hq + 1], AF.Exp, [smX, sinkb], [smX], bias=smX[:, hq, 1:2])
                yield
                for k5 in range(5):
                    tr(pTX[:, k5 * 128:(k5 + 1) * 128], pexpX[:, k5 * 128:(k5 + 1) * 128], ident_b, [pexpX, csb], [pTX])
                yield
                cp("act", pTtX[:].rearrange("p a t -> p (a t)"), pTX[:, 0:640], [pTX], [pTtX])
                tt("dve", smX[:, hq, 4:5], smX[:, hq, 2:3], smX[:, hq, 3:4], ALU.add, [smX], [smX])
                S.op("dve", lambda e_: e_.reciprocal(out=smX[:, hq, 5:6], in_=smX[:, hq, 4:5]), bl([smX]), bl([smX]))
                yield
                for k5 in range(5):
                    vv = v_w[:, k5, kv * 64:(kv + 1) * 64] if k5 < 3 else cvt[:, k5 - 3, kv * 64:(kv + 1) * 64]
                    mm(pO[:, oc:oc + 64], pTtX[:, k5, :], vv, k5 == 0, k5 == 4, [pTtX, v_w, cvt], [pO])
                yield

            for h2 in range(0, 8, 2):
                gens = [head_gen(h2, 0), head_gen(h2 + 1, 1)]
                alive = [True, True]
                while any(alive):
                    for P in range(2):
                        if alive[P]:
                            try:
                                next(gens[P])
                            except StopIteration:
                                alive[P] = False
            for hq in range(8):
                pO = (pE, pF)[hq % 2]
                oc = (hq // 2) * 64
                stt(oat[:, hq * 64:(hq + 1) * 64], pO[:, oc:oc + 64], sm2[hq % 2][:, hq, 5:6], sz_t[:, hq * 64:(hq + 1) * 64],
                    ALU.mult, ALU.mult, [pO, sm2[hq % 2], sz_t], [oat])
            cp("act", pbfA[:, 0:512], oat[:], [oat], [pbfA])
            transpose_store(pbfA[:, 0:512], 4, yT_s, 0, ti, [pbfA])
            ts("dve", u1h[:, :, 0:1], u1h[:, :, 0:1], mt[:, 3 + 2 * NT + ti:4 + 2 * NT + ti], ALU.mult, [u1h, mt], [u1h])
            ts("dve", u1h[:, :, 129:130], u1h[:, :, 129:130], mt[:, 3 + 3 * NT + ti:4 + 3 * NT + ti], ALU.mult, [u1h, mt], [u1h])
            for a in range(4):
                ts("dve", cacc[:, a, :], u1h[:, a, 1:129], cvw[:, a * 4 + 1:a * 4 + 2], ALU.mult, [u1h, cvw], [cacc],
                   cvw[:, a * 4 + 3:a * 4 + 4], ALU.add)
                stt(cacc[:, a, :], u1h[:, a, 0:128], cvw[:, a * 4:a * 4 + 1], cacc[:, a, :], ALU.mult, ALU.add, [u1h, cvw, cacc], [cacc])
                stt(cacc[:, a, :], u1h[:, a, 2:130], cvw[:, a * 4 + 2:a * 4 + 3], cacc[:, a, :], ALU.mult, ALU.add, [u1h, cvw, cacc], [cacc])
            tt("dve", trs[:, 4:8, :], cacc[:], u2t[:], ALU.mult, [cacc, u2t], [trs])
            dma(yT_s.ap[512:1024, t0:t0 + 128].rearrange("(a p) t -> p a t", p=128), trs[:, 4:8, :], [trs], [yT_s.bs[ti]])
        chk('A', l)
        out_proj_residual(l, xsrc, xdst, yT_s, w_out_o.ap[e])
        S.barrier()
        chk('O', l)

    chain = [x_in, xs[0], xs[1], xs[0], y_out]
    try:
        for l in range(4):
            if l % 2 == 0:
                even_layer(l, chain[l], chain[l + 1])
            else:
                odd_layer(l, chain[l], chain[l + 1])
    except _Stop:
        S.barrier()
    S.finish_waits([y_out.b, ngla.b, ns5.b, nkv.b] + y_out.bs)
    S.emit()
    st.close()
    return nc, S


def _consts():
    c = np.zeros((128, NCST), np.float32)
    i = np.arange(128)
    s = i[:, None]
    t = i[None, :]
    same = (s // 64) == (t // 64)
    c[:, C_ID:C_ID + 128] = np.eye(128, dtype=np.float32)
    c[:, C_TIF:C_TIF + 128] = np.where(same & (s <= t), -1.0 / 16, 0.0)
    c[:, C_TRF:C_TRF + 128] = np.where(same & (s > t), -1.0 / 16, 0.0)
    c[:, C_TIB:C_TIB + 128] = np.where(same & (s >= t), -1.0 / 16, 0.0)
    c[:, C_TRB:C_TRB + 128] = np.where(same & (s < t), -1.0 / 16, 0.0)
    c[:, C_MF:C_MF + 128] = np.where(same & (s <= t), 1.0, 0.0)
    c[:, C_MB:C_MB + 128] = np.where(same & (s >= t), 1.0, 0.0)
    c[:, C_ONE:C_ONE + 128] = 1.0
    c[:, C_EPS] = EPS
    c[:, C_EPS + 1] = 1.0
    qi = i[:, None]
    kj = i[None, :]
    NEG = -1e30
    c[:, C_BAND:C_BAND + 128] = np.where(kj >= qi, 0.0, NEG)
    c[:, C_BAND + 128:C_BAND + 256] = 0.0
    c[:, C_BAND + 256:C_BAND + 384] = np.where(kj <= qi, 0.0, NEG)
    c[0:64, C_BLK:C_BLK + 128] = 1.0
    c[64:128, C_BLK + 128:C_BLK + 256] = 1.0
    return c


def _rope_table(T, identity):
    tab = np.zeros((T, 128), np.float32)
    if identity:
        tab[:, :64] = 1.0
        return tab
    pos = np.arange(T)
    row = (pos // 64).astype(np.float32)
    col = (pos % 64).astype(np.float32)
    freq = (10000.0 ** (-np.arange(16, dtype=np.float32) / 16)).astype(np.float32)
    ar = row[:, None] * freq
    ac = col[:, None] * freq
    cos = np.concatenate([np.cos(ar), np.cos(ar), np.cos(ac), np.cos(ac)], axis=1)
    sin = np.concatenate([-np.sin(ar), np.sin(ar), -np.sin(ac), np.sin(ac)], axis=1)
    tab[:, :64] = cos
    tab[:, 64:] = sin
    return tab


def _meta(T, sample):
    NT = T // 128
    m = np.zeros((128, 3 + 4 * NT), np.float32)
    NEG = -1e30
    if sample:
        m[:, 0] = 1.0
        m[:, 1] = 1.0
        m[:, 2] = 0.0
        flp = np.zeros(NT); flp[0] = NEG
        fln = np.zeros(NT); fln[-1] = NEG
        cfl = np.ones(NT); cfl[0] = 0
        cfr = np.ones(NT); cfr[-1] = 0
    else:
        m[:, 0] = 0.0
        m[:, 1] = 0.0
        m[:, 2] = NEG
        flp = np.where(np.arange(NT) % 2 == 0, NEG, 0.0)
        fln = np.where(np.arange(NT) % 2 == 1, NEG, 0.0)
        cfl = np.where(np.arange(NT) % 2 == 0, 0.0, 1.0)
        cfr = np.where(np.arange(NT) % 2 == 1, 0.0, 1.0)
    m[:, 3:3 + NT] = flp
    m[:, 3 + NT:3 + 2 * NT] = fln
    m[:, 3 + 2 * NT:3 + 3 * NT] = cfl
    m[:, 3 + 3 * NT:3 + 4 * NT] = cfr
    return m


def _state_layout(a):
    sh = a.shape[:-2]
    b = a.reshape(sh + (16, 2, 64))
    b = np.moveaxis(b, -3, -1)
    return np.ascontiguousarray(b.reshape(sh + (128, 16)))


_NC_CACHE = {}
LAST_RESULTS = None


def run(inputs, T, n_prompt_per_core):
    f = lambda k: np.asarray(inputs[k], dtype=np.float32)
    x_prompt, x_sample, c = f("x_prompt"), f("x_sample"), f("c")
    NS = T // 256
    if T not in _NC_CACHE:
        _NC_CACHE[T] = build(T)[0]
    nc = _NC_CACHE[T]
    shared = {}
    shared["cst"] = _consts()
    shared["norm_w"] = f("norm_w")
    shared["w_ada"] = f("w_ada")
    shared["b_ada"] = f("b_ada")
    shared["w_in_e"] = f("w_in_e")
    shared["w_out_e"] = f("w_out_e")
    w2 = f("gla_w2"); b2 = f("gla_b2")
    w2cat = np.zeros((2, 64, 512), np.float32)
    w2cat[:, 0:16, 0:256] = w2[:, 0]
    w2cat[:, 16:32, 256:512] = w2[:, 1]
    w2cat[:, 32, 0:256] = b2[:, 0]
    w2cat[:, 32, 256:512] = b2[:, 1]
    shared["w2cat"] = w2cat
    shared["onorm"] = f("gla_onorm").reshape(2, 128, 1)
    lam_re, lam_im, log_dt = f("s5_lam_re"), f("s5_lam_im"), f("s5_log_dt")
    ldt = np.broadcast_to(log_dt[..., None], lam_re.shape)
    shared["s5p"] = np.stack([_state_layout(lam_re), _state_layout(lam_im), _state_layout(ldt)], axis=2)
    def expand_b(b):
        out = np.zeros((2, 128, 16, 128), np.float32)
        for g in range(32):
            j, gs = g // 2, g % 2
            k0 = 16 * (g % 8)
            out[:, gs * 64:(gs + 1) * 64, j, k0:k0 + 16] = b[:, g]
        return out.reshape(2, 128, 16 * 128)
    shared["s5b"] = np.stack([expand_b(f("s5_b_re")), expand_b(f("s5_b_im"))], axis=1)
    def expand_c(cc, sign):
        out = np.zeros((2, 128, 16, 32), np.float32)
        for g in range(32):
            j, gs = g // 2, g % 2
            out[:, gs * 64:(gs + 1) * 64, j, gs * 16:(gs + 1) * 16] = np.swapaxes(cc[:, g], 1, 2)
        if sign < 0:
            out = np.negative(out)
        return out.reshape(2, 128, 16 * 32)
    shared["s5c"] = np.stack([expand_c(f("s5_c_re"), 1), expand_c(f("s5_c_im"), -1)], axis=1)
    shared["s5d"] = np.ascontiguousarray(f("s5_d").reshape(2, 4, 128).transpose(0, 2, 1))
    shared["wglu"] = f("s5_w_glu")
    shared["bglu"] = np.ascontiguousarray(f("s5_b_glu").reshape(2, 4, 128).transpose(0, 2, 1))
    shared["w_in_o"] = f("w_in_o")
    shared["w_out_o"] = f("w_out_o")
    shared["qkw"] = np.concatenate([f("q_norm_w"), f("k_norm_w")], axis=1)
    shared["sink"] = f("sink")
    cw = f("conv_w"); cb = f("conv_b")
    cvw = np.zeros((2, 128, 4, 4), np.float32)
    for a in range(4):
        cvw[:, :, a, 0:3] = cw[:, :, a * 128:(a + 1) * 128].transpose(0, 2, 1)
        cvw[:, :, a, 3] = cb[:, a * 128:(a + 1) * 128]
    shared["convw"] = cvw.reshape(2, 128, 16)

    in_maps = []
    n_sample = x_sample.shape[0]
    for core in range(8):
        m = dict(shared)
        if core < 4:
            b = core
            m["x"] = np.ascontiguousarray(x_sample[b])
            cv = c[b]
            m["meta"] = _meta(T, True)
            m["rope"] = _rope_table(T, False)
            m["gla0"] = np.ascontiguousarray(f("state_gla")[b])
            sre = _state_layout(f("state_s5_re")[b]); sim = _state_layout(f("state_s5_im")[b])
            m["s5h0"] = np.stack([sre, sim], axis=2)
            ck = f("cache_k")[b]; cvv = f("cache_v")[b]
            m["ckv"] = np.ascontiguousarray(np.stack([ck.transpose(0, 2, 1, 3), cvv.transpose(0, 2, 1, 3)], axis=1))
        else:
            pc = core - 4
            xx = np.zeros((T, D), np.float32)
            seqs = x_prompt[pc * n_prompt_per_core:(pc + 1) * n_prompt_per_core]
            xx[:n_prompt_per_core * 256] = seqs.reshape(-1, D)
            m["x"] = xx
            cv = f("c_ctx")
            m["meta"] = _meta(T, False)
            m["rope"] = _rope_table(T, True)
            m["gla0"] = np.zeros((2, 2, 4, 64, 128), np.float32)
            m["s5h0"] = np.zeros((2, 2, 2, 128, 16), np.float32)
            m["ckv"] = np.zeros((2, 2, 256, 2, 64), np.float32)
        m["cvec"] = np.ascontiguousarray(cv.reshape(8, 128).T)
        in_maps.append(m)
    res = run_bass_kernel_spmd(nc, in_maps, core_ids=list(range(8)))
    R = res.results
    global LAST_RESULTS
    LAST_RESULTS = R
    BATCH = x_prompt.shape[0]
    y_sample = np.stack([np.asarray(R[b]["y"]) for b in range(4)], axis=0).astype(np.float32)
    y_prompt = np.zeros_like(x_prompt)
    new_gla = np.zeros((BATCH, 2, 2, 4, 64, 128), np.float32)
    new_re = np.zeros((BATCH, 2, 2, 32, 64), np.float32)
    new_im = np.zeros((BATCH, 2, 2, 32, 64), np.float32)
    new_k = np.zeros((BATCH, 2, 2, 256, 64), np.float32)
    new_v = np.zeros((BATCH, 2, 2, 256, 64), np.float32)
    for pc in range(4):
        r = R[4 + pc]
        y = np.asarray(r["y"]); g = np.asarray(r["ngla"]); s5 = np.asarray(r["ns5"]); kvo = np.asarray(r["nkv"])
        s5 = s5.reshape(2, 2, 2, NS, 16, 2, 64)
        for q in range(n_prompt_per_core):
            bi = pc * n_prompt_per_core + q
            y_prompt[bi] = y[q * 256:(q + 1) * 256]
            new_gla[bi] = g[:, :, q]
            for d in range(2):
                sig = q if d == 0 else NS - 1 - q
                new_re[bi, :, d] = s5[:, d, 0, sig].reshape(2, 32, 64)
                new_im[bi, :, d] = s5[:, d, 1, sig].reshape(2, 32, 64)
            kk = kvo[:, :, q * 256:(q + 1) * 256, :].reshape(2, 2, 256, 2, 64)
            new_k[bi] = kk[:, 0].transpose(0, 2, 1, 3)
            new_v[bi] = kk[:, 1].transpose(0, 2, 1, 3)
    return (y_prompt, y_sample, new_gla, new_re, new_im, new_k, new_v)


def kernel(**inputs):
    return run(inputs, 4096, 8)
```

```python
import math
import os
import numpy as np
import concourse.bass as bass
import concourse.mybir as mybir
from concourse.bass_utils import run_bass_kernel_spmd
from contextlib import ExitStack

F32 = mybir.dt.float32
BF16 = mybir.dt.bfloat16
I32 = mybir.dt.int32
ALU = mybir.AluOpType
AF = mybir.ActivationFunctionType
AX = mybir.AxisListType

D = 1024
EIN = 2592
OIN = 3328
EPS = 1e-6
ENGS = ("pe", "act", "dve", "pool", "sp")
SEM_LIMIT = 30000
N_DMA_SEMS = 12


class Buf:
    __slots__ = ("w", "r")

    def __init__(self):
        self.w = None
        self.r = {}


class Sched:
    def __init__(self, nc, same_engine_sync=True):
        self.nc = nc
        self.q = {e: [] for e in ENGS}
        self.epoch = {e: 0 for e in ENGS}
        self.cnt = {}
        self.seen = {e: {} for e in ENGS}
        self.same = same_engine_sync
        self.nosync = set(os.environ.get("KNOSYNC", "").split(","))
        self.semkeys = []
        for e in ENGS:
            self._newkey((e, 0))
        self.dma_pool = {e: [] for e in ENGS}
        self.dma_rr = {e: 0 for e in ENGS}
        self.n_ops = 0

    def _newkey(self, k):
        self.cnt[k] = 0
        self.semkeys.append(k)

    def _engkey(self, e):
        k = (e, self.epoch[e])
        if self.cnt[k] >= SEM_LIMIT:
            self.epoch[e] += 1
            k = (e, self.epoch[e])
            self._newkey(k)
        return k

    def _need(self, eng, waits, tok, is_dma=False):
        if tok is None:
            return
        k, v = tok
        if (not is_dma) and k[0] == eng and (eng == "pe" or eng in self.nosync):
            return
        if self.seen[eng].get(k, 0) >= v:
            return
        if waits.get(k, 0) < v:
            waits[k] = v

    def _deps(self, eng, reads, writes, is_dma):
        waits = {}
        for b in reads:
            self._need(eng, waits, b.w, is_dma)
        for b in writes:
            self._need(eng, waits, b.w, is_dma)
            for k, v in b.r.items():
                self._need(eng, waits, (k, v), is_dma)
        return waits

    def op(self, eng, fn, reads=(), writes=(), extra=None):
        if extra is None:
            waits = self._deps(eng, reads, writes, False)
        else:
            sv = self.nosync
            self.nosync = set(sv) | {eng}
            waits = self._deps(eng, reads, writes, False)
            self.nosync = sv
            for tok in extra:
                if tok is not None:
                    self._need(eng, waits, tok, True)
        for k, v in waits.items():
            self.seen[eng][k] = v
        key = self._engkey(eng)
        self.cnt[key] += 1
        tok = (key, self.cnt[key])
        for b in reads:
            b.r[key] = tok[1]
        for b in writes:
            b.w = tok
            b.r = {}
        self.q[eng].append((fn, list(waits.items()), key, 1))
        self.n_ops += 1
        return tok

    def dma(self, fn, reads=(), writes=(), eng="sp"):
        pool = self.dma_pool[eng]
        if len(pool) < N_DMA_SEMS:
            key = ("dma", eng, len(pool), 0)
            self._newkey(key)
            pool.append(key)
        else:
            idx = self.dma_rr[eng] % N_DMA_SEMS
            self.dma_rr[eng] += 1
            key = pool[idx]
            if self.cnt[key] >= SEM_LIMIT:
                key = ("dma", eng, idx, key[3] + 1)
                self._newkey(key)
                pool[idx] = key
        waits = self._deps(eng, reads, writes, True)
        if self.cnt[key] > 0:
            self._need(eng, waits, (key, self.cnt[key]), True)
        for k, v in waits.items():
            self.seen[eng][k] = v
        self.cnt[key] += 16
        tok = (key, self.cnt[key])
        for b in reads:
            b.r[key] = tok[1]
        for b in writes:
            b.w = tok
            b.r = {}
        self.q[eng].append((fn, list(waits.items()), key, 16))
        self.n_ops += 1

    def barrier(self):
        for eng in ENGS:
            waits = {}
            for k, v in self.cnt.items():
                if v > 0 and self.seen[eng].get(k, 0) < v:
                    waits[k] = v
                    self.seen[eng][k] = v
            self.q[eng].append((None, list(waits.items()), None, 0))

    def finish_waits(self, bufs, eng="sp"):
        waits = {}
        for b in bufs:
            self._need(eng, waits, b.w, True)
        self.q[eng].append((None, list(waits.items()), None, 0))

    def emit(self):
        nc = self.nc
        with ExitStack() as st:
            sems = {}
            for i, k in enumerate(self.semkeys):
                sems[k] = st.enter_context(nc.semaphore("s%d" % i))
            block = st.enter_context(nc.Block())

            def runner(ename):
                def run(e):
                    for fn, waits, key, inc in self.q[ename]:
                        for k, v in waits:
                            e.wait_ge(sems[k], v)
                        if fn is not None:
                            fn(e).then_inc(sems[key], inc)
                return run

            block.tensor(runner("pe"))
            block.scalar(runner("act"))
            block.vector(runner("dve"))
            block.gpsimd(runner("pool"))
            block.sync(runner("sp"))


class TL:
    def __init__(self, t):
        self.t = t
        self.b = Buf()

    def __getitem__(self, k):
        return self.t[k]


class View:
    def __init__(self, ap):
        self.ap = ap
        self.b = Buf()

    def __getitem__(self, k):
        return self.ap[k]


class DT:
    def __init__(self, ap, nb=1):
        self.ap = ap
        self.bs = [Buf() for _ in range(nb)]
        self.b = self.bs[0]


C_ID, C_TIF, C_TRF, C_TIB, C_TRB, C_MF, C_MB, C_ONE, C_BAND = [128 * i for i in range(9)]
C_BLK = C_BAND + 384
C_EPS = C_BLK + 256
NCST = C_EPS + 2


class _Stop(Exception):
    pass


def build(T, stop=None):
    import os
    stop = stop or os.environ.get("KSTOP")
    NT = T // 128
    NS = T // 256
    NB = T // 512
    nc = bass.Bass("TRN2", target_bir_lowering=False)
    S = Sched(nc, same_engine_sync=bool(int(os.environ.get("KSAME", "0"))))
    st = ExitStack()

    def din(name, shape, dt=F32):
        return DT(nc.dram_tensor(name, list(shape), dt, kind="ExternalInput").ap())

    def dout(name, shape, dt=F32):
        return DT(nc.dram_tensor(name, list(shape), dt, kind="ExternalOutput").ap())

    dbg = bool(os.environ.get("KDBG"))

    def dscr(name, shape, dt=F32, nb=1):
        return DT(nc.dram_tensor(name, list(shape), dt, kind="ExternalOutput" if dbg else "Internal").ap(), nb)

    ARENA_WORDS = 34000
    G_BASE = 29000
    arena_t = st.enter_context(nc.sbuf_tensor("arena", [128, ARENA_WORDS], F32))
    goff = {"G": G_BASE}

    def sb(name, shape, dt=F32, grp=None):
        if grp is None:
            return TL(st.enter_context(nc.sbuf_tensor(name, list(shape), dt)))
        n = 1
        for d_ in shape[1:]:
            n *= d_
        words = n if dt in (F32, I32) else (n + 1) // 2
        off = goff.get(grp, 0)
        goff[grp] = off + words
        assert goff[grp] <= ARENA_WORDS, (grp, name, goff[grp])
        assert grp != "S" or goff[grp] <= G_BASE, (grp, name, goff[grp])
        ap = arena_t[:, off:off + words]
        if dt != F32:
            ap = ap.bitcast(dt)[:, :n]
        if len(shape) > 2:
            names = " ".join("d%d" % i for i in range(len(shape) - 1))
            kw = {"d%d" % i: shape[i + 1] for i in range(len(shape) - 1)}
            ap = ap.rearrange("p (%s) -> p %s" % (names, names), **kw)
        if shape[0] < 128:
            ap = ap[0:shape[0]]
        return View(ap)

    def ps(name, shape, dt=F32):
        return TL(st.enter_context(nc.psum_tensor(name, list(shape), dt)))

    def bl(xs):
        return [x.b if hasattr(x, "b") else x for x in xs]

    def mm(out, lhsT, rhs, start, stop, R, W):
        S.op("pe", lambda e: e.matmul(out, lhsT=lhsT, rhs=rhs, start=start, stop=stop), bl(R), bl(W))

    def tr(out, in_, ident, R, W):
        S.op("pe", lambda e: e.transpose(out, in_, ident), bl(R), bl(W))

    def act(out, in_, func, R, W, **kw):
        S.op("act", lambda e: e.activation(out=out, in_=in_, func=func, **kw), bl(R), bl(W))

    def tt(eng, out, a, b, op, R, W):
        S.op(eng, lambda e: e.tensor_tensor(out=out, in0=a, in1=b, op=op), bl(R), bl(W))

    def ts(eng, out, a, s1, op0, R, W, s2=None, op1=None):
        if op1 is None:
            S.op(eng, lambda e: e.tensor_scalar(out=out, in0=a, scalar1=s1, scalar2=None, op0=op0), bl(R), bl(W))
        else:
            S.op(eng, lambda e: e.tensor_scalar(out=out, in0=a, scalar1=s1, scalar2=s2, op0=op0, op1=op1), bl(R), bl(W))

    def stt(out, in0, scalar, in1, op0, op1, R, W, extra=None):
        return S.op("dve", lambda e: e.scalar_tensor_tensor(out=out, in0=in0, scalar=scalar, in1=in1, op0=op0, op1=op1),
                    bl(R), bl(W), extra=extra)

    def cp(eng, out, in_, R, W):
        if eng == "act":
            S.op("act", lambda e: e.copy(out=out, in_=in_), bl(R), bl(W))
        else:
            S.op(eng, lambda e: e.tensor_copy(out=out, in_=in_), bl(R), bl(W))

    def mset(eng, ap, val, W):
        S.op(eng, lambda e: e.memset(ap, val), [], bl(W))

    def dma(out, in_, R, W, eng="sp", slow=False):
        if slow:
            S.dma(lambda e: e.dma_start(out=out, in_=in_, allow_slow_non_contiguous=True), bl(R), bl(W), eng=eng)
        else:
            S.dma(lambda e: e.dma_start(out=out, in_=in_), bl(R), bl(W), eng=eng)

    x_in = din("x", [T, D])
    cvec = din("cvec", [128, 8])
    cst = din("cst", [128, NCST])
    NMETA = 3 + 4 * NT
    meta = din("meta", [128, NMETA])
    rope = din("rope", [T, 128])
    gla0 = din("gla0", [2, 2, 4, 64, 128])
    s5h0 = din("s5h0", [2, 2, 2, 128, 16])
    ckv = din("ckv", [2, 2, 256, 2, 64])
    norm_w = din("norm_w", [4, D])
    w_ada = din("w_ada", [4, D, 3 * D])
    b_ada = din("b_ada", [4, 3 * D])
    w_in_e = din("w_in_e", [2, D, EIN])
    w_out_e = din("w_out_e", [2, D, D])
    w2cat = din("w2cat", [2, 64, 512])
    onorm = din("onorm", [2, 128, 1])
    s5p = din("s5p", [2, 2, 3, 128, 16])
    s5b = din("s5b", [2, 2, 128, 16 * 128])
    s5c = din("s5c", [2, 2, 128, 16 * 32])
    s5d = din("s5d", [2, 128, 4])
    wglu = din("wglu", [2, 512, 512])
    bglu = din("bglu", [2, 128, 4])
    w_in_o = din("w_in_o", [2, D, OIN])
    w_out_o = din("w_out_o", [2, D, D])
    qkw = din("qkw", [2, 128])
    sinkv = din("sink", [2, 8])
    convw = din("convw", [2, 128, 16])

    y_out = dout("y", [T, D])
    ngla = dout("ngla", [2, 2, NS, 4, 64, 128])
    ns5 = dout("ns5", [2, 2, 2, NS * 16, 128])
    nkv = dout("nkv", [2, 2, T, 128])

    xs = [dscr("xs0", [T, D], nb=NT), dscr("xs1", [T, D], nb=NT)]
    qkT_s = dscr("qkT", [512, T], BF16, NT)
    kv_s = dscr("kvtm", [T, 768], BF16, NT)
    sp_s = dscr("sp", [T, 512], F32, NT)
    zgT_s = dscr("zgT", [512, T], BF16, NT)
    uT_s = dscr("uT", [512, T], BF16, NT)
    z5T_s = dscr("z5T", [512, T], BF16, NT)
    oF_s = dscr("oF", [512, T], F32, NT)
    y5T_s = dscr("y5T", [512, T], F32, 16)
    yT_s = dscr("yT", [1024, T], BF16, NT)
    qT_s = dscr("qTo", [512, T], BF16, NT)
    kT_s = dscr("kTo", [256, T], BF16, NT)
    v_s = dscr("vo", [T, 128], BF16, NT)
    sz_s = dscr("szo", [T, 512], F32, NT)
    u1T_s = dscr("u1T", [512, T], F32, NT)
    u2T_s = dscr("u2T", [512, T], F32, NT)

    cs = sb("cs", [128, NCST])
    csb = sb("csb", [128, NCST], BF16)
    mt = sb("mt", [128, NMETA])
    wbf = sb("wbf", [128, 8, OIN], BF16, grp="P")
    wob = sb("wob", [128, 8, D], BF16, grp="O")
    wst1 = sb("wst0", [128, OIN], grp="P")
    wst = [wst1, wst1]
    wstO = sb("wstO", [128, D], grp="O")
    wstS = sb("wstS", [128, 512], grp="SP")
    pbfA = sb("pbfA", [128, 512], BF16, grp="A")
    modbc = sb("modbc", [128, 3 * D])
    Abc = sb("Abc", [128, D])
    sc8 = sb("sc8", [128, 8])
    screp = sb("screp", [128, 8, 128])
    xts = [sb("xt0", [128, D]), sb("xt1", [128, D])]
    xt = xts[0]
    hb = sb("hb", [128, D], BF16, grp="P")
    hT = sb("hT", [128, 8, 128], BF16, grp="P")
    small = sb("small", [128, 16])
    small2 = sb("small2", [128, 16])
    proj = sb("proj", [128, OIN], grp="P")
    pbf = sb("pbf", [128, OIN], BF16, grp="P")
    trs = sb("trs", [128, 8, 128], BF16)
    trf = sb("trf", [128, 4, 128])
    w1 = sb("w1", [128, 1024])
    w2 = sb("w2", [128, 1024])
    w3 = sb("w3", [128, 1024])
    lrT = sb("lrT", [64, 128], BF16, grp="P")
    w2c = sb("w2c", [64, 512], BF16, grp="P")
    w2cf = sb("w2cf", [64, 512], grp="P")

    projB = sb("projB", [128, OIN], grp="P")
    pbfB = sb("pbfB", [128, OIN], BF16, grp="P")
    proj2 = [proj, projB]
    pbf2 = [pbf, pbfB]
    pA = ps("pA", [128, 512]); pB = ps("pB", [128, 512]); pC = ps("pC", [128, 512])
    pD = ps("pD", [128, 512]); pE = ps("pE", [128, 512]); pF = ps("pF", [128, 512])
    pT = ps("pT", [128, 1024], BF16)
    pTf = ps("pTf", [128, 512])

    class _PT32:
        def __init__(self):
            self.ap = pT[:, :].bitcast(F32)
            self.b = pT.b

        def __getitem__(self, k):
            return self.ap[k]
    pT32 = _PT32()

    class _PTB:
        def __init__(self):
            self.ap = pE[:, :].bitcast(BF16)
            self.b = pE.b

        def __getitem__(self, k):
            return self.ap[k]
    pTb = _PTB()

    class _PTF16:
        def __init__(self):
            self.ap = pTf[:, :].bitcast(BF16)
            self.b = pTf.b

        def __getitem__(self, k):
            return self.ap[k]
    pTf16 = _PTF16()
    trsb = sb("trsb", [128, 4, 128], BF16)
    ident_b = csb[:, C_ID:C_ID + 128]
    ident_f = cs[:, C_ID:C_ID + 128]
    mflag = mt[:, 0:1]

    dma(cs[:], cst.ap[:, :], [], [cs])
    cp("dve", csb[:], cs[:], [cs], [csb])
    dma(mt[:], meta.ap[:, :], [], [mt])
    dma(sc8[:], cvec.ap[:, :], [], [sc8])
    act(sc8[:], sc8[:], AF.Silu, [sc8], [sc8])
    for kt in range(8):
        cp("dve", screp[:, kt, :], sc8[:, kt:kt + 1].to_broadcast([128, 128]), [sc8], [screp])
    band = sb("band", [128, 384])
    ts("dve", band[:], cs[:, C_BAND:C_BAND + 384], mt[:, 1:2], ALU.mult, [cs, mt], [band])

    evac_rr = [0]

    def evac(out, in_, R, W):
        e = ("act", "dve")[evac_rr[0] % 2]
        evac_rr[0] += 1
        cp(e, out, in_, R, W)

    def load_weight_bf16(dst, src_ap, ncols, wsx=None):
        wsx = wsx or wst1
        for kt in range(8):
            dma(wsx[:, :ncols], src_ap[kt * 128:(kt + 1) * 128, :], [], [wsx])
            cp(("act", "dve")[kt % 2], dst[:, kt, :ncols], wsx[:, :ncols], [wsx], [dst])

    def adaln(l):
        banks = [pA, pB, pC, pD, pE, pF]
        for kt in range(8):
            wsx = wst[kt % 2]
            dma(wsx[:, :3 * D], w_ada.ap[l, kt * 128:(kt + 1) * 128, :], [], [wsx])
            for nb_ in range(6):
                mm(banks[nb_][:, :], screp[:, kt, :], wsx[:, nb_ * 512:(nb_ + 1) * 512], kt == 0, kt == 7,
                   [screp, wsx], [banks[nb_]])
        tmpbc = wst1
        dma(tmpbc[:, :3 * D], b_ada.ap[l:l + 1, :].partition_broadcast(128), [], [tmpbc])
        for nb_ in range(6):
            tt("dve", modbc[:, nb_ * 512:(nb_ + 1) * 512], banks[nb_][:, :], tmpbc[:, nb_ * 512:(nb_ + 1) * 512],
               ALU.add, [banks[nb_], tmpbc], [modbc])
        dma(tmpbc[:, :D], norm_w.ap[l:l + 1, :].partition_broadcast(128), [modbc], [tmpbc])
        stt(Abc[:], modbc[:, D:2 * D], 1.0, tmpbc[:, :D], ALU.add, ALU.mult, [modbc, tmpbc], [Abc])

    def load_x(ti, xsrc):
        if ti >= NT:
            return
        t0 = ti * 128
        xt = xts[ti % 2]
        dma(xt[:], xsrc.ap[t0:t0 + 128, :], [xsrc.bs[ti] if len(xsrc.bs) > 1 else xsrc.b], [xt])

    def norm_mod_T(l, ti, xsrc):
        if ti == 0:
            load_x(0, xsrc)
        load_x(ti + 1, xsrc)
        xt = xts[ti % 2]
        mset("dve", small2[:, 0:1], 0.0, [small2])
        act(hb[:], xt[:], AF.Square, [xt], [hb, small2], accum_out=small2[:, 0:1])
        act(small2[:, 1:2], small2[:, 0:1], AF.Ln, [small2], [small2], scale=1.0 / D, bias=cs[:, C_EPS:C_EPS + 1])
        act(small2[:, 2:3], small2[:, 1:2], AF.Exp, [small2], [small2], scale=-0.5)
        stt(w4[:, :D], xt[:], small2[:, 2:3], Abc[:], ALU.mult, ALU.mult, [xt, small2, Abc], [w4])
        tt("dve", hb[:], w4[:, :D], modbc[:, 0:D], ALU.add, [w4, modbc], [hb])
        for kt in range(8):
            tr(pT[:, kt * 128:(kt + 1) * 128], hb[:, kt * 128:(kt + 1) * 128], ident_b, [hb, csb], [pT])
        cp("act", hT[:].rearrange("p a t -> p (a t)"), pT[:, :], [pT], [hT])

    def in_proj(ncols, dst=None):
        dst = dst or proj
        c0 = 0
        k = 0
        while c0 < ncols:
            cw = min(512, ncols - c0)
            pp = (pA, pB)[k % 2]
            for kt in range(8):
                mm(pp[:, :cw], hT[:, kt, :], wbf[:, kt, c0:c0 + cw], kt == 0, kt == 7, [hT, wbf], [pp])
            evac(dst[:, c0:c0 + cw], pp[:, :cw], [pp], [dst])
            c0 += cw
            k += 1

    ts_rr = [0]

    def transpose_store(src_bf_ap, nft, dst, row0, ti, R):
        t0 = ti * 128
        assert nft <= 4
        k = ts_rr[0] % 2
        ts_rr[0] += 1
        pX, tX = (pT, pTb)[k], (trs, trsb)[k]
        for a in range(nft):
            tr(pX[:, a * 128:(a + 1) * 128], src_bf_ap[:, a * 128:(a + 1) * 128], ident_b, R + [csb], [pX])
        cp(("act", "dve")[k], tX[:, :nft, :].rearrange("p a t -> p (a t)"), pX[:, :nft * 128], [pX], [tX])
        dma(dst.ap[row0:row0 + nft * 128, t0:t0 + 128].rearrange("(a p) t -> p a t", p=128), tX[:, :nft, :],
            [tX], [dst.bs[ti]])

    def transpose_store_f32(src_ap, nft, dst, row0, ti, R):
        t0 = ti * 128
        for a in range(nft):
            tr(pTf[:, a * 128:(a + 1) * 128], src_ap[:, a * 128:(a + 1) * 128], ident_f, R + [cs], [pTf])
        cp("act", trf[:, :nft, :].rearrange("p a t -> p (a t)"), pTf[:, :nft * 128], [pTf], [trf])
        dma(dst.ap[row0:row0 + nft * 128, t0:t0 + 128].rearrange("(a p) t -> p a t", p=128), trf[:, :nft, :],
            [trf], [dst.bs[ti]])

    def out_proj_residual(l, xsrc, xdst, ydt, wsrc):
        S.barrier()
        load_weight_bf16(wob, wsrc, D, wstO)
        def loads(ti):
            if ti >= NT:
                return
            t0 = ti * 128
            p = ti % 2
            yt = (sb_yt, sb_yt2)[p]
            dma(yt[:], ydt.ap[:, t0:t0 + 128].rearrange("(a p) t -> p a t", p=128), [ydt.bs[ti]], [yt])
            dma(xts[p][:], xsrc.ap[t0:t0 + 128, :], [xsrc.bs[ti] if len(xsrc.bs) > 1 else xsrc.b], [xts[p]])

        loads(0)
        for ti in range(NT):
            t0 = ti * 128
            p = ti % 2
            loads(ti + 1)
            yt, xt = (sb_yt, sb_yt2)[p], xts[p]
            o1, o2 = ((w1, w2), (w3, w4))[p]
            for cb in range(2):
                pp = ((pA, pB), (pC, pD))[p][cb]
                for kt in range(8):
                    mm(pp[:, :], yt[:, kt, :], wob[:, kt, cb * 512:(cb + 1) * 512], kt == 0, kt == 7, [yt, wob], [pp])
                tt("dve", o1[:, cb * 512:(cb + 1) * 512], pp[:, :], modbc[:, 2 * D + cb * 512:2 * D + (cb + 1) * 512],
                   ALU.mult, [pp, modbc], [o1])
            tt("dve", o2[:, :D], o1[:, :D], xt[:], ALU.add, [o1, xt], [o2])
            dma(xdst.ap[t0:t0 + 128, :], o2[:, :D], [o2], [xdst.bs[ti] if len(xdst.bs) > 1 else xdst.b])

    sb_yt = sb("yt", [128, 8, 128], BF16)
    sb_yt2 = sb("yt2", [128, 8, 128], BF16)
    w4 = sb("w4", [128, 1024])

    qk_t = sb("qk_t", [128, 4, 128], BF16, grp="G")
    kv_t = sb("kv_t", [128, 768], BF16, grp="G")
    sp_t = sb("sp_t", [128, 256], grp="G")
    E1 = sb("E1", [128, 2, 128], grp="G")
    E2 = sb("E2", [128, 2, 128], grp="G")
    E3 = sb("E3", [128, 256], grp="G")
    qbT = sb("qbT", [128, 2, 128], BF16, grp="G")
    kbT = sb("kbT", [128, 2, 128], BF16, grp="G")
    kd = sb("kd", [128, 256], BF16, grp="G")
    attm = sb("attm", [128, 4, 128], BF16, grp="G")
    Sst = sb("Sst", [128, 2, 256], grp="G")
    Sbf = sb("Sbf", [128, 2, 256], BF16, grp="G")
    Sbf1 = sb("Sbf1", [128, 2, 256], BF16, grp="G")
    o_t = sb("o_t", [128, 4, 128], grp="G")
    oF_t = sb("oF_t", [128, 4, 128], grp="G")
    zg_t = sb("zg_t", [128, 4, 128], BF16, grp="G")
    osq = sb("osq", [128, 512], BF16, grp="G")
    onc = sb("onc", [128, 1])
    TX = max(T, 2048)
    Xs = [sb("Xf", [128, 2, TX], grp="S"), sb("Xb", [128, 2, TX], grp="S")]
    u_ft = sb("u_ft", [128, T], BF16, grp="S")
    BT = sb("BT", [128, 2, 2, 16, 128], BF16, grp="S")
    CTm = sb("CTm", [128, 2, 16, 32], grp="S")
    Xblk = [[Buf() for _ in range(max(NB, 1))] for _ in range(2)]

    class _Alias:
        def __init__(self, base, shape):
            self.ap = base.ap.rearrange("p c t -> p (c t)")[:, 0:4096].rearrange("p (a j k) -> p a j k", a=2, j=16)
            self.b = base.b

        def __getitem__(self, k):
            return self.ap[k]
    Bx = _Alias(Xs[0], None)
    Bbar = _Alias(Xs[1], None)
    s5par = sb("s5par", [128, 3, 16], grp="S")
    h0t = sb("h0t", [128, 2, 2, 16], grp="S")
    NPW = 18
    PW = sb("PW", [128, 2, NPW, 3, 16], grp="S")
    PWm = sb("PWm", [128, 2, NPW, 3, 16], grp="S")
    s5tmp = sb("s5tmp", [128, 12, 16], grp="S")
    s5i = sb("s5i", [128, 16], I32, grp="S")
    STG = sb("STG", [128, 2 * 2 * NS * 16], grp="S")
    stgT = sb("stgT", [128, 128], grp="S")
    y5st = sb("y5st", [32, 512], grp="S")
    dcol = sb("dcol", [128, 4], grp="SP")
    bgl = sb("bgl", [128, 4], grp="SP")
    wgb = sb("wgb", [128, 4, 512], BF16, grp="SP")
    y5b = sb("y5b", [128, 4, 512], grp="SP")
    ub = sb("ub", [128, 4, 512], BF16, grp="SP")
    z5b = sb("z5b", [128, 4, 512], BF16, grp="SP")
    zz = sb("zz", [128, 4, 512], grp="SP")
    zzb = sb("zzb", [128, 4, 512], BF16, grp="SP")
    sg = sb("sg", [128, 4, 512], grp="SP")
    y5o = sb("y5o", [128, 4, 512], BF16, grp="SP")

    pw_exps = list(range(1, 9)) + [16, 24, 32, 40, 48, 56, 64, 128, 192, 256]
    pidx = {m: i for i, m in enumerate(pw_exps)}
    assert len(pw_exps) <= NPW

    def s5_post_setup(e):
        dma(dcol[:], s5d.ap[e], [], [dcol])
        dma(bgl[:], bglu.ap[e], [], [bgl])
        for kt in range(4):
            dma(wstS[:, :512], wglu.ap[e, kt * 128:(kt + 1) * 128, :], [], [wstS])
            cp("act", wgb[:, kt, :], wstS[:, :512], [wstS], [wgb])

    def s5_setup(e):
        dma(h0t[:], s5h0.ap[e].rearrange("d c p j -> p d c j"), [], [h0t])
        dma(CTm[:].rearrange("p a j o -> p a (j o)"), s5c.ap[e].rearrange("a p x -> p a x"), [], [CTm])
        dma(Bx[:].rearrange("p a j k -> p a (j k)"), s5b.ap[e].rearrange("a p x -> p a x"), [], [Bx])
        for d in range(2):
            dma(s5par[:], s5p.ap[e, d].rearrange("a p j -> p a j"), [], [s5par])
            lre, lim, ldt = s5par[:, 0, :], s5par[:, 1, :], s5par[:, 2, :]
            tmp = lambda i: s5tmp[:, i, :]
            R = [s5par, s5tmp]
            W = [s5tmp]
            act(tmp(0), ldt, AF.Exp, R, W)
            tt("dve", tmp(1), lre, tmp(0), ALU.mult, R, W)
            tt("dve", tmp(2), lim, tmp(0), ALU.mult, R, W)
            act(tmp(3), tmp(1), AF.Exp, R, W)
            for (dst, shift) in ((4, 0.0), (5, math.pi / 2)):
                ts("dve", tmp(6), tmp(2), shift, ALU.add, R, W, 1.0 / (2 * math.pi), ALU.mult)
                cp("dve", s5i[:], tmp(6), R, [s5i])
                cp("dve", tmp(7), s5i[:], [s5i], W)
                ts("dve", tmp(6), tmp(2), shift, ALU.add, R, W)
                stt(tmp(6), tmp(7), -2 * math.pi, tmp(6), ALU.mult, ALU.add, R, W)
                ts("dve", tmp(6), tmp(6), math.pi, ALU.min, R, W, -math.pi, ALU.max)
                act(tmp(dst), tmp(6), AF.Sin, R, W)
            P = lambda m, c: PW[:, d, pidx[m], c, :]
            RW = [PW, s5tmp]
            tt("dve", P(1, 0), tmp(3), tmp(5), ALU.mult, RW, [PW])
            tt("dve", P(1, 1), tmp(3), tmp(4), ALU.mult, RW, [PW])
            tt("dve", tmp(6), lre, lre, ALU.mult, R, W)
            tt("dve", tmp(7), lim, lim, ALU.mult, R, W)
            tt("dve", tmp(6), tmp(6), tmp(7), ALU.add, R, W)
            S.op("dve", lambda e_, o=tmp(6): e_.reciprocal(out=o, in_=o), bl(R), bl(W))
            ts("dve", tmp(7), P(1, 0), -1.0, ALU.add, RW, W)
            tt("dve", tmp(8), tmp(7), lre, ALU.mult, R, W)
            tt("dve", tmp(9), P(1, 1), lim, ALU.mult, RW + [s5par], W)
            tt("dve", tmp(8), tmp(8), tmp(9), ALU.add, R, W)
            tt("dve", tmp(8), tmp(8), tmp(6), ALU.mult, R, W)
            tt("dve", tmp(9), P(1, 1), lre, ALU.mult, RW + [s5par], W)
            tt("dve", tmp(10), tmp(7), lim, ALU.mult, R, W)
            tt("dve", tmp(9), tmp(9), tmp(10), ALU.subtract, R, W)
            tt("dve", tmp(9), tmp(9), tmp(6), ALU.mult, R, W)
            crb = s5tmp[:, 8, :].unsqueeze(2).to_broadcast([128, 16, 128])
            cib = s5tmp[:, 9, :].unsqueeze(2).to_broadcast([128, 16, 128])
            RB = [Bx, s5tmp, Bbar]
            tt("dve", Bbar[:, 0], Bx[:, 0], crb, ALU.mult, RB, [Bbar])
            tt("dve", Bbar[:, 1], Bx[:, 1], cib, ALU.mult, RB, [Bbar])
            tt("dve", Bbar[:, 0], Bbar[:, 0], Bbar[:, 1], ALU.subtract, RB, [Bbar])
            tt("dve", Bbar[:, 1], Bx[:, 1], crb, ALU.mult, RB, [Bbar])
            for j in range(16):
                stt(Bbar[:, 1, j, :], Bx[:, 0, j, :], s5tmp[:, 9, j:j + 1], Bbar[:, 1, j, :], ALU.mult, ALU.add, RB, [Bbar])
            for c in range(2):
                for j4 in range(4):
                    for jj in range(4):
                        j = j4 * 4 + jj
                        tr(pTf[:, jj * 128:(jj + 1) * 128], Bbar[:, c, j, :], ident_f, [Bbar, cs], [pTf])
                    cp("act", BT[:, d, c, j4 * 4:(j4 + 1) * 4, :].rearrange("p j k -> p (j k)"), pTf[:, :], [pTf], [BT])
            def cmul(mo, ma, mb_):
                a_re, a_im = P(ma, 0), P(ma, 1)
                b_re, b_im = P(mb_, 0), P(mb_, 1)
                tt("dve", tmp(10), a_im, b_im, ALU.mult, RW, W)
                tt("dve", tmp(11), a_re, b_re, ALU.mult, RW, W)
                tt("dve", tmp(6), a_re, b_im, ALU.mult, RW, W)
                tt("dve", tmp(7), a_im, b_re, ALU.mult, RW, W)
                tt("dve", P(mo, 0), tmp(11), tmp(10), ALU.subtract, RW, [PW])
                tt("dve", P(mo, 1), tmp(6), tmp(7), ALU.add, RW, [PW])
            for m in range(2, 9):
                cmul(m, m - 1, 1)
            for m in range(16, 65, 8):
                cmul(m, m - 8, 8)
            cmul(128, 64, 64)
            cmul(192, 128, 64)
            cmul(256, 192, 64)
            ts("dve", PW[:, d, :, 2, :], PW[:, d, :, 1, :], -1.0, ALU.mult, [PW], [PW])
            ts("dve", PWm[:, d], PW[:, d], mflag, ALU.mult, [PW, mt], [PWm])

    def s5_view(X, comp, start, step, count, rev, inner=None):
        base = X[:, comp, 0:1]
        off = base.offset
        pstride = base.ap[0][0]
        if rev:
            o = off + (T - 1 - start)
            dims = [[pstride, 128], [-step, count]]
            if inner is not None:
                dims.append([-inner[0], inner[1]])
        else:
            o = off + start
            dims = [[pstride, 128], [step, count]]
            if inner is not None:
                dims.append([inner[0], inner[1]])
        return bass.AP(base.tensor, o, dims)

    def s5_scan(X, d, j, rev, fence0):
        XW = Xblk[d]
        XB = XW + [PW, PWm]
        stt_ = {"fence": fence0, "C": None, "D": None}

        def cmac(tgt, src, pw_tile, m, chain):
            pr = pw_tile[:, d, pidx[m], 0, j:j + 1]
            pi = pw_tile[:, d, pidx[m], 1, j:j + 1]
            pn = pw_tile[:, d, pidx[m], 2, j:j + 1]
            f = stt_["fence"]
            pc, pd = (stt_["C"], stt_["D"]) if chain else (None, None)
            ta = stt(tgt(0), src(0), pr, tgt(0), ALU.mult, ALU.add, XB, XW, extra=[f, pc])
            yield
            tb = stt(tgt(1), src(0), pi, tgt(1), ALU.mult, ALU.add, XB, XW, extra=[f, pc])
            yield
            tc = stt(tgt(0), src(1), pn, tgt(0), ALU.mult, ALU.add, XB, XW, extra=[f, pd, ta])
            yield
            td = stt(tgt(1), src(1), pr, tgt(1), ALU.mult, ALU.add, XB, XW, extra=[f, pd, tb])
            yield
            stt_["C"], stt_["D"], stt_["last"] = tc, td, td

        def fence():
            stt_["fence"] = stt_.get("last", stt_["fence"])
            stt_["C"] = stt_["D"] = None

        for (s, K) in ((1, 8), (8, 8), (64, 4)):
            n = T // (s * K)
            fence()
            for jj in range(1, K):
                tgt = lambda c, s=s, K=K, jj=jj, n=n: s5_view(X, c, (jj + 1) * s - 1, K * s, n, rev)
                src = lambda c, s=s, K=K, jj=jj, n=n: s5_view(X, c, jj * s - 1, K * s, n, rev)
                yield from cmac(tgt, src, PW, s, True)
        fence()
        for sg_ in range(1, NS):
            tgt = lambda c, sg_=sg_: s5_view(X, c, 256 * (sg_ + 1) - 1, 1, 1, rev)
            src = lambda c, sg_=sg_: s5_view(X, c, 256 * sg_ - 1, 1, 1, rev)
            yield from cmac(tgt, src, PWm, 256, True)
        fence()
        if NS > 1:
            for jj in range(3):
                tgt = lambda c, jj=jj: s5_view(X, c, 256 + 64 * (jj + 1) - 1, 256, NS - 1, rev)
                src = lambda c: s5_view(X, c, 255, 256, NS - 1, rev)
                yield from cmac(tgt, src, PWm, 64 * (jj + 1), False)
        for (s, K, nin) in ((8, 8, 4), (1, 8, 32)):
            fence()
            for jj in range(K - 1):
                m = s * (jj + 1)
                tgt = lambda c, s=s, K=K, jj=jj, nin=nin: s5_view(X, c, s * K + (jj + 1) * s - 1, 256, NS, rev, inner=(s * K, nin - 1))
                src = lambda c, s=s, K=K, nin=nin: s5_view(X, c, s * K - 1, 256, NS, rev, inner=(s * K, nin - 1))
                yield from cmac(tgt, src, PW, m, False)
                if NS > 1:
                    tgt = lambda c, s=s, jj=jj: s5_view(X, c, 256 + (jj + 1) * s - 1, 256, NS - 1, rev)
                    src = lambda c: s5_view(X, c, 255, 256, NS - 1, rev)
                    yield from cmac(tgt, src, PWm, m, False)

    def chk(tag, l):
        if stop == "%s%d" % (tag, l):
            raise _Stop()

    def chk2(tag):
        if stop == tag:
            raise _Stop()

    def even_layer(l, xsrc, xdst):
        e = l // 2
        load_weight_bf16(wbf, w_in_e.ap[e], EIN)
        chk2("W")
        adaln(l)
        chk2("ADA")
        dma(w2cf[:], w2cat.ap[e], [], [w2cf])
        cp("dve", w2c[:], w2cf[:], [w2cf], [w2c])
        dma(onc[:], onorm.ap[e], [], [onc])
        mset("dve", lrT[:], 0.0, [lrT])
        mset("dve", lrT[32:33, :], 1.0, [lrT])
        def front_e(ti):
            norm_mod_T(l, ti, xsrc)
            in_proj(EIN, pbf2[ti % 2])

        front_e(0)
        for ti in range(NT):
            t0 = ti * 128
            if ti + 1 < NT:
                front_e(ti + 1)
            pbf_t = pbf2[ti % 2]
            transpose_store(pbf_t[:, 0:512], 4, qkT_s, 0, ti, [pbf_t])
            chk2("TS")
            dma(kv_s.ap[t0:t0 + 128, :], pbf_t[:, 256:1024], [pbf_t], [kv_s.bs[ti]])
            chk2("KV")
            tr(pT[:32, 0:128], pbf_t[:, 1024:1056], ident_b, [pbf_t, csb], [pT])
            cp("act", lrT[0:32, :], pT[:32, 0:128], [pT], [lrT])
            mm(pC[:, :], lrT[:, :], w2c[:, :], True, True, [lrT, w2c], [pC])
            act(w3[:, :512], pC[:, :], AF.Exp, [pC], [w3], scale=-1.0)
            act(w3[:, 512:1024], w3[:, :512], AF.Ln, [w3], [w3], bias=cs[:, C_ONE:C_ONE + 1])
            dma(sp_s.ap[t0:t0 + 128, :], w3[:, 512:1024], [w3], [sp_s.bs[ti]])
            chk2("GATE")
            transpose_store(pbf_t[:, 1056:1568], 4, zgT_s, 0, ti, [pbf_t])
            transpose_store(pbf_t[:, 1568:2080], 4, uT_s, 0, ti, [pbf_t])
            transpose_store(pbf_t[:, 2080:2592], 4, z5T_s, 0, ti, [pbf_t])
            chk2("T%d" % ti)
        chk('P', l)
        S.barrier()
        def gla_gen():
            for d in range(2):
                TI, TR_, MK = (C_TIF, C_TRF, C_MF) if d == 0 else (C_TIB, C_TRB, C_MB)
                mset("dve", Sst[:], 0.0, [Sst])
                yield
                for h in range(4):
                    pr, hh = h // 2, h % 2
                    dma(Sst[hh * 64:(hh + 1) * 64, pr, hh * 128:(hh + 1) * 128], gla0.ap[e, d, h], [], [Sst])
                    yield
                blkm = cs[:, C_BLK:C_BLK + 256].unsqueeze(1).to_broadcast([128, 2, 256])
                tt("pool", Sbf[:], Sst[:], blkm, ALU.mult, [Sst, cs], [Sbf])
                yield
                order = list(range(NT)) if d == 0 else list(range(NT - 1, -1, -1))
                for oi, ti in enumerate(order):
                    t0 = ti * 128
                    if oi > 0 and oi % 2 == 0:
                        ts("dve", Sst[:], Sst[:], mflag, ALU.mult, [Sst, mt], [Sst])
                        yield
                        tt("pool", Sbf[:], Sst[:], blkm, ALU.mult, [Sst, cs], [Sbf])
                        yield
                    dma(qk_t[:], qkT_s.ap[:, t0:t0 + 128].rearrange("(a p) t -> p a t", p=128), [qkT_s.bs[ti]], [qk_t])
                    yield
                    dma(kv_t[:], kv_s.ap[t0:t0 + 128, :], [kv_s.bs[ti]], [kv_t])
                    yield
                    dma(sp_t[:], sp_s.ap[t0:t0 + 128, d * 256:(d + 1) * 256], [sp_s.bs[ti]], [sp_t])
                    yield
                    yield
                    for pr in range(2):
                        mm(pC[:, pr * 128:(pr + 1) * 128], sp_t[:, pr * 128:(pr + 1) * 128], cs[:, TI:TI + 128], True, True,
                           [sp_t, cs], [pC])
                        yield
                    mm(pD[:, 0:256], cs[:, TR_:TR_ + 128], sp_t[:, :], True, True, [sp_t, cs], [pD])
                    yield
                    yield
                    act(E1[:].rearrange("p a t -> p (a t)"), pC[:, 0:256], AF.Exp, [pC], [E1])
                    yield
                    act(E2[:].rearrange("p a t -> p (a t)"), pC[:, 0:256], AF.Exp, [pC], [E2], scale=-1.0)
                    yield
                    act(E3[:], pD[:, 0:256], AF.Exp, [pD], [E3])
                    yield
                    stt(qbT[:], qk_t[:, 0:2, :], 0.125, E1[:], ALU.mult, ALU.mult, [qk_t, E1], [qbT])
                    yield
                    tt("pool", kbT[:], qk_t[:, 2:4, :], E2[:], ALU.mult, [qk_t, E2], [kbT])
                    yield
                    tt("pool", kd[:], kv_t[:, 0:256], E3[:], ALU.mult, [kv_t, E3], [kd])
                    yield
                    yield
                    for h in range(4):
                        pr, hh = h // 2, h % 2
                        pX = (pE, pA)[hh]
                        mm(pX[:, pr * 128:(pr + 1) * 128], kbT[hh * 64:(hh + 1) * 64, pr, :], qbT[hh * 64:(hh + 1) * 64, pr, :],
                           True, True, [kbT, qbT], [pX])
                        yield
                    yield
                    for hh in range(2):
                        pX = (pE, pA)[hh]
                        av = attm[:].rearrange("p (pr hh) t -> p hh pr t", hh=2)[:, hh]
                        tt("dve", av, pX[:, 0:256].rearrange("p (h t) -> p h t", h=2),
                           cs[:, MK:MK + 128].unsqueeze(1).to_broadcast([128, 2, 128]), ALU.mult, [pX, cs], [attm])
                        yield
                    yield
                    chunks = (0, 1) if d == 0 else (1, 0)
                    for ci, ch in enumerate(chunks):
                        c0 = ch * 64
                        dcolx = (c0 + 63) if d == 0 else c0
                        pUp = (pD, pA)[ch]
                        for pr in range(2):
                            mm(pUp[:, 256:512], kd[c0:c0 + 64, pr * 128:(pr + 1) * 128],
                               kv_t[c0:c0 + 64, 256 + pr * 256:256 + (pr + 1) * 256], True, True, [kd, kv_t], [pUp])
                            yield
                            stt(Sst[:, pr, :], Sst[:, pr, :], E1[:, pr, dcolx:dcolx + 1], pUp[:, 256:512], ALU.mult, ALU.add,
                                [Sst, E1, pUp], [Sst])
                            yield
                        if ci == 0:
                            tt("pool", Sbf1[:], Sst[:], blkm, ALU.mult, [Sst, cs], [Sbf1])
                            yield
                    for h in range(4):
                        pr, hh = h // 2, h % 2
                        mm(pF[:, h * 128:(h + 1) * 128], kv_t[:, 256 + h * 128:256 + (h + 1) * 128], attm[:, h, :],
                           True, False, [kv_t, attm], [pF])
                        yield
                        for ci, ch in enumerate(chunks):
                            c0 = ch * 64
                            Sx = (Sbf, Sbf1)[ci]
                            mm(pF[:, h * 128 + c0:h * 128 + c0 + 64], Sx[:, pr, hh * 128:(hh + 1) * 128],
                               qbT[:, pr, c0:c0 + 64], False, (ci == 1), [Sx, qbT], [pF])
                            yield
                    tt("pool", Sbf[:], Sst[:], blkm, ALU.mult, [Sst, cs, pF], [Sbf])
                    yield
                    yield
                    if oi % 2 == 1:
                        seg = ti // 2
                        for h in range(4):
                            pr, hh = h // 2, h % 2
                            dma(ngla.ap[e, d, seg, h], Sst[hh * 64:(hh + 1) * 64, pr, hh * 128:(hh + 1) * 128], [Sst], [ngla])
                            yield
                    if d == 0:
                        cp("act", o_t[:].rearrange("p h t -> p (h t)"), pF[:, :], [pF], [o_t])
                        yield
                        dma(oF_s.ap[:, t0:t0 + 128].rearrange("(a p) t -> p a t", p=128), o_t[:], [o_t], [oF_s.bs[ti]])
                        yield
                    else:
                        dma(oF_t[:], oF_s.ap[:, t0:t0 + 128].rearrange("(a p) t -> p a t", p=128), [oF_s.bs[ti]], [oF_t])
                        yield
                        dma(zg_t[:], zgT_s.ap[:, t0:t0 + 128].rearrange("(a p) t -> p a t", p=128), [zgT_s.bs[ti]], [zg_t])
                        yield
                        tt("dve", o_t[:].rearrange("p h t -> p (h t)"), pF[:, :], oF_t[:].rearrange("p h t -> p (h t)"),
                           ALU.add, [pF, oF_t], [o_t])
                        yield
                        of = o_t[:].rearrange("p h t -> p (h t)")
                        act(osq[:], of, AF.Square, [o_t], [osq])
                        yield
                        mm(pC[:, :], csb[:, C_ONE:C_ONE + 128], osq[:], True, True, [osq, csb], [pC])
                        yield
                        act(w1[:, :512], pC[:, :], AF.Ln, [pC], [w1], scale=1.0 / 128, bias=cs[:, C_EPS:C_EPS + 1])
                        yield
                        act(w1[:, :512], w1[:, :512], AF.Exp, [w1], [w1], scale=-0.5)
                        yield
                        tt("dve", w1[:, :512], w1[:, :512], of, ALU.mult, [w1, o_t], [w1])
                        yield
                        act(w1[:, 512:1024], zg_t[:].rearrange("p h t -> p (h t)"), AF.Silu, [zg_t], [w1])
                        yield
                        stt(trs[:, 0:4, :].rearrange("p a t -> p (a t)"), w1[:, :512], onc[:, 0:1], w1[:, 512:1024],
                            ALU.mult, ALU.mult, [w1, onc], [trs])
                        yield
                        dma(yT_s.ap[0:512, t0:t0 + 128].rearrange("(a p) t -> p a t", p=128), trs[:, 0:4, :], [trs], [yT_s.bs[ti]])
                        yield
            yield

        def s5_gen():
            s5_setup(e)
            for j in range(16):
                ft = j // 4
                if j % 4 == 0:
                    dma(u_ft[:], uT_s.ap[ft * 128:(ft + 1) * 128, :], uT_s.bs, [u_ft])
                fences = []
                for d in range(2):
                    X = Xs[d]
                    rev = (d == 1)
                    for blk in range(NB):
                        for c in range(2):
                            pp = (pB, pTf)[c]
                            mm(pp[:, :], BT[:, d, c, j, :], u_ft[:, blk * 512:(blk + 1) * 512], True, True, [BT, u_ft], [pp])
                            cp("act", X[:, c, blk * 512:(blk + 1) * 512], pp[:, :], [pp], [X, Xblk[d][blk]])
                            yield
                    v0 = lambda c: s5_view(X, c, 0, 1, 1, rev)
                    pr_ = PW[:, d, pidx[1], 0, j:j + 1]; pi_ = PW[:, d, pidx[1], 1, j:j + 1]; pn_ = PW[:, d, pidx[1], 2, j:j + 1]
                    stt(v0(0), h0t[:, d, 0, j:j + 1], pr_, v0(0), ALU.mult, ALU.add, [h0t, PW] + Xblk[d], Xblk[d])
                    stt(v0(0), h0t[:, d, 1, j:j + 1], pn_, v0(0), ALU.mult, ALU.add, [h0t, PW] + Xblk[d], Xblk[d])
                    stt(v0(1), h0t[:, d, 1, j:j + 1], pr_, v0(1), ALU.mult, ALU.add, [h0t, PW] + Xblk[d], Xblk[d])
                    fences.append(stt(v0(1), h0t[:, d, 0, j:j + 1], pi_, v0(1), ALU.mult, ALU.add, [h0t, PW] + Xblk[d], Xblk[d]))
                gens = [s5_scan(Xs[d], d, j, d == 1, fences[d]) for d in range(2)]
                alive = [True, True]
                while any(alive):
                    for d in range(2):
                        if alive[d]:
                            try:
                                next(gens[d])
                                yield
                            except StopIteration:
                                alive[d] = False
                for d in range(2):
                    X = Xs[d]
                    rev = (d == 1)
                    for c in range(2):
                        col0 = ((d * 2 + c) * NS) * 16 + j
                        dst = bass.AP(STG[:, 0:1].tensor, STG[:, col0:col0 + 1].offset, [list(STG[:, 0:1].ap[0]), [16, NS]])
                        cp("pool", dst, s5_view(X, c, 255, 256, NS, rev), Xblk[d], [STG])
                for blk in range(NB):
                    k = 0
                    for d in range(2):
                        for c in range(2):
                            mm(pT32[0:32, :], CTm[:, c, j, :], Xs[d][:, c, blk * 512:(blk + 1) * 512], k == 0, k == 3,
                               [CTm, Xblk[d][blk]], [pT32])
                            k += 1
                    cp("act", y5st[:, :], pT32[0:32, :], [pT32], [y5st])
                    dma(y5T_s.ap[j * 32:(j + 1) * 32, blk * 512:(blk + 1) * 512], y5st[:, :], [y5st], [y5T_s.bs[j]])
                    yield
            gsz = min(8, NS)
            for d in range(2):
                for c in range(2):
                    for s0 in range(0, NS, gsz):
                        col0 = ((d * 2 + c) * NS + s0) * 16
                        ncol = gsz * 16
                        tr(pTf[:ncol, 0:128], STG[:, col0:col0 + ncol], ident_f, [STG, cs], [pTf])
                        cp("act", stgT[:ncol, :], pTf[:ncol, 0:128], [pTf], [stgT])
                        dma(ns5.ap[e, d, c, s0 * 16:s0 * 16 + ncol, :], stgT[:ncol, :], [stgT], [ns5])
            yield

        gG, gS = gla_gen(), s5_gen()
        aG = aS = True
        RATIO = float(os.environ.get("KRATIO", "2.4"))
        acc = 0.0
        while aG or aS:
            if aG:
                try:
                    next(gG)
                except StopIteration:
                    aG = False
            acc += RATIO if aG else 1000.0
            while acc >= 1.0 and aS:
                acc -= 1.0
                try:
                    next(gS)
                except StopIteration:
                    aS = False
            if not aS:
                acc = 0.0

        chk('S', l)
        S.barrier()
        s5_post_setup(e)
        for blk in range(NB):
            c0 = blk * 512
            tis = list(range(blk * 4, blk * 4 + 4))
            dma(y5b[:], y5T_s.ap[:, c0:c0 + 512].rearrange("(a p) t -> p a t", p=128), y5T_s.bs, [y5b])
            dma(ub[:], uT_s.ap[:, c0:c0 + 512].rearrange("(a p) t -> p a t", p=128), [uT_s.bs[i] for i in tis], [ub])
            dma(z5b[:], z5T_s.ap[:, c0:c0 + 512].rearrange("(a p) t -> p a t", p=128), [z5T_s.bs[i] for i in tis], [z5b])
            for a in range(4):
                stt(y5b[:, a, :], ub[:, a, :], dcol[:, a:a + 1], y5b[:, a, :], ALU.mult, ALU.add, [ub, dcol, y5b], [y5b])
            act(zz[:].rearrange("p a t -> p (a t)"), y5b[:].rearrange("p a t -> p (a t)"), AF.Gelu_apprx_tanh, [y5b], [zz])
            cp("dve", zzb[:].rearrange("p a t -> p (a t)"), zz[:].rearrange("p a t -> p (a t)"), [zz], [zzb])
            for fo in range(4):
                pp = (pA, pB)[fo % 2]
                for kt in range(4):
                    mm(pp[:, :], wgb[:, kt, fo * 128:(fo + 1) * 128], zzb[:, kt, :], kt == 0, kt == 3, [wgb, zzb], [pp])
                act(sg[:, fo, :], pp[:, :], AF.Sigmoid, [pp, bgl], [sg], bias=bgl[:, fo:fo + 1])
            tt("dve", sg[:].rearrange("p a t -> p (a t)"), sg[:].rearrange("p a t -> p (a t)"),
               zz[:].rearrange("p a t -> p (a t)"), ALU.mult, [sg, zz], [sg])
            act(zz[:].rearrange("p a t -> p (a t)"), z5b[:].rearrange("p a t -> p (a t)"), AF.Silu, [z5b], [zz])
            tt("dve", y5o[:].rearrange("p a t -> p (a t)"), sg[:].rearrange("p a t -> p (a t)"),
               zz[:].rearrange("p a t -> p (a t)"), ALU.mult, [sg, zz], [y5o])
            dma(yT_s.ap[512:1024, c0:c0 + 512].rearrange("(a p) t -> p a t", p=128), y5o[:], [y5o],
                [yT_s.bs[i] for i in tis])
        chk('SP', l)
        out_proj_residual(l, xsrc, xdst, yT_s, w_out_e.ap[e])
        S.barrier()
        chk('O', l)

    qkwb = sb("qkwb", [128, 128])
    sinkb = sb("sinkb", [128, 8])
    rp_t = sb("rp_t", [128, 128], grp="P")
    qn = sb("qn", [128, 640], grp="P")
    qr = sb("qr", [128, 640], grp="P")
    kdup = sb("kdup", [128, 2, 2, 64], BF16, grp="P")
    ckT = sb("ckT", [128, 2, 256], BF16)
    cvt = sb("cvt", [128, 2, 128], BF16)
    ckf = sb("ckf", [128, 2, 128], grp="P")
    qT_t2 = [sb("qT_t%d" % i, [128, 4, 128], BF16, grp="A") for i in range(2)]
    kT_w2 = [sb("kT_w%d" % i, [128, 2, 384], BF16, grp="A") for i in range(2)]
    v_w2 = [sb("v_w%d" % i, [128, 3, 128], BF16, grp="A") for i in range(2)]
    scs2 = [sb("scs%d" % i, [128, 640], grp="A") for i in range(2)]
    pexp2 = [sb("pexp%d" % i, [128, 640], BF16, grp="A") for i in range(2)]
    pTt2 = [sb("pTt%d" % i, [128, 5, 128], BF16, grp="A") for i in range(2)]
    sm2 = [sb("sm%d" % i, [128, 8, 8], grp="A") for i in range(2)]
    bandT = sb("bandT", [128, 384], grp="A")
    oat = sb("oat", [128, 512], grp="A")
    sz_t2 = [sb("sz_t%d" % i, [128, 512], grp="A") for i in range(2)]
    cvw = sb("cvw", [128, 16])
    u1h2 = [sb("u1h%d" % i, [128, 4, 130], grp="A") for i in range(2)]
    u2t2 = [sb("u2t%d" % i, [128, 4, 128], grp="A") for i in range(2)]
    cacc = sb("cacc", [128, 4, 128], grp="A")

    def odd_layer(l, xsrc, xdst):
        e = l // 2
        load_weight_bf16(wbf, w_in_o.ap[e], OIN)
        adaln(l)
        dma(qkwb[:], qkw.ap[e:e + 1, :].partition_broadcast(128), [], [qkwb])
        ts("dve", qkwb[:, 0:64], qkwb[:, 0:64], 0.125, ALU.mult, [qkwb], [qkwb])
        dma(sinkb[:], sinkv.ap[e:e + 1, :].partition_broadcast(128), [], [sinkb])
        dma(cvw[:], convw.ap[e], [], [cvw])
        for hf in range(2):
            dma(ckf[:, 0, :], ckv.ap[e, 0, hf * 128:(hf + 1) * 128].rearrange("t k d -> t (k d)"), [], [ckf])
            dma(ckf[:, 1, :], ckv.ap[e, 1, hf * 128:(hf + 1) * 128].rearrange("t k d -> t (k d)"), [], [ckf])
            cp("dve", cvt[:, hf, :], ckf[:, 1, :], [ckf], [cvt])
            for kv in range(2):
                for c in range(2):
                    cp("dve", kdup[:, kv, c, :], ckf[:, 0, kv * 64:(kv + 1) * 64], [ckf], [kdup])
            for kv in range(2):
                tr(pT[:, kv * 128:(kv + 1) * 128], kdup[:, kv].rearrange("p c d -> p (c d)"), ident_b, [kdup, csb], [pT])
            for kv in range(2):
                cp("act", ckT[:, kv, hf * 128:(hf + 1) * 128], pT[:, kv * 128:(kv + 1) * 128], [pT], [ckT])
        def front_o(ti):
            norm_mod_T(l, ti, xsrc)
            in_proj(OIN, proj2[ti % 2])

        front_o(0)
        for ti in range(NT):
            t0 = ti * 128
            if ti + 1 < NT:
                front_o(ti + 1)
            proj_t = proj2[ti % 2]
            pbf_t = pbf2[ti % 2]
            dma(rp_t[:], rope.ap[t0:t0 + 128, :], [], [rp_t])
            act(w1[:, :640], proj_t[:, 0:640], AF.Square, [proj_t], [w1])
            S.op("dve", lambda e_: e_.tensor_reduce(out=small[:, 0:10], in_=w1[:, :640].rearrange("p (h d) -> p h d", d=64),
                                                    axis=AX.X, op=ALU.add), bl([w1]), bl([small]))
            act(small[:, 0:10], small[:, 0:10], AF.Ln, [small], [small], scale=1.0 / 64, bias=cs[:, C_EPS:C_EPS + 1])
            act(small[:, 0:10], small[:, 0:10], AF.Exp, [small], [small], scale=-0.5)
            tt("dve", qn[:].rearrange("p (h d) -> p h d", d=64), proj_t[:, 0:640].rearrange("p (h d) -> p h d", d=64),
               small[:, 0:10].unsqueeze(2).to_broadcast([128, 10, 64]), ALU.mult, [proj_t, small], [qn])
            tt("dve", qn[:, 0:512].rearrange("p (h d) -> p h d", d=64), qn[:, 0:512].rearrange("p (h d) -> p h d", d=64),
               qkwb[:, 0:64].unsqueeze(1).to_broadcast([128, 8, 64]), ALU.mult, [qn, qkwb], [qn])
            tt("dve", qn[:, 512:640].rearrange("p (h d) -> p h d", d=64), qn[:, 512:640].rearrange("p (h d) -> p h d", d=64),
               qkwb[:, 64:128].unsqueeze(1).to_broadcast([128, 2, 64]), ALU.mult, [qn, qkwb], [qn])
            dma(nkv.ap[e, 0, t0:t0 + 128, :], qn[:, 512:640], [qn], [nkv])
            dma(nkv.ap[e, 1, t0:t0 + 128, :], proj_t[:, 640:768], [proj_t], [nkv])
            v5 = lambda tl: tl[:, 0:640].rearrange("p (h a b f) -> p h a b f", a=2, b=2, f=16)
            cosb = rp_t[:, 0:64].rearrange("p (a b f) -> p a b f", a=2, b=2).unsqueeze(1).to_broadcast([128, 10, 2, 2, 16])
            tt("dve", v5(qr), v5(qn), cosb, ALU.mult, [qn, rp_t], [qr])
            for b_ in range(2):
                sinb = rp_t[:, 64:128].rearrange("p (a b f) -> p a b f", a=2, b=2)[:, :, b_, :].unsqueeze(1).to_broadcast([128, 10, 2, 16])
                tt("dve", v5(w1)[:, :, :, b_, :], v5(qn)[:, :, :, 1 - b_, :], sinb, ALU.mult, [qn, rp_t], [w1])
            tt("dve", pbf_t[:, 0:640], qr[:, 0:640], w1[:, 0:640], ALU.add, [qr, w1], [pbf_t])
            transpose_store(pbf_t[:, 0:512], 4, qT_s, 0, ti, [pbf_t])
            for kv in range(2):
                for c in range(2):
                    cp("act", kdup[:, kv, c, :], pbf_t[:, 512 + kv * 64:512 + (kv + 1) * 64], [pbf_t], [kdup])
            transpose_store(kdup[:].rearrange("p k c d -> p (k c d)"), 2, kT_s, 0, ti, [kdup])
            cp("pool", pbf_t[:, 640:768], proj_t[:, 640:768], [proj_t], [pbf_t])
            dma(v_s.ap[t0:t0 + 128, :], pbf_t[:, 640:768], [pbf_t], [v_s.bs[ti]])
            act(w2[:, :512], proj_t[:, 768:1280], AF.Silu, [proj_t], [w2])
            dma(sz_s.ap[t0:t0 + 128, :], w2[:, :512], [w2], [sz_s.bs[ti]])
            tt("dve", w3[:, 0:512], proj_t[:, 2304:2816], proj_t[:, 1280:1792], ALU.mult, [proj_t], [w3])
            act(w3[:, 512:1024], proj_t[:, 2816:3328], AF.Silu, [proj_t], [w3])
            tt("dve", w3[:, 512:1024], w3[:, 512:1024], proj_t[:, 1792:2304], ALU.mult, [w3, proj_t], [w3])
            transpose_store_f32(w3[:, 0:512], 4, u1T_s, 0, ti, [w3])
            transpose_store_f32(w3[:, 512:1024], 4, u2T_s, 0, ti, [w3])
        chk('P', l)
        S.barrier()
        def loadsA(ti):
            if ti >= NT:
                return
            t0 = ti * 128
            tp = max(ti - 1, 0)
            tn = min(ti + 1, NT - 1)
            qT_t, kT_w, v_w, sz_t, u1h, u2t = (x[ti % 2] for x in (qT_t2, kT_w2, v_w2, sz_t2, u1h2, u2t2))
            dma(qT_t[:], qT_s.ap[:, t0:t0 + 128].rearrange("(a p) t -> p a t", p=128), [qT_s.bs[ti]], [qT_t])
            for wi, tw in enumerate((tp, ti, tn)):
                dma(kT_w[:, :, wi * 128:(wi + 1) * 128], kT_s.ap[:, tw * 128:(tw + 1) * 128].rearrange("(k p) t -> p k t", p=128),
                    [kT_s.bs[tw]], [kT_w])
                dma(v_w[:, wi, :], v_s.ap[tw * 128:(tw + 1) * 128, :], [v_s.bs[tw]], [v_w])
            dma(sz_t[:], sz_s.ap[t0:t0 + 128, :], [sz_s.bs[ti]], [sz_t])
            dma(u1h[:, :, 1:129], u1T_s.ap[:, t0:t0 + 128].rearrange("(a p) t -> p a t", p=128), [u1T_s.bs[ti]], [u1h])
            lo = t0 - 1 if ti > 0 else 0
            hi = t0 + 128 if ti < NT - 1 else T - 1
            dma(u1h[:, :, 0:1], u1T_s.ap[:, lo:lo + 1].rearrange("(a p) t -> p a t", p=128), [u1T_s.bs[tp]], [u1h], slow=True)
            dma(u1h[:, :, 129:130], u1T_s.ap[:, hi:hi + 1].rearrange("(a p) t -> p a t", p=128), [u1T_s.bs[tn]], [u1h], slow=True)
            dma(u2t[:], u2T_s.ap[:, t0:t0 + 128].rearrange("(a p) t -> p a t", p=128), [u2T_s.bs[ti]], [u2t])

        loadsA(0)
        for ti in range(NT):
            t0 = ti * 128
            loadsA(ti + 1)
            qT_t, kT_w, v_w, sz_t, u1h, u2t = (x[ti % 2] for x in (qT_t2, kT_w2, v_w2, sz_t2, u1h2, u2t2))
            flp = mt[:, 3 + ti:4 + ti]
            fln = mt[:, 3 + NT + ti:4 + NT + ti]
            ts("dve", bandT[:, 0:128], band[:, 0:128], flp, ALU.add, [band, mt], [bandT])
            cp("dve", bandT[:, 128:256], band[:, 128:256], [band], [bandT])
            ts("dve", bandT[:, 256:384], band[:, 256:384], fln, ALU.add, [band, mt], [bandT])

            def head_gen(hq, P):
                kv, pr, hh = hq // 4, hq // 2, hq % 2
                rows = slice(hh * 64, (hh + 1) * 64)
                pS1, pS2 = ((pC, pD), (pA, pB))[P]
                scsX, pexpX, pTtX, smX = scs2[P], pexp2[P], pTt2[P], sm2[P]
                pTX = (pT, pTf16)[P]
                pO = (pE, pF)[P]
                oc = (hq // 2) * 64
                mm(pS1[:, 0:384], qT_t[rows, pr, :], kT_w[rows, kv, :], True, True, [qT_t, kT_w], [pS1])
                mm(pS2[:, 0:256], qT_t[rows, pr, :], ckT[rows, kv, :], True, True, [qT_t, ckT], [pS2])
                yield
                tt("dve", scsX[:, 0:384], pS1[:, 0:384], bandT[:, :], ALU.add, [pS1, bandT], [scsX])
                yield
                ts("dve", scsX[:, 384:640], pS2[:, 0:256], mt[:, 2:3], ALU.add, [pS2, mt], [scsX])
                yield
                S.op("dve", lambda e_: e_.reduce_max(out=smX[:, hq, 0:1], in_=scsX[:, :], axis=AX.X), bl([scsX]), bl([smX]))
                yield
                tt("dve", smX[:, hq, 0:1], smX[:, hq, 0:1], sinkb[:, hq:hq + 1], ALU.max, [smX, sinkb], [smX])
                ts("dve", smX[:, hq, 1:2], smX[:, hq, 0:1], -1.0, ALU.mult, [smX], [smX])
                mset("dve", smX[:, hq, 2:3], 0.0, [smX])
                yield
                act(pexpX[:], scsX[:], AF.Exp, [scsX, smX], [pexpX, smX], bias=smX[:, hq, 1:2], accum_out=smX[:, hq, 2:3])
                act(smX[:, hq, 3:4], sinkb[:, hq:hq + 1], AF.Exp, [smX, sinkb], [smX], bias=smX[:, hq, 1:2])
                yield
                for k5 in range(5):
                    tr(pTX[:, k5 * 128:(k5 + 1) * 128], pexpX[:, k5 * 128:(k5 + 1) * 128], ident_b, [pexpX, csb], [pTX])
                yield
                cp("act", pTtX[:].rearrange("p a t -> p (a t)"), pTX[:, 0:640], [pTX], [pTtX])
                tt("dve", smX[:, hq, 4:5], smX[:, hq, 2:3], smX[:, hq, 3:4], ALU.add, [smX], [smX])
                S.op("dve", lambda e_: e_.reciprocal(out=smX[:, hq, 5:6], in_=smX[:, hq, 4:5]), bl([smX]), bl([smX]))
                yield
                for k5 in range(5):
                    vv = v_w[:, k5, kv * 64:(kv + 1) * 64] if k5 < 3 else cvt[:, k5 - 3, kv * 64:(kv + 1) * 64]
                    mm(pO[:, oc:oc + 64], pTtX[:, k5, :], vv, k5 == 0, k5 == 4, [pTtX, v_w, cvt], [pO])
                yield

            for h2 in range(0, 8, 2):
                gens = [head_gen(h2, 0), head_gen(h2 + 1, 1)]
                alive = [True, True]
                while any(alive):
                    for P in range(2):
                        if alive[P]:
                            try:
                                next(gens[P])
                            except StopIteration:
                                alive[P] = False
            for hq in range(8):
                pO = (pE, pF)[hq % 2]
                oc = (hq // 2) * 64
                stt(oat[:, hq * 64:(hq + 1) * 64], pO[:, oc:oc + 64], sm2[hq % 2][:, hq, 5:6], sz_t[:, hq * 64:(hq + 1) * 64],
                    ALU.mult, ALU.mult, [pO, sm2[hq % 2], sz_t], [oat])
            cp("act", pbfA[:, 0:512], oat[:], [oat], [pbfA])
            transpose_store(pbfA[:, 0:512], 4, yT_s, 0, ti, [pbfA])
            ts("dve", u1h[:, :, 0:1], u1h[:, :, 0:1], mt[:, 3 + 2 * NT + ti:4 + 2 * NT + ti], ALU.mult, [u1h, mt], [u1h])
            ts("dve", u1h[:, :, 129:130], u1h[:, :, 129:130], mt[:, 3 + 3 * NT + ti:4 + 3 * NT + ti], ALU.mult, [u1h, mt], [u1h])
            for a in range(4):
                ts("dve", cacc[:, a, :], u1h[:, a, 1:129], cvw[:, a * 4 + 1:a * 4 + 2], ALU.mult, [u1h, cvw], [cacc],
                   cvw[:, a * 4 + 3:a * 4 + 4], ALU.add)
                stt(cacc[:, a, :], u1h[:, a, 0:128], cvw[:, a * 4:a * 4 + 1], cacc[:, a, :], ALU.mult, ALU.add, [u1h, cvw, cacc], [cacc])
                stt(cacc[:, a, :], u1h[:, a, 2:130], cvw[:, a * 4 + 2:a * 4 + 3], cacc[:, a, :], ALU.mult, ALU.add, [u1h, cvw, cacc], [cacc])
            tt("dve", trs[:, 4:8, :], cacc[:], u2t[:], ALU.mult, [cacc, u2t], [trs])
            dma(yT_s.ap[512:1024, t0:t0 + 128].rearrange("(a p) t -> p a t", p=128), trs[:, 4:8, :], [trs], [yT_s.bs[ti]])
        chk('A', l)
        out_proj_residual(l, xsrc, xdst, yT_s, w_out_o.ap[e])
        S.barrier()
        chk('O', l)

    chain = [x_in, xs[0], xs[1], xs[0], y_out]
    try:
        for l in range(4):
            if l % 2 == 0:
                even_layer(l, chain[l], chain[l + 1])
            else:
                odd_layer(l, chain[l], chain[l + 1])
    except _Stop:
        S.barrier()
    S.finish_waits([y_out.b, ngla.b, ns5.b, nkv.b] + y_out.bs)
    S.emit()
    st.close()
    return nc, S


def _consts():
    c = np.zeros((128, NCST), np.float32)
    i = np.arange(128)
    s = i[:, None]
    t = i[None, :]
    same = (s // 64) == (t // 64)
    c[:, C_ID:C_ID + 128] = np.eye(128, dtype=np.float32)
    c[:, C_TIF:C_TIF + 128] = np.where(same & (s <= t), -1.0 / 16, 0.0)
    c[:, C_TRF:C_TRF + 128] = np.where(same & (s > t), -1.0 / 16, 0.0)
    c[:, C_TIB:C_TIB + 128] = np.where(same & (s >= t), -1.0 / 16, 0.0)
    c[:, C_TRB:C_TRB + 128] = np.where(same & (s < t), -1.0 / 16, 0.0)
    c[:, C_MF:C_MF + 128] = np.where(same & (s <= t), 1.0, 0.0)
    c[:, C_MB:C_MB + 128] = np.where(same & (s >= t), 1.0, 0.0)
    c[:, C_ONE:C_ONE + 128] = 1.0
    c[:, C_EPS] = EPS
    c[:, C_EPS + 1] = 1.0
    qi = i[:, None]
    kj = i[None, :]
    NEG = -1e30
    c[:, C_BAND:C_BAND + 128] = np.where(kj >= qi, 0.0, NEG)
    c[:, C_BAND + 128:C_BAND + 256] = 0.0
    c[:, C_BAND + 256:C_BAND + 384] = np.where(kj <= qi, 0.0, NEG)
    c[0:64, C_BLK:C_BLK + 128] = 1.0
    c[64:128, C_BLK + 128:C_BLK + 256] = 1.0
    return c


def _rope_table(T, identity):
    tab = np.zeros((T, 128), np.float32)
    if identity:
        tab[:, :64] = 1.0
        return tab
    pos = np.arange(T)
    row = (pos // 64).astype(np.float32)
    col = (pos % 64).astype(np.float32)
    freq = (10000.0 ** (-np.arange(16, dtype=np.float32) / 16)).astype(np.float32)
    ar = row[:, None] * freq
    ac = col[:, None] * freq
    cos = np.concatenate([np.cos(ar), np.cos(ar), np.cos(ac), np.cos(ac)], axis=1)
    sin = np.concatenate([-np.sin(ar), np.sin(ar), -np.sin(ac), np.sin(ac)], axis=1)
    tab[:, :64] = cos
    tab[:, 64:] = sin
    return tab


def _meta(T, sample):
    NT = T // 128
    m = np.zeros((128, 3 + 4 * NT), np.float32)
    NEG = -1e30
    if sample:
        m[:, 0] = 1.0
        m[:, 1] = 1.0
        m[:, 2] = 0.0
        flp = np.zeros(NT); flp[0] = NEG
        fln = np.zeros(NT); fln[-1] = NEG
        cfl = np.ones(NT); cfl[0] = 0
        cfr = np.ones(NT); cfr[-1] = 0
    else:
        m[:, 0] = 0.0
        m[:, 1] = 0.0
        m[:, 2] = NEG
        flp = np.where(np.arange(NT) % 2 == 0, NEG, 0.0)
        fln = np.where(np.arange(NT) % 2 == 1, NEG, 0.0)
        cfl = np.where(np.arange(NT) % 2 == 0, 0.0, 1.0)
        cfr = np.where(np.arange(NT) % 2 == 1, 0.0, 1.0)
    m[:, 3:3 + NT] = flp
    m[:, 3 + NT:3 + 2 * NT] = fln
    m[:, 3 + 2 * NT:3 + 3 * NT] = cfl
    m[:, 3 + 3 * NT:3 + 4 * NT] = cfr
    return m


def _state_layout(a):
    sh = a.shape[:-2]
    b = a.reshape(sh + (16, 2, 64))
    b = np.moveaxis(b, -3, -1)
    return np.ascontiguousarray(b.reshape(sh + (128, 16)))


_NC_CACHE = {}
LAST_RESULTS = None


def run(inputs, T, n_prompt_per_core):
    f = lambda k: np.asarray(inputs[k], dtype=np.float32)
    x_prompt, x_sample, c = f("x_prompt"), f("x_sample"), f("c")
    NS = T // 256
    if T not in _NC_CACHE:
        _NC_CACHE[T] = build(T)[0]
    nc = _NC_CACHE[T]
    shared = {}
    shared["cst"] = _consts()
    shared["norm_w"] = f("norm_w")
    shared["w_ada"] = f("w_ada")
    shared["b_ada"] = f("b_ada")
    shared["w_in_e"] = f("w_in_e")
    shared["w_out_e"] = f("w_out_e")
    w2 = f("gla_w2"); b2 = f("gla_b2")
    w2cat = np.zeros((2, 64, 512), np.float32)
    w2cat[:, 0:16, 0:256] = w2[:, 0]
    w2cat[:, 16:32, 256:512] = w2[:, 1]
    w2cat[:, 32, 0:256] = b2[:, 0]
    w2cat[:, 32, 256:512] = b2[:, 1]
    shared["w2cat"] = w2cat
    shared["onorm"] = f("gla_onorm").reshape(2, 128, 1)
    lam_re, lam_im, log_dt = f("s5_lam_re"), f("s5_lam_im"), f("s5_log_dt")
    ldt = np.broadcast_to(log_dt[..., None], lam_re.shape)
    shared["s5p"] = np.stack([_state_layout(lam_re), _state_layout(lam_im), _state_layout(ldt)], axis=2)
    def expand_b(b):
        out = np.zeros((2, 128, 16, 128), np.float32)
        for g in range(32):
            j, gs = g // 2, g % 2
            k0 = 16 * (g % 8)
            out[:, gs * 64:(gs + 1) * 64, j, k0:k0 + 16] = b[:, g]
        return out.reshape(2, 128, 16 * 128)
    shared["s5b"] = np.stack([expand_b(f("s5_b_re")), expand_b(f("s5_b_im"))], axis=1)
    def expand_c(cc, sign):
        out = np.zeros((2, 128, 16, 32), np.float32)
        for g in range(32):
            j, gs = g // 2, g % 2
            out[:, gs * 64:(gs + 1) * 64, j, gs * 16:(gs + 1) * 16] = np.swapaxes(cc[:, g], 1, 2)
        if sign < 0:
            out = np.negative(out)
        return out.reshape(2, 128, 16 * 32)
    shared["s5c"] = np.stack([expand_c(f("s5_c_re"), 1), expand_c(f("s5_c_im"), -1)], axis=1)
    shared["s5d"] = np.ascontiguousarray(f("s5_d").reshape(2, 4, 128).transpose(0, 2, 1))
    shared["wglu"] = f("s5_w_glu")
    shared["bglu"] = np.ascontiguousarray(f("s5_b_glu").reshape(2, 4, 128).transpose(0, 2, 1))
    shared["w_in_o"] = f("w_in_o")
    shared["w_out_o"] = f("w_out_o")
    shared["qkw"] = np.concatenate([f("q_norm_w"), f("k_norm_w")], axis=1)
    shared["sink"] = f("sink")
    cw = f("conv_w"); cb = f("conv_b")
    cvw = np.zeros((2, 128, 4, 4), np.float32)
    for a in range(4):
        cvw[:, :, a, 0:3] = cw[:, :, a * 128:(a + 1) * 128].transpose(0, 2, 1)
        cvw[:, :, a, 3] = cb[:, a * 128:(a + 1) * 128]
    shared["convw"] = cvw.reshape(2, 128, 16)

    in_maps = []
    n_sample = x_sample.shape[0]
    for core in range(8):
        m = dict(shared)
        if core < 4:
            b = core
            m["x"] = np.ascontiguousarray(x_sample[b])
            cv = c[b]
            m["meta"] = _meta(T, True)
            m["rope"] = _rope_table(T, False)
            m["gla0"] = np.ascontiguousarray(f("state_gla")[b])
            sre = _state_layout(f("state_s5_re")[b]); sim = _state_layout(f("state_s5_im")[b])
            m["s5h0"] = np.stack([sre, sim], axis=2)
            ck = f("cache_k")[b]; cvv = f("cache_v")[b]
            m["ckv"] = np.ascontiguousarray(np.stack([ck.transpose(0, 2, 1, 3), cvv.transpose(0, 2, 1, 3)], axis=1))
        else:
            pc = core - 4
            xx = np.zeros((T, D), np.float32)
            seqs = x_prompt[pc * n_prompt_per_core:(pc + 1) * n_prompt_per_core]
            xx[:n_prompt_per_core * 256] = seqs.reshape(-1, D)
            m["x"] = xx
            cv = f("c_ctx")
            m["meta"] = _meta(T, False)
            m["rope"] = _rope_table(T, True)
            m["gla0"] = np.zeros((2, 2, 4, 64, 128), np.float32)
            m["s5h0"] = np.zeros((2, 2, 2, 128, 16), np.float32)
            m["ckv"] = np.zeros((2, 2, 256, 2, 64), np.float32)
        m["cvec"] = np.ascontiguousarray(cv.reshape(8, 128).T)
        in_maps.append(m)
    res = run_bass_kernel_spmd(nc, in_maps, core_ids=list(range(8)))
    R = res.results
    global LAST_RESULTS
    LAST_RESULTS = R
    BATCH = x_prompt.shape[0]
    y_sample = np.stack([np.asarray(R[b]["y"]) for b in range(4)], axis=0).astype(np.float32)
    y_prompt = np.zeros_like(x_prompt)
    new_gla = np.zeros((BATCH, 2, 2, 4, 64, 128), np.float32)
    new_re = np.zeros((BATCH, 2, 2, 32, 64), np.float32)
    new_im = np.zeros((BATCH, 2, 2, 32, 64), np.float32)
    new_k = np.zeros((BATCH, 2, 2, 256, 64), np.float32)
    new_v = np.zeros((BATCH, 2, 2, 256, 64), np.float32)
    for pc in range(4):
        r = R[4 + pc]
        y = np.asarray(r["y"]); g = np.asarray(r["ngla"]); s5 = np.asarray(r["ns5"]); kvo = np.asarray(r["nkv"])
        s5 = s5.reshape(2, 2, 2, NS, 16, 2, 64)
        for q in range(n_prompt_per_core):
            bi = pc * n_prompt_per_core + q
            y_prompt[bi] = y[q * 256:(q + 1) * 256]
            new_gla[bi] = g[:, :, q]
            for d in range(2):
                sig = q if d == 0 else NS - 1 - q
                new_re[bi, :, d] = s5[:, d, 0, sig].reshape(2, 32, 64)
                new_im[bi, :, d] = s5[:, d, 1, sig].reshape(2, 32, 64)
            kk = kvo[:, :, q * 256:(q + 1) * 256, :].reshape(2, 2, 256, 2, 64)
            new_k[bi] = kk[:, 0].transpose(0, 2, 1, 3)
            new_v[bi] = kk[:, 1].transpose(0, 2, 1, 3)
    return (y_prompt, y_sample, new_gla, new_re, new_im, new_k, new_v)


def kernel(**inputs):
    return run(inputs, 4096, 8)
```
